# Optimizing a Trainium2 kernel written in Bass

```python
import math
import numpy as np
import jax
import jax.numpy as jnp
from jax import lax

D_MODEL = 1024
BATCH = 8
SEQ = 4096
DEPTH = 1

GRID_W = 64
CTX_LEN = 256

RWKV_WIDTH = 512
RWKV_HEAD = 64
RWKV_HEADS = RWKV_WIDTH // RWKV_HEAD
DECAY_LORA = 32
AAA_LORA = 32
GATE_LORA = 96
DIFF_WIDTH = D_MODEL - RWKV_WIDTH
DIFF_HEAD = 64
DIFF_HEADS = DIFF_WIDTH // (2 * DIFF_HEAD)
D_FF = 2816
CONV_W = 3
Q_BLOCK = 128
ROPE_THETA = 10000.0
NORM_EPS = 1e-6
LN_X_EPS = 64e-5
SUBLN_EPS = 1e-5
ATTN_SCALE = DIFF_HEAD ** -0.5

RWKV_SPLITS = (RWKV_WIDTH, 2 * RWKV_WIDTH, 3 * RWKV_WIDTH,
               3 * RWKV_WIDTH + DECAY_LORA, 3 * RWKV_WIDTH + DECAY_LORA + AAA_LORA)
RWKV_COLS = 3 * RWKV_WIDTH + DECAY_LORA + AAA_LORA + GATE_LORA
DIFF_QK = DIFF_HEADS * 2 * DIFF_HEAD
DIFF_V = DIFF_HEADS * 2 * DIFF_HEAD
IN_COLS = RWKV_COLS + 2 * DIFF_QK + DIFF_V

kernel_name = 'hybrid_rwkv7_diffattn_dit_block'


def rmsnorm(x, g, eps=NORM_EPS):
    xf = x.astype(jnp.float32)
    y = xf * lax.rsqrt(jnp.mean(xf * xf, axis=-1, keepdims=True) + eps)
    return (y * g.astype(jnp.float32)).astype(x.dtype)


def modulate(h, shift, scale):
    return h * (1.0 + scale) + shift


def adaln(cond, w_mod, b_mod):
    return jax.nn.silu(cond) @ w_mod + b_mod


def dwconv(x, w):
    return lax.conv_general_dilated(
        x, w[:, None, :].astype(x.dtype), window_strides=(1,),
        padding=((CONV_W // 2, CONV_W // 2),),
        dimension_numbers=('NWC', 'WIO', 'NWC'),
        feature_group_count=x.shape[-1])


def axial_rope(rows):
    n_pair = DIFF_HEAD // 4
    row = jnp.repeat(jnp.arange(rows, dtype=jnp.float32), GRID_W)
    col = jnp.tile(jnp.arange(GRID_W, dtype=jnp.float32), rows)
    inv = ROPE_THETA ** (-jnp.arange(n_pair, dtype=jnp.float32) / n_pair)
    ang = jnp.concatenate([row[:, None] * inv, col[:, None] * inv], axis=-1)
    return jnp.cos(ang), jnp.sin(ang)


def apply_rope(x, cos, sin):
    half = DIFF_HEAD // 2
    x1, x2 = x[..., :half], x[..., half:]
    return jnp.concatenate([x1 * cos - x2 * sin, x1 * sin + x2 * cos], axis=-1).astype(x.dtype)


def rwkv_prepare(rw, p):
    B, T, _ = rw.shape
    heads = lambda t: t.reshape(B, T, RWKV_HEADS, RWKV_HEAD)
    r, k, v, wl, al, gl = jnp.split(rw, RWKV_SPLITS, axis=-1)
    kk = heads(k * p['k_k']).astype(jnp.float32)
    kk = kk / jnp.maximum(jnp.linalg.norm(kk, axis=-1, keepdims=True), 1e-12)
    w_lat = jnp.tanh(wl)
    dirs = []
    for w0, w2, a0, a2 in ((p['w0_fwd'], p['w2_fwd'], p['a0_fwd'], p['a2_fwd']),
                           (p['w0_bwd'], p['w2_bwd'], p['a0_bwd'], p['a2_bwd'])):
        w = -jax.nn.softplus(-(w0 + w_lat @ w2).astype(jnp.float32)) - 0.5
        decay = jnp.exp(-jnp.exp(w))
        a = jax.nn.sigmoid((a0 + al @ a2).astype(jnp.float32))
        k_dir = k * (1.0 + (a - 1.0) * p['k_a'])
        dirs.append((heads(decay), heads(k_dir), heads(a)))
    return heads(r), heads(v), kk, gl, dirs


def rwkv_scan(r, decay, k, v, kk, a, s0, reverse):
    tm = lambda t: jnp.swapaxes(t, 0, 1)

    def step(S, inp):
        r_t, w_t, k_t, v_t, kk_t, b_t = inp
        sa = jnp.einsum('bhvk,bhk->bhv', S, kk_t)
        S = (S * w_t[:, :, None, :] - sa[..., None] * b_t[:, :, None, :]
             + v_t[..., None] * k_t[:, :, None, :])
        return S, jnp.einsum('bhvk,bhk->bhv', S, r_t)

    s_fin, ys = lax.scan(step, s0, (tm(r), tm(decay), tm(k), tm(v), tm(kk), tm(kk * a)),
                         reverse=reverse)
    return tm(ys), s_fin


def rwkv_output(y, r, v, k_bonus, gl, p):
    B, T = y.shape[:2]
    mu = jnp.mean(y, axis=-1, keepdims=True)
    var = jnp.mean(jnp.square(y - mu), axis=-1, keepdims=True)
    yn = ((y - mu) * lax.rsqrt(var + LN_X_EPS)).reshape(B, T, RWKV_WIDTH) * p['ln_x_w'] + p['ln_x_b']
    bonus = (jnp.sum(r * k_bonus * p['r_k'], axis=-1, keepdims=True) * v).reshape(B, T, RWKV_WIDTH)
    g = jax.nn.sigmoid(gl) @ p['g2']
    return ((yn + bonus) * g).astype(r.dtype)


def rwkv_mixer(rw, rw_c, p, need_ctx):
    r, v, kk, gl, dirs = rwkv_prepare(rw, p)
    r_c, v_c, kk_c, gl_c, dirs_c = rwkv_prepare(rw_c, p)
    s0 = jnp.zeros((rw.shape[0], RWKV_HEADS, RWKV_HEAD, RWKV_HEAD), jnp.float32)
    ys, ys_c = [], []
    for d, reverse in enumerate((False, True)):
        dec_c, k_c, a_c = dirs_c[d]
        yd_c, s_ctx = rwkv_scan(r_c, dec_c, k_c, v_c, kk_c, a_c, s0, reverse)
        dec, k_d, a_d = dirs[d]
        yd, _ = rwkv_scan(r, dec, k_d, v, kk, a_d, s_ctx, reverse)
        ys.append(yd)
        ys_c.append(yd_c)
    out = rwkv_output(ys[0] + ys[1], r, v, 0.5 * (dirs[0][1] + dirs[1][1]), gl, p)
    out_c = (rwkv_output(ys_c[0] + ys_c[1], r_c, v_c, 0.5 * (dirs_c[0][1] + dirs_c[1][1]), gl_c, p)
             if need_ctx else None)
    return out, out_c


def diff_split(dq):
    B, T, _ = dq.shape
    q, k, v = jnp.split(dq, (DIFF_QK, 2 * DIFF_QK), axis=-1)
    qk = lambda t: t.reshape(B, T, DIFF_HEADS, 2, DIFF_HEAD).transpose(0, 2, 3, 1, 4)
    return qk(q), qk(k), v.reshape(B, T, DIFF_HEADS, 2 * DIFF_HEAD).transpose(0, 2, 1, 3)


def diff_attention(q, k_all, v_all, lam):
    s = jnp.einsum('bhmqd,bhmkd->bhmqk', q, k_all).astype(jnp.float32) * ATTN_SCALE
    pr = jax.nn.softmax(s, axis=-1)
    a = pr[:, :, 0] - lam * pr[:, :, 1]
    return jnp.einsum('bhqk,bhke->bhqe', a.astype(v_all.dtype), v_all)


def diff_post(o, p, lam_init):
    B, H, T, e = o.shape
    o = rmsnorm(o, p['subln_w'], SUBLN_EPS) * (1.0 - lam_init)
    return o.transpose(0, 2, 1, 3).reshape(B, T, H * e)


def diff_mixer(dq, dq_c, p, lam_init, cos, sin, need_ctx):
    q, k, v = diff_split(dq)
    q_c, k_c, v_c = diff_split(dq_c)
    q, k = apply_rope(q, cos, sin), apply_rope(k, cos, sin)
    f32 = lambda t: t.astype(jnp.float32)
    lam = (jnp.exp(jnp.sum(f32(p['lam_q1']) * f32(p['lam_k1'])))
           - jnp.exp(jnp.sum(f32(p['lam_q2']) * f32(p['lam_k2']))) + lam_init)
    k_all = jnp.concatenate([k_c, k], axis=3)
    v_all = jnp.concatenate([v_c, v], axis=2)
    B, H, _, T, d = q.shape
    nb = T // Q_BLOCK
    qb = q.reshape(B, H, 2, nb, Q_BLOCK, d).transpose(3, 0, 1, 2, 4, 5)
    o = lax.map(lambda qq: diff_attention(qq, k_all, v_all, lam), qb)
    o = o.transpose(1, 2, 0, 3, 4).reshape(B, H, T, 2 * d)
    out = diff_post(o, p, lam_init)
    out_c = diff_post(diff_attention(q_c, k_c, v_c, lam), p, lam_init) if need_ctx else None
    return out, out_c


def token_mixer(h, h_c, p, lam_init, cos, sin, need_ctx):
    parts = h @ p['w_in']
    parts_c = h_c @ p['w_in']
    rw = dwconv(parts[..., :RWKV_COLS], p['rwkv_conv'])
    rw_c = dwconv(parts_c[..., :RWKV_COLS], p['rwkv_conv'])
    y_r, y_r_c = rwkv_mixer(rw, rw_c, p, need_ctx)
    y_d, y_d_c = diff_mixer(parts[..., RWKV_COLS:], parts_c[..., RWKV_COLS:], p, lam_init,
                            cos, sin, need_ctx)
    out = jnp.concatenate([y_r, y_d], axis=-1) @ p['w_out']
    out_c = jnp.concatenate([y_r_c, y_d_c], axis=-1) @ p['w_out'] if need_ctx else None
    return out, out_c


def conv_ffn(h, p):
    u = dwconv(h @ p['w_up'], p['ffn_conv']) + p['ffn_conv_b']
    val, gate = jnp.split(u, 2, axis=-1)
    return (val * jax.nn.silu(gate)) @ p['w_down']


def setup_inputs(seed: int = 0) -> dict:
    key = jax.random.key(seed)
    keys = jax.random.split(key, 36)
    L, D, Rw = DEPTH, D_MODEL, RWKV_WIDTH

    def nrm(i, shape, s):
        return s * jax.random.normal(keys[i], shape, jnp.float32)

    def uni(i, shape, lo, hi):
        return jax.random.uniform(keys[i], shape, jnp.float32, lo, hi)

    conv_base = jnp.array([0.25, 0.5, 0.25], jnp.float32)[None, :, None]
    return {
        'x': nrm(0, (BATCH, SEQ, D), 1.0),
        'c': nrm(1, (BATCH, D), 1.0),
        'ctx': nrm(2, (BATCH, CTX_LEN, D), 1.0),
        'c_ctx': nrm(3, (D,), 1.0),
        'w_mod': nrm(4, (L, D, 6 * D), 0.5 * D ** -0.5),
        'b_mod': nrm(5, (L, 6 * D), 0.02),
        'g_pre_mix': 1.0 + nrm(6, (L, D), 0.05),
        'g_post_mix': 1.0 + nrm(7, (L, D), 0.05),
        'g_pre_ffn': 1.0 + nrm(8, (L, D), 0.05),
        'g_post_ffn': 1.0 + nrm(9, (L, D), 0.05),
        'w_in': nrm(10, (L, D, IN_COLS), D ** -0.5),
        'rwkv_conv': conv_base + nrm(11, (L, CONV_W, RWKV_COLS), 0.1),
        'w0_fwd': uni(12, (L, Rw), -6.5, -1.5),
        'w2_fwd': nrm(13, (L, DECAY_LORA, Rw), 0.1),
        'a0_fwd': nrm(14, (L, Rw), 0.1),
        'a2_fwd': nrm(15, (L, AAA_LORA, Rw), 0.1),
        'w0_bwd': uni(16, (L, Rw), -6.5, -1.5),
        'w2_bwd': nrm(17, (L, DECAY_LORA, Rw), 0.1),
        'a0_bwd': nrm(18, (L, Rw), 0.1),
        'a2_bwd': nrm(19, (L, AAA_LORA, Rw), 0.1),
        'g2': nrm(20, (L, GATE_LORA, Rw), GATE_LORA ** -0.5),
        'k_k': 0.85 + nrm(21, (L, Rw), 0.05),
        'k_a': 1.0 + nrm(22, (L, Rw), 0.05),
        'r_k': nrm(23, (L, RWKV_HEADS, RWKV_HEAD), 0.1),
        'ln_x_w': 1.0 + nrm(24, (L, Rw), 0.05),
        'ln_x_b': nrm(25, (L, Rw), 0.02),
        'lam_q1': nrm(26, (L, DIFF_HEAD), 0.1),
        'lam_k1': nrm(27, (L, DIFF_HEAD), 0.1),
        'lam_q2': nrm(28, (L, DIFF_HEAD), 0.1),
        'lam_k2': nrm(29, (L, DIFF_HEAD), 0.1),
        'subln_w': 1.0 + nrm(30, (L, 2 * DIFF_HEAD), 0.05),
        'w_out': nrm(31, (L, D, D), D ** -0.5),
        'w_up': nrm(32, (L, D, 2 * D_FF), D ** -0.5),
        'ffn_conv': conv_base + nrm(33, (L, CONV_W, 2 * D_FF), 0.1),
        'ffn_conv_b': nrm(34, (L, 2 * D_FF), 0.02),
        'w_down': nrm(35, (L, D_FF, D), D_FF ** -0.5),
    }


def reference(x, c, ctx, c_ctx, w_mod, b_mod, g_pre_mix, g_post_mix, g_pre_ffn, g_post_ffn,
              w_in, rwkv_conv, w0_fwd, w2_fwd, a0_fwd, a2_fwd, w0_bwd, w2_bwd, a0_bwd, a2_bwd,
              g2, k_k, k_a, r_k, ln_x_w, ln_x_b, lam_q1, lam_k1, lam_q2, lam_k2, subln_w,
              w_out, w_up, ffn_conv, ffn_conv_b, w_down):
    n_tok = x.shape[1]
    ROWS = n_tok // GRID_W
    cos, sin = axial_rope(ROWS)
    for l in range(DEPTH):
        p = dict(g_pre_mix=g_pre_mix[l], g_post_mix=g_post_mix[l], g_pre_ffn=g_pre_ffn[l],
                 g_post_ffn=g_post_ffn[l], w_in=w_in[l], rwkv_conv=rwkv_conv[l],
                 w0_fwd=w0_fwd[l], w2_fwd=w2_fwd[l], a0_fwd=a0_fwd[l], a2_fwd=a2_fwd[l],
                 w0_bwd=w0_bwd[l], w2_bwd=w2_bwd[l], a0_bwd=a0_bwd[l], a2_bwd=a2_bwd[l],
                 g2=g2[l], k_k=k_k[l], k_a=k_a[l], r_k=r_k[l], ln_x_w=ln_x_w[l], ln_x_b=ln_x_b[l],
                 lam_q1=lam_q1[l], lam_k1=lam_k1[l], lam_q2=lam_q2[l], lam_k2=lam_k2[l],
                 subln_w=subln_w[l], w_out=w_out[l], w_up=w_up[l], ffn_conv=ffn_conv[l],
                 ffn_conv_b=ffn_conv_b[l], w_down=w_down[l])
        lam_init = 0.8 - 0.6 * math.exp(-0.3 * l)
        need_ctx = l + 1 < DEPTH
        mod = adaln(c, w_mod[l], b_mod[l])[:, None, :]
        mod_c = adaln(c_ctx[None, :], w_mod[l], b_mod[l])[:, None, :]
        sh1, sc1, gt1, sh2, sc2, gt2 = jnp.split(mod, 6, axis=-1)
        sh1c, sc1c, gt1c, sh2c, sc2c, gt2c = jnp.split(mod_c, 6, axis=-1)
        h = modulate(rmsnorm(x, p['g_pre_mix']), sh1, sc1)
        h_c = modulate(rmsnorm(ctx, p['g_pre_mix']), sh1c, sc1c)
        y, y_c = token_mixer(h, h_c, p, lam_init, cos, sin, need_ctx)
        x = x + gt1 * rmsnorm(y, p['g_post_mix'])
        h = modulate(rmsnorm(x, p['g_pre_ffn']), sh2, sc2)
        x = x + gt2 * rmsnorm(conv_ffn(h, p), p['g_post_ffn'])
        if need_ctx:
            ctx = ctx + gt1c * rmsnorm(y_c, p['g_post_mix'])
            h_c = modulate(rmsnorm(ctx, p['g_pre_ffn']), sh2c, sc2c)
            ctx = ctx + gt2c * rmsnorm(conv_ffn(h_c, p), p['g_post_ffn'])
    return x
```

```python
import contextlib
import numpy as np
import concourse.bass as bass
import concourse.mybir as mybir
from concourse.bass_utils import run_bass_kernel_spmd

F32 = mybir.dt.float32
BF16 = mybir.dt.bfloat16
AF = mybir.ActivationFunctionType
ALU = mybir.AluOpType
AX = mybir.AxisListType

SEM_ROLL = 30000


class Buf:
    def __init__(self, name, t=None):
        self.name = name
        self.t = t
        self.last_w = None
        self.readers = []
        self.dma_sem = None
        self.dma_cnt = 0

    def ap(self):
        return self.t[:]

    def __getitem__(self, idx):
        return self.t[idx]


class _Rec:
    def __getattr__(self, name):
        return lambda *a, **k: (name, a, k)


_REC = _Rec()


def _bind(fn):
    name, a, k = fn(_REC)
    return lambda e: getattr(e, name)(*a, **k)


class Op:
    __slots__ = ("eng", "fn", "deps", "signal", "token", "is_dma", "sem_buf", "final", "idx")


class Prog:
    ENGS = ("pe", "act", "dve", "pool", "sp")

    def __init__(self, nc):
        self.nc = nc
        self.stack = contextlib.ExitStack()
        self.ops = {e: [] for e in self.ENGS}
        self.nbuf = 0
        self.all_ops = []
        self.final_ops = []

    def sbuf(self, name, shape, dtype):
        self.nbuf += 1
        name = "s%d_%s" % (self.nbuf, name)
        t = self.stack.enter_context(self.nc.sbuf_tensor(name, list(shape), dtype))
        b = Buf(name, t)
        b.readers = list(getattr(self, "fence", []))
        if hasattr(self, "scope_bufs") and self.scope_bufs:
            self.scope_bufs[-1].append(b)
        return b

    def psum(self, name, shape, dtype):
        t = self.stack.enter_context(self.nc.psum_tensor(name, list(shape), dtype))
        return Buf(name, t)

    def view(self, name):
        return Buf(name)

    @contextlib.contextmanager
    def scope(self):
        old = self.stack
        self.stack = contextlib.ExitStack()
        if not hasattr(self, "scope_bufs"):
            self.scope_bufs = []
        self.scope_bufs.append([])
        try:
            yield
        finally:
            self.stack.close()
            self.stack = old
            bufs = self.scope_bufs.pop()
            ops = list(getattr(self, "fence", []))
            for b in bufs:
                if b.last_w is not None:
                    ops.append(b.last_w)
                ops.extend(b.readers)
            best = {}
            dmas = {}
            for o in ops:
                if o.is_dma:
                    dmas[id(o)] = o
                else:
                    if o.eng not in best or best[o.eng].idx < o.idx:
                        best[o.eng] = o
            self.fence = list(best.values()) + list(dmas.values())

    def _deps(self, o, reads, writes):
        deps = []
        for b in list(reads) + list(writes):
            if b.last_w is not None:
                deps.append(b.last_w)
        for b in writes:
            deps.extend(b.readers)
        for b in writes:
            b.last_w = o
            b.readers = []
        for b in reads:
            b.readers.append(o)
        seen = set()
        out = []
        for d in deps:
            if id(d) in seen or d is o:
                continue
            seen.add(id(d))
            if d.eng == "pe" and o.eng == "pe" and not d.is_dma and not o.is_dma:
                continue
            d.signal = True
            out.append(d)
        o.deps = out

    def op(self, eng, fn, reads=(), writes=()):
        o = Op()
        o.eng = eng
        o.fn = _bind(fn)
        o.signal = False
        o.token = None
        o.is_dma = False
        o.sem_buf = None
        o.final = False
        o.idx = len(self.ops[eng])
        self._deps(o, reads, writes)
        self.ops[eng].append(o)
        return o

    def dma(self, eng, out, in_, reads=(), writes=(), final=False):
        o = Op()
        o.eng = eng
        o.fn = lambda e: e.dma_start(out=out, in_=in_)
        o.signal = True
        o.token = None
        o.is_dma = True
        o.final = final
        cands = [b for b in list(writes) + list(reads) if b.t is not None]
        sb = cands[0] if cands else (list(writes) + list(reads))[0]
        o.sem_buf = sb
        o.idx = len(self.ops[eng])
        self._deps(o, reads, writes)
        self.ops[eng].append(o)
        if final:
            self.final_ops.append(o)
        return o

    def emit(self):
        nc = self.nc
        st = self.stack
        eng_sems = {}
        for e in self.ENGS:
            n = 0
            for o in self.ops[e]:
                if o.is_dma:
                    b = o.sem_buf
                    if b.dma_sem is None:
                        b.dma_sem = st.enter_context(nc.semaphore("d_" + b.name))
                    b.dma_cnt += 16
                    o.token = (b.dma_sem, b.dma_cnt, 16)
                elif o.signal:
                    k = n // SEM_ROLL
                    if (e, k) not in eng_sems:
                        eng_sems[(e, k)] = st.enter_context(nc.semaphore("s_%s_%d" % (e, k)))
                    o.token = (eng_sems[(e, k)], n % SEM_ROLL + 1, 1)
                    n += 1
        print("ops:", {e: len(self.ops[e]) for e in self.ENGS}, "signals:", {e: sum(1 for o in self.ops[e] if o.token is not None) for e in self.ENGS}, "nsems", len(eng_sems))
        all_sems = list(eng_sems.values())
        seen_b = set()
        for e in self.ENGS:
            for o in self.ops[e]:
                if o.is_dma and id(o.sem_buf) not in seen_b:
                    seen_b.add(id(o.sem_buf))
                    all_sems.append(o.sem_buf.dma_sem)
        with nc.Block() as blk0:
            @blk0.sync
            def _(eng):
                for sm in all_sems:
                    eng.sem_clear(sm)
        block = st.enter_context(nc.Block())
        hooks = {"pe": block.tensor, "act": block.scalar, "dve": block.vector,
                 "pool": block.gpsimd, "sp": block.sync}
        final_ops = self.final_ops

        def make(e):
            ops = self.ops[e]

            def body(eng):
                waited = {}
                for o in ops:
                    for d in o.deps:
                        sem, val, _ = d.token
                        if waited.get(id(sem), 0) < val:
                            eng.wait_ge(sem, val)
                            waited[id(sem)] = val
                    ins = o.fn(eng)
                    if o.token is not None:
                        ins.then_inc(o.token[0], o.token[2])
                if e == "sp":
                    for o in final_ops:
                        sem, val, _ = o.token
                        eng.wait_ge(sem, val)
            return body

        for e in self.ENGS:
            hooks[e](make(e))
        st.close()


D_MODEL = 1024
T_LAT = 4096
T_CTX = 256
NB = 256
NLB = T_LAT // NB
D_FF = 2816
EXPM05 = float(np.exp(-0.5))


class Ctx:
    pass


def build_program(nc, stage=99):
    P = Prog(nc)
    C = Ctx()
    C.P = P
    C.nc = nc
    dr = {}

    def din(name, shape, dt=F32):
        dr[name] = nc.dram_tensor(name, list(shape), dt, kind="ExternalInput").ap()

    din("xT", [1024, T_LAT]); din("ctxT", [1024, T_CTX]); din("cc", [128, 8, 2])
    din("w_mod", [1024, 6144]); din("b_modT", [128, 48]); din("gvec", [128, 4, 8])
    din("w_in", [1024, 3232]); din("w_qks", [1024, 1024]); din("convT", [1696, 3])
    din("lw2", [64, 2, 512]); din("w0a0", [128, 4, 4]); din("kvec", [128, 5, 4]); din("g2", [96, 512])
    din("lam", [128, 4, 64]); din("subln", [128, 1])
    din("w_out", [1024, 1024]); din("w_up", [1024, 5632]); din("fconvT", [5632, 3]); din("fbias", [128, 44])
    din("w_down", [2816, 1024]); din("ropeC", [128, T_LAT]); din("ropeS", [128, T_LAT]); din("blkmask", [128, 5, 128])
    dr["outT"] = nc.dram_tensor("outT", [1024, T_LAT], F32, kind="ExternalOutput").ap()
    dr["yf"] = nc.dram_tensor("yf_scr", [4, 128, T_LAT], F32).ap()
    dr["mt"] = nc.dram_tensor("mt_scr", [8, 128, T_LAT], BF16).ap()
    dr["x1s"] = nc.dram_tensor("x1_scr", [8, 128, T_LAT], F32).ap()
    dr["h2s"] = nc.dram_tensor("h2_scr", [8, 128, T_LAT], BF16).ap()
    if stage < 99:
        dr["dbg"] = nc.dram_tensor("dbg", [128, 8, T_LAT + 2], F32, kind="ExternalOutput").ap()
    C.dr = dr
    C.yf_bufs = {}
    C.mt_bufs = {}

    C.banks = [P.psum("pb%d" % i, [128, 512], F32) for i in range(7)]
    C.bankT = P.psum("pbT", [128, 1024], BF16)
    C.bank_i = 0

    def nb():
        b = C.banks[C.bank_i % 7]
        C.bank_i += 1
        return b
    C.nb = nb

    ident = P.sbuf("ident", [128, 128], BF16)
    ones = P.sbuf("ones", [128, 128], BF16)
    bones = P.sbuf("bones", [128, 128], BF16)
    bmean = P.sbuf("bmean", [128, 128], BF16)
    mean128 = P.sbuf("mean128", [128, 128], BF16)
    eps6 = P.sbuf("eps6", [128, 1], F32)
    epsx = P.sbuf("epsx", [128, 1], F32)
    epss = P.sbuf("epss", [128, 1], F32)
    P.op("pool", lambda e: e.memset(ident.ap(), 0.0), writes=[ident])
    P.op("pool", lambda e: e.affine_select(out=ident.ap(), in_=ident.ap(), pattern=[[-1, 128]],
                                           compare_op=ALU.not_equal, fill=1.0, base=0, channel_multiplier=1),
         reads=[ident], writes=[ident])
    P.op("pool", lambda e: e.memset(ones.ap(), 1.0), writes=[ones])
    P.op("pool", lambda e: e.memset(mean128.ap(), 1.0 / 128), writes=[mean128])
    P.op("pool", lambda e: e.memset(bones.ap(), 0.0), writes=[bones])
    P.op("pool", lambda e: e.memset(bones[0:64, 0:64], 1.0), writes=[bones])
    P.op("pool", lambda e: e.memset(bones[64:128, 64:128], 1.0), writes=[bones])
    P.op("pool", lambda e: e.memset(bmean.ap(), 0.0), writes=[bmean])
    P.op("pool", lambda e: e.memset(bmean[0:64, 0:64], 1.0 / 64), writes=[bmean])
    P.op("pool", lambda e: e.memset(bmean[64:128, 64:128], 1.0 / 64), writes=[bmean])
    P.op("pool", lambda e: e.memset(eps6.ap(), 1e-6), writes=[eps6])
    P.op("pool", lambda e: e.memset(epsx.ap(), 64e-5), writes=[epsx])
    P.op("pool", lambda e: e.memset(epss.ap(), 1e-5), writes=[epss])
    C.ident, C.ones, C.bones, C.bmean, C.mean128 = ident, ones, bones, bmean, mean128
    C.eps6, C.epsx, C.epss = eps6, epsx, epss

    stage_mod(C)
    import os
    with P.scope():
        C.HT = P.sbuf("HT", [128, 8, T_LAT + 2], BF16)
        C.HC = P.sbuf("HC", [128, 8, T_CTX + 2], BF16)
        for Hb, n in ((C.HT, T_LAT), (C.HC, T_CTX)):
            P.op("pool", lambda e, Hb=Hb: e.memset(Hb[:, :, 0:1], 0.0), writes=[Hb])
            P.op("pool", lambda e, Hb=Hb, n=n: e.memset(Hb[:, :, n + 1:n + 2], 0.0), writes=[Hb])
        stage_prenorm(C)
        if stage == 1:
            dbg_dump_HT(C)
            P.emit()
            return nc
        if not os.environ.get("SKIP_RWKV"):
            stage_rwkv(C)
        if stage == 21:
            with P.scope():
                for k in range(4):
                    tmp = P.sbuf("dbgy%d" % k, [128, T_LAT], F32)
                    P.dma("sp", tmp.ap(), dr["yf"][k], reads=list(C.yf_bufs.values()), writes=[tmp])
                    P.dma("sp", dr["dbg"][:, k, 0:T_LAT], tmp.ap(), reads=[tmp], final=True)
            P.emit()
            return nc
        if stage == 2:
            dbg_dump_mt(C, 0, 4)
            P.emit()
            return nc
        stage_attn(C)
        if stage == 3:
            dbg_dump_mt(C, 4, 8)
            P.emit()
            return nc
        stage_out_a(C)
    stage_ffn(C)
    P.emit()
    return nc


def dbg_dump_HT(C):
    P = C.P
    with P.scope():
        for k in range(8):
            tmp = P.sbuf("dbgt%d" % k, [128, T_LAT + 2], F32)
            P.op("dve", lambda e, k=k, tmp=tmp: e.tensor_copy(out=tmp.ap(), in_=C.HT[:, k, :]), reads=[C.HT], writes=[tmp])
            P.dma("sp", C.dr["dbg"][:, k, :], tmp.ap(), reads=[tmp], final=True)


def dbg_dump_mt(C, k0, k1):
    P = C.P
    with P.scope():
        for k in range(k0, k1):
            tb = P.sbuf("dbgb%d" % k, [128, T_LAT], BF16)
            tmp = P.sbuf("dbgt%d" % k, [128, T_LAT], F32)
            rd = [b for (kk, _), b in C.mt_bufs.items() if kk == k] + list(C.yf_bufs.values())
            P.dma("sp", tb.ap(), C.dr["mt"][k], reads=rd, writes=[tb])
            P.op("dve", lambda e, tmp=tmp, tb=tb: e.tensor_copy(out=tmp.ap(), in_=tb.ap()), reads=[tb], writes=[tmp])
            P.dma("sp", C.dr["dbg"][:, k, 0:T_LAT], tmp.ap(), reads=[tmp], final=True)


def stage_mod(C):
    P, dr = C.P, C.dr
    cc = P.sbuf("cc", [128, 8, 2], F32)
    scc = P.sbuf("scc", [128, 8, 2], F32)
    bm = P.sbuf("bmodT", [128, 48], F32)
    gv = P.sbuf("gvec", [128, 4, 8], F32)
    modT = P.sbuf("modT", [128, 48, 2], F32)
    P.dma("sp", cc.ap(), dr["cc"], writes=[cc])
    P.dma("sp", bm.ap(), dr["b_modT"], writes=[bm])
    P.dma("sp", gv.ap(), dr["gvec"], writes=[gv])
    P.op("act", lambda e: e.activation(out=scc.ap(), in_=cc.ap(), func=AF.Silu), reads=[cc], writes=[scc])
    pm = C.nb()
    with P.scope():
        wst = [P.sbuf("wmst%d" % i, [128, 8, 512], F32) for i in range(2)]
        for g in range(12):
            w = wst[g % 2]
            P.dma("sp", w.ap(), dr["w_mod"][:, g * 512:(g + 1) * 512].rearrange("(k p) n -> p k n", p=128), writes=[w])
            for jj in range(4):
                j = g * 4 + jj
                for k in range(8):
                    P.op("pe", lambda e, w=w, k=k, jj=jj, j=j: e.matmul(
                        pm[:, 2 * j:2 * j + 2], lhsT=w[:, k, jj * 128:(jj + 1) * 128], rhs=scc[:, k, :],
                        start=(k == 0), stop=(k == 7)), reads=[w, scc], writes=[pm])
    pmv = pm[:, 0:96].rearrange("p (j c) -> p j c", c=2)
    for c in range(2):
        P.op("dve", lambda e, c=c: e.tensor_tensor(out=modT[:, :, c], in0=pmv[:, :, c], in1=bm.ap(), op=ALU.add),
             reads=[bm], writes=[pm, modT])
    C.modT = modT
    A1 = P.sbuf("A1", [128, 2, 8], F32)
    G1 = P.sbuf("G1", [128, 8], F32)
    A2 = P.sbuf("A2", [128, 8], F32)
    G2 = P.sbuf("G2", [128, 8], F32)
    for c in range(2):
        P.op("dve", lambda e, c=c: e.scalar_tensor_tensor(out=A1[:, c, :], in0=modT[:, 8:16, c], scalar=1.0,
                                                          in1=gv[:, 0, :], op0=ALU.add, op1=ALU.mult),
             reads=[modT, gv], writes=[A1])
    P.op("dve", lambda e: e.tensor_tensor(out=G1.ap(), in0=modT[:, 16:24, 0], in1=gv[:, 1, :], op=ALU.mult),
         reads=[modT, gv], writes=[G1])
    P.op("dve", lambda e: e.scalar_tensor_tensor(out=A2.ap(), in0=modT[:, 32:40, 0], scalar=1.0, in1=gv[:, 2, :],
                                                 op0=ALU.add, op1=ALU.mult), reads=[modT, gv], writes=[A2])
    P.op("dve", lambda e: e.tensor_tensor(out=G2.ap(), in0=modT[:, 40:48, 0], in1=gv[:, 3, :], op=ALU.mult),
         reads=[modT, gv], writes=[G2])
    C.A1, C.G1, C.A2, C.G2 = A1, G1, A2, G2


def norm_block(C, xt, nbk, A_ap, sh_ap, out_ap, xt_reads, out_buf, tmpn):
    P = C.P
    sq, rs, tmp = tmpn
    P.op("act", lambda e: e.activation(out=sq[:, :, 0:nbk], in_=xt[:, :, 0:nbk], func=AF.Square),
         reads=[xt], writes=[sq])
    pb = C.nb()
    for k in range(8):
        P.op("pe", lambda e, k=k: e.matmul(pb[:, 0:nbk], lhsT=C.ones.ap(), rhs=sq[:, k, 0:nbk],
                                            start=(k == 0), stop=(k == 7)), reads=[sq, C.ones], writes=[pb])
    P.op("act", lambda e: e.activation(out=rs[:, 0:nbk], in_=pb[:, 0:nbk], func=AF.Sqrt, scale=1.0 / D_MODEL,
                                       bias=C.eps6.ap()), reads=[C.eps6], writes=[pb, rs])
    P.op("dve", lambda e: e.reciprocal(out=rs[:, 0:nbk], in_=rs[:, 0:nbk]), writes=[rs])
    for k in range(8):
        P.op("dve", lambda e, k=k: e.scalar_tensor_tensor(out=tmp[:, k, 0:nbk], in0=xt[:, k, 0:nbk], scalar=A_ap(k),
                                                          in1=rs[:, 0:nbk], op0=ALU.mult, op1=ALU.mult),
             reads=[xt, rs] + xt_reads, writes=[tmp])
        P.op("act", lambda e, k=k: e.activation(out=out_ap(k), in_=tmp[:, k, 0:nbk], func=AF.Identity,
                                                bias=sh_ap(k)), reads=[tmp] + xt_reads, writes=[out_buf])


def stage_prenorm(C):
    P, dr = C.P, C.dr
    with P.scope():
        xts = [P.sbuf("xt%d" % i, [128, 8, NB], F32) for i in range(2)]
        sq = P.sbuf("pn_sq", [128, 8, NB], BF16)
        rs = P.sbuf("pn_rs", [128, NB], F32)
        tmp = P.sbuf("pn_tmp", [128, 8, NB], F32)
        blocks = [("c", 0)] + [("l", i) for i in range(NLB)]
        for bi, (src, i) in enumerate(blocks):
            xt = xts[bi % 2]
            if src == "c":
                P.dma("sp", xt.ap(), dr["ctxT"].rearrange("(k p) t -> p k t", p=128), writes=[xt])
                H, cidx, t0 = C.HC, 1, 0
            else:
                P.dma("sp", xt.ap(), dr["xT"][:, i * NB:(i + 1) * NB].rearrange("(k p) t -> p k t", p=128), writes=[xt])
                H, cidx, t0 = C.HT, 0, i * NB
            norm_block(C, xt, NB, lambda k, cidx=cidx: C.A1[:, cidx, k:k + 1],
                       lambda k, cidx=cidx: C.modT[:, k, cidx:cidx + 1],
                       lambda k, H=H, t0=t0: H[:, k, 1 + t0:1 + t0 + NB], [C.A1, C.modT], H, (sq, rs, tmp))


def _rope_tables():
    n_pair = 16
    rows = T_LAT // 64
    row = np.repeat(np.arange(rows, dtype=np.float32), 64)
    col = np.tile(np.arange(64, dtype=np.float32), rows)
    inv = (np.float32(10000.0) ** (-np.arange(n_pair, dtype=np.float32) / np.float32(n_pair))).astype(np.float32)
    ang = np.concatenate([row[:, None] * inv, col[:, None] * inv], axis=-1).astype(np.float32)
    cos = np.cos(ang).astype(np.float32).T
    sin = np.sin(ang).astype(np.float32).T
    Cc = np.concatenate([cos, cos, cos, cos], axis=0)
    Ss = np.concatenate([-sin, sin, -sin, sin], axis=0)
    return np.ascontiguousarray(Cc), np.ascontiguousarray(Ss)


def prep_inputs(inp):
    f = lambda a: np.ascontiguousarray(np.asarray(a, dtype=np.float32))
    x, c, ctx, c_ctx = f(inp["x"]), f(inp["c"]), f(inp["ctx"]), f(inp["c_ctx"])
    pk = lambda v, n: f(v.reshape(n, 128).T)
    sh = {}
    sh["w_mod"] = f(inp["w_mod"][0])
    sh["b_modT"] = pk(f(inp["b_mod"][0]), 48)
    sh["gvec"] = f(np.stack([pk(f(inp[n][0]), 8) for n in ("g_pre_mix", "g_post_mix", "g_pre_ffn", "g_post_ffn")], axis=1))
    w_in = f(inp["w_in"][0])
    sh["w_in"] = w_in
    perm = np.arange(512).reshape(8, 2, 32)[:, ::-1, :].reshape(512)
    qc, kc = 1696, 1696 + 512
    sh["w_qks"] = f(np.concatenate([w_in[:, qc:qc + 512][:, perm], w_in[:, kc:kc + 512][:, perm]], axis=1))
    sh["convT"] = f(inp["rwkv_conv"][0].T)
    sh["lw2"] = f(np.stack([np.concatenate([f(inp["w2_fwd"][0]), f(inp["a2_fwd"][0])], 0),
                            np.concatenate([f(inp["w2_bwd"][0]), f(inp["a2_bwd"][0])], 0)], axis=1))
    sh["w0a0"] = f(np.stack([pk(f(inp[n][0]), 4) for n in ("w0_fwd", "w0_bwd", "a0_fwd", "a0_bwd")], axis=1))
    sh["kvec"] = f(np.stack([pk(f(inp[n][0]).reshape(512), 4) for n in ("k_k", "k_a", "r_k", "ln_x_w", "ln_x_b")], axis=1))
    sh["g2"] = f(inp["g2"][0])
    sh["lam"] = f(np.broadcast_to(np.stack([f(inp[n][0]) for n in ("lam_q1", "lam_k1", "lam_q2", "lam_k2")], 0)[None], (128, 4, 64)))
    sh["subln"] = f(inp["subln_w"][0].reshape(128, 1))
    sh["w_out"] = f(inp["w_out"][0])
    sh["w_up"] = f(inp["w_up"][0])
    sh["fconvT"] = f(inp["ffn_conv"][0].T)
    sh["fbias"] = pk(f(inp["ffn_conv_b"][0]), 44)
    sh["w_down"] = f(inp["w_down"][0])
    sh["ropeC"], sh["ropeS"] = _rope_tables()
    ii = np.arange(128)
    dm = lambda sz: (ii[:, None] // sz == ii[None, :] // sz).astype(np.float32)
    sh["blkmask"] = f(np.stack([dm(8), dm(16) - dm(8), dm(32) - dm(16), dm(64) - dm(32), dm(128) - dm(64)], axis=1))
    maps = []
    for b in range(8):
        m = dict(sh)
        m["xT"] = f(x[b].T)
        m["ctxT"] = f(ctx[b].T)
        m["cc"] = f(np.stack([c[b], c_ctx], -1).reshape(8, 128, 2).transpose(1, 0, 2))
        maps.append(m)
    return maps


def kernel(**inputs):
    maps = prep_inputs(inputs)
    nc = bass.Bass("TRN2", target_bir_lowering=False)
    build_program(nc)
    res = run_bass_kernel_spmd(nc, maps, core_ids=list(range(8)))
    out = np.stack([np.ascontiguousarray(res.results[b]["outT"].T) for b in range(8)], 0)
    return out.astype(np.float32)


def load_w_bf16(C, dst, dram_cols_ap, ncols, stg):
    P = C.P
    for i, c0 in enumerate(range(0, ncols, 256)):
        c1 = min(ncols, c0 + 256)
        s = stg[C.stg_i % 2]
        C.stg_i += 1
        P.dma("sp", s[:, :, 0:c1 - c0], dram_cols_ap[:, c0:c1].rearrange("(k p) n -> p k n", p=128), writes=[s])
        P.op("pool", lambda e, s=s, c0=c0, c1=c1: e.tensor_copy(out=dst[0][:, :, dst[1] + c0:dst[1] + c1], in_=s[:, :, 0:c1 - c0]),
             reads=[s], writes=[dst[0]])


def proj_conv(C, H, t0, W, c0, m, cw, out, out_buf, bias=None, eng2="dve", bias_buf=None):
    P = C.P
    pb = C.nb()
    for k in range(8):
        P.op("pe", lambda e, k=k: e.matmul(pb[0:m, 0:NB + 2], lhsT=W[:, k, c0:c0 + m], rhs=H[:, k, t0:t0 + NB + 2],
                                            start=(k == 0), stop=(k == 7)), reads=[W, H], writes=[pb])
    if bias is None:
        P.op("act", lambda e: e.activation(out=out, in_=pb[0:m, 1:NB + 1], func=AF.Copy, scale=cw[:, 1:2]),
             reads=[C.cwb], writes=[pb, out_buf])
    else:
        P.op("act", lambda e: e.activation(out=out, in_=pb[0:m, 1:NB + 1], func=AF.Identity, scale=cw[:, 1:2], bias=bias),
             reads=[C.cwb, bias_buf], writes=[pb, out_buf])
    P.op(eng2, lambda e: e.scalar_tensor_tensor(out=out, in0=pb[0:m, 0:NB], scalar=cw[:, 0:1], in1=out,
                                                op0=ALU.mult, op1=ALU.add), reads=[C.cwb], writes=[pb, out_buf])
    P.op(eng2, lambda e: e.scalar_tensor_tensor(out=out, in0=pb[0:m, 2:NB + 2], scalar=cw[:, 2:3], in1=out,
                                                op0=ALU.mult, op1=ALU.add), reads=[C.cwb], writes=[pb, out_buf])


def stage_rwkv(C):
    P, dr = C.P, C.dr
    C.stg_i = 0
    with P.scope():
        stg = [P.sbuf("wstg%d" % i, [128, 8, 256], F32) for i in range(2)]
        WR = P.sbuf("WR", [128, 8, 1696], BF16)
        load_w_bf16(C, (WR, 0), dr["w_in"][:, 0:1696], 1696, stg)
        CW = P.sbuf("CW", [128, 14, 3], F32)
        C.cwb = CW
        P.dma("sp", CW[:, 0:12, :], dr["convT"][0:1536, :].rearrange("(c p) j -> p c j", p=128), writes=[CW])
        P.dma("sp", CW[0:64, 12, :], dr["convT"][1536:1600, :], writes=[CW])
        P.dma("sp", CW[0:96, 13, :], dr["convT"][1600:1696, :], writes=[CW])
        lw2f = P.sbuf("lw2f", [64, 2, 512], F32)
        LW2 = P.sbuf("LW2", [64, 2, 512], BF16)
        P.dma("sp", lw2f.ap(), dr["lw2"], writes=[lw2f])
        P.op("pool", lambda e: e.tensor_copy(out=LW2.ap(), in_=lw2f.ap()), reads=[lw2f], writes=[LW2])
        g2f = P.sbuf("g2f", [96, 512], F32)
        G2W = P.sbuf("G2W", [96, 512], BF16)
        P.dma("sp", g2f.ap(), dr["g2"], writes=[g2f])
        P.op("pool", lambda e: e.tensor_copy(out=G2W.ap(), in_=g2f.ap()), reads=[g2f], writes=[G2W])
        w0a0 = P.sbuf("w0a0", [128, 4, 4], F32)
        kvec = P.sbuf("kvec", [128, 5, 4], F32)
        P.dma("sp", w0a0.ap(), dr["w0a0"], writes=[w0a0])
        P.dma("sp", kvec.ap(), dr["kvec"], writes=[kvec])
        omka = P.sbuf("omka", [128, 4], F32)
        rkh = P.sbuf("rkh", [128, 4], F32)
        P.op("dve", lambda e: e.tensor_scalar(out=omka.ap(), in0=kvec[:, 1, :], scalar1=-1.0, scalar2=1.0,
                                              op0=ALU.mult, op1=ALU.add), reads=[kvec], writes=[omka])
        P.op("dve", lambda e: e.tensor_scalar(out=rkh.ap(), in0=kvec[:, 2, :], scalar1=0.5, scalar2=None,
                                              op0=ALU.mult), reads=[kvec], writes=[rkh])
        ones32 = P.sbuf("ones32", [128, 128], F32)
        P.op("pool", lambda e: e.memset(ones32.ap(), 1.0), writes=[ones32])
        msk = {}
        for nm, cop, sgn in (("SU", ALU.is_gt, -1), ("IU", ALU.is_ge, -1), ("SL", ALU.is_gt, 1), ("IL", ALU.is_ge, 1)):
            mb = P.sbuf("m" + nm, [128, 128], F32)
            P.op("pool", lambda e, mb=mb, cop=cop, sgn=sgn: e.affine_select(out=mb.ap(), in_=ones32.ap(), pattern=[[-sgn, 128]],
                                                                            compare_op=cop, fill=0.0, base=0, channel_multiplier=sgn),
                 reads=[ones32], writes=[mb])
            msk[nm] = mb
        AMM = []
        for d, (s_, i_) in enumerate((("SU", "IU"), ("SL", "IL"))):
            am = P.sbuf("amm%d" % d, [128, 4, 128], F32)
            for q, nm in enumerate((s_, i_, s_, i_)):
                P.op("pool", lambda e, am=am, q=q, nm=nm: e.tensor_copy(out=am[:, q, :], in_=msk[nm].ap()),
                     reads=[msk[nm]], writes=[am])
            AMM.append(am)
        NTM = [msk["SL"], msk["SU"]]
        rmask = P.sbuf("rmask", [128, NB], F32)
        P.op("pool", lambda e: e.memset(rmask.ap(), 1.0), writes=[rmask])
        for c in range(NB // 128):
            P.op("pool", lambda e, c=c: e.memset(rmask[:, c * 128:c * 128 + 1], 0.0), writes=[rmask])

        def f32t(n, shape=(128, NB)):
            return P.sbuf(n, list(shape), F32)

        def b16t(n, shape=(128, NB)):
            return P.sbuf(n, list(shape), BF16)

        la32, LA16 = f32t("la32", (64, NB)), b16t("LA16", (64, NB))
        gl32, sg16 = f32t("gl32", (96, NB)), b16t("sg16", (96, NB))
        r32, k32, v32, v16 = f32t("r32"), f32t("k32"), f32t("v32"), b16t("v16")
        sig, ld, aa, a2_, kraw, ksq, rn, kk = f32t("sig"), f32t("ld"), f32t("aa"), f32t("a2_"), f32t("kraw"), b16t("ksq"), f32t("rn"), f32t("kk")
        fac, kdir, kdir2, bb, cs, LL, Lm = f32t("fac"), f32t("kdir"), f32t("kdir2"), f32t("bb"), f32t("cs"), f32t("LL"), f32t("Lm")
        gg, ginv, gprev = f32t("gg"), f32t("ginv"), f32t("gprev")
        Bt, Kt = b16t("Bt"), b16t("Kt")
        KR = P.sbuf("KR", [128, NB // 128, 2, 128], BF16)
        Bg, Kg = b16t("Bg", (128, 128)), b16t("Kg", (128, 128))
        TT = P.sbuf("TT", [128, 3, 128], BF16)
        AMh = [P.sbuf("AM%d" % h, [128, 4, 128], BF16) for h in range(2)]
        NN = [[P.sbuf("NN%d_%d" % (h, i), [128, 2, 128], F32) for i in range(1)] for h in range(2)]
        PP = [[P.sbuf("PP%d_%d" % (h, i), [128, 128], F32) for i in range(2)] for h in range(2)]
        Tb = [P.sbuf("Tb%d" % h, [128, 128], BF16) for h in range(2)]
        IV = [dict(Nb2=P.sbuf("ivNb2_%d" % h, [128, 2, 128], F32), S2=P.sbuf("ivS2_%d" % h, [128, 2, 128], F32),
                   S4T=P.sbuf("ivS4T_%d" % h, [128, 128], F32), X=P.sbuf("ivX_%d" % h, [128, 128], F32),
                   TTr=P.sbuf("ivTTr_%d" % h, [128, 128], F32), Bt=P.sbuf("ivBt_%d" % h, [128, 4, 128], F32)) for h in range(2)]
        BM = P.sbuf("blkmask", [128, 5, 128], F32)
        P.dma("sp", BM.ap(), dr["blkmask"], writes=[BM])
        ident32 = P.sbuf("ident32", [128, 128], F32)
        P.op("pool", lambda e: e.tensor_copy(out=ident32.ap(), in_=C.ident.ap()), reads=[C.ident], writes=[ident32])
        NZ, UT = b16t("NZ", (128, 128)), b16t("UT", (128, 128))
        S32 = [[f32t("S32_%d_%d" % (d, p), (128, 64)) for p in range(4)] for d in range(2)]
        Sb = [[[b16t("Sb_%d_%d_%d" % (d, p, i), (128, 64)) for i in range(2)] for p in range(4)] for d in range(2)]
        sbi = [[0] * 4 for _ in range(2)]
        ys32, yfl = f32t("ys32"), f32t("yfl")
        ys16, yc, yc2, rstd, yn = b16t("ys16"), f32t("yc"), b16t("yc2"), f32t("rstd"), f32t("yn")
        rk, rk16, bonus, o16 = f32t("rk"), b16t("rk16"), f32t("bonus"), b16t("o16")
        for d in range(2):
            for p in range(4):
                P.op("pool", lambda e, d=d, p=p: e.memset(S32[d][p].ap(), 0.0), writes=[S32[d][p]])
                P.op("pool", lambda e, d=d, p=p: e.memset(Sb[d][p][0].ap(), 0.0), writes=[Sb[d][p][0]])

        NCH = NB // 128
        import os
        lim = int(os.environ.get("RWKV_LIM", "999"))
        cnt = 0
        for d in range(2):
            blocks = [("c", 0)] + ([("l", i) for i in range(NLB)] if d == 0 else [("l", i) for i in reversed(range(NLB))])
            for (src, bi) in blocks:
                cnt += 1
                if cnt > lim:
                    continue
                if os.environ.get("RWKV_SKIPF") and d == 0 and src == "l":
                    continue
                H = C.HC if src == "c" else C.HT
                t0 = 0 if src == "c" else bi * NB
                lat = (src == "l")
                proj_conv(C, H, t0, WR, 1536, 64, CW[0:64, 12, :], la32.ap(), la32)
                P.op("act", lambda e: e.activation(out=LA16[0:32, :], in_=la32[0:32, :], func=AF.Tanh), reads=[la32], writes=[LA16])
                P.op("pool", lambda e: e.tensor_copy(out=LA16[32:64, :], in_=la32[32:64, :]), reads=[la32], writes=[LA16])
                if d == 1 and lat:
                    proj_conv(C, H, t0, WR, 1600, 96, CW[0:96, 13, :], gl32.ap(), gl32)
                    P.op("act", lambda e: e.activation(out=sg16.ap(), in_=gl32.ap(), func=AF.Sigmoid), reads=[gl32], writes=[sg16])
                for p in range(4):
                    proj_conv(C, H, t0, WR, p * 128, 128, CW[:, p, :], r32.ap(), r32)
                    proj_conv(C, H, t0, WR, 512 + p * 128, 128, CW[:, 4 + p, :], k32.ap(), k32)
                    proj_conv(C, H, t0, WR, 1024 + p * 128, 128, CW[:, 8 + p, :], v32.ap(), v32)
                    P.op("pool", lambda e: e.tensor_copy(out=v16.ap(), in_=v32.ap()), reads=[v32], writes=[v16])
                    pz = C.nb()
                    P.op("pe", lambda e, p=p, d=d: e.matmul(pz[:, 0:NB], lhsT=LW2[0:32, d, p * 128:(p + 1) * 128], rhs=LA16[0:32, :],
                                                             start=True, stop=True), reads=[LW2, LA16], writes=[pz])
                    P.op("act", lambda e, p=p, d=d: e.activation(out=sig.ap(), in_=pz[:, 0:NB], func=AF.Sigmoid, bias=w0a0[:, d, p:p + 1]),
                         reads=[w0a0], writes=[pz, sig])
                    P.op("pool", lambda e: e.tensor_scalar(out=ld.ap(), in0=sig.ap(), scalar1=-EXPM05, scalar2=None, op0=ALU.mult),
                         reads=[sig], writes=[ld])
                    pz = C.nb()
                    P.op("pe", lambda e, p=p, d=d, pz=pz: e.matmul(pz[:, 0:NB], lhsT=LW2[32:64, d, p * 128:(p + 1) * 128], rhs=LA16[32:64, :],
                                                                    start=True, stop=True), reads=[LW2, LA16], writes=[pz])
                    P.op("act", lambda e, p=p, d=d, pz=pz: e.activation(out=aa.ap(), in_=pz[:, 0:NB], func=AF.Sigmoid, bias=w0a0[:, 2 + d, p:p + 1]),
                         reads=[w0a0], writes=[pz, aa])
                    P.op("pool", lambda e, p=p: e.tensor_scalar(out=kraw.ap(), in0=k32.ap(), scalar1=kvec[:, 0, p:p + 1], scalar2=None, op0=ALU.mult),
                         reads=[k32, kvec], writes=[kraw])
                    P.op("act", lambda e: e.activation(out=ksq.ap(), in_=kraw.ap(), func=AF.Square), reads=[kraw], writes=[ksq])
                    pz = C.nb()
                    P.op("pe", lambda e, pz=pz: e.matmul(pz[:, 0:NB], lhsT=C.bones.ap(), rhs=ksq.ap(), start=True, stop=True),
                         reads=[C.bones, ksq], writes=[pz])
                    P.op("act", lambda e, pz=pz: e.activation(out=rn.ap(), in_=pz[:, 0:NB], func=AF.Sqrt), writes=[pz, rn])
                    P.op("dve", lambda e: e.tensor_scalar(out=rn.ap(), in0=rn.ap(), scalar1=1e-12, scalar2=None, op0=ALU.max), writes=[rn])
                    P.op("dve", lambda e: e.reciprocal(out=rn.ap(), in_=rn.ap()), writes=[rn])
                    P.op("dve", lambda e: e.tensor_tensor(out=kk.ap(), in0=kraw.ap(), in1=rn.ap(), op=ALU.mult), reads=[kraw, rn], writes=[kk])
                    P.op("dve", lambda e, p=p: e.tensor_scalar(out=fac.ap(), in0=aa.ap(), scalar1=kvec[:, 1, p:p + 1], scalar2=omka[:, p:p + 1],
                                                               op0=ALU.mult, op1=ALU.add), reads=[aa, kvec, omka], writes=[fac])
                    P.op("pool", lambda e: e.tensor_tensor(out=kdir.ap(), in0=k32.ap(), in1=fac.ap(), op=ALU.mult), reads=[k32, fac], writes=[kdir])
                    P.op("pool", lambda e: e.tensor_tensor(out=bb.ap(), in0=kk.ap(), in1=aa.ap(), op=ALU.mult), reads=[kk, aa], writes=[bb])
                    P.op("dve", lambda e: e.tensor_tensor_scan(out=cs.ap(), data0=rmask.ap(), data1=ld.ap(), initial=0.0,
                                                               op0=ALU.mult, op1=ALU.add), reads=[rmask, ld], writes=[cs])
                    if d == 0:
                        Lb = cs
                    else:
                        for c in range(NCH):
                            P.op("dve", lambda e, c=c: e.tensor_scalar(out=LL[:, c * 128:(c + 1) * 128], in0=cs[:, c * 128:(c + 1) * 128],
                                                                       scalar1=-1.0, scalar2=cs[:, c * 128 + 127:c * 128 + 128],
                                                                       op0=ALU.mult, op1=ALU.add), reads=[cs], writes=[LL])
                        P.op("dve", lambda e: e.tensor_tensor(out=LL.ap(), in0=LL.ap(), in1=ld.ap(), op=ALU.add), reads=[ld], writes=[LL])
                        Lb = LL
                    P.op("pool", lambda e, Lb=Lb: e.tensor_tensor(out=Lm.ap(), in0=Lb.ap(), in1=ld.ap(), op=ALU.subtract), reads=[Lb, ld], writes=[Lm])
                    P.op("act", lambda e, Lb=Lb: e.activation(out=gg.ap(), in_=Lb.ap(), func=AF.Exp), reads=[Lb], writes=[gg])
                    P.op("act", lambda e, Lb=Lb: e.activation(out=ginv.ap(), in_=Lb.ap(), func=AF.Exp, scale=-1.0), reads=[Lb], writes=[ginv])
                    P.op("act", lambda e: e.activation(out=gprev.ap(), in_=Lm.ap(), func=AF.Exp), reads=[Lm], writes=[gprev])
                    P.op("dve", lambda e: e.tensor_tensor(out=Bt.ap(), in0=bb.ap(), in1=ginv.ap(), op=ALU.mult), reads=[bb, ginv], writes=[Bt])
                    P.op("dve", lambda e: e.tensor_tensor(out=Kt.ap(), in0=kdir.ap(), in1=ginv.ap(), op=ALU.mult), reads=[kdir, ginv], writes=[Kt])
                    P.op("pool", lambda e: e.tensor_tensor(out=KR[:, :, 0, :], in0=kk.ap().rearrange("p (c t) -> p c t", t=128),
                                                          in1=gprev.ap().rearrange("p (c t) -> p c t", t=128), op=ALU.mult),
                         reads=[kk, gprev], writes=[KR])
                    P.op("pool", lambda e: e.tensor_tensor(out=KR[:, :, 1, :], in0=r32.ap().rearrange("p (c t) -> p c t", t=128),
                                                          in1=gg.ap().rearrange("p (c t) -> p c t", t=128), op=ALU.mult),
                         reads=[r32, gg], writes=[KR])
                    if d == 1 and lat:
                        if (p, bi) in C.yf_bufs:
                            P.dma("sp", yfl.ap(), dr["yf"][p, :, t0:t0 + NB], reads=[C.yf_bufs[(p, bi)]], writes=[yfl])
                        else:
                            P.op("pool", lambda e: e.memset(yfl.ap(), 0.0), writes=[yfl])
                    chunks = list(range(NCH)) if d == 0 else list(reversed(range(NCH)))
                    for c in chunks:
                        csl = slice(c * 128, (c + 1) * 128)
                        gcol = c * 128 + 127 if d == 0 else c * 128
                        P.op("dve", lambda e, csl=csl, gcol=gcol: e.tensor_scalar(out=Bg.ap(), in0=Bt[:, csl], scalar1=gg[:, gcol:gcol + 1],
                                                                                  scalar2=None, op0=ALU.mult), reads=[Bt, gg], writes=[Bg])
                        P.op("dve", lambda e, csl=csl, gcol=gcol: e.tensor_scalar(out=Kg.ap(), in0=Kt[:, csl], scalar1=gg[:, gcol:gcol + 1],
                                                                                  scalar2=None, op0=ALU.mult), reads=[Kt, gg], writes=[Kg])
                        pT = C.bankT
                        P.op("pe", lambda e, csl=csl: e.transpose(pT[:, 0:128], v16[:, csl], C.ident.ap()), reads=[v16, C.ident], writes=[pT])
                        P.op("pe", lambda e: e.transpose(pT[:, 128:256], Bg.ap(), C.ident.ap()), reads=[Bg, C.ident], writes=[pT])
                        P.op("pe", lambda e: e.transpose(pT[:, 256:384], Kg.ap(), C.ident.ap()), reads=[Kg, C.ident], writes=[pT])
                        P.op("act", lambda e: e.activation(out=TT.ap(), in_=pT[:, 0:384].rearrange("p (a b) -> p a b", b=128), func=AF.Copy),
                             writes=[pT, TT])
                        for h in range(2):
                            hs = slice(h * 64, (h + 1) * 64)
                            AM = AMh[h]
                            pg = C.nb()
                            P.op("pe", lambda e, hs=hs, csl=csl, c=c, pg=pg: e.matmul(pg[:, 0:256], lhsT=Bt[hs, csl], rhs=KR[hs, c, :, :],
                                                                                      start=True, stop=True), reads=[Bt, KR], writes=[pg])
                            P.op("pe", lambda e, hs=hs, csl=csl, c=c, pg=pg: e.matmul(pg[:, 256:512], lhsT=Kt[hs, csl], rhs=KR[hs, c, :, :],
                                                                                      start=True, stop=True), reads=[Kt, KR], writes=[pg])
                            N0 = NN[h][0]
                            P.op("dve", lambda e, N0=N0, pg=pg, d=d: e.tensor_tensor(out=N0[:, 0, :], in0=pg[:, 0:128], in1=AMM[d][:, 0, :], op=ALU.mult),
                                 reads=[AMM[d]], writes=[pg, N0])
                            P.op("dve", lambda e, AM=AM, pg=pg, d=d: e.tensor_tensor(out=AM.ap(), in0=pg.ap().rearrange("p (a b) -> p a b", b=128),
                                                                                    in1=AMM[d].ap(), op=ALU.mult), reads=[AMM[d]], writes=[pg, AM])
                            pn = C.nb()
                            P.op("pe", lambda e, hs=hs, csl=csl, c=c, pn=pn: e.matmul(pn[:, 0:128], lhsT=KR[hs, c, 0, :], rhs=Bt[hs, csl],
                                                                                      start=True, stop=True), reads=[Bt, KR], writes=[pn])
                            N0 = NN[h][0]
                            P.op("dve", lambda e, N0=N0, pn=pn, d=d: e.tensor_tensor(out=N0[:, 1, :], in0=pn[:, 0:128], in1=NTM[d].ap(), op=ALU.mult),
                                 reads=[NTM[d]], writes=[pn, N0])
                        def inv_gen(h):
                            N0 = NN[h][0]
                            Nb2, S2, S4T, X_, TTr = IV[h]["Nb2"], IV[h]["S2"], IV[h]["S4T"], IV[h]["X"], IV[h]["TTr"]
                            Btm = IV[h]["Bt"]
                            Pc = [PP[h][0], PP[h][1]]
                            P.op("dve", lambda e: e.tensor_tensor(out=Nb2[:, 0, :], in0=N0[:, 0, :], in1=BM[:, 0, :], op=ALU.mult), reads=[N0, BM], writes=[Nb2])
                            P.op("pool", lambda e: e.tensor_tensor(out=Nb2[:, 1, :], in0=N0[:, 1, :], in1=BM[:, 0, :], op=ALU.mult), reads=[N0, BM], writes=[Nb2])
                            for l in range(4):
                                P.op("pool", lambda e, l=l: e.tensor_tensor(out=Btm[:, l, :], in0=N0[:, 1, :], in1=BM[:, 1 + l, :], op=ALU.mult),
                                     reads=[N0, BM], writes=[Btm])
                            pq = C.nb()
                            P.op("pe", lambda e: e.matmul(pq[:, 0:128], lhsT=Nb2[:, 1, :], rhs=Nb2[:, 0, :], start=True, stop=True), reads=[Nb2], writes=[pq])
                            P.op("pe", lambda e: e.matmul(pq[:, 128:256], lhsT=Nb2[:, 0, :], rhs=Nb2[:, 1, :], start=True, stop=True), reads=[Nb2], writes=[pq])
                            P.op("act", lambda e: e.activation(out=S2.ap(), in_=pq[:, 0:256].rearrange("p (a b) -> p a b", b=128), func=AF.Copy), writes=[pq, S2])
                            P.op("pool", lambda e: e.tensor_tensor(out=Pc[0].ap(), in0=ident32.ap(), in1=Nb2[:, 0, :], op=ALU.subtract),
                                 reads=[Nb2, ident32], writes=[Pc[0]])
                            yield
                            pq2 = C.nb()
                            P.op("pe", lambda e: e.matmul(pq2[:, 0:128], lhsT=S2[:, 0, :], rhs=S2[:, 1, :], start=True, stop=True), reads=[S2], writes=[pq2])
                            P.op("dve", lambda e: e.tensor_copy(out=S4T.ap(), in_=pq2[:, 0:128]), writes=[pq2, S4T])
                            pp_ = C.nb()
                            P.op("pe", lambda e: e.matmul(pp_[:, 0:128], lhsT=S2[:, 1, :], rhs=Pc[0].ap(), start=True, stop=True), reads=[S2, Pc[0]], writes=[pp_])
                            P.op("dve", lambda e: e.tensor_tensor(out=Pc[1].ap(), in0=pp_[:, 0:128], in1=Pc[0].ap(), op=ALU.add), reads=[Pc[0]], writes=[pp_, Pc[1]])
                            yield
                            pp2 = C.nb()
                            P.op("pe", lambda e: e.matmul(pp2[:, 0:128], lhsT=S4T.ap(), rhs=Pc[1].ap(), start=True, stop=True), reads=[S4T, Pc[1]], writes=[pp2])
                            P.op("dve", lambda e: e.tensor_tensor(out=Pc[0].ap(), in0=pp2[:, 0:128], in1=Pc[1].ap(), op=ALU.add), reads=[Pc[1]], writes=[pp2, Pc[0]])
                            cur = 0
                            yield
                            for l in range(4):
                                Tc, Tn = Pc[cur], Pc[1 - cur]
                                ptr = C.nb()
                                P.op("pe", lambda e, Tc=Tc, ptr=ptr: e.transpose(ptr[:, 0:128], Tc.ap(), ident32.ap()), reads=[Tc, ident32], writes=[ptr])
                                P.op("act", lambda e, ptr=ptr: e.activation(out=TTr.ap(), in_=ptr[:, 0:128], func=AF.Copy), writes=[ptr, TTr])
                                px = C.nb()
                                P.op("pe", lambda e, l=l, Tc=Tc, px=px: e.matmul(px[:, 0:128], lhsT=Btm[:, l, :], rhs=Tc.ap(), start=True, stop=True),
                                     reads=[Btm, Tc], writes=[px])
                                P.op("dve", lambda e, px=px: e.tensor_copy(out=X_.ap(), in_=px[:, 0:128]), writes=[px, X_])
                                yield
                                pr = C.nb()
                                P.op("pe", lambda e, pr=pr: e.matmul(pr[:, 0:128], lhsT=TTr.ap(), rhs=X_.ap(), start=True, stop=True), reads=[TTr, X_], writes=[pr])
                                if l < 3:
                                    P.op("dve", lambda e, pr=pr, Tc=Tc, Tn=Tn: e.tensor_tensor(out=Tn.ap(), in0=Tc.ap(), in1=pr[:, 0:128], op=ALU.subtract),
                                         reads=[Tc], writes=[pr, Tn])
                                else:
                                    P.op("dve", lambda e, pr=pr, Tc=Tc: e.tensor_tensor(out=Tb[h].ap(), in0=Tc.ap(), in1=pr[:, 0:128], op=ALU.subtract),
                                         reads=[Tc], writes=[pr, Tb[h]])
                                cur = 1 - cur
                                yield

                        for _ in zip(inv_gen(0), inv_gen(1)):
                            pass
                        Th = Tb
                        So = Sb[d][p][sbi[d][p] % 2]
                        Sn = Sb[d][p][(sbi[d][p] + 1) % 2]
                        sbi[d][p] += 1
                        pz = C.nb()
                        for h in range(2):
                            hs = slice(h * 64, (h + 1) * 64)
                            P.op("pe", lambda e, hs=hs, c=c, pz=pz, So=So: e.matmul(pz[:, hs], lhsT=KR[hs, c, 0, :], rhs=So[hs, :], start=True, stop=False),
                                 reads=[KR, So], writes=[pz])
                            P.op("pe", lambda e, hs=hs, h=h, pz=pz: e.matmul(pz[:, hs], lhsT=AMh[h][:, 2, :], rhs=TT[:, 0, hs], start=False, stop=True),
                                 reads=[AMh[h], TT], writes=[pz])
                        P.op("act", lambda e, pz=pz: e.activation(out=NZ.ap(), in_=pz[:, 0:128], func=AF.Copy, scale=-1.0), writes=[pz, NZ])
                        pu = C.nb()
                        for h in range(2):
                            hs = slice(h * 64, (h + 1) * 64)
                            P.op("pe", lambda e, hs=hs, h=h, pu=pu: e.matmul(pu[:, hs], lhsT=Th[h].ap(), rhs=NZ[:, hs], start=True, stop=True),
                                 reads=[Th[h], NZ], writes=[pu])
                        P.op("dve", lambda e, pu=pu: e.tensor_copy(out=UT.ap(), in_=pu[:, 0:128]), writes=[pu, UT])
                        if lat:
                            py = C.nb()
                            for h in range(2):
                                hs = slice(h * 64, (h + 1) * 64)
                                P.op("pe", lambda e, hs=hs, c=c, py=py, So=So: e.matmul(py[hs, 0:128], lhsT=So[hs, :], rhs=KR[hs, c, 1, :], start=True, stop=False),
                                     reads=[So, KR], writes=[py])
                                P.op("pe", lambda e, hs=hs, h=h, py=py: e.matmul(py[hs, 0:128], lhsT=UT[:, hs], rhs=AMh[h][:, 1, :], start=False, stop=False),
                                     reads=[UT, AMh[h]], writes=[py])
                                P.op("pe", lambda e, hs=hs, h=h, py=py: e.matmul(py[hs, 0:128], lhsT=TT[:, 0, hs], rhs=AMh[h][:, 3, :], start=False, stop=True),
                                     reads=[TT, AMh[h]], writes=[py])
                            if d == 0:
                                P.op("act", lambda e, py=py, csl=csl: e.activation(out=ys32[:, csl], in_=py[:, 0:128], func=AF.Copy), writes=[py, ys32])
                            else:
                                P.op("dve", lambda e, py=py, csl=csl: e.tensor_tensor(out=ys32[:, csl], in0=py[:, 0:128], in1=yfl[:, csl], op=ALU.add),
                                     reads=[yfl], writes=[py, ys32])
                        pS = C.nb()
                        for h in range(2):
                            hs = slice(h * 64, (h + 1) * 64)
                            P.op("pe", lambda e, hs=hs, pS=pS: e.matmul(pS[hs, 0:64], lhsT=TT[:, 1, hs], rhs=UT[:, hs], start=True, stop=False),
                                 reads=[TT, UT], writes=[pS])
                            P.op("pe", lambda e, hs=hs, pS=pS: e.matmul(pS[hs, 0:64], lhsT=TT[:, 2, hs], rhs=TT[:, 0, hs], start=False, stop=True),
                                 reads=[TT], writes=[pS])
                        S3 = S32[d][p]
                        P.op("dve", lambda e, pS=pS, S3=S3, gcol=gcol: e.scalar_tensor_tensor(out=S3.ap(), in0=S3.ap(), scalar=gg[:, gcol:gcol + 1],
                                                                                              in1=pS[:, 0:64], op0=ALU.mult, op1=ALU.add),
                             reads=[gg], writes=[pS, S3])
                        P.op("act", lambda e, S3=S3, Sn=Sn: e.activation(out=Sn.ap(), in_=S3.ap(), func=AF.Copy), reads=[S3], writes=[Sn])
                    if not lat:
                        continue
                    if d == 0:
                        yb = Buf("yf_%d_%d" % (p, bi))
                        C.yf_bufs[(p, bi)] = yb
                        P.dma("sp", dr["yf"][p, :, t0:t0 + NB], ys32.ap(), reads=[ys32], writes=[yb])
                        continue
                    P.op("act", lambda e: e.activation(out=ys16.ap(), in_=ys32.ap(), func=AF.Copy), reads=[ys32], writes=[ys16])
                    pm_ = C.nb()
                    P.op("pe", lambda e, pm_=pm_: e.matmul(pm_[:, 0:NB], lhsT=C.bmean.ap(), rhs=ys16.ap(), start=True, stop=True),
                         reads=[C.bmean, ys16], writes=[pm_])
                    P.op("dve", lambda e, pm_=pm_: e.tensor_tensor(out=yc.ap(), in0=ys32.ap(), in1=pm_[:, 0:NB], op=ALU.subtract),
                         reads=[ys32], writes=[pm_, yc])
                    P.op("act", lambda e: e.activation(out=yc2.ap(), in_=yc.ap(), func=AF.Square), reads=[yc], writes=[yc2])
                    pv = C.nb()
                    P.op("pe", lambda e, pv=pv: e.matmul(pv[:, 0:NB], lhsT=C.bmean.ap(), rhs=yc2.ap(), start=True, stop=True),
                         reads=[C.bmean, yc2], writes=[pv])
                    P.op("act", lambda e, pv=pv: e.activation(out=rstd.ap(), in_=pv[:, 0:NB], func=AF.Sqrt, bias=C.epsx.ap()),
                         reads=[C.epsx], writes=[pv, rstd])
                    P.op("dve", lambda e: e.reciprocal(out=rstd.ap(), in_=rstd.ap()), writes=[rstd])
                    P.op("dve", lambda e: e.tensor_tensor(out=yn.ap(), in0=yc.ap(), in1=rstd.ap(), op=ALU.mult), reads=[yc, rstd], writes=[yn])
                    P.op("dve", lambda e, p=p: e.tensor_scalar(out=yn.ap(), in0=yn.ap(), scalar1=kvec[:, 3, p:p + 1], scalar2=kvec[:, 4, p:p + 1],
                                                               op0=ALU.mult, op1=ALU.add), reads=[kvec], writes=[yn])
                    pz = C.nb()
                    P.op("pe", lambda e, p=p, pz=pz: e.matmul(pz[:, 0:NB], lhsT=LW2[32:64, 0, p * 128:(p + 1) * 128], rhs=LA16[32:64, :],
                                                              start=True, stop=True), reads=[LW2, LA16], writes=[pz])
                    P.op("act", lambda e, p=p, pz=pz: e.activation(out=a2_.ap(), in_=pz[:, 0:NB], func=AF.Sigmoid, bias=w0a0[:, 2, p:p + 1]),
                         reads=[w0a0], writes=[pz, a2_])
                    P.op("dve", lambda e, p=p: e.tensor_scalar(out=a2_.ap(), in0=a2_.ap(), scalar1=kvec[:, 1, p:p + 1], scalar2=omka[:, p:p + 1],
                                                               op0=ALU.mult, op1=ALU.add), reads=[kvec, omka], writes=[a2_])
                    P.op("pool", lambda e: e.tensor_tensor(out=a2_.ap(), in0=a2_.ap(), in1=fac.ap(), op=ALU.add), reads=[fac], writes=[a2_])
                    P.op("pool", lambda e: e.tensor_tensor(out=kdir2.ap(), in0=k32.ap(), in1=a2_.ap(), op=ALU.mult), reads=[k32, a2_], writes=[kdir2])
                    P.op("pool", lambda e: e.tensor_tensor(out=rk.ap(), in0=r32.ap(), in1=kdir2.ap(), op=ALU.mult), reads=[r32, kdir2], writes=[rk])
                    P.op("pool", lambda e, p=p: e.tensor_scalar(out=rk16.ap(), in0=rk.ap(), scalar1=rkh[:, p:p + 1], scalar2=None, op0=ALU.mult),
                         reads=[rk, rkh], writes=[rk16])
                    pbn = C.nb()
                    P.op("pe", lambda e, pbn=pbn: e.matmul(pbn[:, 0:NB], lhsT=C.bones.ap(), rhs=rk16.ap(), start=True, stop=True),
                         reads=[C.bones, rk16], writes=[pbn])
                    P.op("dve", lambda e, pbn=pbn: e.tensor_tensor(out=bonus.ap(), in0=pbn[:, 0:NB], in1=v32.ap(), op=ALU.mult),
                         reads=[v32], writes=[pbn, bonus])
                    P.op("pool", lambda e: e.tensor_tensor(out=bonus.ap(), in0=bonus.ap(), in1=yn.ap(), op=ALU.add), reads=[yn], writes=[bonus])
                    pgt = C.nb()
                    P.op("pe", lambda e, p=p, pgt=pgt: e.matmul(pgt[:, 0:NB], lhsT=G2W[0:96, p * 128:(p + 1) * 128], rhs=sg16.ap(), start=True, stop=True),
                         reads=[G2W, sg16], writes=[pgt])
                    P.op("dve", lambda e, pgt=pgt: e.tensor_tensor(out=o16.ap(), in0=pgt[:, 0:NB], in1=bonus.ap(), op=ALU.mult),
                         reads=[bonus], writes=[pgt, o16])
                    mb = Buf("mt_%d_%d" % (p, bi))
                    C.mt_bufs[(p, bi)] = mb
                    P.dma("sp", dr["mt"][p, :, t0:t0 + NB], o16.ap(), reads=[o16], writes=[mb])


def stage_attn(C):
    P, dr = C.P, C.dr
    HT, HC = C.HT, C.HC
    bk = C.banks
    NKT = (T_CTX + T_LAT) // 128
    with P.scope():
        stg = [P.sbuf("astg%d" % i, [128, 8, 256], F32) for i in range(2)]
        Wh = P.sbuf("Wh", [128, 8, 640], BF16)
        KT = P.sbuf("KT", [128, T_CTX + T_LAT], BF16)
        VT = P.sbuf("VT", [128, NKT, 128], BF16)
        QB = [P.sbuf("QB%d" % i, [128, 512], BF16) for i in range(2)]
        PT = [P.sbuf("PT%d" % i, [128, 512], BF16) for i in range(2)]
        rC = [P.sbuf("rC%d" % i, [128, NB], F32) for i in range(2)]
        rS = [P.sbuf("rS%d" % i, [128, NB], F32) for i in range(2)]
        t1 = P.sbuf("rp_t1", [128, NB], F32)
        t2 = P.sbuf("rp_t2", [128, NB], F32)
        rl = P.sbuf("at_rl", [128, 512], F32)
        on_ = P.sbuf("at_on", [128, 512], F32)
        dif = P.sbuf("at_dif", [128, NB], F32)
        dsq = P.sbuf("at_dsq", [128, NB], BF16)
        drs = P.sbuf("at_drs", [128, NB], F32)
        ao16 = [P.sbuf("at_o16_%d" % i, [128, NB], BF16) for i in range(2)]
        lamt = P.sbuf("lamt", [128, 4, 64], F32)
        lpr = P.sbuf("lpr", [128, 2, 64], F32)
        lsum = P.sbuf("lsum", [128, 2], F32)
        nlam = P.sbuf("nlam", [128, 1], F32)
        sbw = P.sbuf("sbw", [128, 1], F32)
        P.dma("sp", lamt.ap(), dr["lam"], writes=[lamt])
        P.dma("sp", sbw.ap(), dr["subln"], writes=[sbw])
        P.op("dve", lambda e: e.tensor_tensor(out=lpr.ap(), in0=lamt[:, 0:4:2, :], in1=lamt[:, 1:4:2, :], op=ALU.mult),
             reads=[lamt], writes=[lpr])
        P.op("dve", lambda e: e.reduce_sum(out=lsum.ap(), in_=lpr.ap(), axis=AX.X), reads=[lpr], writes=[lsum])
        P.op("act", lambda e: e.activation(out=lsum.ap(), in_=lsum.ap(), func=AF.Exp), writes=[lsum])
        P.op("dve", lambda e: e.tensor_tensor(out=nlam.ap(), in0=lsum[:, 1:2], in1=lsum[:, 0:1], op=ALU.subtract),
             reads=[lsum], writes=[nlam])
        P.op("dve", lambda e: e.tensor_scalar(out=nlam.ap(), in0=nlam.ap(), scalar1=-0.2, scalar2=None, op0=ALU.add), writes=[nlam])
        P.op("dve", lambda e: e.tensor_scalar(out=sbw.ap(), in0=sbw.ap(), scalar1=0.8, scalar2=None, op0=ALU.mult), writes=[sbw])
        for q in QB:
            P.op("pool", lambda e, q=q: e.memset(q.ap(), 0.0), writes=[q])
        C.stg_i = 0
        ri = [0]

        def load_rope(bi):
            i = ri[0] % 2
            ri[0] += 1
            P.dma("sp", rC[i].ap(), dr["ropeC"][:, bi * NB:(bi + 1) * NB], writes=[rC[i]])
            P.dma("sp", rS[i].ap(), dr["ropeS"][:, bi * NB:(bi + 1) * NB], writes=[rS[i]])
            return rC[i], rS[i]

        def proj_rope(bank, c0, bi, rc, rs, outs):
            t0 = bi * NB
            for g in range(2):
                for k in range(8):
                    P.op("pe", lambda e, g=g, k=k: e.matmul(bank[:, g * NB:(g + 1) * NB], lhsT=Wh[:, k, c0 + g * 128:c0 + (g + 1) * 128],
                                                             rhs=HT[:, k, 1 + t0:1 + t0 + NB], start=(k == 0), stop=(k == 7)),
                         reads=[Wh, HT], writes=[bank])
            P.op("dve", lambda e: e.tensor_tensor(out=t1.ap(), in0=bank[:, 0:NB], in1=rc.ap(), op=ALU.mult), reads=[rc], writes=[bank, t1])
            P.op("dve", lambda e: e.tensor_tensor(out=t2.ap(), in0=bank[:, NB:2 * NB], in1=rs.ap(), op=ALU.mult), reads=[rs], writes=[bank, t2])
            for rows, oap, ob in outs:
                P.op("pool", lambda e, rows=rows, oap=oap: e.tensor_tensor(out=oap, in0=t1[rows, :], in1=t2[rows, :], op=ALU.add),
                     reads=[t1, t2], writes=[ob])

        for h in range(4):
            qc = 1696 + h * 128
            kc = 1696 + 512 + h * 128
            vc = 1696 + 1024 + h * 128
            load_w_bf16(C, (Wh, 0), dr["w_in"][:, qc:qc + 128], 128, stg)
            load_w_bf16(C, (Wh, 128), dr["w_qks"][:, h * 128:(h + 1) * 128], 128, stg)
            load_w_bf16(C, (Wh, 256), dr["w_in"][:, kc:kc + 128], 128, stg)
            load_w_bf16(C, (Wh, 384), dr["w_qks"][:, 512 + h * 128:512 + (h + 1) * 128], 128, stg)
            load_w_bf16(C, (Wh, 512), dr["w_in"][:, vc:vc + 128], 128, stg)
            pb = C.nb()
            for k in range(8):
                P.op("pe", lambda e, k=k, pb=pb: e.matmul(pb[:, 0:T_CTX], lhsT=Wh[:, k, 256:384], rhs=HC[:, k, 1:1 + T_CTX],
                                                           start=(k == 0), stop=(k == 7)), reads=[Wh, HC], writes=[pb])
            P.op("act", lambda e, pb=pb: e.activation(out=KT[:, 0:T_CTX], in_=pb[:, 0:T_CTX], func=AF.Copy), writes=[pb, KT])
            for bi in range(NLB):
                rc, rs = load_rope(bi)
                proj_rope(C.nb(), 256, bi, rc, rs, [(slice(0, 128), KT[:, T_CTX + bi * NB:T_CTX + (bi + 1) * NB], KT)])
            for j in range(NKT):
                Hs, c0 = (HC, 1 + j * 128) if j < 2 else (HT, 1 + (j - 2) * 128)
                pv = C.nb()
                for k in range(8):
                    P.op("pe", lambda e, k=k, pv=pv, Hs=Hs, c0=c0: e.matmul(pv[:, 0:128], lhsT=Hs[:, k, c0:c0 + 128], rhs=Wh[:, k, 512:640],
                                                                           start=(k == 0), stop=(k == 7)), reads=[Wh, Hs], writes=[pv])
                eng = "act" if j % 2 == 0 else "dve"
                if eng == "act":
                    P.op("act", lambda e, pv=pv, j=j: e.activation(out=VT[:, j, :], in_=pv[:, 0:128], func=AF.Copy), writes=[pv, VT])
                else:
                    P.op("dve", lambda e, pv=pv, j=j: e.tensor_copy(out=VT[:, j, :], in_=pv[:, 0:128]), writes=[pv, VT])

            def q_proj(bi):
                rc, rs = load_rope(bi)
                qb = QB[bi % 2]
                proj_rope(bk[6], 0, bi, rc, rs, [(slice(0, 64), qb[0:64, 0:NB], qb), (slice(64, 128), qb[64:128, NB:2 * NB], qb)])

            q_proj(0)
            for bi in range(NLB):
                qb = QB[bi % 2]
                po, pl = bk[2 + bi % 2], bk[4 + bi % 2]

                def qk(j):
                    ps = bk[j % 2]
                    P.op("pe", lambda e, j=j, ps=ps: e.matmul(ps.ap(), lhsT=KT[:, j * 128:(j + 1) * 128], rhs=qb.ap(), start=True, stop=True),
                         reads=[KT, qb], writes=[ps])

                qk(0)
                qk(1)
                if bi + 1 < NLB:
                    q_proj(bi + 1)
                for j in range(NKT):
                    ps, pt = bk[j % 2], PT[j % 2]
                    P.op("act", lambda e, ps=ps, pt=pt: e.activation(out=pt.ap(), in_=ps.ap(), func=AF.Exp, scale=0.125), writes=[ps, pt])
                    P.op("pe", lambda e, j=j, pt=pt: e.matmul(po.ap(), lhsT=VT[:, j, :], rhs=pt.ap(), start=(j == 0), stop=(j == NKT - 1)),
                         reads=[VT, pt], writes=[po])
                    P.op("pe", lambda e, j=j, pt=pt: e.matmul(pl.ap(), lhsT=C.ones.ap(), rhs=pt.ap(), start=(j == 0), stop=(j == NKT - 1)),
                         reads=[C.ones, pt], writes=[pl])
                    if j + 2 < NKT:
                        qk(j + 2)
                P.op("dve", lambda e: e.reciprocal(out=rl.ap(), in_=pl.ap()), writes=[pl, rl])
                P.op("dve", lambda e: e.tensor_tensor(out=on_.ap(), in0=po.ap(), in1=rl.ap(), op=ALU.mult), reads=[rl], writes=[po, on_])
                P.op("dve", lambda e: e.scalar_tensor_tensor(out=dif.ap(), in0=on_[:, NB:2 * NB], scalar=nlam.ap(), in1=on_[:, 0:NB],
                                                             op0=ALU.mult, op1=ALU.add), reads=[on_, nlam], writes=[dif])
                P.op("pool", lambda e: e.tensor_tensor(out=dsq.ap(), in0=dif.ap(), in1=dif.ap(), op=ALU.mult), reads=[dif], writes=[dsq])
                pm = bk[6]
                P.op("pe", lambda e: e.matmul(pm[:, 0:NB], lhsT=C.mean128.ap(), rhs=dsq.ap(), start=True, stop=True),
                     reads=[C.mean128, dsq], writes=[pm])
                P.op("act", lambda e: e.activation(out=drs.ap(), in_=pm[:, 0:NB], func=AF.Sqrt, bias=C.epss.ap()), reads=[C.epss], writes=[pm, drs])
                P.op("dve", lambda e: e.reciprocal(out=drs.ap(), in_=drs.ap()), writes=[drs])
                P.op("pool", lambda e: e.tensor_tensor(out=dif.ap(), in0=dif.ap(), in1=drs.ap(), op=ALU.mult), reads=[drs], writes=[dif])
                o16 = ao16[bi % 2]
                P.op("pool", lambda e, o16=o16: e.tensor_scalar(out=o16.ap(), in0=dif.ap(), scalar1=sbw.ap(), scalar2=None, op0=ALU.mult),
                     reads=[dif, sbw], writes=[o16])
                mb = Buf("mt_%d_%d" % (4 + h, bi))
                C.mt_bufs[(4 + h, bi)] = mb
                P.dma("sp", dr["mt"][4 + h, :, bi * NB:(bi + 1) * NB], o16.ap(), reads=[o16], writes=[mb])


def stage_out_a(C):
    P, dr = C.P, C.dr
    C.x1_bufs, C.h2_bufs = {}, {}
    with P.scope():
        C.stg_i = 0
        stg = [P.sbuf("ostg%d" % i, [128, 8, 256], F32) for i in range(2)]
        WO = P.sbuf("WO", [128, 8, 1024], BF16)
        load_w_bf16(C, (WO, 0), dr["w_out"], 1024, stg)
        MTb = [P.sbuf("MTb%d" % i, [128, 8, NB], BF16) for i in range(2)]
        xts = [P.sbuf("oxt%d" % i, [128, 8, NB], F32) for i in range(2)]
        y32 = P.sbuf("oy32", [128, 8, NB], F32)
        x1 = [P.sbuf("ox1_%d" % i, [128, 8, NB], F32) for i in range(2)]
        h2 = [P.sbuf("oh2_%d" % i, [128, 8, NB], BF16) for i in range(2)]
        sq = P.sbuf("o_sq", [128, 8, NB], BF16)
        rs = P.sbuf("o_rs", [128, NB], F32)
        tmp = P.sbuf("o_tmp", [128, 8, NB], F32)
        for bi in range(NLB):
            t0 = bi * NB
            mtb, xt, x1b, h2b = MTb[bi % 2], xts[bi % 2], x1[bi % 2], h2[bi % 2]
            P.dma("sp", mtb.ap(), dr["mt"][:, :, t0:t0 + NB].rearrange("k p t -> p k t"),
                  reads=[C.mt_bufs[(k, bi)] for k in range(8)], writes=[mtb])
            P.dma("sp", xt.ap(), dr["xT"][:, t0:t0 + NB].rearrange("(k p) t -> p k t", p=128), writes=[xt])
            for j in range(8):
                py = C.nb()
                for k in range(8):
                    P.op("pe", lambda e, j=j, k=k, py=py: e.matmul(py[:, 0:NB], lhsT=WO[:, k, j * 128:(j + 1) * 128], rhs=mtb[:, k, :],
                                                                   start=(k == 0), stop=(k == 7)), reads=[WO, mtb], writes=[py])
                if j % 2 == 0:
                    P.op("act", lambda e, j=j, py=py: e.activation(out=y32[:, j, :], in_=py[:, 0:NB], func=AF.Copy), writes=[py, y32])
                else:
                    P.op("dve", lambda e, j=j, py=py: e.tensor_copy(out=y32[:, j, :], in_=py[:, 0:NB]), writes=[py, y32])
            P.op("act", lambda e: e.activation(out=sq.ap(), in_=y32.ap(), func=AF.Square), reads=[y32], writes=[sq])
            pb = C.nb()
            for k in range(8):
                P.op("pe", lambda e, k=k, pb=pb: e.matmul(pb[:, 0:NB], lhsT=C.ones.ap(), rhs=sq[:, k, :], start=(k == 0), stop=(k == 7)),
                     reads=[sq, C.ones], writes=[pb])
            P.op("act", lambda e, pb=pb: e.activation(out=rs.ap(), in_=pb[:, 0:NB], func=AF.Sqrt, scale=1.0 / D_MODEL, bias=C.eps6.ap()),
                 reads=[C.eps6], writes=[pb, rs])
            P.op("dve", lambda e: e.reciprocal(out=rs.ap(), in_=rs.ap()), writes=[rs])
            for j in range(8):
                P.op("dve", lambda e, j=j: e.scalar_tensor_tensor(out=tmp[:, j, :], in0=y32[:, j, :], scalar=C.G1[:, j:j + 1], in1=rs.ap(),
                                                                  op0=ALU.mult, op1=ALU.mult), reads=[y32, rs, C.G1], writes=[tmp])
            P.op("pool", lambda e, x1b=x1b, xt=xt: e.tensor_tensor(out=x1b.ap(), in0=tmp.ap(), in1=xt.ap(), op=ALU.add),
                 reads=[tmp, xt], writes=[x1b])
            xb = Buf("x1s_%d" % bi)
            C.x1_bufs[bi] = xb
            P.dma("sp", dr["x1s"][:, :, t0:t0 + NB].rearrange("k p t -> p k t"), x1b.ap(), reads=[x1b], writes=[xb])
            norm_block(C, x1b, NB, lambda k: C.A2[:, k:k + 1], lambda k: C.modT[:, 24 + k, 0:1],
                       lambda k, h2b=h2b: h2b[:, k, :], [C.A2, C.modT], h2b, (sq, rs, tmp))
            hb = Buf("h2s_%d" % bi)
            C.h2_bufs[bi] = hb
            P.dma("sp", dr["h2s"][:, :, t0:t0 + NB].rearrange("k p t -> p k t"), h2b.ap(), reads=[h2b], writes=[hb])


def stage_ffn(C):
    P, dr = C.P, C.dr
    NJ = D_FF // 128
    with P.scope():
        C.stg_i = 0
        WU = P.sbuf("WU", [128, 8, 2 * D_FF], BF16)
        WD = P.sbuf("WD", [128, NJ, 1024], BF16)
        with P.scope():
            stg = [P.sbuf("fstg%d" % i, [128, 8, 256], F32) for i in range(2)]
            load_w_bf16(C, (WU, 0), dr["w_up"], 2 * D_FF, stg)
            for j in range(NJ):
                s_ = stg[C.stg_i % 2]
                C.stg_i += 1
                sv = s_.ap().rearrange("p a b -> p (a b)")[:, 0:1024]
                P.dma("sp", sv, dr["w_down"][j * 128:(j + 1) * 128, :], writes=[s_])
                P.op("pool", lambda e, j=j, sv=sv: e.tensor_copy(out=WD[:, j, :], in_=sv), reads=[s_], writes=[WD])
        FCW = P.sbuf("FCW", [128, 2 * NJ, 3], F32)
        FB = P.sbuf("FB", [128, 2 * NJ], F32)
        C.cwb = FCW
        P.dma("sp", FCW.ap(), dr["fconvT"].rearrange("(c p) j -> p c j", p=128), writes=[FCW])
        P.dma("sp", FB.ap(), dr["fbias"], writes=[FB])
        h2t = [P.sbuf("fh2_%d" % i, [128, 8, NB + 2], BF16) for i in range(2)]
        x1t = [P.sbuf("fx1_%d" % i, [128, 8, NB], F32) for i in range(2)]
        uv = [P.sbuf("fuv%d" % i, [128, NB], F32) for i in range(2)]
        ug = [P.sbuf("fug%d" % i, [128, NB], F32) for i in range(2)]
        sg = [P.sbuf("fsg%d" % i, [128, NB], F32) for i in range(2)]
        act16 = P.sbuf("fact", [128, NJ, NB], BF16)
        f32t = P.sbuf("ff32", [128, 8, NB], F32)
        sq = P.sbuf("f_sq", [128, 8, NB], BF16)
        rs = P.sbuf("f_rs", [128, NB], F32)
        for bi in range(NLB):
            t0 = bi * NB
            hb, xb = h2t[bi % 2], x1t[bi % 2]
            lo = max(t0 - 1, 0)
            hi = min(t0 + NB + 1, T_LAT)
            rd = [C.h2_bufs[b] for b in (bi - 1, bi, bi + 1) if 0 <= b < NLB]
            if bi == 0:
                P.op("pool", lambda e, hb=hb: e.memset(hb[:, :, 0:1], 0.0), writes=[hb])
            if bi == NLB - 1:
                P.op("pool", lambda e, hb=hb: e.memset(hb[:, :, NB + 1:NB + 2], 0.0), writes=[hb])
            d0 = lo - (t0 - 1)
            P.dma("sp", hb[:, :, d0:d0 + (hi - lo)], dr["h2s"][:, :, lo:hi].rearrange("k p t -> p k t"), reads=rd, writes=[hb])
            P.dma("sp", xb.ap(), dr["x1s"][:, :, t0:t0 + NB].rearrange("k p t -> p k t"), reads=[C.x1_bufs[bi]], writes=[xb])
            for j in range(NJ):
                u, g, s2 = uv[j % 2], ug[j % 2], sg[j % 2]
                proj_conv(C, hb, 0, WU, j * 128, 128, FCW[:, j, :], u.ap(), u, bias=FB[:, j:j + 1], bias_buf=FB)
                proj_conv(C, hb, 0, WU, D_FF + j * 128, 128, FCW[:, NJ + j, :], g.ap(), g, bias=FB[:, NJ + j:NJ + j + 1], bias_buf=FB)
                P.op("act", lambda e, g=g, s2=s2: e.activation(out=s2.ap(), in_=g.ap(), func=AF.Silu), reads=[g], writes=[s2])
                P.op("pool", lambda e, j=j, u=u, s2=s2: e.tensor_tensor(out=act16[:, j, :], in0=u.ap(), in1=s2.ap(), op=ALU.mult),
                     reads=[u, s2], writes=[act16])
            for i in range(8):
                pf = C.nb()
                for j in range(NJ):
                    P.op("pe", lambda e, i=i, j=j, pf=pf: e.matmul(pf[:, 0:NB], lhsT=WD[:, j, i * 128:(i + 1) * 128], rhs=act16[:, j, :],
                                                                   start=(j == 0), stop=(j == NJ - 1)), reads=[WD, act16], writes=[pf])
                if i % 2 == 0:
                    P.op("act", lambda e, i=i, pf=pf: e.activation(out=f32t[:, i, :], in_=pf[:, 0:NB], func=AF.Copy), writes=[pf, f32t])
                else:
                    P.op("dve", lambda e, i=i, pf=pf: e.tensor_copy(out=f32t[:, i, :], in_=pf[:, 0:NB]), writes=[pf, f32t])
            P.op("act", lambda e: e.activation(out=sq.ap(), in_=f32t.ap(), func=AF.Square), reads=[f32t], writes=[sq])
            pb = C.nb()
            for k in range(8):
                P.op("pe", lambda e, k=k, pb=pb: e.matmul(pb[:, 0:NB], lhsT=C.ones.ap(), rhs=sq[:, k, :], start=(k == 0), stop=(k == 7)),
                     reads=[sq, C.ones], writes=[pb])
            P.op("act", lambda e, pb=pb: e.activation(out=rs.ap(), in_=pb[:, 0:NB], func=AF.Sqrt, scale=1.0 / D_MODEL, bias=C.eps6.ap()),
                 reads=[C.eps6], writes=[pb, rs])
            P.op("dve", lambda e: e.reciprocal(out=rs.ap(), in_=rs.ap()), writes=[rs])
            for i in range(8):
                P.op("dve", lambda e, i=i: e.scalar_tensor_tensor(out=f32t[:, i, :], in0=f32t[:, i, :], scalar=C.G2[:, i:i + 1], in1=rs.ap(),
                                                                  op0=ALU.mult, op1=ALU.mult), reads=[rs, C.G2], writes=[f32t])
            P.op("pool", lambda e, xb=xb: e.tensor_tensor(out=xb.ap(), in0=f32t.ap(), in1=xb.ap(), op=ALU.add),
                 reads=[f32t], writes=[xb])
            P.dma("sp", dr["outT"][:, t0:t0 + NB].rearrange("(k p) t -> p k t", p=128), xb.ap(), reads=[xb], final=True)
```

```python
import contextlib
import numpy as np
import concourse.bass as bass
import concourse.mybir as mybir
from concourse.bass_utils import run_bass_kernel_spmd

F32 = mybir.dt.float32
BF16 = mybir.dt.bfloat16
AF = mybir.ActivationFunctionType
ALU = mybir.AluOpType
AX = mybir.AxisListType

SEM_ROLL = 30000


class Buf:
    def __init__(self, name, t=None):
        self.name = name
        self.t = t
        self.last_w = None
        self.readers = []
        self.dma_sem = None
        self.dma_cnt = 0

    def ap(self):
        return self.t[:]

    def __getitem__(self, idx):
        return self.t[idx]


class _Rec:
    def __getattr__(self, name):
        return lambda *a, **k: (name, a, k)


_REC = _Rec()


def _bind(fn):
    name, a, k = fn(_REC)
    return lambda e: getattr(e, name)(*a, **k)


class Op:
    __slots__ = ("eng", "fn", "deps", "signal", "token", "is_dma", "sem_buf", "final", "idx")


class Prog:
    ENGS = ("pe", "act", "dve", "pool", "sp")

    def __init__(self, nc):
        self.nc = nc
        self.stack = contextlib.ExitStack()
        self.ops = {e: [] for e in self.ENGS}
        self.nbuf = 0
        self.all_ops = []
        self.final_ops = []

    def sbuf(self, name, shape, dtype):
        self.nbuf += 1
        name = "s%d_%s" % (self.nbuf, name)
        t = self.stack.enter_context(self.nc.sbuf_tensor(name, list(shape), dtype))
        b = Buf(name, t)
        b.readers = list(getattr(self, "fence", []))
        if hasattr(self, "scope_bufs") and self.scope_bufs:
            self.scope_bufs[-1].append(b)
        return b

    def psum(self, name, shape, dtype):
        t = self.stack.enter_context(self.nc.psum_tensor(name, list(shape), dtype))
        return Buf(name, t)

    def view(self, name):
        return Buf(name)

    @contextlib.contextmanager
    def scope(self):
        old = self.stack
        self.stack = contextlib.ExitStack()
        if not hasattr(self, "scope_bufs"):
            self.scope_bufs = []
        self.scope_bufs.append([])
        try:
            yield
        finally:
            self.stack.close()
            self.stack = old
            bufs = self.scope_bufs.pop()
            ops = list(getattr(self, "fence", []))
            for b in bufs:
                if b.last_w is not None:
                    ops.append(b.last_w)
                ops.extend(b.readers)
            best = {}
            dmas = {}
            for o in ops:
                if o.is_dma:
                    dmas[id(o)] = o
                else:
                    if o.eng not in best or best[o.eng].idx < o.idx:
                        best[o.eng] = o
            self.fence = list(best.values()) + list(dmas.values())

    def _deps(self, o, reads, writes):
        deps = []
        for b in list(reads) + list(writes):
            if b.last_w is not None:
                deps.append(b.last_w)
        for b in writes:
            deps.extend(b.readers)
        for b in writes:
            b.last_w = o
            b.readers = []
        for b in reads:
            b.readers.append(o)
        seen = set()
        out = []
        for d in deps:
            if id(d) in seen or d is o:
                continue
            seen.add(id(d))
            if d.eng == "pe" and o.eng == "pe" and not d.is_dma and not o.is_dma:
                continue
            d.signal = True
            out.append(d)
        o.deps = out

    def op(self, eng, fn, reads=(), writes=()):
        o = Op()
        o.eng = eng
        o.fn = _bind(fn)
        o.signal = False
        o.token = None
        o.is_dma = False
        o.sem_buf = None
        o.final = False
        o.idx = len(self.ops[eng])
        self._deps(o, reads, writes)
        self.ops[eng].append(o)
        return o

    def dma(self, eng, out, in_, reads=(), writes=(), final=False):
        o = Op()
        o.eng = eng
        o.fn = lambda e: e.dma_start(out=out, in_=in_)
        o.signal = True
        o.token = None
        o.is_dma = True
        o.final = final
        cands = [b for b in list(writes) + list(reads) if b.t is not None]
        sb = cands[0] if cands else (list(writes) + list(reads))[0]
        o.sem_buf = sb
        o.idx = len(self.ops[eng])
        self._deps(o, reads, writes)
        self.ops[eng].append(o)
        if final:
            self.final_ops.append(o)
        return o

    def emit(self):
        nc = self.nc
        st = self.stack
        eng_sems = {}
        for e in self.ENGS:
            n = 0
            for o in self.ops[e]:
                if o.is_dma:
                    b = o.sem_buf
                    if b.dma_sem is None:
                        b.dma_sem = st.enter_context(nc.semaphore("d_" + b.name))
                    b.dma_cnt += 16
                    o.token = (b.dma_sem, b.dma_cnt, 16)
                elif o.signal:
                    k = n // SEM_ROLL
                    if (e, k) not in eng_sems:
                        eng_sems[(e, k)] = st.enter_context(nc.semaphore("s_%s_%d" % (e, k)))
                    o.token = (eng_sems[(e, k)], n % SEM_ROLL + 1, 1)
                    n += 1
        print("ops:", {e: len(self.ops[e]) for e in self.ENGS}, "signals:", {e: sum(1 for o in self.ops[e] if o.token is not None) for e in self.ENGS}, "nsems", len(eng_sems))
        all_sems = list(eng_sems.values())
        seen_b = set()
        for e in self.ENGS:
            for o in self.ops[e]:
                if o.is_dma and id(o.sem_buf) not in seen_b:
                    seen_b.add(id(o.sem_buf))
                    all_sems.append(o.sem_buf.dma_sem)
        with nc.Block() as blk0:
            @blk0.sync
            def _(eng):
                for sm in all_sems:
                    eng.sem_clear(sm)
        block = st.enter_context(nc.Block())
        hooks = {"pe": block.tensor, "act": block.scalar, "dve": block.vector,
                 "pool": block.gpsimd, "sp": block.sync}
        final_ops = self.final_ops

        def make(e):
            ops = self.ops[e]

            def body(eng):
                waited = {}
                for o in ops:
                    for d in o.deps:
                        sem, val, _ = d.token
                        if waited.get(id(sem), 0) < val:
                            eng.wait_ge(sem, val)
                            waited[id(sem)] = val
                    ins = o.fn(eng)
                    if o.token is not None:
                        ins.then_inc(o.token[0], o.token[2])
                if e == "sp":
                    for o in final_ops:
                        sem, val, _ = o.token
                        eng.wait_ge(sem, val)
            return body

        for e in self.ENGS:
            hooks[e](make(e))
        st.close()


D_MODEL = 1024
T_LAT = 4096
T_CTX = 256
NB = 256
NLB = T_LAT // NB
D_FF = 2816
EXPM05 = float(np.exp(-0.5))


class Ctx:
    pass


def build_program(nc, stage=99):
    P = Prog(nc)
    C = Ctx()
    C.P = P
    C.nc = nc
    dr = {}

    def din(name, shape, dt=F32):
        dr[name] = nc.dram_tensor(name, list(shape), dt, kind="ExternalInput").ap()

    din("xT", [1024, T_LAT]); din("ctxT", [1024, T_CTX]); din("cc", [128, 8, 2])
    din("w_mod", [1024, 6144]); din("b_modT", [128, 48]); din("gvec", [128, 4, 8])
    din("w_in", [1024, 3232]); din("w_qks", [1024, 1024]); din("convT", [1696, 3])
    din("lw2", [64, 2, 512]); din("w0a0", [128, 4, 4]); din("kvec", [128, 5, 4]); din("g2", [96, 512])
    din("lam", [128, 4, 64]); din("subln", [128, 1])
    din("w_out", [1024, 1024]); din("w_up", [1024, 5632]); din("fconvT", [5632, 3]); din("fbias", [128, 44])
    din("w_down", [2816, 1024]); din("ropeC", [128, T_LAT]); din("ropeS", [128, T_LAT]); din("blkmask", [128, 5, 128])
    dr["outT"] = nc.dram_tensor("outT", [1024, T_LAT], F32, kind="ExternalOutput").ap()
    dr["yf"] = nc.dram_tensor("yf_scr", [4, 128, T_LAT], F32).ap()
    dr["mt"] = nc.dram_tensor("mt_scr", [8, 128, T_LAT], BF16).ap()
    dr["x1s"] = nc.dram_tensor("x1_scr", [8, 128, T_LAT], F32).ap()
    dr["h2s"] = nc.dram_tensor("h2_scr", [8, 128, T_LAT], BF16).ap()
    if stage < 99:
        dr["dbg"] = nc.dram_tensor("dbg", [128, 8, T_LAT + 2], F32, kind="ExternalOutput").ap()
    C.dr = dr
    C.yf_bufs = {}
    C.mt_bufs = {}

    C.banks = [P.psum("pb%d" % i, [128, 512], F32) for i in range(7)]
    C.bankT = P.psum("pbT", [128, 1024], BF16)
    C.bank_i = 0

    def nb():
        b = C.banks[C.bank_i % 7]
        C.bank_i += 1
        return b
    C.nb = nb

    ident = P.sbuf("ident", [128, 128], BF16)
    ones = P.sbuf("ones", [128, 128], BF16)
    bones = P.sbuf("bones", [128, 128], BF16)
    bmean = P.sbuf("bmean", [128, 128], BF16)
    mean128 = P.sbuf("mean128", [128, 128], BF16)
    eps6 = P.sbuf("eps6", [128, 1], F32)
    epsx = P.sbuf("epsx", [128, 1], F32)
    epss = P.sbuf("epss", [128, 1], F32)
    P.op("pool", lambda e: e.memset(ident.ap(), 0.0), writes=[ident])
    P.op("pool", lambda e: e.affine_select(out=ident.ap(), in_=ident.ap(), pattern=[[-1, 128]],
                                           compare_op=ALU.not_equal, fill=1.0, base=0, channel_multiplier=1),
         reads=[ident], writes=[ident])
    P.op("pool", lambda e: e.memset(ones.ap(), 1.0), writes=[ones])
    P.op("pool", lambda e: e.memset(mean128.ap(), 1.0 / 128), writes=[mean128])
    P.op("pool", lambda e: e.memset(bones.ap(), 0.0), writes=[bones])
    P.op("pool", lambda e: e.memset(bones[0:64, 0:64], 1.0), writes=[bones])
    P.op("pool", lambda e: e.memset(bones[64:128, 64:128], 1.0), writes=[bones])
    P.op("pool", lambda e: e.memset(bmean.ap(), 0.0), writes=[bmean])
    P.op("pool", lambda e: e.memset(bmean[0:64, 0:64], 1.0 / 64), writes=[bmean])
    P.op("pool", lambda e: e.memset(bmean[64:128, 64:128], 1.0 / 64), writes=[bmean])
    P.op("pool", lambda e: e.memset(eps6.ap(), 1e-6), writes=[eps6])
    P.op("pool", lambda e: e.memset(epsx.ap(), 64e-5), writes=[epsx])
    P.op("pool", lambda e: e.memset(epss.ap(), 1e-5), writes=[epss])
    C.ident, C.ones, C.bones, C.bmean, C.mean128 = ident, ones, bones, bmean, mean128
    C.eps6, C.epsx, C.epss = eps6, epsx, epss

    stage_mod(C)
    import os
    with P.scope():
        C.HT = P.sbuf("HT", [128, 8, T_LAT + 2], BF16)
        C.HC = P.sbuf("HC", [128, 8, T_CTX + 2], BF16)
        for Hb, n in ((C.HT, T_LAT), (C.HC, T_CTX)):
            P.op("pool", lambda e, Hb=Hb: e.memset(Hb[:, :, 0:1], 0.0), writes=[Hb])
            P.op("pool", lambda e, Hb=Hb, n=n: e.memset(Hb[:, :, n + 1:n + 2], 0.0), writes=[Hb])
        stage_prenorm(C)
        if stage == 1:
            dbg_dump_HT(C)
            P.emit()
            return nc
        if not os.environ.get("SKIP_RWKV"):
            stage_rwkv(C)
        if stage == 21:
            with P.scope():
                for k in range(4):
                    tmp = P.sbuf("dbgy%d" % k, [128, T_LAT], F32)
                    P.dma("sp", tmp.ap(), dr["yf"][k], reads=list(C.yf_bufs.values()), writes=[tmp])
                    P.dma("sp", dr["dbg"][:, k, 0:T_LAT], tmp.ap(), reads=[tmp], final=True)
            P.emit()
            return nc
        if stage == 2:
            dbg_dump_mt(C, 0, 4)
            P.emit()
            return nc
        stage_attn(C)
        if stage == 3:
            dbg_dump_mt(C, 4, 8)
            P.emit()
            return nc
        stage_out_a(C)
    stage_ffn(C)
    P.emit()
    return nc


def dbg_dump_HT(C):
    P = C.P
    with P.scope():
        for k in range(8):
            tmp = P.sbuf("dbgt%d" % k, [128, T_LAT + 2], F32)
            P.op("dve", lambda e, k=k, tmp=tmp: e.tensor_copy(out=tmp.ap(), in_=C.HT[:, k, :]), reads=[C.HT], writes=[tmp])
            P.dma("sp", C.dr["dbg"][:, k, :], tmp.ap(), reads=[tmp], final=True)


def dbg_dump_mt(C, k0, k1):
    P = C.P
    with P.scope():
        for k in range(k0, k1):
            tb = P.sbuf("dbgb%d" % k, [128, T_LAT], BF16)
            tmp = P.sbuf("dbgt%d" % k, [128, T_LAT], F32)
            rd = [b for (kk, _), b in C.mt_bufs.items() if kk == k] + list(C.yf_bufs.values())
            P.dma("sp", tb.ap(), C.dr["mt"][k], reads=rd, writes=[tb])
            P.op("dve", lambda e, tmp=tmp, tb=tb: e.tensor_copy(out=tmp.ap(), in_=tb.ap()), reads=[tb], writes=[tmp])
            P.dma("sp", C.dr["dbg"][:, k, 0:T_LAT], tmp.ap(), reads=[tmp], final=True)


def stage_mod(C):
    P, dr = C.P, C.dr
    cc = P.sbuf("cc", [128, 8, 2], F32)
    scc = P.sbuf("scc", [128, 8, 2], F32)
    bm = P.sbuf("bmodT", [128, 48], F32)
    gv = P.sbuf("gvec", [128, 4, 8], F32)
    modT = P.sbuf("modT", [128, 48, 2], F32)
    P.dma("sp", cc.ap(), dr["cc"], writes=[cc])
    P.dma("sp", bm.ap(), dr["b_modT"], writes=[bm])
    P.dma("sp", gv.ap(), dr["gvec"], writes=[gv])
    P.op("act", lambda e: e.activation(out=scc.ap(), in_=cc.ap(), func=AF.Silu), reads=[cc], writes=[scc])
    pm = C.nb()
    with P.scope():
        wst = [P.sbuf("wmst%d" % i, [128, 8, 512], F32) for i in range(2)]
        for g in range(12):
            w = wst[g % 2]
            P.dma("sp", w.ap(), dr["w_mod"][:, g * 512:(g + 1) * 512].rearrange("(k p) n -> p k n", p=128), writes=[w])
            for jj in range(4):
                j = g * 4 + jj
                for k in range(8):
                    P.op("pe", lambda e, w=w, k=k, jj=jj, j=j: e.matmul(
                        pm[:, 2 * j:2 * j + 2], lhsT=w[:, k, jj * 128:(jj + 1) * 128], rhs=scc[:, k, :],
                        start=(k == 0), stop=(k == 7)), reads=[w, scc], writes=[pm])
    pmv = pm[:, 0:96].rearrange("p (j c) -> p j c", c=2)
    for c in range(2):
        P.op("dve", lambda e, c=c: e.tensor_tensor(out=modT[:, :, c], in0=pmv[:, :, c], in1=bm.ap(), op=ALU.add),
             reads=[bm], writes=[pm, modT])
    C.modT = modT
    A1 = P.sbuf("A1", [128, 2, 8], F32)
    G1 = P.sbuf("G1", [128, 8], F32)
    A2 = P.sbuf("A2", [128, 8], F32)
    G2 = P.sbuf("G2", [128, 8], F32)
    for c in range(2):
        P.op("dve", lambda e, c=c: e.scalar_tensor_tensor(out=A1[:, c, :], in0=modT[:, 8:16, c], scalar=1.0,
                                                          in1=gv[:, 0, :], op0=ALU.add, op1=ALU.mult),
             reads=[modT, gv], writes=[A1])
    P.op("dve", lambda e: e.tensor_tensor(out=G1.ap(), in0=modT[:, 16:24, 0], in1=gv[:, 1, :], op=ALU.mult),
         reads=[modT, gv], writes=[G1])
    P.op("dve", lambda e: e.scalar_tensor_tensor(out=A2.ap(), in0=modT[:, 32:40, 0], scalar=1.0, in1=gv[:, 2, :],
                                                 op0=ALU.add, op1=ALU.mult), reads=[modT, gv], writes=[A2])
    P.op("dve", lambda e: e.tensor_tensor(out=G2.ap(), in0=modT[:, 40:48, 0], in1=gv[:, 3, :], op=ALU.mult),
         reads=[modT, gv], writes=[G2])
    C.A1, C.G1, C.A2, C.G2 = A1, G1, A2, G2


def norm_block(C, xt, nbk, A_ap, sh_ap, out_ap, xt_reads, out_buf, tmpn):
    P = C.P
    sq, rs, tmp = tmpn
    P.op("act", lambda e: e.activation(out=sq[:, :, 0:nbk], in_=xt[:, :, 0:nbk], func=AF.Square),
         reads=[xt], writes=[sq])
    pb = C.nb()
    for k in range(8):
        P.op("pe", lambda e, k=k: e.matmul(pb[:, 0:nbk], lhsT=C.ones.ap(), rhs=sq[:, k, 0:nbk],
                                            start=(k == 0), stop=(k == 7)), reads=[sq, C.ones], writes=[pb])
    P.op("act", lambda e: e.activation(out=rs[:, 0:nbk], in_=pb[:, 0:nbk], func=AF.Sqrt, scale=1.0 / D_MODEL,
                                       bias=C.eps6.ap()), reads=[C.eps6], writes=[pb, rs])
    P.op("dve", lambda e: e.reciprocal(out=rs[:, 0:nbk], in_=rs[:, 0:nbk]), writes=[rs])
    for k in range(8):
        P.op("dve", lambda e, k=k: e.scalar_tensor_tensor(out=tmp[:, k, 0:nbk], in0=xt[:, k, 0:nbk], scalar=A_ap(k),
                                                          in1=rs[:, 0:nbk], op0=ALU.mult, op1=ALU.mult),
             reads=[xt, rs] + xt_reads, writes=[tmp])
        P.op("act", lambda e, k=k: e.activation(out=out_ap(k), in_=tmp[:, k, 0:nbk], func=AF.Identity,
                                                bias=sh_ap(k)), reads=[tmp] + xt_reads, writes=[out_buf])


def stage_prenorm(C):
    P, dr = C.P, C.dr
    with P.scope():
        xts = [P.sbuf("xt%d" % i, [128, 8, NB], F32) for i in range(2)]
        sq = P.sbuf("pn_sq", [128, 8, NB], BF16)
        rs = P.sbuf("pn_rs", [128, NB], F32)
        tmp = P.sbuf("pn_tmp", [128, 8, NB], F32)
        blocks = [("c", 0)] + [("l", i) for i in range(NLB)]
        for bi, (src, i) in enumerate(blocks):
            xt = xts[bi % 2]
            if src == "c":
                P.dma("sp", xt.ap(), dr["ctxT"].rearrange("(k p) t -> p k t", p=128), writes=[xt])
                H, cidx, t0 = C.HC, 1, 0
            else:
                P.dma("sp", xt.ap(), dr["xT"][:, i * NB:(i + 1) * NB].rearrange("(k p) t -> p k t", p=128), writes=[xt])
                H, cidx, t0 = C.HT, 0, i * NB
            norm_block(C, xt, NB, lambda k, cidx=cidx: C.A1[:, cidx, k:k + 1],
                       lambda k, cidx=cidx: C.modT[:, k, cidx:cidx + 1],
                       lambda k, H=H, t0=t0: H[:, k, 1 + t0:1 + t0 + NB], [C.A1, C.modT], H, (sq, rs, tmp))


def _rope_tables():
    n_pair = 16
    rows = T_LAT // 64
    row = np.repeat(np.arange(rows, dtype=np.float32), 64)
    col = np.tile(np.arange(64, dtype=np.float32), rows)
    inv = (np.float32(10000.0) ** (-np.arange(n_pair, dtype=np.float32) / np.float32(n_pair))).astype(np.float32)
    ang = np.concatenate([row[:, None] * inv, col[:, None] * inv], axis=-1).astype(np.float32)
    cos = np.cos(ang).astype(np.float32).T
    sin = np.sin(ang).astype(np.float32).T
    Cc = np.concatenate([cos, cos, cos, cos], axis=0)
    Ss = np.concatenate([-sin, sin, -sin, sin], axis=0)
    return np.ascontiguousarray(Cc), np.ascontiguousarray(Ss)


def prep_inputs(inp):
    f = lambda a: np.ascontiguousarray(np.asarray(a, dtype=np.float32))
    x, c, ctx, c_ctx = f(inp["x"]), f(inp["c"]), f(inp["ctx"]), f(inp["c_ctx"])
    pk = lambda v, n: f(v.reshape(n, 128).T)
    sh = {}
    sh["w_mod"] = f(inp["w_mod"][0])
    sh["b_modT"] = pk(f(inp["b_mod"][0]), 48)
    sh["gvec"] = f(np.stack([pk(f(inp[n][0]), 8) for n in ("g_pre_mix", "g_post_mix", "g_pre_ffn", "g_post_ffn")], axis=1))
    w_in = f(inp["w_in"][0])
    sh["w_in"] = w_in
    perm = np.arange(512).reshape(8, 2, 32)[:, ::-1, :].reshape(512)
    qc, kc = 1696, 1696 + 512
    sh["w_qks"] = f(np.concatenate([w_in[:, qc:qc + 512][:, perm], w_in[:, kc:kc + 512][:, perm]], axis=1))
    sh["convT"] = f(inp["rwkv_conv"][0].T)
    sh["lw2"] = f(np.stack([np.concatenate([f(inp["w2_fwd"][0]), f(inp["a2_fwd"][0])], 0),
                            np.concatenate([f(inp["w2_bwd"][0]), f(inp["a2_bwd"][0])], 0)], axis=1))
    sh["w0a0"] = f(np.stack([pk(f(inp[n][0]), 4) for n in ("w0_fwd", "w0_bwd", "a0_fwd", "a0_bwd")], axis=1))
    sh["kvec"] = f(np.stack([pk(f(inp[n][0]).reshape(512), 4) for n in ("k_k", "k_a", "r_k", "ln_x_w", "ln_x_b")], axis=1))
    sh["g2"] = f(inp["g2"][0])
    sh["lam"] = f(np.broadcast_to(np.stack([f(inp[n][0]) for n in ("lam_q1", "lam_k1", "lam_q2", "lam_k2")], 0)[None], (128, 4, 64)))
    sh["subln"] = f(inp["subln_w"][0].reshape(128, 1))
    sh["w_out"] = f(inp["w_out"][0])
    sh["w_up"] = f(inp["w_up"][0])
    sh["fconvT"] = f(inp["ffn_conv"][0].T)
    sh["fbias"] = pk(f(inp["ffn_conv_b"][0]), 44)
    sh["w_down"] = f(inp["w_down"][0])
    sh["ropeC"], sh["ropeS"] = _rope_tables()
    ii = np.arange(128)
    dm = lambda sz: (ii[:, None] // sz == ii[None, :] // sz).astype(np.float32)
    sh["blkmask"] = f(np.stack([dm(8), dm(16) - dm(8), dm(32) - dm(16), dm(64) - dm(32), dm(128) - dm(64)], axis=1))
    maps = []
    for b in range(8):
        m = dict(sh)
        m["xT"] = f(x[b].T)
        m["ctxT"] = f(ctx[b].T)
        m["cc"] = f(np.stack([c[b], c_ctx], -1).reshape(8, 128, 2).transpose(1, 0, 2))
        maps.append(m)
    return maps


def kernel(**inputs):
    maps = prep_inputs(inputs)
    nc = bass.Bass("TRN2", target_bir_lowering=False)
    build_program(nc)
    res = run_bass_kernel_spmd(nc, maps, core_ids=list(range(8)))
    out = np.stack([np.ascontiguousarray(res.results[b]["outT"].T) for b in range(8)], 0)
    return out.astype(np.float32)


def load_w_bf16(C, dst, dram_cols_ap, ncols, stg):
    P = C.P
    for i, c0 in enumerate(range(0, ncols, 256)):
        c1 = min(ncols, c0 + 256)
        s = stg[C.stg_i % 2]
        C.stg_i += 1
        P.dma("sp", s[:, :, 0:c1 - c0], dram_cols_ap[:, c0:c1].rearrange("(k p) n -> p k n", p=128), writes=[s])
        P.op("pool", lambda e, s=s, c0=c0, c1=c1: e.tensor_copy(out=dst[0][:, :, dst[1] + c0:dst[1] + c1], in_=s[:, :, 0:c1 - c0]),
             reads=[s], writes=[dst[0]])


def proj_conv(C, H, t0, W, c0, m, cw, out, out_buf, bias=None, eng2="dve", bias_buf=None):
    P = C.P
    pb = C.nb()
    for k in range(8):
        P.op("pe", lambda e, k=k: e.matmul(pb[0:m, 0:NB + 2], lhsT=W[:, k, c0:c0 + m], rhs=H[:, k, t0:t0 + NB + 2],
                                            start=(k == 0), stop=(k == 7)), reads=[W, H], writes=[pb])
    if bias is None:
        P.op("act", lambda e: e.activation(out=out, in_=pb[0:m, 1:NB + 1], func=AF.Copy, scale=cw[:, 1:2]),
             reads=[C.cwb], writes=[pb, out_buf])
    else:
        P.op("act", lambda e: e.activation(out=out, in_=pb[0:m, 1:NB + 1], func=AF.Identity, scale=cw[:, 1:2], bias=bias),
             reads=[C.cwb, bias_buf], writes=[pb, out_buf])
    P.op(eng2, lambda e: e.scalar_tensor_tensor(out=out, in0=pb[0:m, 0:NB], scalar=cw[:, 0:1], in1=out,
                                                op0=ALU.mult, op1=ALU.add), reads=[C.cwb], writes=[pb, out_buf])
    P.op(eng2, lambda e: e.scalar_tensor_tensor(out=out, in0=pb[0:m, 2:NB + 2], scalar=cw[:, 2:3], in1=out,
                                                op0=ALU.mult, op1=ALU.add), reads=[C.cwb], writes=[pb, out_buf])


def stage_rwkv(C):
    P, dr = C.P, C.dr
    C.stg_i = 0
    with P.scope():
        stg = [P.sbuf("wstg%d" % i, [128, 8, 256], F32) for i in range(2)]
        WR = P.sbuf("WR", [128, 8, 1696], BF16)
        load_w_bf16(C, (WR, 0), dr["w_in"][:, 0:1696], 1696, stg)
        CW = P.sbuf("CW", [128, 14, 3], F32)
        C.cwb = CW
        P.dma("sp", CW[:, 0:12, :], dr["convT"][0:1536, :].rearrange("(c p) j -> p c j", p=128), writes=[CW])
        P.dma("sp", CW[0:64, 12, :], dr["convT"][1536:1600, :], writes=[CW])
        P.dma("sp", CW[0:96, 13, :], dr["convT"][1600:1696, :], writes=[CW])
        lw2f = P.sbuf("lw2f", [64, 2, 512], F32)
        LW2 = P.sbuf("LW2", [64, 2, 512], BF16)
        P.dma("sp", lw2f.ap(), dr["lw2"], writes=[lw2f])
        P.op("pool", lambda e: e.tensor_copy(out=LW2.ap(), in_=lw2f.ap()), reads=[lw2f], writes=[LW2])
        g2f = P.sbuf("g2f", [96, 512], F32)
        G2W = P.sbuf("G2W", [96, 512], BF16)
        P.dma("sp", g2f.ap(), dr["g2"], writes=[g2f])
        P.op("pool", lambda e: e.tensor_copy(out=G2W.ap(), in_=g2f.ap()), reads=[g2f], writes=[G2W])
        w0a0 = P.sbuf("w0a0", [128, 4, 4], F32)
        kvec = P.sbuf("kvec", [128, 5, 4], F32)
        P.dma("sp", w0a0.ap(), dr["w0a0"], writes=[w0a0])
        P.dma("sp", kvec.ap(), dr["kvec"], writes=[kvec])
        omka = P.sbuf("omka", [128, 4], F32)
        rkh = P.sbuf("rkh", [128, 4], F32)
        P.op("dve", lambda e: e.tensor_scalar(out=omka.ap(), in0=kvec[:, 1, :], scalar1=-1.0, scalar2=1.0,
                                              op0=ALU.mult, op1=ALU.add), reads=[kvec], writes=[omka])
        P.op("dve", lambda e: e.tensor_scalar(out=rkh.ap(), in0=kvec[:, 2, :], scalar1=0.5, scalar2=None,
                                              op0=ALU.mult), reads=[kvec], writes=[rkh])
        ones32 = P.sbuf("ones32", [128, 128], F32)
        P.op("pool", lambda e: e.memset(ones32.ap(), 1.0), writes=[ones32])
        msk = {}
        for nm, cop, sgn in (("SU", ALU.is_gt, -1), ("IU", ALU.is_ge, -1), ("SL", ALU.is_gt, 1), ("IL", ALU.is_ge, 1)):
            mb = P.sbuf("m" + nm, [128, 128], F32)
            P.op("pool", lambda e, mb=mb, cop=cop, sgn=sgn: e.affine_select(out=mb.ap(), in_=ones32.ap(), pattern=[[-sgn, 128]],
                                                                            compare_op=cop, fill=0.0, base=0, channel_multiplier=sgn),
                 reads=[ones32], writes=[mb])
            msk[nm] = mb
        AMM = []
        for d, (s_, i_) in enumerate((("SU", "IU"), ("SL", "IL"))):
            am = P.sbuf("amm%d" % d, [128, 4, 128], F32)
            for q, nm in enumerate((s_, i_, s_, i_)):
                P.op("pool", lambda e, am=am, q=q, nm=nm: e.tensor_copy(out=am[:, q, :], in_=msk[nm].ap()),
                     reads=[msk[nm]], writes=[am])
            AMM.append(am)
        NTM = [msk["SL"], msk["SU"]]
        rmask = P.sbuf("rmask", [128, NB], F32)
        P.op("pool", lambda e: e.memset(rmask.ap(), 1.0), writes=[rmask])
        for c in range(NB // 128):
            P.op("pool", lambda e, c=c: e.memset(rmask[:, c * 128:c * 128 + 1], 0.0), writes=[rmask])

        def f32t(n, shape=(128, NB)):
            return P.sbuf(n, list(shape), F32)

        def b16t(n, shape=(128, NB)):
            return P.sbuf(n, list(shape), BF16)

        la32, LA16 = f32t("la32", (64, NB)), b16t("LA16", (64, NB))
        gl32, sg16 = f32t("gl32", (96, NB)), b16t("sg16", (96, NB))
        r32, k32, v32, v16 = f32t("r32"), f32t("k32"), f32t("v32"), b16t("v16")
        sig, ld, aa, a2_, kraw, ksq, rn, kk = f32t("sig"), f32t("ld"), f32t("aa"), f32t("a2_"), f32t("kraw"), b16t("ksq"), f32t("rn"), f32t("kk")
        fac, kdir, kdir2, bb, cs, LL, Lm = f32t("fac"), f32t("kdir"), f32t("kdir2"), f32t("bb"), f32t("cs"), f32t("LL"), f32t("Lm")
        gg, ginv, gprev = f32t("gg"), f32t("ginv"), f32t("gprev")
        Bt, Kt = b16t("Bt"), b16t("Kt")
        KR = P.sbuf("KR", [128, NB // 128, 2, 128], BF16)
        Bg, Kg = b16t("Bg", (128, 128)), b16t("Kg", (128, 128))
        TT = P.sbuf("TT", [128, 3, 128], BF16)
        AMh = [P.sbuf("AM%d" % h, [128, 4, 128], BF16) for h in range(2)]
        NN = [[P.sbuf("NN%d_%d" % (h, i), [128, 2, 128], F32) for i in range(1)] for h in range(2)]
        PP = [[P.sbuf("PP%d_%d" % (h, i), [128, 128], F32) for i in range(2)] for h in range(2)]
        Tb = [P.sbuf("Tb%d" % h, [128, 128], BF16) for h in range(2)]
        IV = [dict(Nb2=P.sbuf("ivNb2_%d" % h, [128, 2, 128], F32), S2=P.sbuf("ivS2_%d" % h, [128, 2, 128], F32),
                   S4T=P.sbuf("ivS4T_%d" % h, [128, 128], F32), X=P.sbuf("ivX_%d" % h, [128, 128], F32),
                   TTr=P.sbuf("ivTTr_%d" % h, [128, 128], F32), Bt=P.sbuf("ivBt_%d" % h, [128, 4, 128], F32)) for h in range(2)]
        BM = P.sbuf("blkmask", [128, 5, 128], F32)
        P.dma("sp", BM.ap(), dr["blkmask"], writes=[BM])
        ident32 = P.sbuf("ident32", [128, 128], F32)
        P.op("pool", lambda e: e.tensor_copy(out=ident32.ap(), in_=C.ident.ap()), reads=[C.ident], writes=[ident32])
        NZ, UT = b16t("NZ", (128, 128)), b16t("UT", (128, 128))
        S32 = [[f32t("S32_%d_%d" % (d, p), (128, 64)) for p in range(4)] for d in range(2)]
        Sb = [[[b16t("Sb_%d_%d_%d" % (d, p, i), (128, 64)) for i in range(2)] for p in range(4)] for d in range(2)]
        sbi = [[0] * 4 for _ in range(2)]
        ys32, yfl = f32t("ys32"), f32t("yfl")
        ys16, yc, yc2, rstd, yn = b16t("ys16"), f32t("yc"), b16t("yc2"), f32t("rstd"), f32t("yn")
        rk, rk16, bonus, o16 = f32t("rk"), b16t("rk16"), f32t("bonus"), b16t("o16")
        for d in range(2):
            for p in range(4):
                P.op("pool", lambda e, d=d, p=p: e.memset(S32[d][p].ap(), 0.0), writes=[S32[d][p]])
                P.op("pool", lambda e, d=d, p=p: e.memset(Sb[d][p][0].ap(), 0.0), writes=[Sb[d][p][0]])

        NCH = NB // 128
        import os
        lim = int(os.environ.get("RWKV_LIM", "999"))
        cnt = 0
        for d in range(2):
            blocks = [("c", 0)] + ([("l", i) for i in range(NLB)] if d == 0 else [("l", i) for i in reversed(range(NLB))])
            for (src, bi) in blocks:
                cnt += 1
                if cnt > lim:
                    continue
                if os.environ.get("RWKV_SKIPF") and d == 0 and src == "l":
                    continue
                H = C.HC if src == "c" else C.HT
                t0 = 0 if src == "c" else bi * NB
                lat = (src == "l")
                proj_conv(C, H, t0, WR, 1536, 64, CW[0:64, 12, :], la32.ap(), la32)
                P.op("act", lambda e: e.activation(out=LA16[0:32, :], in_=la32[0:32, :], func=AF.Tanh), reads=[la32], writes=[LA16])
                P.op("pool", lambda e: e.tensor_copy(out=LA16[32:64, :], in_=la32[32:64, :]), reads=[la32], writes=[LA16])
                if d == 1 and lat:
                    proj_conv(C, H, t0, WR, 1600, 96, CW[0:96, 13, :], gl32.ap(), gl32)
                    P.op("act", lambda e: e.activation(out=sg16.ap(), in_=gl32.ap(), func=AF.Sigmoid), reads=[gl32], writes=[sg16])
                for p in range(4):
                    proj_conv(C, H, t0, WR, p * 128, 128, CW[:, p, :], r32.ap(), r32)
                    proj_conv(C, H, t0, WR, 512 + p * 128, 128, CW[:, 4 + p, :], k32.ap(), k32)
                    proj_conv(C, H, t0, WR, 1024 + p * 128, 128, CW[:, 8 + p, :], v32.ap(), v32)
                    P.op("pool", lambda e: e.tensor_copy(out=v16.ap(), in_=v32.ap()), reads=[v32], writes=[v16])
                    pz = C.nb()
                    P.op("pe", lambda e, p=p, d=d: e.matmul(pz[:, 0:NB], lhsT=LW2[0:32, d, p * 128:(p + 1) * 128], rhs=LA16[0:32, :],
                                                             start=True, stop=True), reads=[LW2, LA16], writes=[pz])
                    P.op("act", lambda e, p=p, d=d: e.activation(out=sig.ap(), in_=pz[:, 0:NB], func=AF.Sigmoid, bias=w0a0[:, d, p:p + 1]),
                         reads=[w0a0], writes=[pz, sig])
                    P.op("pool", lambda e: e.tensor_scalar(out=ld.ap(), in0=sig.ap(), scalar1=-EXPM05, scalar2=None, op0=ALU.mult),
                         reads=[sig], writes=[ld])
                    pz = C.nb()
                    P.op("pe", lambda e, p=p, d=d, pz=pz: e.matmul(pz[:, 0:NB], lhsT=LW2[32:64, d, p * 128:(p + 1) * 128], rhs=LA16[32:64, :],
                                                                    start=True, stop=True), reads=[LW2, LA16], writes=[pz])
                    P.op("act", lambda e, p=p, d=d, pz=pz: e.activation(out=aa.ap(), in_=pz[:, 0:NB], func=AF.Sigmoid, bias=w0a0[:, 2 + d, p:p + 1]),
                         reads=[w0a0], writes=[pz, aa])
                    P.op("pool", lambda e, p=p: e.tensor_scalar(out=kraw.ap(), in0=k32.ap(), scalar1=kvec[:, 0, p:p + 1], scalar2=None, op0=ALU.mult),
                         reads=[k32, kvec], writes=[kraw])
                    P.op("act", lambda e: e.activation(out=ksq.ap(), in_=kraw.ap(), func=AF.Square), reads=[kraw], writes=[ksq])
                    pz = C.nb()
                    P.op("pe", lambda e, pz=pz: e.matmul(pz[:, 0:NB], lhsT=C.bones.ap(), rhs=ksq.ap(), start=True, stop=True),
                         reads=[C.bones, ksq], writes=[pz])
                    P.op("act", lambda e, pz=pz: e.activation(out=rn.ap(), in_=pz[:, 0:NB], func=AF.Sqrt), writes=[pz, rn])
                    P.op("dve", lambda e: e.tensor_scalar(out=rn.ap(), in0=rn.ap(), scalar1=1e-12, scalar2=None, op0=ALU.max), writes=[rn])
                    P.op("dve", lambda e: e.reciprocal(out=rn.ap(), in_=rn.ap()), writes=[rn])
                    P.op("dve", lambda e: e.tensor_tensor(out=kk.ap(), in0=kraw.ap(), in1=rn.ap(), op=ALU.mult), reads=[kraw, rn], writes=[kk])
                    P.op("dve", lambda e, p=p: e.tensor_scalar(out=fac.ap(), in0=aa.ap(), scalar1=kvec[:, 1, p:p + 1], scalar2=omka[:, p:p + 1],
                                                               op0=ALU.mult, op1=ALU.add), reads=[aa, kvec, omka], writes=[fac])
                    P.op("pool", lambda e: e.tensor_tensor(out=kdir.ap(), in0=k32.ap(), in1=fac.ap(), op=ALU.mult), reads=[k32, fac], writes=[kdir])
                    P.op("pool", lambda e: e.tensor_tensor(out=bb.ap(), in0=kk.ap(), in1=aa.ap(), op=ALU.mult), reads=[kk, aa], writes=[bb])
                    P.op("dve", lambda e: e.tensor_tensor_scan(out=cs.ap(), data0=rmask.ap(), data1=ld.ap(), initial=0.0,
                                                               op0=ALU.mult, op1=ALU.add), reads=[rmask, ld], writes=[cs])
                    if d == 0:
                        Lb = cs
                    else:
                        for c in range(NCH):
                            P.op("dve", lambda e, c=c: e.tensor_scalar(out=LL[:, c * 128:(c + 1) * 128], in0=cs[:, c * 128:(c + 1) * 128],
                                                                       scalar1=-1.0, scalar2=cs[:, c * 128 + 127:c * 128 + 128],
                                                                       op0=ALU.mult, op1=ALU.add), reads=[cs], writes=[LL])
                        P.op("dve", lambda e: e.tensor_tensor(out=LL.ap(), in0=LL.ap(), in1=ld.ap(), op=ALU.add), reads=[ld], writes=[LL])
                        Lb = LL
                    P.op("pool", lambda e, Lb=Lb: e.tensor_tensor(out=Lm.ap(), in0=Lb.ap(), in1=ld.ap(), op=ALU.subtract), reads=[Lb, ld], writes=[Lm])
                    P.op("act", lambda e, Lb=Lb: e.activation(out=gg.ap(), in_=Lb.ap(), func=AF.Exp), reads=[Lb], writes=[gg])
                    P.op("act", lambda e, Lb=Lb: e.activation(out=ginv.ap(), in_=Lb.ap(), func=AF.Exp, scale=-1.0), reads=[Lb], writes=[ginv])
                    P.op("act", lambda e: e.activation(out=gprev.ap(), in_=Lm.ap(), func=AF.Exp), reads=[Lm], writes=[gprev])
                    P.op("dve", lambda e: e.tensor_tensor(out=Bt.ap(), in0=bb.ap(), in1=ginv.ap(), op=ALU.mult), reads=[bb, ginv], writes=[Bt])
                    P.op("dve", lambda e: e.tensor_tensor(out=Kt.ap(), in0=kdir.ap(), in1=ginv.ap(), op=ALU.mult), reads=[kdir, ginv], writes=[Kt])
                    P.op("pool", lambda e: e.tensor_tensor(out=KR[:, :, 0, :], in0=kk.ap().rearrange("p (c t) -> p c t", t=128),
                                                          in1=gprev.ap().rearrange("p (c t) -> p c t", t=128), op=ALU.mult),
                         reads=[kk, gprev], writes=[KR])
                    P.op("pool", lambda e: e.tensor_tensor(out=KR[:, :, 1, :], in0=r32.ap().rearrange("p (c t) -> p c t", t=128),
                                                          in1=gg.ap().rearrange("p (c t) -> p c t", t=128), op=ALU.mult),
                         reads=[r32, gg], writes=[KR])
                    if d == 1 and lat:
                        if (p, bi) in C.yf_bufs:
                            P.dma("sp", yfl.ap(), dr["yf"][p, :, t0:t0 + NB], reads=[C.yf_bufs[(p, bi)]], writes=[yfl])
                        else:
                            P.op("pool", lambda e: e.memset(yfl.ap(), 0.0), writes=[yfl])
                    chunks = list(range(NCH)) if d == 0 else list(reversed(range(NCH)))
                    for c in chunks:
                        csl = slice(c * 128, (c + 1) * 128)
                        gcol = c * 128 + 127 if d == 0 else c * 128
                        P.op("dve", lambda e, csl=csl, gcol=gcol: e.tensor_scalar(out=Bg.ap(), in0=Bt[:, csl], scalar1=gg[:, gcol:gcol + 1],
                                                                                  scalar2=None, op0=ALU.mult), reads=[Bt, gg], writes=[Bg])
                        P.op("dve", lambda e, csl=csl, gcol=gcol: e.tensor_scalar(out=Kg.ap(), in0=Kt[:, csl], scalar1=gg[:, gcol:gcol + 1],
                                                                                  scalar2=None, op0=ALU.mult), reads=[Kt, gg], writes=[Kg])
                        pT = C.bankT
                        P.op("pe", lambda e, csl=csl: e.transpose(pT[:, 0:128], v16[:, csl], C.ident.ap()), reads=[v16, C.ident], writes=[pT])
                        P.op("pe", lambda e: e.transpose(pT[:, 128:256], Bg.ap(), C.ident.ap()), reads=[Bg, C.ident], writes=[pT])
                        P.op("pe", lambda e: e.transpose(pT[:, 256:384], Kg.ap(), C.ident.ap()), reads=[Kg, C.ident], writes=[pT])
                        P.op("act", lambda e: e.activation(out=TT.ap(), in_=pT[:, 0:384].rearrange("p (a b) -> p a b", b=128), func=AF.Copy),
                             writes=[pT, TT])
                        for h in range(2):
                            hs = slice(h * 64, (h + 1) * 64)
                            AM = AMh[h]
                            pg = C.nb()
                            P.op("pe", lambda e, hs=hs, csl=csl, c=c, pg=pg: e.matmul(pg[:, 0:256], lhsT=Bt[hs, csl], rhs=KR[hs, c, :, :],
                                                                                      start=True, stop=True), reads=[Bt, KR], writes=[pg])
                            P.op("pe", lambda e, hs=hs, csl=csl, c=c, pg=pg: e.matmul(pg[:, 256:512], lhsT=Kt[hs, csl], rhs=KR[hs, c, :, :],
                                                                                      start=True, stop=True), reads=[Kt, KR], writes=[pg])
                            N0 = NN[h][0]
                            P.op("dve", lambda e, N0=N0, pg=pg, d=d: e.tensor_tensor(out=N0[:, 0, :], in0=pg[:, 0:128], in1=AMM[d][:, 0, :], op=ALU.mult),
                                 reads=[AMM[d]], writes=[pg, N0])
                            P.op("dve", lambda e, AM=AM, pg=pg, d=d: e.tensor_tensor(out=AM.ap(), in0=pg.ap().rearrange("p (a b) -> p a b", b=128),
                                                                                    in1=AMM[d].ap(), op=ALU.mult), reads=[AMM[d]], writes=[pg, AM])
                            pn = C.nb()
                            P.op("pe", lambda e, hs=hs, csl=csl, c=c, pn=pn: e.matmul(pn[:, 0:128], lhsT=KR[hs, c, 0, :], rhs=Bt[hs, csl],
                                                                                      start=True, stop=True), reads=[Bt, KR], writes=[pn])
                            N0 = NN[h][0]
                            P.op("dve", lambda e, N0=N0, pn=pn, d=d: e.tensor_tensor(out=N0[:, 1, :], in0=pn[:, 0:128], in1=NTM[d].ap(), op=ALU.mult),
                                 reads=[NTM[d]], writes=[pn, N0])
                        def inv_gen(h):
                            N0 = NN[h][0]
                            Nb2, S2, S4T, X_, TTr = IV[h]["Nb2"], IV[h]["S2"], IV[h]["S4T"], IV[h]["X"], IV[h]["TTr"]
                            Btm = IV[h]["Bt"]
                            Pc = [PP[h][0], PP[h][1]]
                            P.op("dve", lambda e: e.tensor_tensor(out=Nb2[:, 0, :], in0=N0[:, 0, :], in1=BM[:, 0, :], op=ALU.mult), reads=[N0, BM], writes=[Nb2])
                            P.op("pool", lambda e: e.tensor_tensor(out=Nb2[:, 1, :], in0=N0[:, 1, :], in1=BM[:, 0, :], op=ALU.mult), reads=[N0, BM], writes=[Nb2])
                            for l in range(4):
                                P.op("pool", lambda e, l=l: e.tensor_tensor(out=Btm[:, l, :], in0=N0[:, 1, :], in1=BM[:, 1 + l, :], op=ALU.mult),
                                     reads=[N0, BM], writes=[Btm])
                            pq = C.nb()
                            P.op("pe", lambda e: e.matmul(pq[:, 0:128], lhsT=Nb2[:, 1, :], rhs=Nb2[:, 0, :], start=True, stop=True), reads=[Nb2], writes=[pq])
                            P.op("pe", lambda e: e.matmul(pq[:, 128:256], lhsT=Nb2[:, 0, :], rhs=Nb2[:, 1, :], start=True, stop=True), reads=[Nb2], writes=[pq])
                            P.op("act", lambda e: e.activation(out=S2.ap(), in_=pq[:, 0:256].rearrange("p (a b) -> p a b", b=128), func=AF.Copy), writes=[pq, S2])
                            P.op("pool", lambda e: e.tensor_tensor(out=Pc[0].ap(), in0=ident32.ap(), in1=Nb2[:, 0, :], op=ALU.subtract),
                                 reads=[Nb2, ident32], writes=[Pc[0]])
                            yield
                            pq2 = C.nb()
                            P.op("pe", lambda e: e.matmul(pq2[:, 0:128], lhsT=S2[:, 0, :], rhs=S2[:, 1, :], start=True, stop=True), reads=[S2], writes=[pq2])
                            P.op("dve", lambda e: e.tensor_copy(out=S4T.ap(), in_=pq2[:, 0:128]), writes=[pq2, S4T])
                            pp_ = C.nb()
                            P.op("pe", lambda e: e.matmul(pp_[:, 0:128], lhsT=S2[:, 1, :], rhs=Pc[0].ap(), start=True, stop=True), reads=[S2, Pc[0]], writes=[pp_])
                            P.op("dve", lambda e: e.tensor_tensor(out=Pc[1].ap(), in0=pp_[:, 0:128], in1=Pc[0].ap(), op=ALU.add), reads=[Pc[0]], writes=[pp_, Pc[1]])
                            yield
                            pp2 = C.nb()
                            P.op("pe", lambda e: e.matmul(pp2[:, 0:128], lhsT=S4T.ap(), rhs=Pc[1].ap(), start=True, stop=True), reads=[S4T, Pc[1]], writes=[pp2])
                            P.op("dve", lambda e: e.tensor_tensor(out=Pc[0].ap(), in0=pp2[:, 0:128], in1=Pc[1].ap(), op=ALU.add), reads=[Pc[1]], writes=[pp2, Pc[0]])
                            cur = 0
                            yield
                            for l in range(4):
                                Tc, Tn = Pc[cur], Pc[1 - cur]
                                ptr = C.nb()
                                P.op("pe", lambda e, Tc=Tc, ptr=ptr: e.transpose(ptr[:, 0:128], Tc.ap(), ident32.ap()), reads=[Tc, ident32], writes=[ptr])
                                P.op("act", lambda e, ptr=ptr: e.activation(out=TTr.ap(), in_=ptr[:, 0:128], func=AF.Copy), writes=[ptr, TTr])
                                px = C.nb()
                                P.op("pe", lambda e, l=l, Tc=Tc, px=px: e.matmul(px[:, 0:128], lhsT=Btm[:, l, :], rhs=Tc.ap(), start=True, stop=True),
                                     reads=[Btm, Tc], writes=[px])
                                P.op("dve", lambda e, px=px: e.tensor_copy(out=X_.ap(), in_=px[:, 0:128]), writes=[px, X_])
                                yield
                                pr = C.nb()
                                P.op("pe", lambda e, pr=pr: e.matmul(pr[:, 0:128], lhsT=TTr.ap(), rhs=X_.ap(), start=True, stop=True), reads=[TTr, X_], writes=[pr])
                                if l < 3:
                                    P.op("dve", lambda e, pr=pr, Tc=Tc, Tn=Tn: e.tensor_tensor(out=Tn.ap(), in0=Tc.ap(), in1=pr[:, 0:128], op=ALU.subtract),
                                         reads=[Tc], writes=[pr, Tn])
                                else:
                                    P.op("dve", lambda e, pr=pr, Tc=Tc: e.tensor_tensor(out=Tb[h].ap(), in0=Tc.ap(), in1=pr[:, 0:128], op=ALU.subtract),
                                         reads=[Tc], writes=[pr, Tb[h]])
                                cur = 1 - cur
                                yield

                        for _ in zip(inv_gen(0), inv_gen(1)):
                            pass
                        Th = Tb
                        So = Sb[d][p][sbi[d][p] % 2]
                        Sn = Sb[d][p][(sbi[d][p] + 1) % 2]
                        sbi[d][p] += 1
                        pz = C.nb()
                        for h in range(2):
                            hs = slice(h * 64, (h + 1) * 64)
                            P.op("pe", lambda e, hs=hs, c=c, pz=pz, So=So: e.matmul(pz[:, hs], lhsT=KR[hs, c, 0, :], rhs=So[hs, :], start=True, stop=False),
                                 reads=[KR, So], writes=[pz])
                            P.op("pe", lambda e, hs=hs, h=h, pz=pz: e.matmul(pz[:, hs], lhsT=AMh[h][:, 2, :], rhs=TT[:, 0, hs], start=False, stop=True),
                                 reads=[AMh[h], TT], writes=[pz])
                        P.op("act", lambda e, pz=pz: e.activation(out=NZ.ap(), in_=pz[:, 0:128], func=AF.Copy, scale=-1.0), writes=[pz, NZ])
                        pu = C.nb()
                        for h in range(2):
                            hs = slice(h * 64, (h + 1) * 64)
                            P.op("pe", lambda e, hs=hs, h=h, pu=pu: e.matmul(pu[:, hs], lhsT=Th[h].ap(), rhs=NZ[:, hs], start=True, stop=True),
                                 reads=[Th[h], NZ], writes=[pu])
                        P.op("dve", lambda e, pu=pu: e.tensor_copy(out=UT.ap(), in_=pu[:, 0:128]), writes=[pu, UT])
                        if lat:
                            py = C.nb()
                            for h in range(2):
                                hs = slice(h * 64, (h + 1) * 64)
                                P.op("pe", lambda e, hs=hs, c=c, py=py, So=So: e.matmul(py[hs, 0:128], lhsT=So[hs, :], rhs=KR[hs, c, 1, :], start=True, stop=False),
                                     reads=[So, KR], writes=[py])
                                P.op("pe", lambda e, hs=hs, h=h, py=py: e.matmul(py[hs, 0:128], lhsT=UT[:, hs], rhs=AMh[h][:, 1, :], start=False, stop=False),
                                     reads=[UT, AMh[h]], writes=[py])
                                P.op("pe", lambda e, hs=hs, h=h, py=py: e.matmul(py[hs, 0:128], lhsT=TT[:, 0, hs], rhs=AMh[h][:, 3, :], start=False, stop=True),
                                     reads=[TT, AMh[h]], writes=[py])
                            if d == 0:
                                P.op("act", lambda e, py=py, csl=csl: e.activation(out=ys32[:, csl], in_=py[:, 0:128], func=AF.Copy), writes=[py, ys32])
                            else:
                                P.op("dve", lambda e, py=py, csl=csl: e.tensor_tensor(out=ys32[:, csl], in0=py[:, 0:128], in1=yfl[:, csl], op=ALU.add),
                                     reads=[yfl], writes=[py, ys32])
                        pS = C.nb()
                        for h in range(2):
                            hs = slice(h * 64, (h + 1) * 64)
                            P.op("pe", lambda e, hs=hs, pS=pS: e.matmul(pS[hs, 0:64], lhsT=TT[:, 1, hs], rhs=UT[:, hs], start=True, stop=False),
                                 reads=[TT, UT], writes=[pS])
                            P.op("pe", lambda e, hs=hs, pS=pS: e.matmul(pS[hs, 0:64], lhsT=TT[:, 2, hs], rhs=TT[:, 0, hs], start=False, stop=True),
                                 reads=[TT], writes=[pS])
                        S3 = S32[d][p]
                        P.op("dve", lambda e, pS=pS, S3=S3, gcol=gcol: e.scalar_tensor_tensor(out=S3.ap(), in0=S3.ap(), scalar=gg[:, gcol:gcol + 1],
                                                                                              in1=pS[:, 0:64], op0=ALU.mult, op1=ALU.add),
                             reads=[gg], writes=[pS, S3])
                        P.op("act", lambda e, S3=S3, Sn=Sn: e.activation(out=Sn.ap(), in_=S3.ap(), func=AF.Copy), reads=[S3], writes=[Sn])
                    if not lat:
                        continue
                    if d == 0:
                        yb = Buf("yf_%d_%d" % (p, bi))
                        C.yf_bufs[(p, bi)] = yb
                        P.dma("sp", dr["yf"][p, :, t0:t0 + NB], ys32.ap(), reads=[ys32], writes=[yb])
                        continue
                    P.op("act", lambda e: e.activation(out=ys16.ap(), in_=ys32.ap(), func=AF.Copy), reads=[ys32], writes=[ys16])
                    pm_ = C.nb()
                    P.op("pe", lambda e, pm_=pm_: e.matmul(pm_[:, 0:NB], lhsT=C.bmean.ap(), rhs=ys16.ap(), start=True, stop=True),
                         reads=[C.bmean, ys16], writes=[pm_])
                    P.op("dve", lambda e, pm_=pm_: e.tensor_tensor(out=yc.ap(), in0=ys32.ap(), in1=pm_[:, 0:NB], op=ALU.subtract),
                         reads=[ys32], writes=[pm_, yc])
                    P.op("act", lambda e: e.activation(out=yc2.ap(), in_=yc.ap(), func=AF.Square), reads=[yc], writes=[yc2])
                    pv = C.nb()
                    P.op("pe", lambda e, pv=pv: e.matmul(pv[:, 0:NB], lhsT=C.bmean.ap(), rhs=yc2.ap(), start=True, stop=True),
                         reads=[C.bmean, yc2], writes=[pv])
                    P.op("act", lambda e, pv=pv: e.activation(out=rstd.ap(), in_=pv[:, 0:NB], func=AF.Sqrt, bias=C.epsx.ap()),
                         reads=[C.epsx], writes=[pv, rstd])
                    P.op("dve", lambda e: e.reciprocal(out=rstd.ap(), in_=rstd.ap()), writes=[rstd])
                    P.op("dve", lambda e: e.tensor_tensor(out=yn.ap(), in0=yc.ap(), in1=rstd.ap(), op=ALU.mult), reads=[yc, rstd], writes=[yn])
                    P.op("dve", lambda e, p=p: e.tensor_scalar(out=yn.ap(), in0=yn.ap(), scalar1=kvec[:, 3, p:p + 1], scalar2=kvec[:, 4, p:p + 1],
                                                               op0=ALU.mult, op1=ALU.add), reads=[kvec], writes=[yn])
                    pz = C.nb()
                    P.op("pe", lambda e, p=p, pz=pz: e.matmul(pz[:, 0:NB], lhsT=LW2[32:64, 0, p * 128:(p + 1) * 128], rhs=LA16[32:64, :],
                                                              start=True, stop=True), reads=[LW2, LA16], writes=[pz])
                    P.op("act", lambda e, p=p, pz=pz: e.activation(out=a2_.ap(), in_=pz[:, 0:NB], func=AF.Sigmoid, bias=w0a0[:, 2, p:p + 1]),
                         reads=[w0a0], writes=[pz, a2_])
                    P.op("dve", lambda e, p=p: e.tensor_scalar(out=a2_.ap(), in0=a2_.ap(), scalar1=kvec[:, 1, p:p + 1], scalar2=omka[:, p:p + 1],
                                                               op0=ALU.mult, op1=ALU.add), reads=[kvec, omka], writes=[a2_])
                    P.op("pool", lambda e: e.tensor_tensor(out=a2_.ap(), in0=a2_.ap(), in1=fac.ap(), op=ALU.add), reads=[fac], writes=[a2_])
                    P.op("pool", lambda e: e.tensor_tensor(out=kdir2.ap(), in0=k32.ap(), in1=a2_.ap(), op=ALU.mult), reads=[k32, a2_], writes=[kdir2])
                    P.op("pool", lambda e: e.tensor_tensor(out=rk.ap(), in0=r32.ap(), in1=kdir2.ap(), op=ALU.mult), reads=[r32, kdir2], writes=[rk])
                    P.op("pool", lambda e, p=p: e.tensor_scalar(out=rk16.ap(), in0=rk.ap(), scalar1=rkh[:, p:p + 1], scalar2=None, op0=ALU.mult),
                         reads=[rk, rkh], writes=[rk16])
                    pbn = C.nb()
                    P.op("pe", lambda e, pbn=pbn: e.matmul(pbn[:, 0:NB], lhsT=C.bones.ap(), rhs=rk16.ap(), start=True, stop=True),
                         reads=[C.bones, rk16], writes=[pbn])
                    P.op("dve", lambda e, pbn=pbn: e.tensor_tensor(out=bonus.ap(), in0=pbn[:, 0:NB], in1=v32.ap(), op=ALU.mult),
                         reads=[v32], writes=[pbn, bonus])
                    P.op("pool", lambda e: e.tensor_tensor(out=bonus.ap(), in0=bonus.ap(), in1=yn.ap(), op=ALU.add), reads=[yn], writes=[bonus])
                    pgt = C.nb()
                    P.op("pe", lambda e, p=p, pgt=pgt: e.matmul(pgt[:, 0:NB], lhsT=G2W[0:96, p * 128:(p + 1) * 128], rhs=sg16.ap(), start=True, stop=True),
                         reads=[G2W, sg16], writes=[pgt])
                    P.op("dve", lambda e, pgt=pgt: e.tensor_tensor(out=o16.ap(), in0=pgt[:, 0:NB], in1=bonus.ap(), op=ALU.mult),
                         reads=[bonus], writes=[pgt, o16])
                    mb = Buf("mt_%d_%d" % (p, bi))
                    C.mt_bufs[(p, bi)] = mb
                    P.dma("sp", dr["mt"][p, :, t0:t0 + NB], o16.ap(), reads=[o16], writes=[mb])


def stage_attn(C):
    P, dr = C.P, C.dr
    HT, HC = C.HT, C.HC
    bk = C.banks
    NKT = (T_CTX + T_LAT) // 128
    with P.scope():
        stg = [P.sbuf("astg%d" % i, [128, 8, 256], F32) for i in range(2)]
        Wh = P.sbuf("Wh", [128, 8, 640], BF16)
        KT = P.sbuf("KT", [128, T_CTX + T_LAT], BF16)
        VT = P.sbuf("VT", [128, NKT, 128], BF16)
        QB = [P.sbuf("QB%d" % i, [128, 512], BF16) for i in range(2)]
        PT = [P.sbuf("PT%d" % i, [128, 512], BF16) for i in range(3)]
        accD = P.sbuf("accD", [128, 512], F32)
        accP = P.sbuf("accP", [128, 512], F32)
        ones32a = P.sbuf("ones32a", [128, 128], F32)
        P.op("pool", lambda e: e.memset(ones32a.ap(), 1.0), writes=[ones32a])
        rC = [P.sbuf("rC%d" % i, [128, NB], F32) for i in range(2)]
        rS = [P.sbuf("rS%d" % i, [128, NB], F32) for i in range(2)]
        t1 = P.sbuf("rp_t1", [128, NB], F32)
        t2 = P.sbuf("rp_t2", [128, NB], F32)
        rl = P.sbuf("at_rl", [128, 512], F32)
        on_ = P.sbuf("at_on", [128, 512], F32)
        dif = P.sbuf("at_dif", [128, NB], F32)
        dsq = P.sbuf("at_dsq", [128, NB], BF16)
        drs = P.sbuf("at_drs", [128, NB], F32)
        ao16 = [P.sbuf("at_o16_%d" % i, [128, NB], BF16) for i in range(2)]
        lamt = P.sbuf("lamt", [128, 4, 64], F32)
        lpr = P.sbuf("lpr", [128, 2, 64], F32)
        lsum = P.sbuf("lsum", [128, 2], F32)
        nlam = P.sbuf("nlam", [128, 1], F32)
        sbw = P.sbuf("sbw", [128, 1], F32)
        P.dma("sp", lamt.ap(), dr["lam"], writes=[lamt])
        P.dma("sp", sbw.ap(), dr["subln"], writes=[sbw])
        P.op("dve", lambda e: e.tensor_tensor(out=lpr.ap(), in0=lamt[:, 0:4:2, :], in1=lamt[:, 1:4:2, :], op=ALU.mult),
             reads=[lamt], writes=[lpr])
        P.op("dve", lambda e: e.reduce_sum(out=lsum.ap(), in_=lpr.ap(), axis=AX.X), reads=[lpr], writes=[lsum])
        P.op("act", lambda e: e.activation(out=lsum.ap(), in_=lsum.ap(), func=AF.Exp), writes=[lsum])
        P.op("dve", lambda e: e.tensor_tensor(out=nlam.ap(), in0=lsum[:, 1:2], in1=lsum[:, 0:1], op=ALU.subtract),
             reads=[lsum], writes=[nlam])
        P.op("dve", lambda e: e.tensor_scalar(out=nlam.ap(), in0=nlam.ap(), scalar1=-0.2, scalar2=None, op0=ALU.add), writes=[nlam])
        P.op("dve", lambda e: e.tensor_scalar(out=sbw.ap(), in0=sbw.ap(), scalar1=0.8, scalar2=None, op0=ALU.mult), writes=[sbw])
        for q in QB:
            P.op("pool", lambda e, q=q: e.memset(q.ap(), 0.0), writes=[q])
        C.stg_i = 0
        ri = [0]

        def load_rope(bi):
            i = ri[0] % 2
            ri[0] += 1
            P.dma("sp", rC[i].ap(), dr["ropeC"][:, bi * NB:(bi + 1) * NB], writes=[rC[i]])
            P.dma("sp", rS[i].ap(), dr["ropeS"][:, bi * NB:(bi + 1) * NB], writes=[rS[i]])
            return rC[i], rS[i]

        def proj_rope(bank, c0, bi, rc, rs, outs):
            t0 = bi * NB
            for g in range(2):
                for k in range(8):
                    P.op("pe", lambda e, g=g, k=k: e.matmul(bank[:, g * NB:(g + 1) * NB], lhsT=Wh[:, k, c0 + g * 128:c0 + (g + 1) * 128],
                                                             rhs=HT[:, k, 1 + t0:1 + t0 + NB], start=(k == 0), stop=(k == 7)),
                         reads=[Wh, HT], writes=[bank])
            P.op("dve", lambda e: e.tensor_tensor(out=t1.ap(), in0=bank[:, 0:NB], in1=rc.ap(), op=ALU.mult), reads=[rc], writes=[bank, t1])
            P.op("dve", lambda e: e.tensor_tensor(out=t2.ap(), in0=bank[:, NB:2 * NB], in1=rs.ap(), op=ALU.mult), reads=[rs], writes=[bank, t2])
            for rows, oap, ob in outs:
                P.op("pool", lambda e, rows=rows, oap=oap: e.tensor_tensor(out=oap, in0=t1[rows, :], in1=t2[rows, :], op=ALU.add),
                     reads=[t1, t2], writes=[ob])

        for h in range(4):
            qc = 1696 + h * 128
            kc = 1696 + 512 + h * 128
            vc = 1696 + 1024 + h * 128
            load_w_bf16(C, (Wh, 0), dr["w_in"][:, qc:qc + 128], 128, stg)
            load_w_bf16(C, (Wh, 128), dr["w_qks"][:, h * 128:(h + 1) * 128], 128, stg)
            load_w_bf16(C, (Wh, 256), dr["w_in"][:, kc:kc + 128], 128, stg)
            load_w_bf16(C, (Wh, 384), dr["w_qks"][:, 512 + h * 128:512 + (h + 1) * 128], 128, stg)
            load_w_bf16(C, (Wh, 512), dr["w_in"][:, vc:vc + 128], 128, stg)
            pb = C.nb()
            for k in range(8):
                P.op("pe", lambda e, k=k, pb=pb: e.matmul(pb[:, 0:T_CTX], lhsT=Wh[:, k, 256:384], rhs=HC[:, k, 1:1 + T_CTX],
                                                           start=(k == 0), stop=(k == 7)), reads=[Wh, HC], writes=[pb])
            P.op("act", lambda e, pb=pb: e.activation(out=KT[:, 0:T_CTX], in_=pb[:, 0:T_CTX], func=AF.Copy), writes=[pb, KT])
            for bi in range(NLB):
                rc, rs = load_rope(bi)
                proj_rope(C.nb(), 256, bi, rc, rs, [(slice(0, 128), KT[:, T_CTX + bi * NB:T_CTX + (bi + 1) * NB], KT)])
            for j in range(NKT):
                Hs, c0 = (HC, 1 + j * 128) if j < 2 else (HT, 1 + (j - 2) * 128)
                pv = C.nb()
                for k in range(8):
                    P.op("pe", lambda e, k=k, pv=pv, Hs=Hs, c0=c0: e.matmul(pv[:, 0:128], lhsT=Hs[:, k, c0:c0 + 128], rhs=Wh[:, k, 512:640],
                                                                           start=(k == 0), stop=(k == 7)), reads=[Wh, Hs], writes=[pv])
                eng = "act" if j % 2 == 0 else "dve"
                if eng == "act":
                    P.op("act", lambda e, pv=pv, j=j: e.activation(out=VT[:, j, :], in_=pv[:, 0:128], func=AF.Copy), writes=[pv, VT])
                else:
                    P.op("dve", lambda e, pv=pv, j=j: e.tensor_copy(out=VT[:, j, :], in_=pv[:, 0:128]), writes=[pv, VT])

            def q_proj(bi):
                rc, rs = load_rope(bi)
                qb = QB[bi % 2]
                proj_rope(bk[6], 0, bi, rc, rs, [(slice(0, 64), qb[0:64, 0:NB], qb), (slice(64, 128), qb[64:128, NB:2 * NB], qb)])

            q_proj(0)
            for bi in range(NLB):
                qb = QB[bi % 2]
                po, pl = bk[3 + bi % 2], bk[5]

                def qk(j):
                    ps = bk[j % 3]
                    P.op("pe", lambda e, j=j, ps=ps: e.matmul(ps.ap(), lhsT=KT[:, j * 128:(j + 1) * 128], rhs=qb.ap(), start=True, stop=True),
                         reads=[KT, qb], writes=[ps])

                qk(0)
                qk(1)
                qk(2)
                if bi + 1 < NLB:
                    q_proj(bi + 1)
                nD = nP = 0
                for j in range(NKT):
                    ps, pt = bk[j % 3], PT[j % 3]
                    P.op("act", lambda e, ps=ps, pt=pt: e.activation(out=pt.ap(), in_=ps.ap(), func=AF.Exp, scale=0.125), writes=[ps, pt])
                    P.op("pe", lambda e, j=j, pt=pt: e.matmul(po.ap(), lhsT=VT[:, j, :], rhs=pt.ap(), start=(j == 0), stop=(j == NKT - 1)),
                         reads=[VT, pt], writes=[po])
                    if j % 3 == 2:
                        if nP == 0:
                            P.op("pool", lambda e, pt=pt: e.tensor_copy(out=accP.ap(), in_=pt.ap()), reads=[pt], writes=[accP])
                        else:
                            P.op("pool", lambda e, pt=pt: e.tensor_tensor(out=accP.ap(), in0=accP.ap(), in1=pt.ap(), op=ALU.add), reads=[pt], writes=[accP])
                        nP += 1
                    else:
                        if nD == 0:
                            P.op("dve", lambda e, pt=pt: e.tensor_copy(out=accD.ap(), in_=pt.ap()), reads=[pt], writes=[accD])
                        else:
                            P.op("dve", lambda e, pt=pt: e.tensor_tensor(out=accD.ap(), in0=accD.ap(), in1=pt.ap(), op=ALU.add), reads=[pt], writes=[accD])
                        nD += 1
                    if j + 3 < NKT:
                        qk(j + 3)
                P.op("pe", lambda e: e.matmul(pl.ap(), lhsT=ones32a.ap(), rhs=accD.ap(), start=True, stop=False), reads=[ones32a, accD], writes=[pl])
                P.op("pe", lambda e: e.matmul(pl.ap(), lhsT=ones32a.ap(), rhs=accP.ap(), start=False, stop=True), reads=[ones32a, accP], writes=[pl])
                P.op("dve", lambda e: e.reciprocal(out=rl.ap(), in_=pl.ap()), writes=[pl, rl])
                P.op("dve", lambda e: e.tensor_tensor(out=on_.ap(), in0=po.ap(), in1=rl.ap(), op=ALU.mult), reads=[rl], writes=[po, on_])
                P.op("dve", lambda e: e.scalar_tensor_tensor(out=dif.ap(), in0=on_[:, NB:2 * NB], scalar=nlam.ap(), in1=on_[:, 0:NB],
                                                             op0=ALU.mult, op1=ALU.add), reads=[on_, nlam], writes=[dif])
                P.op("pool", lambda e: e.tensor_tensor(out=dsq.ap(), in0=dif.ap(), in1=dif.ap(), op=ALU.mult), reads=[dif], writes=[dsq])
                pm = bk[6]
                P.op("pe", lambda e: e.matmul(pm[:, 0:NB], lhsT=C.mean128.ap(), rhs=dsq.ap(), start=True, stop=True),
                     reads=[C.mean128, dsq], writes=[pm])
                P.op("act", lambda e: e.activation(out=drs.ap(), in_=pm[:, 0:NB], func=AF.Sqrt, bias=C.epss.ap()), reads=[C.epss], writes=[pm, drs])
                P.op("dve", lambda e: e.reciprocal(out=drs.ap(), in_=drs.ap()), writes=[drs])
                P.op("pool", lambda e: e.tensor_tensor(out=dif.ap(), in0=dif.ap(), in1=drs.ap(), op=ALU.mult), reads=[drs], writes=[dif])
                o16 = ao16[bi % 2]
                P.op("pool", lambda e, o16=o16: e.tensor_scalar(out=o16.ap(), in0=dif.ap(), scalar1=sbw.ap(), scalar2=None, op0=ALU.mult),
                     reads=[dif, sbw], writes=[o16])
                mb = Buf("mt_%d_%d" % (4 + h, bi))
                C.mt_bufs[(4 + h, bi)] = mb
                P.dma("sp", dr["mt"][4 + h, :, bi * NB:(bi + 1) * NB], o16.ap(), reads=[o16], writes=[mb])


def stage_out_a(C):
    P, dr = C.P, C.dr
    C.x1_bufs, C.h2_bufs = {}, {}
    with P.scope():
        C.stg_i = 0
        stg = [P.sbuf("ostg%d" % i, [128, 8, 256], F32) for i in range(2)]
        WO = P.sbuf("WO", [128, 8, 1024], BF16)
        load_w_bf16(C, (WO, 0), dr["w_out"], 1024, stg)
        MTb = [P.sbuf("MTb%d" % i, [128, 8, NB], BF16) for i in range(2)]
        xts = [P.sbuf("oxt%d" % i, [128, 8, NB], F32) for i in range(2)]
        y32 = P.sbuf("oy32", [128, 8, NB], F32)
        x1 = [P.sbuf("ox1_%d" % i, [128, 8, NB], F32) for i in range(2)]
        h2 = [P.sbuf("oh2_%d" % i, [128, 8, NB], BF16) for i in range(2)]
        sq = P.sbuf("o_sq", [128, 8, NB], BF16)
        rs = P.sbuf("o_rs", [128, NB], F32)
        tmp = P.sbuf("o_tmp", [128, 8, NB], F32)
        def ld_a(bi):
            t0 = bi * NB
            P.dma("sp", MTb[bi % 2].ap(), dr["mt"][:, :, t0:t0 + NB].rearrange("k p t -> p k t"),
                  reads=[C.mt_bufs[(k, bi)] for k in range(8)], writes=[MTb[bi % 2]])
            P.dma("sp", xts[bi % 2].ap(), dr["xT"][:, t0:t0 + NB].rearrange("(k p) t -> p k t", p=128), writes=[xts[bi % 2]])

        for bi in range(NLB):
            t0 = bi * NB
            mtb, xt, x1b, h2b = MTb[bi % 2], xts[bi % 2], x1[bi % 2], h2[bi % 2]
            if bi == 0:
                ld_a(0)
            if bi + 1 < NLB:
                ld_a(bi + 1)
            for j in range(8):
                py = C.nb()
                for k in range(8):
                    P.op("pe", lambda e, j=j, k=k, py=py: e.matmul(py[:, 0:NB], lhsT=WO[:, k, j * 128:(j + 1) * 128], rhs=mtb[:, k, :],
                                                                   start=(k == 0), stop=(k == 7)), reads=[WO, mtb], writes=[py])
                if j % 2 == 0:
                    P.op("act", lambda e, j=j, py=py: e.activation(out=y32[:, j, :], in_=py[:, 0:NB], func=AF.Copy), writes=[py, y32])
                else:
                    P.op("dve", lambda e, j=j, py=py: e.tensor_copy(out=y32[:, j, :], in_=py[:, 0:NB]), writes=[py, y32])
            P.op("act", lambda e: e.activation(out=sq.ap(), in_=y32.ap(), func=AF.Square), reads=[y32], writes=[sq])
            pb = C.nb()
            for k in range(8):
                P.op("pe", lambda e, k=k, pb=pb: e.matmul(pb[:, 0:NB], lhsT=C.ones.ap(), rhs=sq[:, k, :], start=(k == 0), stop=(k == 7)),
                     reads=[sq, C.ones], writes=[pb])
            P.op("act", lambda e, pb=pb: e.activation(out=rs.ap(), in_=pb[:, 0:NB], func=AF.Sqrt, scale=1.0 / D_MODEL, bias=C.eps6.ap()),
                 reads=[C.eps6], writes=[pb, rs])
            P.op("dve", lambda e: e.reciprocal(out=rs.ap(), in_=rs.ap()), writes=[rs])
            for j in range(8):
                P.op("dve", lambda e, j=j: e.scalar_tensor_tensor(out=tmp[:, j, :], in0=y32[:, j, :], scalar=C.G1[:, j:j + 1], in1=rs.ap(),
                                                                  op0=ALU.mult, op1=ALU.mult), reads=[y32, rs, C.G1], writes=[tmp])
            P.op("pool", lambda e, x1b=x1b, xt=xt: e.tensor_tensor(out=x1b.ap(), in0=tmp.ap(), in1=xt.ap(), op=ALU.add),
                 reads=[tmp, xt], writes=[x1b])
            xb = Buf("x1s_%d" % bi)
            C.x1_bufs[bi] = xb
            P.dma("sp", dr["x1s"][:, :, t0:t0 + NB].rearrange("k p t -> p k t"), x1b.ap(), reads=[x1b], writes=[xb])
            norm_block(C, x1b, NB, lambda k: C.A2[:, k:k + 1], lambda k: C.modT[:, 24 + k, 0:1],
                       lambda k, h2b=h2b: h2b[:, k, :], [C.A2, C.modT], h2b, (sq, rs, tmp))
            hb = Buf("h2s_%d" % bi)
            C.h2_bufs[bi] = hb
            P.dma("sp", dr["h2s"][:, :, t0:t0 + NB].rearrange("k p t -> p k t"), h2b.ap(), reads=[h2b], writes=[hb])


def stage_ffn(C):
    P, dr = C.P, C.dr
    NJ = D_FF // 128
    with P.scope():
        C.stg_i = 0
        WU = P.sbuf("WU", [128, 8, 2 * D_FF], BF16)
        WD = P.sbuf("WD", [128, NJ, 1024], BF16)
        with P.scope():
            stg = [P.sbuf("fstg%d" % i, [128, 8, 256], F32) for i in range(2)]
            load_w_bf16(C, (WU, 0), dr["w_up"], 2 * D_FF, stg)
            for j in range(NJ):
                s_ = stg[C.stg_i % 2]
                C.stg_i += 1
                sv = s_.ap().rearrange("p a b -> p (a b)")[:, 0:1024]
                P.dma("sp", sv, dr["w_down"][j * 128:(j + 1) * 128, :], writes=[s_])
                P.op("pool", lambda e, j=j, sv=sv: e.tensor_copy(out=WD[:, j, :], in_=sv), reads=[s_], writes=[WD])
        FCW = P.sbuf("FCW", [128, 2 * NJ, 3], F32)
        FB = P.sbuf("FB", [128, 2 * NJ], F32)
        C.cwb = FCW
        P.dma("sp", FCW.ap(), dr["fconvT"].rearrange("(c p) j -> p c j", p=128), writes=[FCW])
        P.dma("sp", FB.ap(), dr["fbias"], writes=[FB])
        h2t = [P.sbuf("fh2_%d" % i, [128, 8, NB + 2], BF16) for i in range(2)]
        x1t = [P.sbuf("fx1_%d" % i, [128, 8, NB], F32) for i in range(2)]
        uv = [P.sbuf("fuv%d" % i, [128, NB], F32) for i in range(3)]
        ug = [P.sbuf("fug%d" % i, [128, NB], F32) for i in range(3)]
        sg = [P.sbuf("fsg%d" % i, [128, NB], F32) for i in range(3)]
        act16 = P.sbuf("fact", [128, NJ, NB], BF16)
        f32t = P.sbuf("ff32", [128, 8, NB], F32)
        sq = P.sbuf("f_sq", [128, 8, NB], BF16)
        rs = P.sbuf("f_rs", [128, NB], F32)
        def load_block(bi):
            t0 = bi * NB
            hb, xb = h2t[bi % 2], x1t[bi % 2]
            lo = max(t0 - 1, 0)
            hi = min(t0 + NB + 1, T_LAT)
            rd = [C.h2_bufs[b] for b in (bi - 1, bi, bi + 1) if 0 <= b < NLB]
            if bi == 0:
                P.op("pool", lambda e, hb=hb: e.memset(hb[:, :, 0:1], 0.0), writes=[hb])
            if bi == NLB - 1:
                P.op("pool", lambda e, hb=hb: e.memset(hb[:, :, NB + 1:NB + 2], 0.0), writes=[hb])
            d0 = lo - (t0 - 1)
            P.dma("sp", hb[:, :, d0:d0 + (hi - lo)], dr["h2s"][:, :, lo:hi].rearrange("k p t -> p k t"), reads=rd, writes=[hb])
            P.dma("sp", xb.ap(), dr["x1s"][:, :, t0:t0 + NB].rearrange("k p t -> p k t"), reads=[C.x1_bufs[bi]], writes=[xb])

        def gate_mul(j, u, g, s2):
            P.op("act", lambda e: e.activation(out=s2.ap(), in_=g.ap(), func=AF.Silu), reads=[g], writes=[s2])
            P.op("pool", lambda e: e.tensor_tensor(out=act16[:, j, :], in0=u.ap(), in1=s2.ap(), op=ALU.mult),
                 reads=[u, s2], writes=[act16])

        load_block(0)
        for bi in range(NLB):
            t0 = bi * NB
            hb, xb = h2t[bi % 2], x1t[bi % 2]
            if bi + 1 < NLB:
                load_block(bi + 1)
            prev = None
            for j in range(NJ):
                u, g, s2 = uv[j % 3], ug[j % 3], sg[j % 3]
                proj_conv(C, hb, 0, WU, j * 128, 128, FCW[:, j, :], u.ap(), u, bias=FB[:, j:j + 1], bias_buf=FB)
                proj_conv(C, hb, 0, WU, D_FF + j * 128, 128, FCW[:, NJ + j, :], g.ap(), g, bias=FB[:, NJ + j:NJ + j + 1], bias_buf=FB)
                if prev is not None:
                    gate_mul(*prev)
                prev = (j, u, g, s2)
            gate_mul(*prev)
            for i in range(8):
                pf = C.nb()
                for j in range(NJ):
                    P.op("pe", lambda e, i=i, j=j, pf=pf: e.matmul(pf[:, 0:NB], lhsT=WD[:, j, i * 128:(i + 1) * 128], rhs=act16[:, j, :],
                                                                   start=(j == 0), stop=(j == NJ - 1)), reads=[WD, act16], writes=[pf])
                if i % 2 == 0:
                    P.op("act", lambda e, i=i, pf=pf: e.activation(out=f32t[:, i, :], in_=pf[:, 0:NB], func=AF.Copy), writes=[pf, f32t])
                else:
                    P.op("dve", lambda e, i=i, pf=pf: e.tensor_copy(out=f32t[:, i, :], in_=pf[:, 0:NB]), writes=[pf, f32t])
            P.op("act", lambda e: e.activation(out=sq.ap(), in_=f32t.ap(), func=AF.Square), reads=[f32t], writes=[sq])
            pb = C.nb()
            for k in range(8):
                P.op("pe", lambda e, k=k, pb=pb: e.matmul(pb[:, 0:NB], lhsT=C.ones.ap(), rhs=sq[:, k, :], start=(k == 0), stop=(k == 7)),
                     reads=[sq, C.ones], writes=[pb])
            P.op("act", lambda e, pb=pb: e.activation(out=rs.ap(), in_=pb[:, 0:NB], func=AF.Sqrt, scale=1.0 / D_MODEL, bias=C.eps6.ap()),
                 reads=[C.eps6], writes=[pb, rs])
            P.op("dve", lambda e: e.reciprocal(out=rs.ap(), in_=rs.ap()), writes=[rs])
            for i in range(8):
                P.op("dve", lambda e, i=i: e.scalar_tensor_tensor(out=f32t[:, i, :], in0=f32t[:, i, :], scalar=C.G2[:, i:i + 1], in1=rs.ap(),
                                                                  op0=ALU.mult, op1=ALU.mult), reads=[rs, C.G2], writes=[f32t])
            P.op("pool", lambda e, xb=xb: e.tensor_tensor(out=xb.ap(), in0=f32t.ap(), in1=xb.ap(), op=ALU.add),
                 reads=[f32t], writes=[xb])
            P.dma("sp", dr["outT"][:, t0:t0 + NB].rearrange("(k p) t -> p k t", p=128), xb.ap(), reads=[xb], final=True)
```

```python
import contextlib
import numpy as np
import concourse.bass as bass
import concourse.mybir as mybir
from concourse.bass_utils import run_bass_kernel_spmd

F32 = mybir.dt.float32
BF16 = mybir.dt.bfloat16
AF = mybir.ActivationFunctionType
ALU = mybir.AluOpType
AX = mybir.AxisListType

SEM_ROLL = 30000


class Buf:
    def __init__(self, name, t=None):
        self.name = name
        self.t = t
        self.last_w = None
        self.readers = []
        self.dma_sem = None
        self.dma_cnt = 0

    def ap(self):
        return self.t[:]

    def __getitem__(self, idx):
        return self.t[idx]


class _Rec:
    def __getattr__(self, name):
        return lambda *a, **k: (name, a, k)


_REC = _Rec()


def _bind(fn):
    name, a, k = fn(_REC)
    return lambda e: getattr(e, name)(*a, **k)


class Op:
    __slots__ = ("eng", "fn", "deps", "signal", "token", "is_dma", "sem_buf", "final", "idx")


class Prog:
    ENGS = ("pe", "act", "dve", "pool", "sp")

    def __init__(self, nc):
        self.nc = nc
        self.stack = contextlib.ExitStack()
        self.ops = {e: [] for e in self.ENGS}
        self.nbuf = 0
        self.all_ops = []
        self.final_ops = []

    def sbuf(self, name, shape, dtype):
        self.nbuf += 1
        name = "s%d_%s" % (self.nbuf, name)
        t = self.stack.enter_context(self.nc.sbuf_tensor(name, list(shape), dtype))
        b = Buf(name, t)
        b.readers = list(getattr(self, "fence", []))
        if hasattr(self, "scope_bufs") and self.scope_bufs:
            self.scope_bufs[-1].append(b)
        return b

    def psum(self, name, shape, dtype):
        t = self.stack.enter_context(self.nc.psum_tensor(name, list(shape), dtype))
        return Buf(name, t)

    def view(self, name):
        return Buf(name)

    @contextlib.contextmanager
    def scope(self):
        old = self.stack
        self.stack = contextlib.ExitStack()
        if not hasattr(self, "scope_bufs"):
            self.scope_bufs = []
        self.scope_bufs.append([])
        try:
            yield
        finally:
            self.stack.close()
            self.stack = old
            bufs = self.scope_bufs.pop()
            ops = list(getattr(self, "fence", []))
            for b in bufs:
                if b.last_w is not None:
                    ops.append(b.last_w)
                ops.extend(b.readers)
            best = {}
            dmas = {}
            for o in ops:
                if o.is_dma:
                    dmas[id(o)] = o
                else:
                    if o.eng not in best or best[o.eng].idx < o.idx:
                        best[o.eng] = o
            self.fence = list(best.values()) + list(dmas.values())

    def _deps(self, o, reads, writes):
        deps = []
        for b in list(reads) + list(writes):
            if b.last_w is not None:
                deps.append(b.last_w)
        for b in writes:
            deps.extend(b.readers)
        for b in writes:
            b.last_w = o
            b.readers = []
        for b in reads:
            b.readers.append(o)
        seen = set()
        out = []
        for d in deps:
            if id(d) in seen or d is o:
                continue
            seen.add(id(d))
            if d.eng == "pe" and o.eng == "pe" and not d.is_dma and not o.is_dma:
                continue
            d.signal = True
            out.append(d)
        o.deps = out

    def op(self, eng, fn, reads=(), writes=()):
        o = Op()
        o.eng = eng
        o.fn = _bind(fn)
        o.signal = False
        o.token = None
        o.is_dma = False
        o.sem_buf = None
        o.final = False
        o.idx = len(self.ops[eng])
        self._deps(o, reads, writes)
        self.ops[eng].append(o)
        return o

    def dma(self, eng, out, in_, reads=(), writes=(), final=False):
        o = Op()
        o.eng = eng
        o.fn = lambda e: e.dma_start(out=out, in_=in_)
        o.signal = True
        o.token = None
        o.is_dma = True
        o.final = final
        cands = [b for b in list(writes) + list(reads) if b.t is not None]
        sb = cands[0] if cands else (list(writes) + list(reads))[0]
        o.sem_buf = sb
        o.idx = len(self.ops[eng])
        self._deps(o, reads, writes)
        self.ops[eng].append(o)
        if final:
            self.final_ops.append(o)
        return o

    def emit(self):
        nc = self.nc
        st = self.stack
        eng_sems = {}
        for e in self.ENGS:
            n = 0
            for o in self.ops[e]:
                if o.is_dma:
                    b = o.sem_buf
                    if b.dma_sem is None:
                        b.dma_sem = st.enter_context(nc.semaphore("d_" + b.name))
                    b.dma_cnt += 16
                    o.token = (b.dma_sem, b.dma_cnt, 16)
                elif o.signal:
                    k = n // SEM_ROLL
                    if (e, k) not in eng_sems:
                        eng_sems[(e, k)] = st.enter_context(nc.semaphore("s_%s_%d" % (e, k)))
                    o.token = (eng_sems[(e, k)], n % SEM_ROLL + 1, 1)
                    n += 1
        print("ops:", {e: len(self.ops[e]) for e in self.ENGS}, "signals:", {e: sum(1 for o in self.ops[e] if o.token is not None) for e in self.ENGS}, "nsems", len(eng_sems))
        all_sems = list(eng_sems.values())
        seen_b = set()
        for e in self.ENGS:
            for o in self.ops[e]:
                if o.is_dma and id(o.sem_buf) not in seen_b:
                    seen_b.add(id(o.sem_buf))
                    all_sems.append(o.sem_buf.dma_sem)
        with nc.Block() as blk0:
            @blk0.sync
            def _(eng):
                for sm in all_sems:
                    eng.sem_clear(sm)
        block = st.enter_context(nc.Block())
        hooks = {"pe": block.tensor, "act": block.scalar, "dve": block.vector,
                 "pool": block.gpsimd, "sp": block.sync}
        final_ops = self.final_ops

        def make(e):
            ops = self.ops[e]

            def body(eng):
                waited = {}
                for o in ops:
                    for d in o.deps:
                        sem, val, _ = d.token
                        if waited.get(id(sem), 0) < val:
                            eng.wait_ge(sem, val)
                            waited[id(sem)] = val
                    ins = o.fn(eng)
                    if o.token is not None:
                        ins.then_inc(o.token[0], o.token[2])
                if e == "sp":
                    for o in final_ops:
                        sem, val, _ = o.token
                        eng.wait_ge(sem, val)
            return body

        for e in self.ENGS:
            hooks[e](make(e))
        st.close()


D_MODEL = 1024
T_LAT = 4096
T_CTX = 256
NB = 256
NLB = T_LAT // NB
D_FF = 2816
EXPM05 = float(np.exp(-0.5))


class Ctx:
    pass


def build_program(nc, stage=99):
    P = Prog(nc)
    C = Ctx()
    C.P = P
    C.nc = nc
    dr = {}

    def din(name, shape, dt=F32):
        dr[name] = nc.dram_tensor(name, list(shape), dt, kind="ExternalInput").ap()

    din("xT", [1024, T_LAT]); din("ctxT", [1024, T_CTX]); din("cc", [128, 8, 2])
    din("w_mod", [1024, 6144]); din("b_modT", [128, 48]); din("gvec", [128, 4, 8])
    din("w_in", [1024, 3232]); din("w_qks", [1024, 1024]); din("convT", [1696, 3])
    din("lw2", [64, 2, 512]); din("w0a0", [128, 4, 4]); din("kvec", [128, 5, 4]); din("g2", [96, 512])
    din("lam", [128, 4, 64]); din("subln", [128, 1])
    din("w_out", [1024, 1024]); din("w_up", [1024, 5632]); din("fconvT", [5632, 3]); din("fbias", [128, 44])
    din("w_down", [2816, 1024]); din("ropeC", [128, T_LAT]); din("ropeS", [128, T_LAT]); din("blkmask", [128, 5, 128])
    dr["outT"] = nc.dram_tensor("outT", [1024, T_LAT], F32, kind="ExternalOutput").ap()
    dr["yf"] = nc.dram_tensor("yf_scr", [4, 128, T_LAT], F32).ap()
    dr["mt"] = nc.dram_tensor("mt_scr", [8, 128, T_LAT], BF16).ap()
    dr["x1s"] = nc.dram_tensor("x1_scr", [8, 128, T_LAT], F32).ap()
    dr["h2s"] = nc.dram_tensor("h2_scr", [8, 128, T_LAT], BF16).ap()
    if stage < 99:
        dr["dbg"] = nc.dram_tensor("dbg", [128, 8, T_LAT + 2], F32, kind="ExternalOutput").ap()
    C.dr = dr
    C.yf_bufs = {}
    C.mt_bufs = {}

    C.banks = [P.psum("pb%d" % i, [128, 512], F32) for i in range(7)]
    C.bankT = P.psum("pbT", [128, 1024], BF16)
    C.bank_i = 0

    def nb():
        b = C.banks[C.bank_i % 7]
        C.bank_i += 1
        return b
    C.nb = nb

    ident = P.sbuf("ident", [128, 128], BF16)
    ones = P.sbuf("ones", [128, 128], BF16)
    bones = P.sbuf("bones", [128, 128], BF16)
    bmean = P.sbuf("bmean", [128, 128], BF16)
    mean128 = P.sbuf("mean128", [128, 128], BF16)
    eps6 = P.sbuf("eps6", [128, 1], F32)
    epsx = P.sbuf("epsx", [128, 1], F32)
    epss = P.sbuf("epss", [128, 1], F32)
    P.op("pool", lambda e: e.memset(ident.ap(), 0.0), writes=[ident])
    P.op("pool", lambda e: e.affine_select(out=ident.ap(), in_=ident.ap(), pattern=[[-1, 128]],
                                           compare_op=ALU.not_equal, fill=1.0, base=0, channel_multiplier=1),
         reads=[ident], writes=[ident])
    P.op("pool", lambda e: e.memset(ones.ap(), 1.0), writes=[ones])
    P.op("pool", lambda e: e.memset(mean128.ap(), 1.0 / 128), writes=[mean128])
    P.op("pool", lambda e: e.memset(bones.ap(), 0.0), writes=[bones])
    P.op("pool", lambda e: e.memset(bones[0:64, 0:64], 1.0), writes=[bones])
    P.op("pool", lambda e: e.memset(bones[64:128, 64:128], 1.0), writes=[bones])
    P.op("pool", lambda e: e.memset(bmean.ap(), 0.0), writes=[bmean])
    P.op("pool", lambda e: e.memset(bmean[0:64, 0:64], 1.0 / 64), writes=[bmean])
    P.op("pool", lambda e: e.memset(bmean[64:128, 64:128], 1.0 / 64), writes=[bmean])
    P.op("pool", lambda e: e.memset(eps6.ap(), 1e-6), writes=[eps6])
    P.op("pool", lambda e: e.memset(epsx.ap(), 64e-5), writes=[epsx])
    P.op("pool", lambda e: e.memset(epss.ap(), 1e-5), writes=[epss])
    C.ident, C.ones, C.bones, C.bmean, C.mean128 = ident, ones, bones, bmean, mean128
    C.eps6, C.epsx, C.epss = eps6, epsx, epss

    stage_mod(C)
    import os
    with P.scope():
        C.HT = P.sbuf("HT", [128, 8, T_LAT + 2], BF16)
        C.HC = P.sbuf("HC", [128, 8, T_CTX + 2], BF16)
        for Hb, n in ((C.HT, T_LAT), (C.HC, T_CTX)):
            P.op("pool", lambda e, Hb=Hb: e.memset(Hb[:, :, 0:1], 0.0), writes=[Hb])
            P.op("pool", lambda e, Hb=Hb, n=n: e.memset(Hb[:, :, n + 1:n + 2], 0.0), writes=[Hb])
        stage_prenorm(C)
        if stage == 1:
            dbg_dump_HT(C)
            P.emit()
            return nc
        if not os.environ.get("SKIP_RWKV"):
            stage_rwkv(C)
        if stage == 21:
            with P.scope():
                for k in range(4):
                    tmp = P.sbuf("dbgy%d" % k, [128, T_LAT], F32)
                    P.dma("sp", tmp.ap(), dr["yf"][k], reads=list(C.yf_bufs.values()), writes=[tmp])
                    P.dma("sp", dr["dbg"][:, k, 0:T_LAT], tmp.ap(), reads=[tmp], final=True)
            P.emit()
            return nc
        if stage == 2:
            dbg_dump_mt(C, 0, 4)
            P.emit()
            return nc
        stage_attn(C)
        if stage == 3:
            dbg_dump_mt(C, 4, 8)
            P.emit()
            return nc
        stage_out_a(C)
    stage_ffn(C)
    P.emit()
    return nc


def dbg_dump_HT(C):
    P = C.P
    with P.scope():
        for k in range(8):
            tmp = P.sbuf("dbgt%d" % k, [128, T_LAT + 2], F32)
            P.op("dve", lambda e, k=k, tmp=tmp: e.tensor_copy(out=tmp.ap(), in_=C.HT[:, k, :]), reads=[C.HT], writes=[tmp])
            P.dma("sp", C.dr["dbg"][:, k, :], tmp.ap(), reads=[tmp], final=True)


def dbg_dump_mt(C, k0, k1):
    P = C.P
    with P.scope():
        for k in range(k0, k1):
            tb = P.sbuf("dbgb%d" % k, [128, T_LAT], BF16)
            tmp = P.sbuf("dbgt%d" % k, [128, T_LAT], F32)
            rd = [b for (kk, _), b in C.mt_bufs.items() if kk == k] + list(C.yf_bufs.values())
            P.dma("sp", tb.ap(), C.dr["mt"][k], reads=rd, writes=[tb])
            P.op("dve", lambda e, tmp=tmp, tb=tb: e.tensor_copy(out=tmp.ap(), in_=tb.ap()), reads=[tb], writes=[tmp])
            P.dma("sp", C.dr["dbg"][:, k, 0:T_LAT], tmp.ap(), reads=[tmp], final=True)


def stage_mod(C):
    P, dr = C.P, C.dr
    cc = P.sbuf("cc", [128, 8, 2], F32)
    scc = P.sbuf("scc", [128, 8, 2], F32)
    bm = P.sbuf("bmodT", [128, 48], F32)
    gv = P.sbuf("gvec", [128, 4, 8], F32)
    modT = P.sbuf("modT", [128, 48, 2], F32)
    P.dma("sp", cc.ap(), dr["cc"], writes=[cc])
    P.dma("sp", bm.ap(), dr["b_modT"], writes=[bm])
    P.dma("sp", gv.ap(), dr["gvec"], writes=[gv])
    P.op("act", lambda e: e.activation(out=scc.ap(), in_=cc.ap(), func=AF.Silu), reads=[cc], writes=[scc])
    pm = C.nb()
    with P.scope():
        wst = [P.sbuf("wmst%d" % i, [128, 8, 512], F32) for i in range(2)]
        for g in range(12):
            w = wst[g % 2]
            P.dma("sp", w.ap(), dr["w_mod"][:, g * 512:(g + 1) * 512].rearrange("(k p) n -> p k n", p=128), writes=[w])
            for jj in range(4):
                j = g * 4 + jj
                for k in range(8):
                    P.op("pe", lambda e, w=w, k=k, jj=jj, j=j: e.matmul(
                        pm[:, 2 * j:2 * j + 2], lhsT=w[:, k, jj * 128:(jj + 1) * 128], rhs=scc[:, k, :],
                        start=(k == 0), stop=(k == 7)), reads=[w, scc], writes=[pm])
    pmv = pm[:, 0:96].rearrange("p (j c) -> p j c", c=2)
    for c in range(2):
        P.op("dve", lambda e, c=c: e.tensor_tensor(out=modT[:, :, c], in0=pmv[:, :, c], in1=bm.ap(), op=ALU.add),
             reads=[bm], writes=[pm, modT])
    C.modT = modT
    A1 = P.sbuf("A1", [128, 2, 8], F32)
    G1 = P.sbuf("G1", [128, 8], F32)
    A2 = P.sbuf("A2", [128, 8], F32)
    G2 = P.sbuf("G2", [128, 8], F32)
    for c in range(2):
        P.op("dve", lambda e, c=c: e.scalar_tensor_tensor(out=A1[:, c, :], in0=modT[:, 8:16, c], scalar=1.0,
                                                          in1=gv[:, 0, :], op0=ALU.add, op1=ALU.mult),
             reads=[modT, gv], writes=[A1])
    P.op("dve", lambda e: e.tensor_tensor(out=G1.ap(), in0=modT[:, 16:24, 0], in1=gv[:, 1, :], op=ALU.mult),
         reads=[modT, gv], writes=[G1])
    P.op("dve", lambda e: e.scalar_tensor_tensor(out=A2.ap(), in0=modT[:, 32:40, 0], scalar=1.0, in1=gv[:, 2, :],
                                                 op0=ALU.add, op1=ALU.mult), reads=[modT, gv], writes=[A2])
    P.op("dve", lambda e: e.tensor_tensor(out=G2.ap(), in0=modT[:, 40:48, 0], in1=gv[:, 3, :], op=ALU.mult),
         reads=[modT, gv], writes=[G2])
    C.A1, C.G1, C.A2, C.G2 = A1, G1, A2, G2


def norm_block(C, xt, nbk, A_ap, sh_ap, out_ap, xt_reads, out_buf, tmpn):
    P = C.P
    sq, rs, tmp = tmpn
    P.op("act", lambda e: e.activation(out=sq[:, :, 0:nbk], in_=xt[:, :, 0:nbk], func=AF.Square),
         reads=[xt], writes=[sq])
    pb = C.nb()
    for k in range(8):
        P.op("pe", lambda e, k=k: e.matmul(pb[:, 0:nbk], lhsT=C.ones.ap(), rhs=sq[:, k, 0:nbk],
                                            start=(k == 0), stop=(k == 7)), reads=[sq, C.ones], writes=[pb])
    P.op("act", lambda e: e.activation(out=rs[:, 0:nbk], in_=pb[:, 0:nbk], func=AF.Sqrt, scale=1.0 / D_MODEL,
                                       bias=C.eps6.ap()), reads=[C.eps6], writes=[pb, rs])
    P.op("dve", lambda e: e.reciprocal(out=rs[:, 0:nbk], in_=rs[:, 0:nbk]), writes=[rs])
    for k in range(8):
        P.op("dve", lambda e, k=k: e.scalar_tensor_tensor(out=tmp[:, k, 0:nbk], in0=xt[:, k, 0:nbk], scalar=A_ap(k),
                                                          in1=rs[:, 0:nbk], op0=ALU.mult, op1=ALU.mult),
             reads=[xt, rs] + xt_reads, writes=[tmp])
        P.op("act", lambda e, k=k: e.activation(out=out_ap(k), in_=tmp[:, k, 0:nbk], func=AF.Identity,
                                                bias=sh_ap(k)), reads=[tmp] + xt_reads, writes=[out_buf])


def stage_prenorm(C):
    P, dr = C.P, C.dr
    with P.scope():
        xts = [P.sbuf("xt%d" % i, [128, 8, NB], F32) for i in range(2)]
        sq = P.sbuf("pn_sq", [128, 8, NB], BF16)
        rs = P.sbuf("pn_rs", [128, NB], F32)
        tmp = P.sbuf("pn_tmp", [128, 8, NB], F32)
        blocks = [("c", 0)] + [("l", i) for i in range(NLB)]
        for bi, (src, i) in enumerate(blocks):
            xt = xts[bi % 2]
            if src == "c":
                P.dma("sp", xt.ap(), dr["ctxT"].rearrange("(k p) t -> p k t", p=128), writes=[xt])
                H, cidx, t0 = C.HC, 1, 0
            else:
                P.dma("sp", xt.ap(), dr["xT"][:, i * NB:(i + 1) * NB].rearrange("(k p) t -> p k t", p=128), writes=[xt])
                H, cidx, t0 = C.HT, 0, i * NB
            norm_block(C, xt, NB, lambda k, cidx=cidx: C.A1[:, cidx, k:k + 1],
                       lambda k, cidx=cidx: C.modT[:, k, cidx:cidx + 1],
                       lambda k, H=H, t0=t0: H[:, k, 1 + t0:1 + t0 + NB], [C.A1, C.modT], H, (sq, rs, tmp))


def _rope_tables():
    n_pair = 16
    rows = T_LAT // 64
    row = np.repeat(np.arange(rows, dtype=np.float32), 64)
    col = np.tile(np.arange(64, dtype=np.float32), rows)
    inv = (np.float32(10000.0) ** (-np.arange(n_pair, dtype=np.float32) / np.float32(n_pair))).astype(np.float32)
    ang = np.concatenate([row[:, None] * inv, col[:, None] * inv], axis=-1).astype(np.float32)
    cos = np.cos(ang).astype(np.float32).T
    sin = np.sin(ang).astype(np.float32).T
    Cc = np.concatenate([cos, cos, cos, cos], axis=0)
    Ss = np.concatenate([-sin, sin, -sin, sin], axis=0)
    return np.ascontiguousarray(Cc), np.ascontiguousarray(Ss)


def prep_inputs(inp):
    f = lambda a: np.ascontiguousarray(np.asarray(a, dtype=np.float32))
    x, c, ctx, c_ctx = f(inp["x"]), f(inp["c"]), f(inp["ctx"]), f(inp["c_ctx"])
    pk = lambda v, n: f(v.reshape(n, 128).T)
    sh = {}
    sh["w_mod"] = f(inp["w_mod"][0])
    sh["b_modT"] = pk(f(inp["b_mod"][0]), 48)
    sh["gvec"] = f(np.stack([pk(f(inp[n][0]), 8) for n in ("g_pre_mix", "g_post_mix", "g_pre_ffn", "g_post_ffn")], axis=1))
    w_in = f(inp["w_in"][0])
    sh["w_in"] = w_in
    perm = np.arange(512).reshape(8, 2, 32)[:, ::-1, :].reshape(512)
    qc, kc = 1696, 1696 + 512
    sh["w_qks"] = f(np.concatenate([w_in[:, qc:qc + 512][:, perm], w_in[:, kc:kc + 512][:, perm]], axis=1))
    sh["convT"] = f(inp["rwkv_conv"][0].T)
    sh["lw2"] = f(np.stack([np.concatenate([f(inp["w2_fwd"][0]), f(inp["a2_fwd"][0])], 0),
                            np.concatenate([f(inp["w2_bwd"][0]), f(inp["a2_bwd"][0])], 0)], axis=1))
    sh["w0a0"] = f(np.stack([pk(f(inp[n][0]), 4) for n in ("w0_fwd", "w0_bwd", "a0_fwd", "a0_bwd")], axis=1))
    sh["kvec"] = f(np.stack([pk(f(inp[n][0]).reshape(512), 4) for n in ("k_k", "k_a", "r_k", "ln_x_w", "ln_x_b")], axis=1))
    sh["g2"] = f(inp["g2"][0])
    sh["lam"] = f(np.broadcast_to(np.stack([f(inp[n][0]) for n in ("lam_q1", "lam_k1", "lam_q2", "lam_k2")], 0)[None], (128, 4, 64)))
    sh["subln"] = f(inp["subln_w"][0].reshape(128, 1))
    sh["w_out"] = f(inp["w_out"][0])
    sh["w_up"] = f(inp["w_up"][0])
    sh["fconvT"] = f(inp["ffn_conv"][0].T)
    sh["fbias"] = pk(f(inp["ffn_conv_b"][0]), 44)
    sh["w_down"] = f(inp["w_down"][0])
    sh["ropeC"], sh["ropeS"] = _rope_tables()
    ii = np.arange(128)
    dm = lambda sz: (ii[:, None] // sz == ii[None, :] // sz).astype(np.float32)
    sh["blkmask"] = f(np.stack([dm(8), dm(16) - dm(8), dm(32) - dm(16), dm(64) - dm(32), dm(128) - dm(64)], axis=1))
    maps = []
    for b in range(8):
        m = dict(sh)
        m["xT"] = f(x[b].T)
        m["ctxT"] = f(ctx[b].T)
        m["cc"] = f(np.stack([c[b], c_ctx], -1).reshape(8, 128, 2).transpose(1, 0, 2))
        maps.append(m)
    return maps


def kernel(**inputs):
    maps = prep_inputs(inputs)
    nc = bass.Bass("TRN2", target_bir_lowering=False)
    build_program(nc)
    res = run_bass_kernel_spmd(nc, maps, core_ids=list(range(8)))
    out = np.stack([np.ascontiguousarray(res.results[b]["outT"].T) for b in range(8)], 0)
    return out.astype(np.float32)


def load_w_bf16(C, dst, dram_cols_ap, ncols, stg):
    P = C.P
    for i, c0 in enumerate(range(0, ncols, 256)):
        c1 = min(ncols, c0 + 256)
        s = stg[C.stg_i % 2]
        C.stg_i += 1
        P.dma("sp", s[:, :, 0:c1 - c0], dram_cols_ap[:, c0:c1].rearrange("(k p) n -> p k n", p=128), writes=[s])
        P.op("pool", lambda e, s=s, c0=c0, c1=c1: e.tensor_copy(out=dst[0][:, :, dst[1] + c0:dst[1] + c1], in_=s[:, :, 0:c1 - c0]),
             reads=[s], writes=[dst[0]])


def proj_conv(C, H, t0, W, c0, m, cw, out, out_buf, bias=None, eng2="dve", bias_buf=None):
    P = C.P
    pb = C.nb()
    for k in range(8):
        P.op("pe", lambda e, k=k: e.matmul(pb[0:m, 0:NB + 2], lhsT=W[:, k, c0:c0 + m], rhs=H[:, k, t0:t0 + NB + 2],
                                            start=(k == 0), stop=(k == 7)), reads=[W, H], writes=[pb])
    if bias is None:
        P.op("act", lambda e: e.activation(out=out, in_=pb[0:m, 1:NB + 1], func=AF.Copy, scale=cw[:, 1:2]),
             reads=[C.cwb], writes=[pb, out_buf])
    else:
        P.op("act", lambda e: e.activation(out=out, in_=pb[0:m, 1:NB + 1], func=AF.Identity, scale=cw[:, 1:2], bias=bias),
             reads=[C.cwb, bias_buf], writes=[pb, out_buf])
    P.op(eng2, lambda e: e.scalar_tensor_tensor(out=out, in0=pb[0:m, 0:NB], scalar=cw[:, 0:1], in1=out,
                                                op0=ALU.mult, op1=ALU.add), reads=[C.cwb], writes=[pb, out_buf])
    P.op(eng2, lambda e: e.scalar_tensor_tensor(out=out, in0=pb[0:m, 2:NB + 2], scalar=cw[:, 2:3], in1=out,
                                                op0=ALU.mult, op1=ALU.add), reads=[C.cwb], writes=[pb, out_buf])


def stage_rwkv(C):
    P, dr = C.P, C.dr
    C.stg_i = 0
    with P.scope():
        WR = P.sbuf("WR", [128, 8, 1696], BF16)
        with P.scope():
            stg = [P.sbuf("wstg%d" % i, [128, 8, 256], F32) for i in range(2)]
            load_w_bf16(C, (WR, 0), dr["w_in"][:, 0:1696], 1696, stg)
        CW = P.sbuf("CW", [128, 14, 3], F32)
        C.cwb = CW
        P.dma("sp", CW[:, 0:12, :], dr["convT"][0:1536, :].rearrange("(c p) j -> p c j", p=128), writes=[CW])
        P.dma("sp", CW[0:64, 12, :], dr["convT"][1536:1600, :], writes=[CW])
        P.dma("sp", CW[0:96, 13, :], dr["convT"][1600:1696, :], writes=[CW])
        lw2f = P.sbuf("lw2f", [64, 2, 512], F32)
        LW2 = P.sbuf("LW2", [64, 2, 512], BF16)
        P.dma("sp", lw2f.ap(), dr["lw2"], writes=[lw2f])
        P.op("pool", lambda e: e.tensor_copy(out=LW2.ap(), in_=lw2f.ap()), reads=[lw2f], writes=[LW2])
        g2f = P.sbuf("g2f", [96, 512], F32)
        G2W = P.sbuf("G2W", [96, 512], BF16)
        P.dma("sp", g2f.ap(), dr["g2"], writes=[g2f])
        P.op("pool", lambda e: e.tensor_copy(out=G2W.ap(), in_=g2f.ap()), reads=[g2f], writes=[G2W])
        w0a0 = P.sbuf("w0a0", [128, 4, 4], F32)
        kvec = P.sbuf("kvec", [128, 5, 4], F32)
        P.dma("sp", w0a0.ap(), dr["w0a0"], writes=[w0a0])
        P.dma("sp", kvec.ap(), dr["kvec"], writes=[kvec])
        omka = P.sbuf("omka", [128, 4], F32)
        rkh = P.sbuf("rkh", [128, 4], F32)
        P.op("dve", lambda e: e.tensor_scalar(out=omka.ap(), in0=kvec[:, 1, :], scalar1=-1.0, scalar2=1.0,
                                              op0=ALU.mult, op1=ALU.add), reads=[kvec], writes=[omka])
        P.op("dve", lambda e: e.tensor_scalar(out=rkh.ap(), in0=kvec[:, 2, :], scalar1=0.5, scalar2=None,
                                              op0=ALU.mult), reads=[kvec], writes=[rkh])
        ones32 = P.sbuf("ones32", [128, 128], F32)
        P.op("pool", lambda e: e.memset(ones32.ap(), 1.0), writes=[ones32])
        msk = {}
        for nm, cop, sgn in (("SU", ALU.is_gt, -1), ("IU", ALU.is_ge, -1), ("SL", ALU.is_gt, 1), ("IL", ALU.is_ge, 1)):
            mb = P.sbuf("m" + nm, [128, 128], F32)
            P.op("pool", lambda e, mb=mb, cop=cop, sgn=sgn: e.affine_select(out=mb.ap(), in_=ones32.ap(), pattern=[[-sgn, 128]],
                                                                            compare_op=cop, fill=0.0, base=0, channel_multiplier=sgn),
                 reads=[ones32], writes=[mb])
            msk[nm] = mb
        AMM = []
        for d, (s_, i_) in enumerate((("SU", "IU"), ("SL", "IL"))):
            am = P.sbuf("amm%d" % d, [128, 4, 128], F32)
            for q, nm in enumerate((s_, i_, s_, i_)):
                P.op("pool", lambda e, am=am, q=q, nm=nm: e.tensor_copy(out=am[:, q, :], in_=msk[nm].ap()),
                     reads=[msk[nm]], writes=[am])
            AMM.append(am)
        NTM = [msk["SL"], msk["SU"]]
        rmask = P.sbuf("rmask", [128, NB], F32)
        P.op("pool", lambda e: e.memset(rmask.ap(), 1.0), writes=[rmask])
        for c in range(NB // 128):
            P.op("pool", lambda e, c=c: e.memset(rmask[:, c * 128:c * 128 + 1], 0.0), writes=[rmask])

        def f32t(n, shape=(128, NB)):
            return P.sbuf(n, list(shape), F32)

        def b16t(n, shape=(128, NB)):
            return P.sbuf(n, list(shape), BF16)

        la32, LA16 = f32t("la32", (64, NB)), b16t("LA16", (64, NB))
        gl32, sg16 = f32t("gl32", (96, NB)), b16t("sg16", (96, NB))
        BM = P.sbuf("blkmask", [128, 5, 128], F32)
        P.dma("sp", BM.ap(), dr["blkmask"], writes=[BM])
        ident32 = P.sbuf("ident32", [128, 128], F32)
        P.op("pool", lambda e: e.tensor_copy(out=ident32.ap(), in_=C.ident.ap()), reads=[C.ident], writes=[ident32])
        S32 = [[f32t("S32_%d_%d" % (d, p), (128, 64)) for p in range(4)] for d in range(2)]
        Sb = [[[b16t("Sb_%d_%d_%d" % (d, p, i), (128, 64)) for i in range(2)] for p in range(4)] for d in range(2)]
        sbi = [[0] * 4 for _ in range(2)]
        def make_set(si):
            sfx = "q%d_" % si
            o_ = Ctx()
            r32, k32, v32, v16 = f32t(sfx + "r32"), f32t(sfx + "k32"), f32t(sfx + "v32"), b16t(sfx + "v16")
            sig, aa, kraw, ksq, rn = f32t(sfx + "sig"), f32t(sfx + "aa"), f32t(sfx + "kraw"), b16t(sfx + "ksq"), f32t(sfx + "rn")
            fac, kdir, bb, cs, LL, Lm = f32t(sfx + "fac"), f32t(sfx + "kdir"), f32t(sfx + "bb"), f32t(sfx + "cs"), f32t(sfx + "LL"), f32t(sfx + "Lm")
            gg, ginv, gprev = f32t(sfx + "gg"), f32t(sfx + "ginv"), f32t(sfx + "gprev")
            ld, kk = sig, kraw
            yc, rstd, yn, rk, bonus, a2_, kdir2 = LL, rn, Lm, ginv, gprev, bb, kdir
            Bt, Kt = b16t(sfx + "Bt"), b16t(sfx + "Kt")
            KR = P.sbuf(sfx + "KR", [128, NB // 128, 2, 128], BF16)
            Bg, Kg = b16t(sfx + "Bg", (128, 128)), b16t(sfx + "Kg", (128, 128))
            TT = P.sbuf(sfx + "TT", [128, 3, 128], BF16)
            AMh = [P.sbuf(sfx + "AM%d" % h, [128, 4, 128], BF16) for h in range(2)]
            NN = [[P.sbuf(sfx + "NN%d_%d" % (h, i), [128, 2, 128], F32) for i in range(1)] for h in range(2)]
            PP = [[P.sbuf(sfx + "PP%d_%d" % (h, i), [128, 128], F32) for i in range(2)] for h in range(2)]
            Tb = [P.sbuf(sfx + "Tb%d" % h, [128, 128], BF16) for h in range(2)]
            IV = [dict(Nb2=P.sbuf(sfx + "ivNb2_%d" % h, [128, 2, 128], F32), S2=P.sbuf(sfx + "ivS2_%d" % h, [128, 2, 128], F32),
                       S4T=P.sbuf(sfx + "ivS4T_%d" % h, [128, 128], F32), X=P.sbuf(sfx + "ivX_%d" % h, [128, 128], F32),
                       TTr=P.sbuf(sfx + "ivTTr_%d" % h, [128, 128], F32), Bt=P.sbuf(sfx + "ivBt_%d" % h, [128, 1, 128], F32)) for h in range(2)]
            NZ, UT = b16t(sfx + "NZ", (128, 128)), b16t(sfx + "UT", (128, 128))
            ys32, yfl = f32t(sfx + "ys32"), f32t(sfx + "yfl")
            ys16, yc2 = b16t(sfx + "ys16"), b16t(sfx + "yc2")
            rk16, o16 = b16t(sfx + "rk16"), b16t(sfx + "o16")
            for _n in ['r32', 'k32', 'v32', 'v16', 'sig', 'ld', 'aa', 'a2_', 'kraw', 'ksq', 'rn', 'kk', 'fac', 'kdir', 'kdir2', 'bb', 'cs', 'LL', 'Lm', 'gg', 'ginv', 'gprev', 'Bt', 'Kt', 'KR', 'Bg', 'Kg', 'TT', 'AMh', 'NN', 'PP', 'Tb', 'IV', 'NZ', 'UT', 'ys32', 'yfl', 'ys16', 'yc', 'yc2', 'rstd', 'yn', 'rk', 'rk16', 'bonus', 'o16']:
                setattr(o_, _n, locals()[_n])
            return o_

        TS = [make_set(0), make_set(1)]
        for d in range(2):
            for p in range(4):
                P.op("pool", lambda e, d=d, p=p: e.memset(S32[d][p].ap(), 0.0), writes=[S32[d][p]])
                P.op("pool", lambda e, d=d, p=p: e.memset(Sb[d][p][0].ap(), 0.0), writes=[Sb[d][p][0]])

        NCH = NB // 128
        import os
        lim = int(os.environ.get("RWKV_LIM", "999"))
        cnt = 0
        for d in range(2):
            blocks = [("c", 0)] + ([("l", i) for i in range(NLB)] if d == 0 else [("l", i) for i in reversed(range(NLB))])
            for (src, bi) in blocks:
                cnt += 1
                if cnt > lim:
                    continue
                if os.environ.get("RWKV_SKIPF") and d == 0 and src == "l":
                    continue
                H = C.HC if src == "c" else C.HT
                t0 = 0 if src == "c" else bi * NB
                lat = (src == "l")
                proj_conv(C, H, t0, WR, 1536, 64, CW[0:64, 12, :], la32.ap(), la32)
                P.op("act", lambda e: e.activation(out=LA16[0:32, :], in_=la32[0:32, :], func=AF.Tanh), reads=[la32], writes=[LA16])
                P.op("pool", lambda e: e.tensor_copy(out=LA16[32:64, :], in_=la32[32:64, :]), reads=[la32], writes=[LA16])
                if d == 1 and lat:
                    proj_conv(C, H, t0, WR, 1600, 96, CW[0:96, 13, :], gl32.ap(), gl32)
                    P.op("act", lambda e: e.activation(out=sg16.ap(), in_=gl32.ap(), func=AF.Sigmoid), reads=[gl32], writes=[sg16])
                def pair_gen(p, ts_):
                    proj_conv(C, H, t0, WR, p * 128, 128, CW[:, p, :], ts_.r32.ap(), ts_.r32)
                    yield
                    proj_conv(C, H, t0, WR, 512 + p * 128, 128, CW[:, 4 + p, :], ts_.k32.ap(), ts_.k32)
                    yield
                    proj_conv(C, H, t0, WR, 1024 + p * 128, 128, CW[:, 8 + p, :], ts_.v32.ap(), ts_.v32)
                    yield
                    P.op("pool", lambda e: e.tensor_copy(out=ts_.v16.ap(), in_=ts_.v32.ap()), reads=[ts_.v32], writes=[ts_.v16])
                    pz = C.nb()
                    P.op("pe", lambda e, p=p, d=d: e.matmul(pz[:, 0:NB], lhsT=LW2[0:32, d, p * 128:(p + 1) * 128], rhs=LA16[0:32, :],
                                                             start=True, stop=True), reads=[LW2, LA16], writes=[pz])
                    P.op("act", lambda e, p=p, d=d: e.activation(out=ts_.sig.ap(), in_=pz[:, 0:NB], func=AF.Sigmoid, bias=w0a0[:, d, p:p + 1]),
                         reads=[w0a0], writes=[pz, ts_.sig])
                    P.op("pool", lambda e: e.tensor_scalar(out=ts_.ld.ap(), in0=ts_.sig.ap(), scalar1=-EXPM05, scalar2=None, op0=ALU.mult),
                         reads=[ts_.sig], writes=[ts_.ld])
                    pz = C.nb()
                    P.op("pe", lambda e, p=p, d=d, pz=pz: e.matmul(pz[:, 0:NB], lhsT=LW2[32:64, d, p * 128:(p + 1) * 128], rhs=LA16[32:64, :],
                                                                    start=True, stop=True), reads=[LW2, LA16], writes=[pz])
                    P.op("act", lambda e, p=p, d=d, pz=pz: e.activation(out=ts_.aa.ap(), in_=pz[:, 0:NB], func=AF.Sigmoid, bias=w0a0[:, 2 + d, p:p + 1]),
                         reads=[w0a0], writes=[pz, ts_.aa])
                    yield
                    P.op("pool", lambda e, p=p: e.tensor_scalar(out=ts_.kraw.ap(), in0=ts_.k32.ap(), scalar1=kvec[:, 0, p:p + 1], scalar2=None, op0=ALU.mult),
                         reads=[ts_.k32, kvec], writes=[ts_.kraw])
                    P.op("act", lambda e: e.activation(out=ts_.ksq.ap(), in_=ts_.kraw.ap(), func=AF.Square), reads=[ts_.kraw], writes=[ts_.ksq])
                    pz = C.nb()
                    P.op("pe", lambda e, pz=pz: e.matmul(pz[:, 0:NB], lhsT=C.bones.ap(), rhs=ts_.ksq.ap(), start=True, stop=True),
                         reads=[C.bones, ts_.ksq], writes=[pz])
                    P.op("act", lambda e, pz=pz: e.activation(out=ts_.rn.ap(), in_=pz[:, 0:NB], func=AF.Sqrt), writes=[pz, ts_.rn])
                    P.op("dve", lambda e: e.tensor_scalar(out=ts_.rn.ap(), in0=ts_.rn.ap(), scalar1=1e-12, scalar2=None, op0=ALU.max), writes=[ts_.rn])
                    P.op("dve", lambda e: e.reciprocal(out=ts_.rn.ap(), in_=ts_.rn.ap()), writes=[ts_.rn])
                    P.op("dve", lambda e: e.tensor_tensor(out=ts_.kk.ap(), in0=ts_.kraw.ap(), in1=ts_.rn.ap(), op=ALU.mult), reads=[ts_.kraw, ts_.rn], writes=[ts_.kk])
                    yield
                    P.op("dve", lambda e, p=p: e.tensor_scalar(out=ts_.fac.ap(), in0=ts_.aa.ap(), scalar1=kvec[:, 1, p:p + 1], scalar2=omka[:, p:p + 1],
                                                               op0=ALU.mult, op1=ALU.add), reads=[ts_.aa, kvec, omka], writes=[ts_.fac])
                    P.op("pool", lambda e: e.tensor_tensor(out=ts_.kdir.ap(), in0=ts_.k32.ap(), in1=ts_.fac.ap(), op=ALU.mult), reads=[ts_.k32, ts_.fac], writes=[ts_.kdir])
                    P.op("pool", lambda e: e.tensor_tensor(out=ts_.bb.ap(), in0=ts_.kk.ap(), in1=ts_.aa.ap(), op=ALU.mult), reads=[ts_.kk, ts_.aa], writes=[ts_.bb])
                    P.op("dve", lambda e: e.tensor_tensor_scan(out=ts_.cs.ap(), data0=rmask.ap(), data1=ts_.ld.ap(), initial=0.0,
                                                               op0=ALU.mult, op1=ALU.add), reads=[rmask, ts_.ld], writes=[ts_.cs])
                    if d == 0:
                        Lb = ts_.cs
                    else:
                        for c in range(NCH):
                            P.op("dve", lambda e, c=c: e.tensor_scalar(out=ts_.LL[:, c * 128:(c + 1) * 128], in0=ts_.cs[:, c * 128:(c + 1) * 128],
                                                                       scalar1=-1.0, scalar2=ts_.cs[:, c * 128 + 127:c * 128 + 128],
                                                                       op0=ALU.mult, op1=ALU.add), reads=[ts_.cs], writes=[ts_.LL])
                        P.op("dve", lambda e: e.tensor_tensor(out=ts_.LL.ap(), in0=ts_.LL.ap(), in1=ts_.ld.ap(), op=ALU.add), reads=[ts_.ld], writes=[ts_.LL])
                        Lb = ts_.LL
                    P.op("pool", lambda e, Lb=Lb: e.tensor_tensor(out=ts_.Lm.ap(), in0=Lb.ap(), in1=ts_.ld.ap(), op=ALU.subtract), reads=[Lb, ts_.ld], writes=[ts_.Lm])
                    P.op("act", lambda e, Lb=Lb: e.activation(out=ts_.gg.ap(), in_=Lb.ap(), func=AF.Exp), reads=[Lb], writes=[ts_.gg])
                    P.op("act", lambda e, Lb=Lb: e.activation(out=ts_.ginv.ap(), in_=Lb.ap(), func=AF.Exp, scale=-1.0), reads=[Lb], writes=[ts_.ginv])
                    P.op("act", lambda e: e.activation(out=ts_.gprev.ap(), in_=ts_.Lm.ap(), func=AF.Exp), reads=[ts_.Lm], writes=[ts_.gprev])
                    yield
                    P.op("dve", lambda e: e.tensor_tensor(out=ts_.Bt.ap(), in0=ts_.bb.ap(), in1=ts_.ginv.ap(), op=ALU.mult), reads=[ts_.bb, ts_.ginv], writes=[ts_.Bt])
                    P.op("dve", lambda e: e.tensor_tensor(out=ts_.Kt.ap(), in0=ts_.kdir.ap(), in1=ts_.ginv.ap(), op=ALU.mult), reads=[ts_.kdir, ts_.ginv], writes=[ts_.Kt])
                    P.op("pool", lambda e: e.tensor_tensor(out=ts_.KR[:, :, 0, :], in0=ts_.kk.ap().rearrange("p (c t) -> p c t", t=128),
                                                          in1=ts_.gprev.ap().rearrange("p (c t) -> p c t", t=128), op=ALU.mult),
                         reads=[ts_.kk, ts_.gprev], writes=[ts_.KR])
                    P.op("pool", lambda e: e.tensor_tensor(out=ts_.KR[:, :, 1, :], in0=ts_.r32.ap().rearrange("p (c t) -> p c t", t=128),
                                                          in1=ts_.gg.ap().rearrange("p (c t) -> p c t", t=128), op=ALU.mult),
                         reads=[ts_.r32, ts_.gg], writes=[ts_.KR])
                    yield
                    if d == 1 and lat:
                        if (p, bi) in C.yf_bufs:
                            P.dma("sp", ts_.yfl.ap(), dr["yf"][p, :, t0:t0 + NB], reads=[C.yf_bufs[(p, bi)]], writes=[ts_.yfl])
                        else:
                            P.op("pool", lambda e: e.memset(ts_.yfl.ap(), 0.0), writes=[ts_.yfl])
                    chunks = list(range(NCH)) if d == 0 else list(reversed(range(NCH)))
                    for c in chunks:
                        csl = slice(c * 128, (c + 1) * 128)
                        gcol = c * 128 + 127 if d == 0 else c * 128
                        P.op("dve", lambda e, csl=csl, gcol=gcol: e.tensor_scalar(out=ts_.Bg.ap(), in0=ts_.Bt[:, csl], scalar1=ts_.gg[:, gcol:gcol + 1],
                                                                                  scalar2=None, op0=ALU.mult), reads=[ts_.Bt, ts_.gg], writes=[ts_.Bg])
                        P.op("dve", lambda e, csl=csl, gcol=gcol: e.tensor_scalar(out=ts_.Kg.ap(), in0=ts_.Kt[:, csl], scalar1=ts_.gg[:, gcol:gcol + 1],
                                                                                  scalar2=None, op0=ALU.mult), reads=[ts_.Kt, ts_.gg], writes=[ts_.Kg])
                        pT = C.bankT
                        P.op("pe", lambda e, csl=csl: e.transpose(pT[:, 0:128], ts_.v16[:, csl], C.ident.ap()), reads=[ts_.v16, C.ident], writes=[pT])
                        P.op("pe", lambda e: e.transpose(pT[:, 128:256], ts_.Bg.ap(), C.ident.ap()), reads=[ts_.Bg, C.ident], writes=[pT])
                        P.op("pe", lambda e: e.transpose(pT[:, 256:384], ts_.Kg.ap(), C.ident.ap()), reads=[ts_.Kg, C.ident], writes=[pT])
                        P.op("act", lambda e: e.activation(out=ts_.TT.ap(), in_=pT[:, 0:384].rearrange("p (a b) -> p a b", b=128), func=AF.Copy),
                             writes=[pT, ts_.TT])
                        yield
                        for h in range(2):
                            hs = slice(h * 64, (h + 1) * 64)
                            AM = ts_.AMh[h]
                            pg = C.nb()
                            P.op("pe", lambda e, hs=hs, csl=csl, c=c, pg=pg: e.matmul(pg[:, 0:256], lhsT=ts_.Bt[hs, csl], rhs=ts_.KR[hs, c, :, :],
                                                                                      start=True, stop=True), reads=[ts_.Bt, ts_.KR], writes=[pg])
                            P.op("pe", lambda e, hs=hs, csl=csl, c=c, pg=pg: e.matmul(pg[:, 256:512], lhsT=ts_.Kt[hs, csl], rhs=ts_.KR[hs, c, :, :],
                                                                                      start=True, stop=True), reads=[ts_.Kt, ts_.KR], writes=[pg])
                            N0 = ts_.NN[h][0]
                            P.op("dve", lambda e, N0=N0, pg=pg, d=d: e.tensor_tensor(out=N0[:, 0, :], in0=pg[:, 0:128], in1=AMM[d][:, 0, :], op=ALU.mult),
                                 reads=[AMM[d]], writes=[pg, N0])
                            P.op("dve", lambda e, AM=AM, pg=pg, d=d: e.tensor_tensor(out=AM.ap(), in0=pg.ap().rearrange("p (a b) -> p a b", b=128),
                                                                                    in1=AMM[d].ap(), op=ALU.mult), reads=[AMM[d]], writes=[pg, AM])
                            pn = C.nb()
                            P.op("pe", lambda e, hs=hs, csl=csl, c=c, pn=pn: e.matmul(pn[:, 0:128], lhsT=ts_.KR[hs, c, 0, :], rhs=ts_.Bt[hs, csl],
                                                                                      start=True, stop=True), reads=[ts_.Bt, ts_.KR], writes=[pn])
                            N0 = ts_.NN[h][0]
                            P.op("dve", lambda e, N0=N0, pn=pn, d=d: e.tensor_tensor(out=N0[:, 1, :], in0=pn[:, 0:128], in1=NTM[d].ap(), op=ALU.mult),
                                 reads=[NTM[d]], writes=[pn, N0])
                        def inv_gen(h):
                            N0 = ts_.NN[h][0]
                            Nb2, S2, S4T, X_, TTr = ts_.IV[h]["Nb2"], ts_.IV[h]["S2"], ts_.IV[h]["S4T"], ts_.IV[h]["X"], ts_.IV[h]["TTr"]
                            Btm = ts_.IV[h]["Bt"]
                            Pc = [ts_.PP[h][0], ts_.PP[h][1]]
                            P.op("dve", lambda e: e.tensor_tensor(out=Nb2[:, 0, :], in0=N0[:, 0, :], in1=BM[:, 0, :], op=ALU.mult), reads=[N0, BM], writes=[Nb2])
                            P.op("pool", lambda e: e.tensor_tensor(out=Nb2[:, 1, :], in0=N0[:, 1, :], in1=BM[:, 0, :], op=ALU.mult), reads=[N0, BM], writes=[Nb2])
                            pq = C.nb()
                            P.op("pe", lambda e: e.matmul(pq[:, 0:128], lhsT=Nb2[:, 1, :], rhs=Nb2[:, 0, :], start=True, stop=True), reads=[Nb2], writes=[pq])
                            P.op("pe", lambda e: e.matmul(pq[:, 128:256], lhsT=Nb2[:, 0, :], rhs=Nb2[:, 1, :], start=True, stop=True), reads=[Nb2], writes=[pq])
                            P.op("act", lambda e: e.activation(out=S2.ap(), in_=pq[:, 0:256].rearrange("p (a b) -> p a b", b=128), func=AF.Copy), writes=[pq, S2])
                            P.op("pool", lambda e: e.tensor_tensor(out=Pc[0].ap(), in0=ident32.ap(), in1=Nb2[:, 0, :], op=ALU.subtract),
                                 reads=[Nb2, ident32], writes=[Pc[0]])
                            yield
                            pq2 = C.nb()
                            P.op("pe", lambda e: e.matmul(pq2[:, 0:128], lhsT=S2[:, 0, :], rhs=S2[:, 1, :], start=True, stop=True), reads=[S2], writes=[pq2])
                            P.op("dve", lambda e: e.tensor_copy(out=S4T.ap(), in_=pq2[:, 0:128]), writes=[pq2, S4T])
                            pp_ = C.nb()
                            P.op("pe", lambda e: e.matmul(pp_[:, 0:128], lhsT=S2[:, 1, :], rhs=Pc[0].ap(), start=True, stop=True), reads=[S2, Pc[0]], writes=[pp_])
                            P.op("dve", lambda e: e.tensor_tensor(out=Pc[1].ap(), in0=pp_[:, 0:128], in1=Pc[0].ap(), op=ALU.add), reads=[Pc[0]], writes=[pp_, Pc[1]])
                            yield
                            pp2 = C.nb()
                            P.op("pe", lambda e: e.matmul(pp2[:, 0:128], lhsT=S4T.ap(), rhs=Pc[1].ap(), start=True, stop=True), reads=[S4T, Pc[1]], writes=[pp2])
                            P.op("dve", lambda e: e.tensor_tensor(out=Pc[0].ap(), in0=pp2[:, 0:128], in1=Pc[1].ap(), op=ALU.add), reads=[Pc[1]], writes=[pp2, Pc[0]])
                            cur = 0
                            yield
                            for l in range(4):
                                Tc, Tn = Pc[cur], Pc[1 - cur]
                                ptr = C.nb()
                                P.op("pe", lambda e, Tc=Tc, ptr=ptr: e.transpose(ptr[:, 0:128], Tc.ap(), ident32.ap()), reads=[Tc, ident32], writes=[ptr])
                                P.op("act", lambda e, ptr=ptr: e.activation(out=TTr.ap(), in_=ptr[:, 0:128], func=AF.Copy), writes=[ptr, TTr])
                                px = C.nb()
                                P.op("pool", lambda e, l=l: e.tensor_tensor(out=Btm[:, 0, :], in0=N0[:, 1, :], in1=BM[:, 1 + l, :], op=ALU.mult),
                                     reads=[N0, BM], writes=[Btm])
                                P.op("pe", lambda e, l=l, Tc=Tc, px=px: e.matmul(px[:, 0:128], lhsT=Btm[:, 0, :], rhs=Tc.ap(), start=True, stop=True),
                                     reads=[Btm, Tc], writes=[px])
                                P.op("dve", lambda e, px=px: e.tensor_copy(out=X_.ap(), in_=px[:, 0:128]), writes=[px, X_])
                                yield
                                pr = C.nb()
                                P.op("pe", lambda e, pr=pr: e.matmul(pr[:, 0:128], lhsT=TTr.ap(), rhs=X_.ap(), start=True, stop=True), reads=[TTr, X_], writes=[pr])
                                if l < 3:
                                    P.op("dve", lambda e, pr=pr, Tc=Tc, Tn=Tn: e.tensor_tensor(out=Tn.ap(), in0=Tc.ap(), in1=pr[:, 0:128], op=ALU.subtract),
                                         reads=[Tc], writes=[pr, Tn])
                                else:
                                    P.op("dve", lambda e, pr=pr, Tc=Tc: e.tensor_tensor(out=ts_.Tb[h].ap(), in0=Tc.ap(), in1=pr[:, 0:128], op=ALU.subtract),
                                         reads=[Tc], writes=[pr, ts_.Tb[h]])
                                cur = 1 - cur
                                yield

                        for _ in zip(inv_gen(0), inv_gen(1)):
                            yield
                        Th = ts_.Tb
                        So = Sb[d][p][sbi[d][p] % 2]
                        Sn = Sb[d][p][(sbi[d][p] + 1) % 2]
                        sbi[d][p] += 1
                        pz = C.nb()
                        for h in range(2):
                            hs = slice(h * 64, (h + 1) * 64)
                            P.op("pe", lambda e, hs=hs, c=c, pz=pz, So=So: e.matmul(pz[:, hs], lhsT=ts_.KR[hs, c, 0, :], rhs=So[hs, :], start=True, stop=False),
                                 reads=[ts_.KR, So], writes=[pz])
                            P.op("pe", lambda e, hs=hs, h=h, pz=pz: e.matmul(pz[:, hs], lhsT=ts_.AMh[h][:, 2, :], rhs=ts_.TT[:, 0, hs], start=False, stop=True),
                                 reads=[ts_.AMh[h], ts_.TT], writes=[pz])
                        P.op("act", lambda e, pz=pz: e.activation(out=ts_.NZ.ap(), in_=pz[:, 0:128], func=AF.Copy, scale=-1.0), writes=[pz, ts_.NZ])
                        yield
                        pu = C.nb()
                        for h in range(2):
                            hs = slice(h * 64, (h + 1) * 64)
                            P.op("pe", lambda e, hs=hs, h=h, pu=pu: e.matmul(pu[:, hs], lhsT=Th[h].ap(), rhs=ts_.NZ[:, hs], start=True, stop=True),
                                 reads=[Th[h], ts_.NZ], writes=[pu])
                        P.op("dve", lambda e, pu=pu: e.tensor_copy(out=ts_.UT.ap(), in_=pu[:, 0:128]), writes=[pu, ts_.UT])
                        yield
                        if lat:
                            py = C.nb()
                            for h in range(2):
                                hs = slice(h * 64, (h + 1) * 64)
                                P.op("pe", lambda e, hs=hs, c=c, py=py, So=So: e.matmul(py[hs, 0:128], lhsT=So[hs, :], rhs=ts_.KR[hs, c, 1, :], start=True, stop=False),
                                     reads=[So, ts_.KR], writes=[py])
                                P.op("pe", lambda e, hs=hs, h=h, py=py: e.matmul(py[hs, 0:128], lhsT=ts_.UT[:, hs], rhs=ts_.AMh[h][:, 1, :], start=False, stop=False),
                                     reads=[ts_.UT, ts_.AMh[h]], writes=[py])
                                P.op("pe", lambda e, hs=hs, h=h, py=py: e.matmul(py[hs, 0:128], lhsT=ts_.TT[:, 0, hs], rhs=ts_.AMh[h][:, 3, :], start=False, stop=True),
                                     reads=[ts_.TT, ts_.AMh[h]], writes=[py])
                            if d == 0:
                                P.op("act", lambda e, py=py, csl=csl: e.activation(out=ts_.ys32[:, csl], in_=py[:, 0:128], func=AF.Copy), writes=[py, ts_.ys32])
                            else:
                                P.op("dve", lambda e, py=py, csl=csl: e.tensor_tensor(out=ts_.ys32[:, csl], in0=py[:, 0:128], in1=ts_.yfl[:, csl], op=ALU.add),
                                     reads=[ts_.yfl], writes=[py, ts_.ys32])
                        pS = C.nb()
                        for h in range(2):
                            hs = slice(h * 64, (h + 1) * 64)
                            P.op("pe", lambda e, hs=hs, pS=pS: e.matmul(pS[hs, 0:64], lhsT=ts_.TT[:, 1, hs], rhs=ts_.UT[:, hs], start=True, stop=False),
                                 reads=[ts_.TT, ts_.UT], writes=[pS])
                            P.op("pe", lambda e, hs=hs, pS=pS: e.matmul(pS[hs, 0:64], lhsT=ts_.TT[:, 2, hs], rhs=ts_.TT[:, 0, hs], start=False, stop=True),
                                 reads=[ts_.TT], writes=[pS])
                        S3 = S32[d][p]
                        P.op("dve", lambda e, pS=pS, S3=S3, gcol=gcol: e.scalar_tensor_tensor(out=S3.ap(), in0=S3.ap(), scalar=ts_.gg[:, gcol:gcol + 1],
                                                                                              in1=pS[:, 0:64], op0=ALU.mult, op1=ALU.add),
                             reads=[ts_.gg], writes=[pS, S3])
                        P.op("act", lambda e, S3=S3, Sn=Sn: e.activation(out=Sn.ap(), in_=S3.ap(), func=AF.Copy), reads=[S3], writes=[Sn])
                        yield
                    if not lat:
                        return
                    if d == 0:
                        yb = Buf("yf_%d_%d" % (p, bi))
                        C.yf_bufs[(p, bi)] = yb
                        P.dma("sp", dr["yf"][p, :, t0:t0 + NB], ts_.ys32.ap(), reads=[ts_.ys32], writes=[yb])
                        return
                    P.op("act", lambda e: e.activation(out=ts_.ys16.ap(), in_=ts_.ys32.ap(), func=AF.Copy), reads=[ts_.ys32], writes=[ts_.ys16])
                    pm_ = C.nb()
                    P.op("pe", lambda e, pm_=pm_: e.matmul(pm_[:, 0:NB], lhsT=C.bmean.ap(), rhs=ts_.ys16.ap(), start=True, stop=True),
                         reads=[C.bmean, ts_.ys16], writes=[pm_])
                    P.op("dve", lambda e, pm_=pm_: e.tensor_tensor(out=ts_.yc.ap(), in0=ts_.ys32.ap(), in1=pm_[:, 0:NB], op=ALU.subtract),
                         reads=[ts_.ys32], writes=[pm_, ts_.yc])
                    yield
                    P.op("act", lambda e: e.activation(out=ts_.yc2.ap(), in_=ts_.yc.ap(), func=AF.Square), reads=[ts_.yc], writes=[ts_.yc2])
                    pv = C.nb()
                    P.op("pe", lambda e, pv=pv: e.matmul(pv[:, 0:NB], lhsT=C.bmean.ap(), rhs=ts_.yc2.ap(), start=True, stop=True),
                         reads=[C.bmean, ts_.yc2], writes=[pv])
                    P.op("act", lambda e, pv=pv: e.activation(out=ts_.rstd.ap(), in_=pv[:, 0:NB], func=AF.Sqrt, bias=C.epsx.ap()),
                         reads=[C.epsx], writes=[pv, ts_.rstd])
                    P.op("dve", lambda e: e.reciprocal(out=ts_.rstd.ap(), in_=ts_.rstd.ap()), writes=[ts_.rstd])
                    yield
                    P.op("dve", lambda e: e.tensor_tensor(out=ts_.yn.ap(), in0=ts_.yc.ap(), in1=ts_.rstd.ap(), op=ALU.mult), reads=[ts_.yc, ts_.rstd], writes=[ts_.yn])
                    P.op("dve", lambda e, p=p: e.tensor_scalar(out=ts_.yn.ap(), in0=ts_.yn.ap(), scalar1=kvec[:, 3, p:p + 1], scalar2=kvec[:, 4, p:p + 1],
                                                               op0=ALU.mult, op1=ALU.add), reads=[kvec], writes=[ts_.yn])
                    pz = C.nb()
                    P.op("pe", lambda e, p=p, pz=pz: e.matmul(pz[:, 0:NB], lhsT=LW2[32:64, 0, p * 128:(p + 1) * 128], rhs=LA16[32:64, :],
                                                              start=True, stop=True), reads=[LW2, LA16], writes=[pz])
                    P.op("act", lambda e, p=p, pz=pz: e.activation(out=ts_.a2_.ap(), in_=pz[:, 0:NB], func=AF.Sigmoid, bias=w0a0[:, 2, p:p + 1]),
                         reads=[w0a0], writes=[pz, ts_.a2_])
                    yield
                    P.op("dve", lambda e, p=p: e.tensor_scalar(out=ts_.a2_.ap(), in0=ts_.a2_.ap(), scalar1=kvec[:, 1, p:p + 1], scalar2=omka[:, p:p + 1],
                                                               op0=ALU.mult, op1=ALU.add), reads=[kvec, omka], writes=[ts_.a2_])
                    P.op("pool", lambda e: e.tensor_tensor(out=ts_.a2_.ap(), in0=ts_.a2_.ap(), in1=ts_.fac.ap(), op=ALU.add), reads=[ts_.fac], writes=[ts_.a2_])
                    P.op("pool", lambda e: e.tensor_tensor(out=ts_.kdir2.ap(), in0=ts_.k32.ap(), in1=ts_.a2_.ap(), op=ALU.mult), reads=[ts_.k32, ts_.a2_], writes=[ts_.kdir2])
                    P.op("pool", lambda e: e.tensor_tensor(out=ts_.rk.ap(), in0=ts_.r32.ap(), in1=ts_.kdir2.ap(), op=ALU.mult), reads=[ts_.r32, ts_.kdir2], writes=[ts_.rk])
                    P.op("pool", lambda e, p=p: e.tensor_scalar(out=ts_.rk16.ap(), in0=ts_.rk.ap(), scalar1=rkh[:, p:p + 1], scalar2=None, op0=ALU.mult),
                         reads=[ts_.rk, rkh], writes=[ts_.rk16])
                    pbn = C.nb()
                    P.op("pe", lambda e, pbn=pbn: e.matmul(pbn[:, 0:NB], lhsT=C.bones.ap(), rhs=ts_.rk16.ap(), start=True, stop=True),
                         reads=[C.bones, ts_.rk16], writes=[pbn])
                    P.op("dve", lambda e, pbn=pbn: e.tensor_tensor(out=ts_.bonus.ap(), in0=pbn[:, 0:NB], in1=ts_.v32.ap(), op=ALU.mult),
                         reads=[ts_.v32], writes=[pbn, ts_.bonus])
                    yield
                    P.op("pool", lambda e: e.tensor_tensor(out=ts_.bonus.ap(), in0=ts_.bonus.ap(), in1=ts_.yn.ap(), op=ALU.add), reads=[ts_.yn], writes=[ts_.bonus])
                    pgt = C.nb()
                    P.op("pe", lambda e, p=p, pgt=pgt: e.matmul(pgt[:, 0:NB], lhsT=G2W[0:96, p * 128:(p + 1) * 128], rhs=sg16.ap(), start=True, stop=True),
                         reads=[G2W, sg16], writes=[pgt])
                    P.op("dve", lambda e, pgt=pgt: e.tensor_tensor(out=ts_.o16.ap(), in0=pgt[:, 0:NB], in1=ts_.bonus.ap(), op=ALU.mult),
                         reads=[ts_.bonus], writes=[pgt, ts_.o16])
                    mb = Buf("mt_%d_%d" % (p, bi))
                    C.mt_bufs[(p, bi)] = mb
                    P.dma("sp", dr["mt"][p, :, t0:t0 + NB], ts_.o16.ap(), reads=[ts_.o16], writes=[mb])
                import itertools
                for pp0 in (0, 2):
                    for _ in itertools.zip_longest(pair_gen(pp0, TS[0]), pair_gen(pp0 + 1, TS[1])):
                        pass


def stage_attn(C):
    P, dr = C.P, C.dr
    HT, HC = C.HT, C.HC
    bk = C.banks
    NKT = (T_CTX + T_LAT) // 128
    with P.scope():
        stg = [P.sbuf("astg%d" % i, [128, 8, 256], F32) for i in range(2)]
        Wh = P.sbuf("Wh", [128, 8, 640], BF16)
        KT = P.sbuf("KT", [128, T_CTX + T_LAT], BF16)
        VT = P.sbuf("VT", [128, NKT, 128], BF16)
        QB = [P.sbuf("QB%d" % i, [128, 512], BF16) for i in range(2)]
        PT = [P.sbuf("PT%d" % i, [128, 512], BF16) for i in range(3)]
        accD = P.sbuf("accD", [128, 512], F32)
        accP = P.sbuf("accP", [128, 512], F32)
        ones32a = P.sbuf("ones32a", [128, 128], F32)
        P.op("pool", lambda e: e.memset(ones32a.ap(), 1.0), writes=[ones32a])
        rC = [P.sbuf("rC%d" % i, [128, NB], F32) for i in range(2)]
        rS = [P.sbuf("rS%d" % i, [128, NB], F32) for i in range(2)]
        t1 = P.sbuf("rp_t1", [128, NB], F32)
        t2 = P.sbuf("rp_t2", [128, NB], F32)
        rl = P.sbuf("at_rl", [128, 512], F32)
        on_ = P.sbuf("at_on", [128, 512], F32)
        dif = P.sbuf("at_dif", [128, NB], F32)
        dsq = P.sbuf("at_dsq", [128, NB], BF16)
        drs = P.sbuf("at_drs", [128, NB], F32)
        ao16 = [P.sbuf("at_o16_%d" % i, [128, NB], BF16) for i in range(2)]
        lamt = P.sbuf("lamt", [128, 4, 64], F32)
        lpr = P.sbuf("lpr", [128, 2, 64], F32)
        lsum = P.sbuf("lsum", [128, 2], F32)
        nlam = P.sbuf("nlam", [128, 1], F32)
        sbw = P.sbuf("sbw", [128, 1], F32)
        P.dma("sp", lamt.ap(), dr["lam"], writes=[lamt])
        P.dma("sp", sbw.ap(), dr["subln"], writes=[sbw])
        P.op("dve", lambda e: e.tensor_tensor(out=lpr.ap(), in0=lamt[:, 0:4:2, :], in1=lamt[:, 1:4:2, :], op=ALU.mult),
             reads=[lamt], writes=[lpr])
        P.op("dve", lambda e: e.reduce_sum(out=lsum.ap(), in_=lpr.ap(), axis=AX.X), reads=[lpr], writes=[lsum])
        P.op("act", lambda e: e.activation(out=lsum.ap(), in_=lsum.ap(), func=AF.Exp), writes=[lsum])
        P.op("dve", lambda e: e.tensor_tensor(out=nlam.ap(), in0=lsum[:, 1:2], in1=lsum[:, 0:1], op=ALU.subtract),
             reads=[lsum], writes=[nlam])
        P.op("dve", lambda e: e.tensor_scalar(out=nlam.ap(), in0=nlam.ap(), scalar1=-0.2, scalar2=None, op0=ALU.add), writes=[nlam])
        P.op("dve", lambda e: e.tensor_scalar(out=sbw.ap(), in0=sbw.ap(), scalar1=0.8, scalar2=None, op0=ALU.mult), writes=[sbw])
        for q in QB:
            P.op("pool", lambda e, q=q: e.memset(q.ap(), 0.0), writes=[q])
        C.stg_i = 0
        ri = [0]

        def load_rope(bi):
            i = ri[0] % 2
            ri[0] += 1
            P.dma("sp", rC[i].ap(), dr["ropeC"][:, bi * NB:(bi + 1) * NB], writes=[rC[i]])
            P.dma("sp", rS[i].ap(), dr["ropeS"][:, bi * NB:(bi + 1) * NB], writes=[rS[i]])
            return rC[i], rS[i]

        def proj_rope(bank, c0, bi, rc, rs, outs):
            t0 = bi * NB
            for g in range(2):
                for k in range(8):
                    P.op("pe", lambda e, g=g, k=k: e.matmul(bank[:, g * NB:(g + 1) * NB], lhsT=Wh[:, k, c0 + g * 128:c0 + (g + 1) * 128],
                                                             rhs=HT[:, k, 1 + t0:1 + t0 + NB], start=(k == 0), stop=(k == 7)),
                         reads=[Wh, HT], writes=[bank])
            P.op("dve", lambda e: e.tensor_tensor(out=t1.ap(), in0=bank[:, 0:NB], in1=rc.ap(), op=ALU.mult), reads=[rc], writes=[bank, t1])
            P.op("dve", lambda e: e.tensor_tensor(out=t2.ap(), in0=bank[:, NB:2 * NB], in1=rs.ap(), op=ALU.mult), reads=[rs], writes=[bank, t2])
            for rows, oap, ob in outs:
                P.op("pool", lambda e, rows=rows, oap=oap: e.tensor_tensor(out=oap, in0=t1[rows, :], in1=t2[rows, :], op=ALU.add),
                     reads=[t1, t2], writes=[ob])

        for h in range(4):
            qc = 1696 + h * 128
            kc = 1696 + 512 + h * 128
            vc = 1696 + 1024 + h * 128
            load_w_bf16(C, (Wh, 0), dr["w_in"][:, qc:qc + 128], 128, stg)
            load_w_bf16(C, (Wh, 128), dr["w_qks"][:, h * 128:(h + 1) * 128], 128, stg)
            load_w_bf16(C, (Wh, 256), dr["w_in"][:, kc:kc + 128], 128, stg)
            load_w_bf16(C, (Wh, 384), dr["w_qks"][:, 512 + h * 128:512 + (h + 1) * 128], 128, stg)
            load_w_bf16(C, (Wh, 512), dr["w_in"][:, vc:vc + 128], 128, stg)
            pb = C.nb()
            for k in range(8):
                P.op("pe", lambda e, k=k, pb=pb: e.matmul(pb[:, 0:T_CTX], lhsT=Wh[:, k, 256:384], rhs=HC[:, k, 1:1 + T_CTX],
                                                           start=(k == 0), stop=(k == 7)), reads=[Wh, HC], writes=[pb])
            P.op("act", lambda e, pb=pb: e.activation(out=KT[:, 0:T_CTX], in_=pb[:, 0:T_CTX], func=AF.Copy), writes=[pb, KT])
            for bi in range(NLB):
                rc, rs = load_rope(bi)
                proj_rope(C.nb(), 256, bi, rc, rs, [(slice(0, 128), KT[:, T_CTX + bi * NB:T_CTX + (bi + 1) * NB], KT)])
            for j in range(NKT):
                Hs, c0 = (HC, 1 + j * 128) if j < 2 else (HT, 1 + (j - 2) * 128)
                pv = C.nb()
                for k in range(8):
                    P.op("pe", lambda e, k=k, pv=pv, Hs=Hs, c0=c0: e.matmul(pv[:, 0:128], lhsT=Hs[:, k, c0:c0 + 128], rhs=Wh[:, k, 512:640],
                                                                           start=(k == 0), stop=(k == 7)), reads=[Wh, Hs], writes=[pv])
                eng = "act" if j % 2 == 0 else "dve"
                if eng == "act":
                    P.op("act", lambda e, pv=pv, j=j: e.activation(out=VT[:, j, :], in_=pv[:, 0:128], func=AF.Copy), writes=[pv, VT])
                else:
                    P.op("dve", lambda e, pv=pv, j=j: e.tensor_copy(out=VT[:, j, :], in_=pv[:, 0:128]), writes=[pv, VT])

            def q_proj(bi):
                rc, rs = load_rope(bi)
                qb = QB[bi % 2]
                proj_rope(bk[6], 0, bi, rc, rs, [(slice(0, 64), qb[0:64, 0:NB], qb), (slice(64, 128), qb[64:128, NB:2 * NB], qb)])

            q_proj(0)
            for bi in range(NLB):
                qb = QB[bi % 2]
                po, pl = bk[3 + bi % 2], bk[5]

                def qk(j):
                    ps = bk[j % 3]
                    P.op("pe", lambda e, j=j, ps=ps: e.matmul(ps.ap(), lhsT=KT[:, j * 128:(j + 1) * 128], rhs=qb.ap(), start=True, stop=True),
                         reads=[KT, qb], writes=[ps])

                qk(0)
                qk(1)
                qk(2)
                if bi + 1 < NLB:
                    q_proj(bi + 1)
                nD = nP = 0
                for j in range(NKT):
                    ps, pt = bk[j % 3], PT[j % 3]
                    P.op("act", lambda e, ps=ps, pt=pt: e.activation(out=pt.ap(), in_=ps.ap(), func=AF.Exp, scale=0.125), writes=[ps, pt])
                    P.op("pe", lambda e, j=j, pt=pt: e.matmul(po.ap(), lhsT=VT[:, j, :], rhs=pt.ap(), start=(j == 0), stop=(j == NKT - 1)),
                         reads=[VT, pt], writes=[po])
                    if j % 3 == 2:
                        if nP == 0:
                            P.op("pool", lambda e, pt=pt: e.tensor_copy(out=accP.ap(), in_=pt.ap()), reads=[pt], writes=[accP])
                        else:
                            P.op("pool", lambda e, pt=pt: e.tensor_tensor(out=accP.ap(), in0=accP.ap(), in1=pt.ap(), op=ALU.add), reads=[pt], writes=[accP])
                        nP += 1
                    else:
                        if nD == 0:
                            P.op("dve", lambda e, pt=pt: e.tensor_copy(out=accD.ap(), in_=pt.ap()), reads=[pt], writes=[accD])
                        else:
                            P.op("dve", lambda e, pt=pt: e.tensor_tensor(out=accD.ap(), in0=accD.ap(), in1=pt.ap(), op=ALU.add), reads=[pt], writes=[accD])
                        nD += 1
                    if j + 3 < NKT:
                        qk(j + 3)
                P.op("pe", lambda e: e.matmul(pl.ap(), lhsT=ones32a.ap(), rhs=accD.ap(), start=True, stop=False), reads=[ones32a, accD], writes=[pl])
                P.op("pe", lambda e: e.matmul(pl.ap(), lhsT=ones32a.ap(), rhs=accP.ap(), start=False, stop=True), reads=[ones32a, accP], writes=[pl])
                P.op("dve", lambda e: e.reciprocal(out=rl.ap(), in_=pl.ap()), writes=[pl, rl])
                P.op("dve", lambda e: e.tensor_tensor(out=on_.ap(), in0=po.ap(), in1=rl.ap(), op=ALU.mult), reads=[rl], writes=[po, on_])
                P.op("dve", lambda e: e.scalar_tensor_tensor(out=dif.ap(), in0=on_[:, NB:2 * NB], scalar=nlam.ap(), in1=on_[:, 0:NB],
                                                             op0=ALU.mult, op1=ALU.add), reads=[on_, nlam], writes=[dif])
                P.op("pool", lambda e: e.tensor_tensor(out=dsq.ap(), in0=dif.ap(), in1=dif.ap(), op=ALU.mult), reads=[dif], writes=[dsq])
                pm = bk[6]
                P.op("pe", lambda e: e.matmul(pm[:, 0:NB], lhsT=C.mean128.ap(), rhs=dsq.ap(), start=True, stop=True),
                     reads=[C.mean128, dsq], writes=[pm])
                P.op("act", lambda e: e.activation(out=drs.ap(), in_=pm[:, 0:NB], func=AF.Sqrt, bias=C.epss.ap()), reads=[C.epss], writes=[pm, drs])
                P.op("dve", lambda e: e.reciprocal(out=drs.ap(), in_=drs.ap()), writes=[drs])
                P.op("pool", lambda e: e.tensor_tensor(out=dif.ap(), in0=dif.ap(), in1=drs.ap(), op=ALU.mult), reads=[drs], writes=[dif])
                o16 = ao16[bi % 2]
                P.op("pool", lambda e, o16=o16: e.tensor_scalar(out=o16.ap(), in0=dif.ap(), scalar1=sbw.ap(), scalar2=None, op0=ALU.mult),
                     reads=[dif, sbw], writes=[o16])
                mb = Buf("mt_%d_%d" % (4 + h, bi))
                C.mt_bufs[(4 + h, bi)] = mb
                P.dma("sp", dr["mt"][4 + h, :, bi * NB:(bi + 1) * NB], o16.ap(), reads=[o16], writes=[mb])


def stage_out_a(C):
    P, dr = C.P, C.dr
    C.x1_bufs, C.h2_bufs = {}, {}
    with P.scope():
        C.stg_i = 0
        stg = [P.sbuf("ostg%d" % i, [128, 8, 256], F32) for i in range(2)]
        WO = P.sbuf("WO", [128, 8, 1024], BF16)
        load_w_bf16(C, (WO, 0), dr["w_out"], 1024, stg)
        MTb = [P.sbuf("MTb%d" % i, [128, 8, NB], BF16) for i in range(2)]
        xts = [P.sbuf("oxt%d" % i, [128, 8, NB], F32) for i in range(2)]
        y32 = P.sbuf("oy32", [128, 8, NB], F32)
        x1 = [P.sbuf("ox1_%d" % i, [128, 8, NB], F32) for i in range(2)]
        h2 = [P.sbuf("oh2_%d" % i, [128, 8, NB], BF16) for i in range(2)]
        sq = P.sbuf("o_sq", [128, 8, NB], BF16)
        rs = P.sbuf("o_rs", [128, NB], F32)
        tmp = P.sbuf("o_tmp", [128, 8, NB], F32)
        def ld_a(bi):
            t0 = bi * NB
            P.dma("sp", MTb[bi % 2].ap(), dr["mt"][:, :, t0:t0 + NB].rearrange("k p t -> p k t"),
                  reads=[C.mt_bufs[(k, bi)] for k in range(8)], writes=[MTb[bi % 2]])
            P.dma("sp", xts[bi % 2].ap(), dr["xT"][:, t0:t0 + NB].rearrange("(k p) t -> p k t", p=128), writes=[xts[bi % 2]])

        for bi in range(NLB):
            t0 = bi * NB
            mtb, xt, x1b, h2b = MTb[bi % 2], xts[bi % 2], x1[bi % 2], h2[bi % 2]
            if bi == 0:
                ld_a(0)
            if bi + 1 < NLB:
                ld_a(bi + 1)
            for j in range(8):
                py = C.nb()
                for k in range(8):
                    P.op("pe", lambda e, j=j, k=k, py=py: e.matmul(py[:, 0:NB], lhsT=WO[:, k, j * 128:(j + 1) * 128], rhs=mtb[:, k, :],
                                                                   start=(k == 0), stop=(k == 7)), reads=[WO, mtb], writes=[py])
                if j % 2 == 0:
                    P.op("act", lambda e, j=j, py=py: e.activation(out=y32[:, j, :], in_=py[:, 0:NB], func=AF.Copy), writes=[py, y32])
                else:
                    P.op("dve", lambda e, j=j, py=py: e.tensor_copy(out=y32[:, j, :], in_=py[:, 0:NB]), writes=[py, y32])
            P.op("act", lambda e: e.activation(out=sq.ap(), in_=y32.ap(), func=AF.Square), reads=[y32], writes=[sq])
            pb = C.nb()
            for k in range(8):
                P.op("pe", lambda e, k=k, pb=pb: e.matmul(pb[:, 0:NB], lhsT=C.ones.ap(), rhs=sq[:, k, :], start=(k == 0), stop=(k == 7)),
                     reads=[sq, C.ones], writes=[pb])
            P.op("act", lambda e, pb=pb: e.activation(out=rs.ap(), in_=pb[:, 0:NB], func=AF.Sqrt, scale=1.0 / D_MODEL, bias=C.eps6.ap()),
                 reads=[C.eps6], writes=[pb, rs])
            P.op("dve", lambda e: e.reciprocal(out=rs.ap(), in_=rs.ap()), writes=[rs])
            for j in range(8):
                P.op("dve", lambda e, j=j: e.scalar_tensor_tensor(out=tmp[:, j, :], in0=y32[:, j, :], scalar=C.G1[:, j:j + 1], in1=rs.ap(),
                                                                  op0=ALU.mult, op1=ALU.mult), reads=[y32, rs, C.G1], writes=[tmp])
            P.op("pool", lambda e, x1b=x1b, xt=xt: e.tensor_tensor(out=x1b.ap(), in0=tmp.ap(), in1=xt.ap(), op=ALU.add),
                 reads=[tmp, xt], writes=[x1b])
            xb = Buf("x1s_%d" % bi)
            C.x1_bufs[bi] = xb
            P.dma("sp", dr["x1s"][:, :, t0:t0 + NB].rearrange("k p t -> p k t"), x1b.ap(), reads=[x1b], writes=[xb])
            norm_block(C, x1b, NB, lambda k: C.A2[:, k:k + 1], lambda k: C.modT[:, 24 + k, 0:1],
                       lambda k, h2b=h2b: h2b[:, k, :], [C.A2, C.modT], h2b, (sq, rs, tmp))
            hb = Buf("h2s_%d" % bi)
            C.h2_bufs[bi] = hb
            P.dma("sp", dr["h2s"][:, :, t0:t0 + NB].rearrange("k p t -> p k t"), h2b.ap(), reads=[h2b], writes=[hb])


def stage_ffn(C):
    P, dr = C.P, C.dr
    NJ = D_FF // 128
    with P.scope():
        C.stg_i = 0
        WU = P.sbuf("WU", [128, 8, 2 * D_FF], BF16)
        WD = P.sbuf("WD", [128, NJ, 1024], BF16)
        with P.scope():
            stg = [P.sbuf("fstg%d" % i, [128, 8, 256], F32) for i in range(2)]
            load_w_bf16(C, (WU, 0), dr["w_up"], 2 * D_FF, stg)
            for j in range(NJ):
                s_ = stg[C.stg_i % 2]
                C.stg_i += 1
                sv = s_.ap().rearrange("p a b -> p (a b)")[:, 0:1024]
                P.dma("sp", sv, dr["w_down"][j * 128:(j + 1) * 128, :], writes=[s_])
                P.op("pool", lambda e, j=j, sv=sv: e.tensor_copy(out=WD[:, j, :], in_=sv), reads=[s_], writes=[WD])
        FCW = P.sbuf("FCW", [128, 2 * NJ, 3], F32)
        FB = P.sbuf("FB", [128, 2 * NJ], F32)
        C.cwb = FCW
        P.dma("sp", FCW.ap(), dr["fconvT"].rearrange("(c p) j -> p c j", p=128), writes=[FCW])
        P.dma("sp", FB.ap(), dr["fbias"], writes=[FB])
        h2t = [P.sbuf("fh2_%d" % i, [128, 8, NB + 2], BF16) for i in range(2)]
        x1t = [P.sbuf("fx1_%d" % i, [128, 8, NB], F32) for i in range(2)]
        uv = [P.sbuf("fuv%d" % i, [128, NB], F32) for i in range(3)]
        ug = [P.sbuf("fug%d" % i, [128, NB], F32) for i in range(3)]
        sg = [P.sbuf("fsg%d" % i, [128, NB], F32) for i in range(3)]
        act16 = P.sbuf("fact", [128, NJ, NB], BF16)
        f32t = P.sbuf("ff32", [128, 8, NB], F32)
        sq = P.sbuf("f_sq", [128, 8, NB], BF16)
        rs = P.sbuf("f_rs", [128, NB], F32)
        def load_block(bi):
            t0 = bi * NB
            hb, xb = h2t[bi % 2], x1t[bi % 2]
            lo = max(t0 - 1, 0)
            hi = min(t0 + NB + 1, T_LAT)
            rd = [C.h2_bufs[b] for b in (bi - 1, bi, bi + 1) if 0 <= b < NLB]
            if bi == 0:
                P.op("pool", lambda e, hb=hb: e.memset(hb[:, :, 0:1], 0.0), writes=[hb])
            if bi == NLB - 1:
                P.op("pool", lambda e, hb=hb: e.memset(hb[:, :, NB + 1:NB + 2], 0.0), writes=[hb])
            d0 = lo - (t0 - 1)
            P.dma("sp", hb[:, :, d0:d0 + (hi - lo)], dr["h2s"][:, :, lo:hi].rearrange("k p t -> p k t"), reads=rd, writes=[hb])
            P.dma("sp", xb.ap(), dr["x1s"][:, :, t0:t0 + NB].rearrange("k p t -> p k t"), reads=[C.x1_bufs[bi]], writes=[xb])

        def gate_mul(j, u, g, s2):
            P.op("act", lambda e: e.activation(out=s2.ap(), in_=g.ap(), func=AF.Silu), reads=[g], writes=[s2])
            P.op("pool", lambda e: e.tensor_tensor(out=act16[:, j, :], in0=u.ap(), in1=s2.ap(), op=ALU.mult),
                 reads=[u, s2], writes=[act16])

        load_block(0)
        for bi in range(NLB):
            t0 = bi * NB
            hb, xb = h2t[bi % 2], x1t[bi % 2]
            if bi + 1 < NLB:
                load_block(bi + 1)
            prev = None
            for j in range(NJ):
                u, g, s2 = uv[j % 3], ug[j % 3], sg[j % 3]
                proj_conv(C, hb, 0, WU, j * 128, 128, FCW[:, j, :], u.ap(), u, bias=FB[:, j:j + 1], bias_buf=FB)
                proj_conv(C, hb, 0, WU, D_FF + j * 128, 128, FCW[:, NJ + j, :], g.ap(), g, bias=FB[:, NJ + j:NJ + j + 1], bias_buf=FB)
                if prev is not None:
                    gate_mul(*prev)
                prev = (j, u, g, s2)
            gate_mul(*prev)
            for i in range(8):
                pf = C.nb()
                for j in range(NJ):
                    P.op("pe", lambda e, i=i, j=j, pf=pf: e.matmul(pf[:, 0:NB], lhsT=WD[:, j, i * 128:(i + 1) * 128], rhs=act16[:, j, :],
                                                                   start=(j == 0), stop=(j == NJ - 1)), reads=[WD, act16], writes=[pf])
                if i % 2 == 0:
                    P.op("act", lambda e, i=i, pf=pf: e.activation(out=f32t[:, i, :], in_=pf[:, 0:NB], func=AF.Copy), writes=[pf, f32t])
                else:
                    P.op("dve", lambda e, i=i, pf=pf: e.tensor_copy(out=f32t[:, i, :], in_=pf[:, 0:NB]), writes=[pf, f32t])
            P.op("act", lambda e: e.activation(out=sq.ap(), in_=f32t.ap(), func=AF.Square), reads=[f32t], writes=[sq])
            pb = C.nb()
            for k in range(8):
                P.op("pe", lambda e, k=k, pb=pb: e.matmul(pb[:, 0:NB], lhsT=C.ones.ap(), rhs=sq[:, k, :], start=(k == 0), stop=(k == 7)),
                     reads=[sq, C.ones], writes=[pb])
            P.op("act", lambda e, pb=pb: e.activation(out=rs.ap(), in_=pb[:, 0:NB], func=AF.Sqrt, scale=1.0 / D_MODEL, bias=C.eps6.ap()),
                 reads=[C.eps6], writes=[pb, rs])
            P.op("dve", lambda e: e.reciprocal(out=rs.ap(), in_=rs.ap()), writes=[rs])
            for i in range(8):
                P.op("dve", lambda e, i=i: e.scalar_tensor_tensor(out=f32t[:, i, :], in0=f32t[:, i, :], scalar=C.G2[:, i:i + 1], in1=rs.ap(),
                                                                  op0=ALU.mult, op1=ALU.mult), reads=[rs, C.G2], writes=[f32t])
            P.op("pool", lambda e, xb=xb: e.tensor_tensor(out=xb.ap(), in0=f32t.ap(), in1=xb.ap(), op=ALU.add),
                 reads=[f32t], writes=[xb])
            P.dma("sp", dr["outT"][:, t0:t0 + NB].rearrange("(k p) t -> p k t", p=128), xb.ap(), reads=[xb], final=True)
```

```python
import contextlib
import numpy as np
import concourse.bass as bass
import concourse.mybir as mybir
from concourse.bass_utils import run_bass_kernel_spmd

F32 = mybir.dt.float32
BF16 = mybir.dt.bfloat16
AF = mybir.ActivationFunctionType
ALU = mybir.AluOpType
AX = mybir.AxisListType

SEM_ROLL = 30000


class Buf:
    def __init__(self, name, t=None):
        self.name = name
        self.t = t
        self.last_w = None
        self.readers = []
        self.dma_sem = None
        self.dma_cnt = 0

    def ap(self):
        return self.t[:]

    def __getitem__(self, idx):
        return self.t[idx]


class _Rec:
    def __getattr__(self, name):
        return lambda *a, **k: (name, a, k)


_REC = _Rec()


def _bind(fn):
    name, a, k = fn(_REC)
    return lambda e: getattr(e, name)(*a, **k)


class Op:
    __slots__ = ("eng", "fn", "deps", "signal", "token", "is_dma", "sem_buf", "final", "idx")


class Prog:
    ENGS = ("pe", "act", "dve", "pool", "sp")

    def __init__(self, nc):
        self.nc = nc
        self.stack = contextlib.ExitStack()
        self.ops = {e: [] for e in self.ENGS}
        self.nbuf = 0
        self.all_ops = []
        self.final_ops = []

    def sbuf(self, name, shape, dtype):
        self.nbuf += 1
        name = "s%d_%s" % (self.nbuf, name)
        t = self.stack.enter_context(self.nc.sbuf_tensor(name, list(shape), dtype))
        b = Buf(name, t)
        b.readers = list(getattr(self, "fence", []))
        if hasattr(self, "scope_bufs") and self.scope_bufs:
            self.scope_bufs[-1].append(b)
        return b

    def psum(self, name, shape, dtype):
        t = self.stack.enter_context(self.nc.psum_tensor(name, list(shape), dtype))
        return Buf(name, t)

    def view(self, name):
        return Buf(name)

    @contextlib.contextmanager
    def scope(self):
        old = self.stack
        self.stack = contextlib.ExitStack()
        if not hasattr(self, "scope_bufs"):
            self.scope_bufs = []
        self.scope_bufs.append([])
        try:
            yield
        finally:
            self.stack.close()
            self.stack = old
            bufs = self.scope_bufs.pop()
            ops = list(getattr(self, "fence", []))
            for b in bufs:
                if b.last_w is not None:
                    ops.append(b.last_w)
                ops.extend(b.readers)
            best = {}
            dmas = {}
            for o in ops:
                if o.is_dma:
                    dmas[id(o)] = o
                else:
                    if o.eng not in best or best[o.eng].idx < o.idx:
                        best[o.eng] = o
            self.fence = list(best.values()) + list(dmas.values())

    def _deps(self, o, reads, writes):
        deps = []
        for b in list(reads) + list(writes):
            if b.last_w is not None:
                deps.append(b.last_w)
        for b in writes:
            deps.extend(b.readers)
        for b in writes:
            b.last_w = o
            b.readers = []
        for b in reads:
            b.readers.append(o)
        seen = set()
        out = []
        for d in deps:
            if id(d) in seen or d is o:
                continue
            seen.add(id(d))
            if d.eng == "pe" and o.eng == "pe" and not d.is_dma and not o.is_dma:
                continue
            d.signal = True
            out.append(d)
        o.deps = out

    def op(self, eng, fn, reads=(), writes=()):
        o = Op()
        o.eng = eng
        o.fn = _bind(fn)
        o.signal = False
        o.token = None
        o.is_dma = False
        o.sem_buf = None
        o.final = False
        o.idx = len(self.ops[eng])
        self._deps(o, reads, writes)
        self.ops[eng].append(o)
        return o

    def dma(self, eng, out, in_, reads=(), writes=(), final=False):
        o = Op()
        o.eng = eng
        o.fn = lambda e: e.dma_start(out=out, in_=in_)
        o.signal = True
        o.token = None
        o.is_dma = True
        o.final = final
        cands = [b for b in list(writes) + list(reads) if b.t is not None]
        sb = cands[0] if cands else (list(writes) + list(reads))[0]
        o.sem_buf = sb
        o.idx = len(self.ops[eng])
        self._deps(o, reads, writes)
        self.ops[eng].append(o)
        if final:
            self.final_ops.append(o)
        return o

    def emit(self):
        nc = self.nc
        st = self.stack
        eng_sems = {}
        for e in self.ENGS:
            n = 0
            for o in self.ops[e]:
                if o.is_dma:
                    b = o.sem_buf
                    if b.dma_sem is None:
                        b.dma_sem = st.enter_context(nc.semaphore("d_" + b.name))
                    b.dma_cnt += 16
                    o.token = (b.dma_sem, b.dma_cnt, 16)
                elif o.signal:
                    k = n // SEM_ROLL
                    if (e, k) not in eng_sems:
                        eng_sems[(e, k)] = st.enter_context(nc.semaphore("s_%s_%d" % (e, k)))
                    o.token = (eng_sems[(e, k)], n % SEM_ROLL + 1, 1)
                    n += 1
        print("ops:", {e: len(self.ops[e]) for e in self.ENGS}, "signals:", {e: sum(1 for o in self.ops[e] if o.token is not None) for e in self.ENGS}, "nsems", len(eng_sems))
        all_sems = list(eng_sems.values())
        seen_b = set()
        for e in self.ENGS:
            for o in self.ops[e]:
                if o.is_dma and id(o.sem_buf) not in seen_b:
                    seen_b.add(id(o.sem_buf))
                    all_sems.append(o.sem_buf.dma_sem)
        with nc.Block() as blk0:
            @blk0.sync
            def _(eng):
                for sm in all_sems:
                    eng.sem_clear(sm)
        block = st.enter_context(nc.Block())
        hooks = {"pe": block.tensor, "act": block.scalar, "dve": block.vector,
                 "pool": block.gpsimd, "sp": block.sync}
        final_ops = self.final_ops

        def make(e):
            ops = self.ops[e]

            def body(eng):
                waited = {}
                for o in ops:
                    for d in o.deps:
                        sem, val, _ = d.token
                        if waited.get(id(sem), 0) < val:
                            eng.wait_ge(sem, val)
                            waited[id(sem)] = val
                    ins = o.fn(eng)
                    if o.token is not None:
                        ins.then_inc(o.token[0], o.token[2])
                if e == "sp":
                    for o in final_ops:
                        sem, val, _ = o.token
                        eng.wait_ge(sem, val)
            return body

        for e in self.ENGS:
            hooks[e](make(e))
        st.close()


D_MODEL = 1024
T_LAT = 4096
T_CTX = 256
NB = 256
NLB = T_LAT // NB
D_FF = 2816
EXPM05 = float(np.exp(-0.5))


class Ctx:
    pass


def build_program(nc, stage=99):
    P = Prog(nc)
    C = Ctx()
    C.P = P
    C.nc = nc
    dr = {}

    def din(name, shape, dt=F32):
        dr[name] = nc.dram_tensor(name, list(shape), dt, kind="ExternalInput").ap()

    din("xT", [1024, T_LAT]); din("ctxT", [1024, T_CTX]); din("cc", [128, 8, 2])
    din("w_mod", [1024, 6144]); din("b_modT", [128, 48]); din("gvec", [128, 4, 8])
    din("w_in", [1024, 3232]); din("w_qks", [1024, 1024]); din("convT", [1696, 3])
    din("lw2", [64, 2, 512]); din("w0a0", [128, 4, 4]); din("kvec", [128, 5, 4]); din("g2", [96, 512])
    din("lam", [128, 4, 64]); din("subln", [128, 1])
    din("w_out", [1024, 1024]); din("w_up", [1024, 5632]); din("fconvT", [5632, 3]); din("fbias", [128, 44])
    din("w_down", [2816, 1024]); din("ropeC", [128, T_LAT]); din("ropeS", [128, T_LAT]); din("blkmask", [128, 5, 128])
    dr["outT"] = nc.dram_tensor("outT", [1024, T_LAT], F32, kind="ExternalOutput").ap()
    dr["yf"] = nc.dram_tensor("yf_scr", [4, 128, T_LAT], F32).ap()
    dr["mt"] = nc.dram_tensor("mt_scr", [8, 128, T_LAT], BF16).ap()
    dr["x1s"] = nc.dram_tensor("x1_scr", [8, 128, T_LAT], F32).ap()
    dr["h2s"] = nc.dram_tensor("h2_scr", [8, 128, T_LAT], BF16).ap()
    if stage < 99:
        dr["dbg"] = nc.dram_tensor("dbg", [128, 8, T_LAT + 2], F32, kind="ExternalOutput").ap()
    C.dr = dr
    C.yf_bufs = {}
    C.mt_bufs = {}

    C.banks = [P.psum("pb%d" % i, [128, 512], F32) for i in range(7)]
    C.bankT = P.psum("pbT", [128, 1024], BF16)
    C.bank_i = 0

    def nb():
        b = C.banks[C.bank_i % 7]
        C.bank_i += 1
        return b
    C.nb = nb

    ident = P.sbuf("ident", [128, 128], BF16)
    ones = P.sbuf("ones", [128, 128], BF16)
    bones = P.sbuf("bones", [128, 128], BF16)
    bmean = P.sbuf("bmean", [128, 128], BF16)
    mean128 = P.sbuf("mean128", [128, 128], BF16)
    eps6 = P.sbuf("eps6", [128, 1], F32)
    epsx = P.sbuf("epsx", [128, 1], F32)
    epss = P.sbuf("epss", [128, 1], F32)
    P.op("pool", lambda e: e.memset(ident.ap(), 0.0), writes=[ident])
    P.op("pool", lambda e: e.affine_select(out=ident.ap(), in_=ident.ap(), pattern=[[-1, 128]],
                                           compare_op=ALU.not_equal, fill=1.0, base=0, channel_multiplier=1),
         reads=[ident], writes=[ident])
    P.op("pool", lambda e: e.memset(ones.ap(), 1.0), writes=[ones])
    P.op("pool", lambda e: e.memset(mean128.ap(), 1.0 / 128), writes=[mean128])
    P.op("pool", lambda e: e.memset(bones.ap(), 0.0), writes=[bones])
    P.op("pool", lambda e: e.memset(bones[0:64, 0:64], 1.0), writes=[bones])
    P.op("pool", lambda e: e.memset(bones[64:128, 64:128], 1.0), writes=[bones])
    P.op("pool", lambda e: e.memset(bmean.ap(), 0.0), writes=[bmean])
    P.op("pool", lambda e: e.memset(bmean[0:64, 0:64], 1.0 / 64), writes=[bmean])
    P.op("pool", lambda e: e.memset(bmean[64:128, 64:128], 1.0 / 64), writes=[bmean])
    P.op("pool", lambda e: e.memset(eps6.ap(), 1e-6), writes=[eps6])
    P.op("pool", lambda e: e.memset(epsx.ap(), 64e-5), writes=[epsx])
    P.op("pool", lambda e: e.memset(epss.ap(), 1e-5), writes=[epss])
    C.ident, C.ones, C.bones, C.bmean, C.mean128 = ident, ones, bones, bmean, mean128
    C.eps6, C.epsx, C.epss = eps6, epsx, epss

    stage_mod(C)
    import os
    with P.scope():
        C.HT = P.sbuf("HT", [128, 8, T_LAT + 2], BF16)
        C.HC = P.sbuf("HC", [128, 8, T_CTX + 2], BF16)
        for Hb, n in ((C.HT, T_LAT), (C.HC, T_CTX)):
            P.op("pool", lambda e, Hb=Hb: e.memset(Hb[:, :, 0:1], 0.0), writes=[Hb])
            P.op("pool", lambda e, Hb=Hb, n=n: e.memset(Hb[:, :, n + 1:n + 2], 0.0), writes=[Hb])
        stage_prenorm(C)
        if stage == 1:
            dbg_dump_HT(C)
            P.emit()
            return nc
        if not os.environ.get("SKIP_RWKV"):
            stage_rwkv(C)
        if stage == 21:
            with P.scope():
                for k in range(4):
                    tmp = P.sbuf("dbgy%d" % k, [128, T_LAT], F32)
                    P.dma("sp", tmp.ap(), dr["yf"][k], reads=list(C.yf_bufs.values()), writes=[tmp])
                    P.dma("sp", dr["dbg"][:, k, 0:T_LAT], tmp.ap(), reads=[tmp], final=True)
            P.emit()
            return nc
        if stage == 2:
            dbg_dump_mt(C, 0, 4)
            P.emit()
            return nc
        stage_attn(C)
        if stage == 3:
            dbg_dump_mt(C, 4, 8)
            P.emit()
            return nc
        stage_out_a(C)
    stage_ffn(C)
    P.emit()
    return nc


def dbg_dump_HT(C):
    P = C.P
    with P.scope():
        for k in range(8):
            tmp = P.sbuf("dbgt%d" % k, [128, T_LAT + 2], F32)
            P.op("dve", lambda e, k=k, tmp=tmp: e.tensor_copy(out=tmp.ap(), in_=C.HT[:, k, :]), reads=[C.HT], writes=[tmp])
            P.dma("sp", C.dr["dbg"][:, k, :], tmp.ap(), reads=[tmp], final=True)


def dbg_dump_mt(C, k0, k1):
    P = C.P
    with P.scope():
        for k in range(k0, k1):
            tb = P.sbuf("dbgb%d" % k, [128, T_LAT], BF16)
            tmp = P.sbuf("dbgt%d" % k, [128, T_LAT], F32)
            rd = [b for (kk, _), b in C.mt_bufs.items() if kk == k] + list(C.yf_bufs.values())
            P.dma("sp", tb.ap(), C.dr["mt"][k], reads=rd, writes=[tb])
            P.op("dve", lambda e, tmp=tmp, tb=tb: e.tensor_copy(out=tmp.ap(), in_=tb.ap()), reads=[tb], writes=[tmp])
            P.dma("sp", C.dr["dbg"][:, k, 0:T_LAT], tmp.ap(), reads=[tmp], final=True)


def stage_mod(C):
    P, dr = C.P, C.dr
    cc = P.sbuf("cc", [128, 8, 2], F32)
    scc = P.sbuf("scc", [128, 8, 2], F32)
    bm = P.sbuf("bmodT", [128, 48], F32)
    gv = P.sbuf("gvec", [128, 4, 8], F32)
    modT = P.sbuf("modT", [128, 48, 2], F32)
    P.dma("sp", cc.ap(), dr["cc"], writes=[cc])
    P.dma("sp", bm.ap(), dr["b_modT"], writes=[bm])
    P.dma("sp", gv.ap(), dr["gvec"], writes=[gv])
    P.op("act", lambda e: e.activation(out=scc.ap(), in_=cc.ap(), func=AF.Silu), reads=[cc], writes=[scc])
    pm = C.nb()
    with P.scope():
        wst = [P.sbuf("wmst%d" % i, [128, 8, 512], F32) for i in range(2)]
        for g in range(12):
            w = wst[g % 2]
            P.dma("sp", w.ap(), dr["w_mod"][:, g * 512:(g + 1) * 512].rearrange("(k p) n -> p k n", p=128), writes=[w])
            for jj in range(4):
                j = g * 4 + jj
                for k in range(8):
                    P.op("pe", lambda e, w=w, k=k, jj=jj, j=j: e.matmul(
                        pm[:, 2 * j:2 * j + 2], lhsT=w[:, k, jj * 128:(jj + 1) * 128], rhs=scc[:, k, :],
                        start=(k == 0), stop=(k == 7)), reads=[w, scc], writes=[pm])
    pmv = pm[:, 0:96].rearrange("p (j c) -> p j c", c=2)
    for c in range(2):
        P.op("dve", lambda e, c=c: e.tensor_tensor(out=modT[:, :, c], in0=pmv[:, :, c], in1=bm.ap(), op=ALU.add),
             reads=[bm], writes=[pm, modT])
    C.modT = modT
    A1 = P.sbuf("A1", [128, 2, 8], F32)
    G1 = P.sbuf("G1", [128, 8], F32)
    A2 = P.sbuf("A2", [128, 8], F32)
    G2 = P.sbuf("G2", [128, 8], F32)
    for c in range(2):
        P.op("dve", lambda e, c=c: e.scalar_tensor_tensor(out=A1[:, c, :], in0=modT[:, 8:16, c], scalar=1.0,
                                                          in1=gv[:, 0, :], op0=ALU.add, op1=ALU.mult),
             reads=[modT, gv], writes=[A1])
    P.op("dve", lambda e: e.tensor_tensor(out=G1.ap(), in0=modT[:, 16:24, 0], in1=gv[:, 1, :], op=ALU.mult),
         reads=[modT, gv], writes=[G1])
    P.op("dve", lambda e: e.scalar_tensor_tensor(out=A2.ap(), in0=modT[:, 32:40, 0], scalar=1.0, in1=gv[:, 2, :],
                                                 op0=ALU.add, op1=ALU.mult), reads=[modT, gv], writes=[A2])
    P.op("dve", lambda e: e.tensor_tensor(out=G2.ap(), in0=modT[:, 40:48, 0], in1=gv[:, 3, :], op=ALU.mult),
         reads=[modT, gv], writes=[G2])
    C.A1, C.G1, C.A2, C.G2 = A1, G1, A2, G2


def norm_block(C, xt, nbk, A_ap, sh_ap, out_ap, xt_reads, out_buf, tmpn):
    P = C.P
    sq, rs, tmp = tmpn
    P.op("act", lambda e: e.activation(out=sq[:, :, 0:nbk], in_=xt[:, :, 0:nbk], func=AF.Square),
         reads=[xt], writes=[sq])
    pb = C.nb()
    for k in range(8):
        P.op("pe", lambda e, k=k: e.matmul(pb[:, 0:nbk], lhsT=C.ones.ap(), rhs=sq[:, k, 0:nbk],
                                            start=(k == 0), stop=(k == 7)), reads=[sq, C.ones], writes=[pb])
    P.op("act", lambda e: e.activation(out=rs[:, 0:nbk], in_=pb[:, 0:nbk], func=AF.Sqrt, scale=1.0 / D_MODEL,
                                       bias=C.eps6.ap()), reads=[C.eps6], writes=[pb, rs])
    P.op("dve", lambda e: e.reciprocal(out=rs[:, 0:nbk], in_=rs[:, 0:nbk]), writes=[rs])
    for k in range(8):
        P.op("dve", lambda e, k=k: e.scalar_tensor_tensor(out=tmp[:, k, 0:nbk], in0=xt[:, k, 0:nbk], scalar=A_ap(k),
                                                          in1=rs[:, 0:nbk], op0=ALU.mult, op1=ALU.mult),
             reads=[xt, rs] + xt_reads, writes=[tmp])
        P.op("act", lambda e, k=k: e.activation(out=out_ap(k), in_=tmp[:, k, 0:nbk], func=AF.Identity,
                                                bias=sh_ap(k)), reads=[tmp] + xt_reads, writes=[out_buf])


def stage_prenorm(C):
    P, dr = C.P, C.dr
    with P.scope():
        xts = [P.sbuf("xt%d" % i, [128, 8, NB], F32) for i in range(2)]
        sq = P.sbuf("pn_sq", [128, 8, NB], BF16)
        rs = P.sbuf("pn_rs", [128, NB], F32)
        tmp = P.sbuf("pn_tmp", [128, 8, NB], F32)
        blocks = [("c", 0)] + [("l", i) for i in range(NLB)]
        for bi, (src, i) in enumerate(blocks):
            xt = xts[bi % 2]
            if src == "c":
                P.dma("sp", xt.ap(), dr["ctxT"].rearrange("(k p) t -> p k t", p=128), writes=[xt])
                H, cidx, t0 = C.HC, 1, 0
            else:
                P.dma("sp", xt.ap(), dr["xT"][:, i * NB:(i + 1) * NB].rearrange("(k p) t -> p k t", p=128), writes=[xt])
                H, cidx, t0 = C.HT, 0, i * NB
            norm_block(C, xt, NB, lambda k, cidx=cidx: C.A1[:, cidx, k:k + 1],
                       lambda k, cidx=cidx: C.modT[:, k, cidx:cidx + 1],
                       lambda k, H=H, t0=t0: H[:, k, 1 + t0:1 + t0 + NB], [C.A1, C.modT], H, (sq, rs, tmp))


def _rope_tables():
    n_pair = 16
    rows = T_LAT // 64
    row = np.repeat(np.arange(rows, dtype=np.float32), 64)
    col = np.tile(np.arange(64, dtype=np.float32), rows)
    inv = (np.float32(10000.0) ** (-np.arange(n_pair, dtype=np.float32) / np.float32(n_pair))).astype(np.float32)
    ang = np.concatenate([row[:, None] * inv, col[:, None] * inv], axis=-1).astype(np.float32)
    cos = np.cos(ang).astype(np.float32).T
    sin = np.sin(ang).astype(np.float32).T
    Cc = np.concatenate([cos, cos, cos, cos], axis=0)
    Ss = np.concatenate([-sin, sin, -sin, sin], axis=0)
    return np.ascontiguousarray(Cc), np.ascontiguousarray(Ss)


def prep_inputs(inp):
    f = lambda a: np.ascontiguousarray(np.asarray(a, dtype=np.float32))
    x, c, ctx, c_ctx = f(inp["x"]), f(inp["c"]), f(inp["ctx"]), f(inp["c_ctx"])
    pk = lambda v, n: f(v.reshape(n, 128).T)
    sh = {}
    sh["w_mod"] = f(inp["w_mod"][0])
    sh["b_modT"] = pk(f(inp["b_mod"][0]), 48)
    sh["gvec"] = f(np.stack([pk(f(inp[n][0]), 8) for n in ("g_pre_mix", "g_post_mix", "g_pre_ffn", "g_post_ffn")], axis=1))
    w_in = f(inp["w_in"][0])
    sh["w_in"] = w_in
    perm = np.arange(512).reshape(8, 2, 32)[:, ::-1, :].reshape(512)
    qc, kc = 1696, 1696 + 512
    sh["w_qks"] = f(np.concatenate([w_in[:, qc:qc + 512][:, perm], w_in[:, kc:kc + 512][:, perm]], axis=1))
    sh["convT"] = f(inp["rwkv_conv"][0].T)
    sh["lw2"] = f(np.stack([np.concatenate([f(inp["w2_fwd"][0]), f(inp["a2_fwd"][0])], 0),
                            np.concatenate([f(inp["w2_bwd"][0]), f(inp["a2_bwd"][0])], 0)], axis=1))
    sh["w0a0"] = f(np.stack([pk(f(inp[n][0]), 4) for n in ("w0_fwd", "w0_bwd", "a0_fwd", "a0_bwd")], axis=1))
    sh["kvec"] = f(np.stack([pk(f(inp[n][0]).reshape(512), 4) for n in ("k_k", "k_a", "r_k", "ln_x_w", "ln_x_b")], axis=1))
    sh["g2"] = f(inp["g2"][0])
    sh["lam"] = f(np.broadcast_to(np.stack([f(inp[n][0]) for n in ("lam_q1", "lam_k1", "lam_q2", "lam_k2")], 0)[None], (128, 4, 64)))
    sh["subln"] = f(inp["subln_w"][0].reshape(128, 1))
    sh["w_out"] = f(inp["w_out"][0])
    sh["w_up"] = f(inp["w_up"][0])
    sh["fconvT"] = f(inp["ffn_conv"][0].T)
    sh["fbias"] = pk(f(inp["ffn_conv_b"][0]), 44)
    sh["w_down"] = f(inp["w_down"][0])
    sh["ropeC"], sh["ropeS"] = _rope_tables()
    ii = np.arange(128)
    dm = lambda sz: (ii[:, None] // sz == ii[None, :] // sz).astype(np.float32)
    sh["blkmask"] = f(np.stack([dm(8), dm(16) - dm(8), dm(32) - dm(16), dm(64) - dm(32), dm(128) - dm(64)], axis=1))
    maps = []
    for b in range(8):
        m = dict(sh)
        m["xT"] = f(x[b].T)
        m["ctxT"] = f(ctx[b].T)
        m["cc"] = f(np.stack([c[b], c_ctx], -1).reshape(8, 128, 2).transpose(1, 0, 2))
        maps.append(m)
    return maps


def kernel(**inputs):
    maps = prep_inputs(inputs)
    nc = bass.Bass("TRN2", target_bir_lowering=False)
    build_program(nc)
    res = run_bass_kernel_spmd(nc, maps, core_ids=list(range(8)))
    out = np.stack([np.ascontiguousarray(res.results[b]["outT"].T) for b in range(8)], 0)
    return out.astype(np.float32)


def load_w_bf16(C, dst, dram_cols_ap, ncols, stg):
    P = C.P
    for i, c0 in enumerate(range(0, ncols, 256)):
        c1 = min(ncols, c0 + 256)
        s = stg[C.stg_i % 2]
        C.stg_i += 1
        P.dma("sp", s[:, :, 0:c1 - c0], dram_cols_ap[:, c0:c1].rearrange("(k p) n -> p k n", p=128), writes=[s])
        P.op("pool", lambda e, s=s, c0=c0, c1=c1: e.tensor_copy(out=dst[0][:, :, dst[1] + c0:dst[1] + c1], in_=s[:, :, 0:c1 - c0]),
             reads=[s], writes=[dst[0]])


def proj_conv(C, H, t0, W, c0, m, cw, out, out_buf, bias=None, eng2="dve", bias_buf=None):
    P = C.P
    pb = C.nb()
    for k in range(8):
        P.op("pe", lambda e, k=k: e.matmul(pb[0:m, 0:NB + 2], lhsT=W[:, k, c0:c0 + m], rhs=H[:, k, t0:t0 + NB + 2],
                                            start=(k == 0), stop=(k == 7)), reads=[W, H], writes=[pb])
    if bias is None:
        P.op("act", lambda e: e.activation(out=out, in_=pb[0:m, 1:NB + 1], func=AF.Copy, scale=cw[:, 1:2]),
             reads=[C.cwb], writes=[pb, out_buf])
    else:
        P.op("act", lambda e: e.activation(out=out, in_=pb[0:m, 1:NB + 1], func=AF.Identity, scale=cw[:, 1:2], bias=bias),
             reads=[C.cwb, bias_buf], writes=[pb, out_buf])
    P.op(eng2, lambda e: e.scalar_tensor_tensor(out=out, in0=pb[0:m, 0:NB], scalar=cw[:, 0:1], in1=out,
                                                op0=ALU.mult, op1=ALU.add), reads=[C.cwb], writes=[pb, out_buf])
    P.op(eng2, lambda e: e.scalar_tensor_tensor(out=out, in0=pb[0:m, 2:NB + 2], scalar=cw[:, 2:3], in1=out,
                                                op0=ALU.mult, op1=ALU.add), reads=[C.cwb], writes=[pb, out_buf])


def stage_rwkv(C):
    P, dr = C.P, C.dr
    C.stg_i = 0
    with P.scope():
        WR = P.sbuf("WR", [128, 8, 1696], BF16)
        with P.scope():
            stg = [P.sbuf("wstg%d" % i, [128, 8, 256], F32) for i in range(2)]
            load_w_bf16(C, (WR, 0), dr["w_in"][:, 0:1696], 1696, stg)
        CW = P.sbuf("CW", [128, 14, 3], F32)
        C.cwb = CW
        P.dma("sp", CW[:, 0:12, :], dr["convT"][0:1536, :].rearrange("(c p) j -> p c j", p=128), writes=[CW])
        P.dma("sp", CW[0:64, 12, :], dr["convT"][1536:1600, :], writes=[CW])
        P.dma("sp", CW[0:96, 13, :], dr["convT"][1600:1696, :], writes=[CW])
        lw2f = P.sbuf("lw2f", [64, 2, 512], F32)
        LW2 = P.sbuf("LW2", [64, 2, 512], BF16)
        P.dma("sp", lw2f.ap(), dr["lw2"], writes=[lw2f])
        P.op("pool", lambda e: e.tensor_copy(out=LW2.ap(), in_=lw2f.ap()), reads=[lw2f], writes=[LW2])
        g2f = P.sbuf("g2f", [96, 512], F32)
        G2W = P.sbuf("G2W", [96, 512], BF16)
        P.dma("sp", g2f.ap(), dr["g2"], writes=[g2f])
        P.op("pool", lambda e: e.tensor_copy(out=G2W.ap(), in_=g2f.ap()), reads=[g2f], writes=[G2W])
        w0a0 = P.sbuf("w0a0", [128, 4, 4], F32)
        kvec = P.sbuf("kvec", [128, 5, 4], F32)
        P.dma("sp", w0a0.ap(), dr["w0a0"], writes=[w0a0])
        P.dma("sp", kvec.ap(), dr["kvec"], writes=[kvec])
        omka = P.sbuf("omka", [128, 4], F32)
        rkh = P.sbuf("rkh", [128, 4], F32)
        P.op("dve", lambda e: e.tensor_scalar(out=omka.ap(), in0=kvec[:, 1, :], scalar1=-1.0, scalar2=1.0,
                                              op0=ALU.mult, op1=ALU.add), reads=[kvec], writes=[omka])
        P.op("dve", lambda e: e.tensor_scalar(out=rkh.ap(), in0=kvec[:, 2, :], scalar1=0.5, scalar2=None,
                                              op0=ALU.mult), reads=[kvec], writes=[rkh])
        ones32 = P.sbuf("ones32", [128, 128], F32)
        P.op("pool", lambda e: e.memset(ones32.ap(), 1.0), writes=[ones32])
        msk = {}
        for nm, cop, sgn in (("SU", ALU.is_gt, -1), ("IU", ALU.is_ge, -1), ("SL", ALU.is_gt, 1), ("IL", ALU.is_ge, 1)):
            mb = P.sbuf("m" + nm, [128, 128], F32)
            P.op("pool", lambda e, mb=mb, cop=cop, sgn=sgn: e.affine_select(out=mb.ap(), in_=ones32.ap(), pattern=[[-sgn, 128]],
                                                                            compare_op=cop, fill=0.0, base=0, channel_multiplier=sgn),
                 reads=[ones32], writes=[mb])
            msk[nm] = mb
        AMM = []
        for d, (s_, i_) in enumerate((("SU", "IU"), ("SL", "IL"))):
            am = P.sbuf("amm%d" % d, [128, 4, 128], F32)
            for q, nm in enumerate((s_, i_, s_, i_)):
                P.op("pool", lambda e, am=am, q=q, nm=nm: e.tensor_copy(out=am[:, q, :], in_=msk[nm].ap()),
                     reads=[msk[nm]], writes=[am])
            AMM.append(am)
        NTM = [msk["SL"], msk["SU"]]
        rmask = P.sbuf("rmask", [128, NB], F32)
        P.op("pool", lambda e: e.memset(rmask.ap(), 1.0), writes=[rmask])
        for c in range(NB // 128):
            P.op("pool", lambda e, c=c: e.memset(rmask[:, c * 128:c * 128 + 1], 0.0), writes=[rmask])

        def f32t(n, shape=(128, NB)):
            return P.sbuf(n, list(shape), F32)

        def b16t(n, shape=(128, NB)):
            return P.sbuf(n, list(shape), BF16)

        la32, LA16 = f32t("la32", (64, NB)), b16t("LA16", (64, NB))
        gl32, sg16 = f32t("gl32", (96, NB)), b16t("sg16", (96, NB))
        BM2 = P.sbuf("blkmask2", [128, 5, 2, 128], F32)
        P.dma("sp", BM2[:, :, 0, :], dr["blkmask"], writes=[BM2])
        P.dma("sp", BM2[:, :, 1, :], dr["blkmask"], writes=[BM2])
        I2 = P.sbuf("ident2", [128, 2, 128], F32)
        for q_ in range(2):
            P.op("pool", lambda e, q_=q_: e.tensor_copy(out=I2[:, q_, :], in_=C.ident.ap()), reads=[C.ident], writes=[I2])
        ident32 = P.sbuf("ident32", [128, 128], F32)
        P.op("pool", lambda e: e.tensor_copy(out=ident32.ap(), in_=C.ident.ap()), reads=[C.ident], writes=[ident32])
        S32 = [[f32t("S32_%d_%d" % (d, p), (128, 64)) for p in range(4)] for d in range(2)]
        Sb = [[[b16t("Sb_%d_%d_%d" % (d, p, i), (128, 64)) for i in range(2)] for p in range(4)] for d in range(2)]
        sbi = [[0] * 4 for _ in range(2)]
        def make_set(si):
            sfx = "q%d_" % si
            o_ = Ctx()
            r32, k32, v32, v16 = f32t(sfx + "r32"), f32t(sfx + "k32"), f32t(sfx + "v32"), b16t(sfx + "v16")
            sig, aa, kraw, ksq, rn = f32t(sfx + "sig"), f32t(sfx + "aa"), f32t(sfx + "kraw"), b16t(sfx + "ksq"), f32t(sfx + "rn")
            fac, kdir, bb, cs, LL, Lm = f32t(sfx + "fac"), f32t(sfx + "kdir"), f32t(sfx + "bb"), f32t(sfx + "cs"), f32t(sfx + "LL"), f32t(sfx + "Lm")
            gg, ginv, gprev = f32t(sfx + "gg"), f32t(sfx + "ginv"), f32t(sfx + "gprev")
            ld, kk = sig, kraw
            yc, rstd, yn, rk, bonus, a2_, kdir2 = LL, rn, Lm, ginv, gprev, bb, kdir
            Bt, Kt = b16t(sfx + "Bt"), b16t(sfx + "Kt")
            KR = P.sbuf(sfx + "KR", [128, NB // 128, 2, 128], BF16)
            Bg, Kg = b16t(sfx + "Bg", (128, 128)), b16t(sfx + "Kg", (128, 128))
            TT = P.sbuf(sfx + "TT", [128, 3, 128], BF16)
            AMh = [P.sbuf(sfx + "AM%d" % h, [128, 4, 128], BF16) for h in range(2)]
            NN = [[P.sbuf(sfx + "NN%d_%d" % (h, i), [128, 2, 128], F32) for i in range(1)] for h in range(2)]
            PP = None
            Tb = [P.sbuf(sfx + "Tb%d" % h, [128, 128], BF16) for h in range(2)]
            IV = [dict((nm, P.sbuf(sfx + "iv%s_%d" % (nm, h), [128, 2, 128], BF16)) for nm in ("Nb2", "S2", "S4", "X2", "NBl", "Pa", "Pb"))
                  for h in range(2)]
            NZ, UT = b16t(sfx + "NZ", (128, 128)), b16t(sfx + "UT", (128, 128))
            ys32, yfl = f32t(sfx + "ys32"), f32t(sfx + "yfl")
            ys16, yc2 = b16t(sfx + "ys16"), b16t(sfx + "yc2")
            rk16, o16 = b16t(sfx + "rk16"), b16t(sfx + "o16")
            for _n in ['r32', 'k32', 'v32', 'v16', 'sig', 'ld', 'aa', 'a2_', 'kraw', 'ksq', 'rn', 'kk', 'fac', 'kdir', 'kdir2', 'bb', 'cs', 'LL', 'Lm', 'gg', 'ginv', 'gprev', 'Bt', 'Kt', 'KR', 'Bg', 'Kg', 'TT', 'AMh', 'NN', 'PP', 'Tb', 'IV', 'NZ', 'UT', 'ys32', 'yfl', 'ys16', 'yc', 'yc2', 'rstd', 'yn', 'rk', 'rk16', 'bonus', 'o16']:
                setattr(o_, _n, locals()[_n])
            return o_

        TS = [make_set(0), make_set(1)]
        for d in range(2):
            for p in range(4):
                P.op("pool", lambda e, d=d, p=p: e.memset(S32[d][p].ap(), 0.0), writes=[S32[d][p]])
                P.op("pool", lambda e, d=d, p=p: e.memset(Sb[d][p][0].ap(), 0.0), writes=[Sb[d][p][0]])

        NCH = NB // 128
        import os
        lim = int(os.environ.get("RWKV_LIM", "999"))
        cnt = 0
        for d in range(2):
            blocks = [("c", 0)] + ([("l", i) for i in range(NLB)] if d == 0 else [("l", i) for i in reversed(range(NLB))])
            for (src, bi) in blocks:
                cnt += 1
                if cnt > lim:
                    continue
                if os.environ.get("RWKV_SKIPF") and d == 0 and src == "l":
                    continue
                H = C.HC if src == "c" else C.HT
                t0 = 0 if src == "c" else bi * NB
                lat = (src == "l")
                proj_conv(C, H, t0, WR, 1536, 64, CW[0:64, 12, :], la32.ap(), la32)
                P.op("act", lambda e: e.activation(out=LA16[0:32, :], in_=la32[0:32, :], func=AF.Tanh), reads=[la32], writes=[LA16])
                P.op("pool", lambda e: e.tensor_copy(out=LA16[32:64, :], in_=la32[32:64, :]), reads=[la32], writes=[LA16])
                if d == 1 and lat:
                    proj_conv(C, H, t0, WR, 1600, 96, CW[0:96, 13, :], gl32.ap(), gl32)
                    P.op("act", lambda e: e.activation(out=sg16.ap(), in_=gl32.ap(), func=AF.Sigmoid), reads=[gl32], writes=[sg16])
                def pair_gen(p, ts_):
                    proj_conv(C, H, t0, WR, p * 128, 128, CW[:, p, :], ts_.r32.ap(), ts_.r32)
                    yield
                    proj_conv(C, H, t0, WR, 512 + p * 128, 128, CW[:, 4 + p, :], ts_.k32.ap(), ts_.k32)
                    yield
                    proj_conv(C, H, t0, WR, 1024 + p * 128, 128, CW[:, 8 + p, :], ts_.v32.ap(), ts_.v32)
                    yield
                    P.op("pool", lambda e: e.tensor_copy(out=ts_.v16.ap(), in_=ts_.v32.ap()), reads=[ts_.v32], writes=[ts_.v16])
                    pz = C.nb()
                    P.op("pe", lambda e, p=p, d=d: e.matmul(pz[:, 0:NB], lhsT=LW2[0:32, d, p * 128:(p + 1) * 128], rhs=LA16[0:32, :],
                                                             start=True, stop=True), reads=[LW2, LA16], writes=[pz])
                    P.op("act", lambda e, p=p, d=d: e.activation(out=ts_.sig.ap(), in_=pz[:, 0:NB], func=AF.Sigmoid, bias=w0a0[:, d, p:p + 1]),
                         reads=[w0a0], writes=[pz, ts_.sig])
                    P.op("pool", lambda e: e.tensor_scalar(out=ts_.ld.ap(), in0=ts_.sig.ap(), scalar1=-EXPM05, scalar2=None, op0=ALU.mult),
                         reads=[ts_.sig], writes=[ts_.ld])
                    pz = C.nb()
                    P.op("pe", lambda e, p=p, d=d, pz=pz: e.matmul(pz[:, 0:NB], lhsT=LW2[32:64, d, p * 128:(p + 1) * 128], rhs=LA16[32:64, :],
                                                                    start=True, stop=True), reads=[LW2, LA16], writes=[pz])
                    P.op("act", lambda e, p=p, d=d, pz=pz: e.activation(out=ts_.aa.ap(), in_=pz[:, 0:NB], func=AF.Sigmoid, bias=w0a0[:, 2 + d, p:p + 1]),
                         reads=[w0a0], writes=[pz, ts_.aa])
                    yield
                    P.op("pool", lambda e, p=p: e.tensor_scalar(out=ts_.kraw.ap(), in0=ts_.k32.ap(), scalar1=kvec[:, 0, p:p + 1], scalar2=None, op0=ALU.mult),
                         reads=[ts_.k32, kvec], writes=[ts_.kraw])
                    P.op("act", lambda e: e.activation(out=ts_.ksq.ap(), in_=ts_.kraw.ap(), func=AF.Square), reads=[ts_.kraw], writes=[ts_.ksq])
                    pz = C.nb()
                    P.op("pe", lambda e, pz=pz: e.matmul(pz[:, 0:NB], lhsT=C.bones.ap(), rhs=ts_.ksq.ap(), start=True, stop=True),
                         reads=[C.bones, ts_.ksq], writes=[pz])
                    P.op("act", lambda e, pz=pz: e.activation(out=ts_.rn.ap(), in_=pz[:, 0:NB], func=AF.Sqrt), writes=[pz, ts_.rn])
                    P.op("dve", lambda e: e.tensor_scalar(out=ts_.rn.ap(), in0=ts_.rn.ap(), scalar1=1e-12, scalar2=None, op0=ALU.max), writes=[ts_.rn])
                    P.op("dve", lambda e: e.reciprocal(out=ts_.rn.ap(), in_=ts_.rn.ap()), writes=[ts_.rn])
                    P.op("dve", lambda e: e.tensor_tensor(out=ts_.kk.ap(), in0=ts_.kraw.ap(), in1=ts_.rn.ap(), op=ALU.mult), reads=[ts_.kraw, ts_.rn], writes=[ts_.kk])
                    yield
                    P.op("dve", lambda e, p=p: e.tensor_scalar(out=ts_.fac.ap(), in0=ts_.aa.ap(), scalar1=kvec[:, 1, p:p + 1], scalar2=omka[:, p:p + 1],
                                                               op0=ALU.mult, op1=ALU.add), reads=[ts_.aa, kvec, omka], writes=[ts_.fac])
                    P.op("pool", lambda e: e.tensor_tensor(out=ts_.kdir.ap(), in0=ts_.k32.ap(), in1=ts_.fac.ap(), op=ALU.mult), reads=[ts_.k32, ts_.fac], writes=[ts_.kdir])
                    P.op("pool", lambda e: e.tensor_tensor(out=ts_.bb.ap(), in0=ts_.kk.ap(), in1=ts_.aa.ap(), op=ALU.mult), reads=[ts_.kk, ts_.aa], writes=[ts_.bb])
                    P.op("dve", lambda e: e.tensor_tensor_scan(out=ts_.cs.ap(), data0=rmask.ap(), data1=ts_.ld.ap(), initial=0.0,
                                                               op0=ALU.mult, op1=ALU.add), reads=[rmask, ts_.ld], writes=[ts_.cs])
                    if d == 0:
                        Lb = ts_.cs
                    else:
                        for c in range(NCH):
                            P.op("dve", lambda e, c=c: e.tensor_scalar(out=ts_.LL[:, c * 128:(c + 1) * 128], in0=ts_.cs[:, c * 128:(c + 1) * 128],
                                                                       scalar1=-1.0, scalar2=ts_.cs[:, c * 128 + 127:c * 128 + 128],
                                                                       op0=ALU.mult, op1=ALU.add), reads=[ts_.cs], writes=[ts_.LL])
                        P.op("dve", lambda e: e.tensor_tensor(out=ts_.LL.ap(), in0=ts_.LL.ap(), in1=ts_.ld.ap(), op=ALU.add), reads=[ts_.ld], writes=[ts_.LL])
                        Lb = ts_.LL
                    P.op("pool", lambda e, Lb=Lb: e.tensor_tensor(out=ts_.Lm.ap(), in0=Lb.ap(), in1=ts_.ld.ap(), op=ALU.subtract), reads=[Lb, ts_.ld], writes=[ts_.Lm])
                    P.op("act", lambda e, Lb=Lb: e.activation(out=ts_.gg.ap(), in_=Lb.ap(), func=AF.Exp), reads=[Lb], writes=[ts_.gg])
                    P.op("act", lambda e, Lb=Lb: e.activation(out=ts_.ginv.ap(), in_=Lb.ap(), func=AF.Exp, scale=-1.0), reads=[Lb], writes=[ts_.ginv])
                    P.op("act", lambda e: e.activation(out=ts_.gprev.ap(), in_=ts_.Lm.ap(), func=AF.Exp), reads=[ts_.Lm], writes=[ts_.gprev])
                    yield
                    P.op("dve", lambda e: e.tensor_tensor(out=ts_.Bt.ap(), in0=ts_.bb.ap(), in1=ts_.ginv.ap(), op=ALU.mult), reads=[ts_.bb, ts_.ginv], writes=[ts_.Bt])
                    P.op("dve", lambda e: e.tensor_tensor(out=ts_.Kt.ap(), in0=ts_.kdir.ap(), in1=ts_.ginv.ap(), op=ALU.mult), reads=[ts_.kdir, ts_.ginv], writes=[ts_.Kt])
                    P.op("pool", lambda e: e.tensor_tensor(out=ts_.KR[:, :, 0, :], in0=ts_.kk.ap().rearrange("p (c t) -> p c t", t=128),
                                                          in1=ts_.gprev.ap().rearrange("p (c t) -> p c t", t=128), op=ALU.mult),
                         reads=[ts_.kk, ts_.gprev], writes=[ts_.KR])
                    P.op("pool", lambda e: e.tensor_tensor(out=ts_.KR[:, :, 1, :], in0=ts_.r32.ap().rearrange("p (c t) -> p c t", t=128),
                                                          in1=ts_.gg.ap().rearrange("p (c t) -> p c t", t=128), op=ALU.mult),
                         reads=[ts_.r32, ts_.gg], writes=[ts_.KR])
                    yield
                    if d == 1 and lat:
                        if (p, bi) in C.yf_bufs:
                            P.dma("sp", ts_.yfl.ap(), dr["yf"][p, :, t0:t0 + NB], reads=[C.yf_bufs[(p, bi)]], writes=[ts_.yfl])
                        else:
                            P.op("pool", lambda e: e.memset(ts_.yfl.ap(), 0.0), writes=[ts_.yfl])
                    chunks = list(range(NCH)) if d == 0 else list(reversed(range(NCH)))
                    for c in chunks:
                        csl = slice(c * 128, (c + 1) * 128)
                        gcol = c * 128 + 127 if d == 0 else c * 128
                        P.op("dve", lambda e, csl=csl, gcol=gcol: e.tensor_scalar(out=ts_.Bg.ap(), in0=ts_.Bt[:, csl], scalar1=ts_.gg[:, gcol:gcol + 1],
                                                                                  scalar2=None, op0=ALU.mult), reads=[ts_.Bt, ts_.gg], writes=[ts_.Bg])
                        P.op("dve", lambda e, csl=csl, gcol=gcol: e.tensor_scalar(out=ts_.Kg.ap(), in0=ts_.Kt[:, csl], scalar1=ts_.gg[:, gcol:gcol + 1],
                                                                                  scalar2=None, op0=ALU.mult), reads=[ts_.Kt, ts_.gg], writes=[ts_.Kg])
                        pT = C.bankT
                        P.op("pe", lambda e, csl=csl: e.transpose(pT[:, 0:128], ts_.v16[:, csl], C.ident.ap()), reads=[ts_.v16, C.ident], writes=[pT])
                        P.op("pe", lambda e: e.transpose(pT[:, 128:256], ts_.Bg.ap(), C.ident.ap()), reads=[ts_.Bg, C.ident], writes=[pT])
                        P.op("pe", lambda e: e.transpose(pT[:, 256:384], ts_.Kg.ap(), C.ident.ap()), reads=[ts_.Kg, C.ident], writes=[pT])
                        P.op("act", lambda e: e.activation(out=ts_.TT.ap(), in_=pT[:, 0:384].rearrange("p (a b) -> p a b", b=128), func=AF.Copy),
                             writes=[pT, ts_.TT])
                        yield
                        for h in range(2):
                            hs = slice(h * 64, (h + 1) * 64)
                            AM = ts_.AMh[h]
                            pg = C.nb()
                            P.op("pe", lambda e, hs=hs, csl=csl, c=c, pg=pg: e.matmul(pg[:, 0:256], lhsT=ts_.Bt[hs, csl], rhs=ts_.KR[hs, c, :, :],
                                                                                      start=True, stop=True), reads=[ts_.Bt, ts_.KR], writes=[pg])
                            P.op("pe", lambda e, hs=hs, csl=csl, c=c, pg=pg: e.matmul(pg[:, 256:512], lhsT=ts_.Kt[hs, csl], rhs=ts_.KR[hs, c, :, :],
                                                                                      start=True, stop=True), reads=[ts_.Kt, ts_.KR], writes=[pg])
                            N0 = ts_.NN[h][0]
                            P.op("dve", lambda e, N0=N0, pg=pg, d=d: e.tensor_tensor(out=N0[:, 0, :], in0=pg[:, 0:128], in1=AMM[d][:, 0, :], op=ALU.mult),
                                 reads=[AMM[d]], writes=[pg, N0])
                            P.op("dve", lambda e, AM=AM, pg=pg, d=d: e.tensor_tensor(out=AM.ap(), in0=pg.ap().rearrange("p (a b) -> p a b", b=128),
                                                                                    in1=AMM[d].ap(), op=ALU.mult), reads=[AMM[d]], writes=[pg, AM])
                            pn = C.nb()
                            P.op("pe", lambda e, hs=hs, csl=csl, c=c, pn=pn: e.matmul(pn[:, 0:128], lhsT=ts_.KR[hs, c, 0, :], rhs=ts_.Bt[hs, csl],
                                                                                      start=True, stop=True), reads=[ts_.Bt, ts_.KR], writes=[pn])
                            N0 = ts_.NN[h][0]
                            P.op("dve", lambda e, N0=N0, pn=pn, d=d: e.tensor_tensor(out=N0[:, 1, :], in0=pn[:, 0:128], in1=NTM[d].ap(), op=ALU.mult),
                                 reads=[NTM[d]], writes=[pn, N0])
                        def inv_gen(h):
                            N0 = ts_.NN[h][0]
                            iv = ts_.IV[h]
                            Nb2, S2, S4, X2, NBl = iv["Nb2"], iv["S2"], iv["S4"], iv["X2"], iv["NBl"]
                            Pc = [iv["Pa"], iv["Pb"]]
                            P.op("dve", lambda e: e.tensor_tensor(out=Nb2.ap(), in0=N0.ap(), in1=BM2[:, 0, :, :], op=ALU.mult), reads=[N0, BM2], writes=[Nb2])
                            pq = C.nb()
                            P.op("pe", lambda e: e.matmul(pq[:, 0:128], lhsT=Nb2[:, 1, :], rhs=Nb2[:, 0, :], start=True, stop=True), reads=[Nb2], writes=[pq])
                            P.op("pe", lambda e: e.matmul(pq[:, 128:256], lhsT=Nb2[:, 0, :], rhs=Nb2[:, 1, :], start=True, stop=True), reads=[Nb2], writes=[pq])
                            P.op("act", lambda e: e.activation(out=S2.ap(), in_=pq[:, 0:256].rearrange("p (a b) -> p a b", b=128), func=AF.Copy), writes=[pq, S2])
                            P.op("pool", lambda e: e.tensor_tensor(out=Pc[0].ap(), in0=I2.ap(), in1=Nb2.ap(), op=ALU.subtract),
                                 reads=[Nb2, I2], writes=[Pc[0]])
                            yield
                            pq2 = C.nb()
                            P.op("pe", lambda e: e.matmul(pq2[:, 0:128], lhsT=S2[:, 1, :], rhs=S2[:, 0, :], start=True, stop=True), reads=[S2], writes=[pq2])
                            P.op("pe", lambda e: e.matmul(pq2[:, 128:256], lhsT=S2[:, 0, :], rhs=S2[:, 1, :], start=True, stop=True), reads=[S2], writes=[pq2])
                            P.op("act", lambda e: e.activation(out=S4.ap(), in_=pq2[:, 0:256].rearrange("p (a b) -> p a b", b=128), func=AF.Copy), writes=[pq2, S4])
                            pp_ = C.nb()
                            P.op("pe", lambda e: e.matmul(pp_[:, 0:128], lhsT=S2[:, 1, :], rhs=Pc[0][:, 0, :], start=True, stop=True), reads=[S2, Pc[0]], writes=[pp_])
                            P.op("pe", lambda e: e.matmul(pp_[:, 128:256], lhsT=S2[:, 0, :], rhs=Pc[0][:, 1, :], start=True, stop=True), reads=[S2, Pc[0]], writes=[pp_])
                            P.op("dve", lambda e: e.tensor_tensor(out=Pc[1].ap(), in0=pp_[:, 0:256].rearrange("p (a b) -> p a b", b=128), in1=Pc[0].ap(), op=ALU.add),
                                 reads=[Pc[0]], writes=[pp_, Pc[1]])
                            yield
                            pp2 = C.nb()
                            P.op("pe", lambda e: e.matmul(pp2[:, 0:128], lhsT=S4[:, 1, :], rhs=Pc[1][:, 0, :], start=True, stop=True), reads=[S4, Pc[1]], writes=[pp2])
                            P.op("pe", lambda e: e.matmul(pp2[:, 128:256], lhsT=S4[:, 0, :], rhs=Pc[1][:, 1, :], start=True, stop=True), reads=[S4, Pc[1]], writes=[pp2])
                            P.op("dve", lambda e: e.tensor_tensor(out=Pc[0].ap(), in0=pp2[:, 0:256].rearrange("p (a b) -> p a b", b=128), in1=Pc[1].ap(), op=ALU.add),
                                 reads=[Pc[1]], writes=[pp2, Pc[0]])
                            cur = 0
                            yield
                            for l in range(4):
                                Tc, Tn = Pc[cur], Pc[1 - cur]
                                last = (l == 3)
                                P.op("pool", lambda e, l=l: e.tensor_tensor(out=NBl.ap(), in0=N0.ap(), in1=BM2[:, 1 + l, :, :], op=ALU.mult),
                                     reads=[N0, BM2], writes=[NBl])
                                px = C.nb()
                                P.op("pe", lambda e, Tc=Tc, px=px: e.matmul(px[:, 0:128], lhsT=NBl[:, 1, :], rhs=Tc[:, 0, :], start=True, stop=True),
                                     reads=[NBl, Tc], writes=[px])
                                if not last:
                                    P.op("pe", lambda e, Tc=Tc, px=px: e.matmul(px[:, 128:256], lhsT=NBl[:, 0, :], rhs=Tc[:, 1, :], start=True, stop=True),
                                         reads=[NBl, Tc], writes=[px])
                                    P.op("act", lambda e, px=px: e.activation(out=X2.ap(), in_=px[:, 0:256].rearrange("p (a b) -> p a b", b=128), func=AF.Copy),
                                         writes=[px, X2])
                                else:
                                    P.op("act", lambda e, px=px: e.activation(out=X2[:, 0, :], in_=px[:, 0:128], func=AF.Copy), writes=[px, X2])
                                yield
                                pr = C.nb()
                                P.op("pe", lambda e, Tc=Tc, pr=pr: e.matmul(pr[:, 0:128], lhsT=Tc[:, 1, :], rhs=X2[:, 0, :], start=True, stop=True),
                                     reads=[Tc, X2], writes=[pr])
                                if not last:
                                    P.op("pe", lambda e, Tc=Tc, pr=pr: e.matmul(pr[:, 128:256], lhsT=Tc[:, 0, :], rhs=X2[:, 1, :], start=True, stop=True),
                                         reads=[Tc, X2], writes=[pr])
                                    P.op("dve", lambda e, pr=pr, Tc=Tc, Tn=Tn: e.tensor_tensor(out=Tn.ap(), in0=Tc.ap(),
                                                                                              in1=pr[:, 0:256].rearrange("p (a b) -> p a b", b=128), op=ALU.subtract),
                                         reads=[Tc], writes=[pr, Tn])
                                else:
                                    P.op("dve", lambda e, pr=pr, Tc=Tc: e.tensor_tensor(out=ts_.Tb[h].ap(), in0=Tc[:, 0, :], in1=pr[:, 0:128], op=ALU.subtract),
                                         reads=[Tc], writes=[pr, ts_.Tb[h]])
                                cur = 1 - cur
                                yield

                        for _ in zip(inv_gen(0), inv_gen(1)):
                            yield
                        Th = ts_.Tb
                        So = Sb[d][p][sbi[d][p] % 2]
                        Sn = Sb[d][p][(sbi[d][p] + 1) % 2]
                        sbi[d][p] += 1
                        pz = C.nb()
                        for h in range(2):
                            hs = slice(h * 64, (h + 1) * 64)
                            P.op("pe", lambda e, hs=hs, c=c, pz=pz, So=So: e.matmul(pz[:, hs], lhsT=ts_.KR[hs, c, 0, :], rhs=So[hs, :], start=True, stop=False),
                                 reads=[ts_.KR, So], writes=[pz])
                            P.op("pe", lambda e, hs=hs, h=h, pz=pz: e.matmul(pz[:, hs], lhsT=ts_.AMh[h][:, 2, :], rhs=ts_.TT[:, 0, hs], start=False, stop=True),
                                 reads=[ts_.AMh[h], ts_.TT], writes=[pz])
                        P.op("act", lambda e, pz=pz: e.activation(out=ts_.NZ.ap(), in_=pz[:, 0:128], func=AF.Copy, scale=-1.0), writes=[pz, ts_.NZ])
                        yield
                        pu = C.nb()
                        for h in range(2):
                            hs = slice(h * 64, (h + 1) * 64)
                            P.op("pe", lambda e, hs=hs, h=h, pu=pu: e.matmul(pu[:, hs], lhsT=Th[h].ap(), rhs=ts_.NZ[:, hs], start=True, stop=True),
                                 reads=[Th[h], ts_.NZ], writes=[pu])
                        P.op("dve", lambda e, pu=pu: e.tensor_copy(out=ts_.UT.ap(), in_=pu[:, 0:128]), writes=[pu, ts_.UT])
                        yield
                        if lat:
                            py = C.nb()
                            for h in range(2):
                                hs = slice(h * 64, (h + 1) * 64)
                                P.op("pe", lambda e, hs=hs, c=c, py=py, So=So: e.matmul(py[hs, 0:128], lhsT=So[hs, :], rhs=ts_.KR[hs, c, 1, :], start=True, stop=False),
                                     reads=[So, ts_.KR], writes=[py])
                                P.op("pe", lambda e, hs=hs, h=h, py=py: e.matmul(py[hs, 0:128], lhsT=ts_.UT[:, hs], rhs=ts_.AMh[h][:, 1, :], start=False, stop=False),
                                     reads=[ts_.UT, ts_.AMh[h]], writes=[py])
                                P.op("pe", lambda e, hs=hs, h=h, py=py: e.matmul(py[hs, 0:128], lhsT=ts_.TT[:, 0, hs], rhs=ts_.AMh[h][:, 3, :], start=False, stop=True),
                                     reads=[ts_.TT, ts_.AMh[h]], writes=[py])
                            if d == 0:
                                P.op("act", lambda e, py=py, csl=csl: e.activation(out=ts_.ys32[:, csl], in_=py[:, 0:128], func=AF.Copy), writes=[py, ts_.ys32])
                            else:
                                P.op("dve", lambda e, py=py, csl=csl: e.tensor_tensor(out=ts_.ys32[:, csl], in0=py[:, 0:128], in1=ts_.yfl[:, csl], op=ALU.add),
                                     reads=[ts_.yfl], writes=[py, ts_.ys32])
                        pS = C.nb()
                        for h in range(2):
                            hs = slice(h * 64, (h + 1) * 64)
                            P.op("pe", lambda e, hs=hs, pS=pS: e.matmul(pS[hs, 0:64], lhsT=ts_.TT[:, 1, hs], rhs=ts_.UT[:, hs], start=True, stop=False),
                                 reads=[ts_.TT, ts_.UT], writes=[pS])
                            P.op("pe", lambda e, hs=hs, pS=pS: e.matmul(pS[hs, 0:64], lhsT=ts_.TT[:, 2, hs], rhs=ts_.TT[:, 0, hs], start=False, stop=True),
                                 reads=[ts_.TT], writes=[pS])
                        S3 = S32[d][p]
                        P.op("dve", lambda e, pS=pS, S3=S3, gcol=gcol: e.scalar_tensor_tensor(out=S3.ap(), in0=S3.ap(), scalar=ts_.gg[:, gcol:gcol + 1],
                                                                                              in1=pS[:, 0:64], op0=ALU.mult, op1=ALU.add),
                             reads=[ts_.gg], writes=[pS, S3])
                        P.op("act", lambda e, S3=S3, Sn=Sn: e.activation(out=Sn.ap(), in_=S3.ap(), func=AF.Copy), reads=[S3], writes=[Sn])
                        yield
                    if not lat:
                        return
                    if d == 0:
                        yb = Buf("yf_%d_%d" % (p, bi))
                        C.yf_bufs[(p, bi)] = yb
                        P.dma("sp", dr["yf"][p, :, t0:t0 + NB], ts_.ys32.ap(), reads=[ts_.ys32], writes=[yb])
                        return
                    P.op("act", lambda e: e.activation(out=ts_.ys16.ap(), in_=ts_.ys32.ap(), func=AF.Copy), reads=[ts_.ys32], writes=[ts_.ys16])
                    pm_ = C.nb()
                    P.op("pe", lambda e, pm_=pm_: e.matmul(pm_[:, 0:NB], lhsT=C.bmean.ap(), rhs=ts_.ys16.ap(), start=True, stop=True),
                         reads=[C.bmean, ts_.ys16], writes=[pm_])
                    P.op("dve", lambda e, pm_=pm_: e.tensor_tensor(out=ts_.yc.ap(), in0=ts_.ys32.ap(), in1=pm_[:, 0:NB], op=ALU.subtract),
                         reads=[ts_.ys32], writes=[pm_, ts_.yc])
                    yield
                    P.op("act", lambda e: e.activation(out=ts_.yc2.ap(), in_=ts_.yc.ap(), func=AF.Square), reads=[ts_.yc], writes=[ts_.yc2])
                    pv = C.nb()
                    P.op("pe", lambda e, pv=pv: e.matmul(pv[:, 0:NB], lhsT=C.bmean.ap(), rhs=ts_.yc2.ap(), start=True, stop=True),
                         reads=[C.bmean, ts_.yc2], writes=[pv])
                    P.op("act", lambda e, pv=pv: e.activation(out=ts_.rstd.ap(), in_=pv[:, 0:NB], func=AF.Sqrt, bias=C.epsx.ap()),
                         reads=[C.epsx], writes=[pv, ts_.rstd])
                    P.op("dve", lambda e: e.reciprocal(out=ts_.rstd.ap(), in_=ts_.rstd.ap()), writes=[ts_.rstd])
                    yield
                    P.op("dve", lambda e: e.tensor_tensor(out=ts_.yn.ap(), in0=ts_.yc.ap(), in1=ts_.rstd.ap(), op=ALU.mult), reads=[ts_.yc, ts_.rstd], writes=[ts_.yn])
                    P.op("dve", lambda e, p=p: e.tensor_scalar(out=ts_.yn.ap(), in0=ts_.yn.ap(), scalar1=kvec[:, 3, p:p + 1], scalar2=kvec[:, 4, p:p + 1],
                                                               op0=ALU.mult, op1=ALU.add), reads=[kvec], writes=[ts_.yn])
                    pz = C.nb()
                    P.op("pe", lambda e, p=p, pz=pz: e.matmul(pz[:, 0:NB], lhsT=LW2[32:64, 0, p * 128:(p + 1) * 128], rhs=LA16[32:64, :],
                                                              start=True, stop=True), reads=[LW2, LA16], writes=[pz])
                    P.op("act", lambda e, p=p, pz=pz: e.activation(out=ts_.a2_.ap(), in_=pz[:, 0:NB], func=AF.Sigmoid, bias=w0a0[:, 2, p:p + 1]),
                         reads=[w0a0], writes=[pz, ts_.a2_])
                    yield
                    P.op("dve", lambda e, p=p: e.tensor_scalar(out=ts_.a2_.ap(), in0=ts_.a2_.ap(), scalar1=kvec[:, 1, p:p + 1], scalar2=omka[:, p:p + 1],
                                                               op0=ALU.mult, op1=ALU.add), reads=[kvec, omka], writes=[ts_.a2_])
                    P.op("pool", lambda e: e.tensor_tensor(out=ts_.a2_.ap(), in0=ts_.a2_.ap(), in1=ts_.fac.ap(), op=ALU.add), reads=[ts_.fac], writes=[ts_.a2_])
                    P.op("pool", lambda e: e.tensor_tensor(out=ts_.kdir2.ap(), in0=ts_.k32.ap(), in1=ts_.a2_.ap(), op=ALU.mult), reads=[ts_.k32, ts_.a2_], writes=[ts_.kdir2])
                    P.op("pool", lambda e: e.tensor_tensor(out=ts_.rk.ap(), in0=ts_.r32.ap(), in1=ts_.kdir2.ap(), op=ALU.mult), reads=[ts_.r32, ts_.kdir2], writes=[ts_.rk])
                    P.op("pool", lambda e, p=p: e.tensor_scalar(out=ts_.rk16.ap(), in0=ts_.rk.ap(), scalar1=rkh[:, p:p + 1], scalar2=None, op0=ALU.mult),
                         reads=[ts_.rk, rkh], writes=[ts_.rk16])
                    pbn = C.nb()
                    P.op("pe", lambda e, pbn=pbn: e.matmul(pbn[:, 0:NB], lhsT=C.bones.ap(), rhs=ts_.rk16.ap(), start=True, stop=True),
                         reads=[C.bones, ts_.rk16], writes=[pbn])
                    P.op("dve", lambda e, pbn=pbn: e.tensor_tensor(out=ts_.bonus.ap(), in0=pbn[:, 0:NB], in1=ts_.v32.ap(), op=ALU.mult),
                         reads=[ts_.v32], writes=[pbn, ts_.bonus])
                    yield
                    P.op("pool", lambda e: e.tensor_tensor(out=ts_.bonus.ap(), in0=ts_.bonus.ap(), in1=ts_.yn.ap(), op=ALU.add), reads=[ts_.yn], writes=[ts_.bonus])
                    pgt = C.nb()
                    P.op("pe", lambda e, p=p, pgt=pgt: e.matmul(pgt[:, 0:NB], lhsT=G2W[0:96, p * 128:(p + 1) * 128], rhs=sg16.ap(), start=True, stop=True),
                         reads=[G2W, sg16], writes=[pgt])
                    P.op("dve", lambda e, pgt=pgt: e.tensor_tensor(out=ts_.o16.ap(), in0=pgt[:, 0:NB], in1=ts_.bonus.ap(), op=ALU.mult),
                         reads=[ts_.bonus], writes=[pgt, ts_.o16])
                    mb = Buf("mt_%d_%d" % (p, bi))
                    C.mt_bufs[(p, bi)] = mb
                    P.dma("sp", dr["mt"][p, :, t0:t0 + NB], ts_.o16.ap(), reads=[ts_.o16], writes=[mb])
                import itertools
                for pp0 in (0, 2):
                    for _ in itertools.zip_longest(pair_gen(pp0, TS[0]), pair_gen(pp0 + 1, TS[1])):
                        pass


def stage_attn(C):
    P, dr = C.P, C.dr
    HT, HC = C.HT, C.HC
    bk = C.banks
    NKT = (T_CTX + T_LAT) // 128
    with P.scope():
        stg = [P.sbuf("astg%d" % i, [128, 8, 256], F32) for i in range(2)]
        Wh = P.sbuf("Wh", [128, 8, 640], BF16)
        KT = P.sbuf("KT", [128, T_CTX + T_LAT], BF16)
        VT = P.sbuf("VT", [128, NKT, 128], BF16)
        QB = [P.sbuf("QB%d" % i, [128, 512], BF16) for i in range(2)]
        PT = [P.sbuf("PT%d" % i, [128, 512], BF16) for i in range(3)]
        accD = P.sbuf("accD", [128, 512], F32)
        accP = P.sbuf("accP", [128, 512], F32)
        ones32a = P.sbuf("ones32a", [128, 128], F32)
        P.op("pool", lambda e: e.memset(ones32a.ap(), 1.0), writes=[ones32a])
        rC = [P.sbuf("rC%d" % i, [128, NB], F32) for i in range(2)]
        rS = [P.sbuf("rS%d" % i, [128, NB], F32) for i in range(2)]
        t1 = P.sbuf("rp_t1", [128, NB], F32)
        t2 = P.sbuf("rp_t2", [128, NB], F32)
        rl = P.sbuf("at_rl", [128, 512], F32)
        on_ = P.sbuf("at_on", [128, 512], F32)
        dif = P.sbuf("at_dif", [128, NB], F32)
        dsq = P.sbuf("at_dsq", [128, NB], BF16)
        drs = P.sbuf("at_drs", [128, NB], F32)
        ao16 = [P.sbuf("at_o16_%d" % i, [128, NB], BF16) for i in range(2)]
        lamt = P.sbuf("lamt", [128, 4, 64], F32)
        lpr = P.sbuf("lpr", [128, 2, 64], F32)
        lsum = P.sbuf("lsum", [128, 2], F32)
        nlam = P.sbuf("nlam", [128, 1], F32)
        sbw = P.sbuf("sbw", [128, 1], F32)
        P.dma("sp", lamt.ap(), dr["lam"], writes=[lamt])
        P.dma("sp", sbw.ap(), dr["subln"], writes=[sbw])
        P.op("dve", lambda e: e.tensor_tensor(out=lpr.ap(), in0=lamt[:, 0:4:2, :], in1=lamt[:, 1:4:2, :], op=ALU.mult),
             reads=[lamt], writes=[lpr])
        P.op("dve", lambda e: e.reduce_sum(out=lsum.ap(), in_=lpr.ap(), axis=AX.X), reads=[lpr], writes=[lsum])
        P.op("act", lambda e: e.activation(out=lsum.ap(), in_=lsum.ap(), func=AF.Exp), writes=[lsum])
        P.op("dve", lambda e: e.tensor_tensor(out=nlam.ap(), in0=lsum[:, 1:2], in1=lsum[:, 0:1], op=ALU.subtract),
             reads=[lsum], writes=[nlam])
        P.op("dve", lambda e: e.tensor_scalar(out=nlam.ap(), in0=nlam.ap(), scalar1=-0.2, scalar2=None, op0=ALU.add), writes=[nlam])
        P.op("dve", lambda e: e.tensor_scalar(out=sbw.ap(), in0=sbw.ap(), scalar1=0.8, scalar2=None, op0=ALU.mult), writes=[sbw])
        for q in QB:
            P.op("pool", lambda e, q=q: e.memset(q.ap(), 0.0), writes=[q])
        C.stg_i = 0
        ri = [0]

        def load_rope(bi):
            i = ri[0] % 2
            ri[0] += 1
            P.dma("sp", rC[i].ap(), dr["ropeC"][:, bi * NB:(bi + 1) * NB], writes=[rC[i]])
            P.dma("sp", rS[i].ap(), dr["ropeS"][:, bi * NB:(bi + 1) * NB], writes=[rS[i]])
            return rC[i], rS[i]

        def proj_rope(bank, c0, bi, rc, rs, outs):
            t0 = bi * NB
            for g in range(2):
                for k in range(8):
                    P.op("pe", lambda e, g=g, k=k: e.matmul(bank[:, g * NB:(g + 1) * NB], lhsT=Wh[:, k, c0 + g * 128:c0 + (g + 1) * 128],
                                                             rhs=HT[:, k, 1 + t0:1 + t0 + NB], start=(k == 0), stop=(k == 7)),
                         reads=[Wh, HT], writes=[bank])
            P.op("dve", lambda e: e.tensor_tensor(out=t1.ap(), in0=bank[:, 0:NB], in1=rc.ap(), op=ALU.mult), reads=[rc], writes=[bank, t1])
            P.op("dve", lambda e: e.tensor_tensor(out=t2.ap(), in0=bank[:, NB:2 * NB], in1=rs.ap(), op=ALU.mult), reads=[rs], writes=[bank, t2])
            for rows, oap, ob in outs:
                P.op("pool", lambda e, rows=rows, oap=oap: e.tensor_tensor(out=oap, in0=t1[rows, :], in1=t2[rows, :], op=ALU.add),
                     reads=[t1, t2], writes=[ob])

        for h in range(4):
            qc = 1696 + h * 128
            kc = 1696 + 512 + h * 128
            vc = 1696 + 1024 + h * 128
            load_w_bf16(C, (Wh, 0), dr["w_in"][:, qc:qc + 128], 128, stg)
            load_w_bf16(C, (Wh, 128), dr["w_qks"][:, h * 128:(h + 1) * 128], 128, stg)
            load_w_bf16(C, (Wh, 256), dr["w_in"][:, kc:kc + 128], 128, stg)
            load_w_bf16(C, (Wh, 384), dr["w_qks"][:, 512 + h * 128:512 + (h + 1) * 128], 128, stg)
            load_w_bf16(C, (Wh, 512), dr["w_in"][:, vc:vc + 128], 128, stg)
            pb = C.nb()
            for k in range(8):
                P.op("pe", lambda e, k=k, pb=pb: e.matmul(pb[:, 0:T_CTX], lhsT=Wh[:, k, 256:384], rhs=HC[:, k, 1:1 + T_CTX],
                                                           start=(k == 0), stop=(k == 7)), reads=[Wh, HC], writes=[pb])
            P.op("act", lambda e, pb=pb: e.activation(out=KT[:, 0:T_CTX], in_=pb[:, 0:T_CTX], func=AF.Copy), writes=[pb, KT])
            for bi in range(NLB):
                rc, rs = load_rope(bi)
                proj_rope(C.nb(), 256, bi, rc, rs, [(slice(0, 128), KT[:, T_CTX + bi * NB:T_CTX + (bi + 1) * NB], KT)])
            for j in range(NKT):
                Hs, c0 = (HC, 1 + j * 128) if j < 2 else (HT, 1 + (j - 2) * 128)
                pv = C.nb()
                for k in range(8):
                    P.op("pe", lambda e, k=k, pv=pv, Hs=Hs, c0=c0: e.matmul(pv[:, 0:128], lhsT=Hs[:, k, c0:c0 + 128], rhs=Wh[:, k, 512:640],
                                                                           start=(k == 0), stop=(k == 7)), reads=[Wh, Hs], writes=[pv])
                eng = "act" if j % 2 == 0 else "dve"
                if eng == "act":
                    P.op("act", lambda e, pv=pv, j=j: e.activation(out=VT[:, j, :], in_=pv[:, 0:128], func=AF.Copy), writes=[pv, VT])
                else:
                    P.op("dve", lambda e, pv=pv, j=j: e.tensor_copy(out=VT[:, j, :], in_=pv[:, 0:128]), writes=[pv, VT])

            def q_proj(bi):
                rc, rs = load_rope(bi)
                qb = QB[bi % 2]
                proj_rope(bk[6], 0, bi, rc, rs, [(slice(0, 64), qb[0:64, 0:NB], qb), (slice(64, 128), qb[64:128, NB:2 * NB], qb)])

            q_proj(0)
            for bi in range(NLB):
                qb = QB[bi % 2]
                po, pl = bk[3 + bi % 2], bk[5]

                def qk(j):
                    ps = bk[j % 3]
                    P.op("pe", lambda e, j=j, ps=ps: e.matmul(ps.ap(), lhsT=KT[:, j * 128:(j + 1) * 128], rhs=qb.ap(), start=True, stop=True),
                         reads=[KT, qb], writes=[ps])

                qk(0)
                qk(1)
                qk(2)
                if bi + 1 < NLB:
                    q_proj(bi + 1)
                nD = nP = 0
                for j in range(NKT):
                    ps, pt = bk[j % 3], PT[j % 3]
                    P.op("act", lambda e, ps=ps, pt=pt: e.activation(out=pt.ap(), in_=ps.ap(), func=AF.Exp, scale=0.125), writes=[ps, pt])
                    P.op("pe", lambda e, j=j, pt=pt: e.matmul(po.ap(), lhsT=VT[:, j, :], rhs=pt.ap(), start=(j == 0), stop=(j == NKT - 1)),
                         reads=[VT, pt], writes=[po])
                    if j % 3 == 2:
                        if nP == 0:
                            P.op("pool", lambda e, pt=pt: e.tensor_copy(out=accP.ap(), in_=pt.ap()), reads=[pt], writes=[accP])
                        else:
                            P.op("pool", lambda e, pt=pt: e.tensor_tensor(out=accP.ap(), in0=accP.ap(), in1=pt.ap(), op=ALU.add), reads=[pt], writes=[accP])
                        nP += 1
                    else:
                        if nD == 0:
                            P.op("dve", lambda e, pt=pt: e.tensor_copy(out=accD.ap(), in_=pt.ap()), reads=[pt], writes=[accD])
                        else:
                            P.op("dve", lambda e, pt=pt: e.tensor_tensor(out=accD.ap(), in0=accD.ap(), in1=pt.ap(), op=ALU.add), reads=[pt], writes=[accD])
                        nD += 1
                    if j + 3 < NKT:
                        qk(j + 3)
                P.op("pe", lambda e: e.matmul(pl.ap(), lhsT=ones32a.ap(), rhs=accD.ap(), start=True, stop=False), reads=[ones32a, accD], writes=[pl])
                P.op("pe", lambda e: e.matmul(pl.ap(), lhsT=ones32a.ap(), rhs=accP.ap(), start=False, stop=True), reads=[ones32a, accP], writes=[pl])
                P.op("dve", lambda e: e.reciprocal(out=rl.ap(), in_=pl.ap()), writes=[pl, rl])
                P.op("dve", lambda e: e.tensor_tensor(out=on_.ap(), in0=po.ap(), in1=rl.ap(), op=ALU.mult), reads=[rl], writes=[po, on_])
                P.op("dve", lambda e: e.scalar_tensor_tensor(out=dif.ap(), in0=on_[:, NB:2 * NB], scalar=nlam.ap(), in1=on_[:, 0:NB],
                                                             op0=ALU.mult, op1=ALU.add), reads=[on_, nlam], writes=[dif])
                P.op("pool", lambda e: e.tensor_tensor(out=dsq.ap(), in0=dif.ap(), in1=dif.ap(), op=ALU.mult), reads=[dif], writes=[dsq])
                pm = bk[6]
                P.op("pe", lambda e: e.matmul(pm[:, 0:NB], lhsT=C.mean128.ap(), rhs=dsq.ap(), start=True, stop=True),
                     reads=[C.mean128, dsq], writes=[pm])
                P.op("act", lambda e: e.activation(out=drs.ap(), in_=pm[:, 0:NB], func=AF.Sqrt, bias=C.epss.ap()), reads=[C.epss], writes=[pm, drs])
                P.op("dve", lambda e: e.reciprocal(out=drs.ap(), in_=drs.ap()), writes=[drs])
                P.op("pool", lambda e: e.tensor_tensor(out=dif.ap(), in0=dif.ap(), in1=drs.ap(), op=ALU.mult), reads=[drs], writes=[dif])
                o16 = ao16[bi % 2]
                P.op("pool", lambda e, o16=o16: e.tensor_scalar(out=o16.ap(), in0=dif.ap(), scalar1=sbw.ap(), scalar2=None, op0=ALU.mult),
                     reads=[dif, sbw], writes=[o16])
                mb = Buf("mt_%d_%d" % (4 + h, bi))
                C.mt_bufs[(4 + h, bi)] = mb
                P.dma("sp", dr["mt"][4 + h, :, bi * NB:(bi + 1) * NB], o16.ap(), reads=[o16], writes=[mb])


def stage_out_a(C):
    P, dr = C.P, C.dr
    C.x1_bufs, C.h2_bufs = {}, {}
    with P.scope():
        C.stg_i = 0
        stg = [P.sbuf("ostg%d" % i, [128, 8, 256], F32) for i in range(2)]
        WO = P.sbuf("WO", [128, 8, 1024], BF16)
        load_w_bf16(C, (WO, 0), dr["w_out"], 1024, stg)
        MTb = [P.sbuf("MTb%d" % i, [128, 8, NB], BF16) for i in range(2)]
        xts = [P.sbuf("oxt%d" % i, [128, 8, NB], F32) for i in range(2)]
        y32 = P.sbuf("oy32", [128, 8, NB], F32)
        x1 = [P.sbuf("ox1_%d" % i, [128, 8, NB], F32) for i in range(2)]
        h2 = [P.sbuf("oh2_%d" % i, [128, 8, NB], BF16) for i in range(2)]
        sq = P.sbuf("o_sq", [128, 8, NB], BF16)
        rs = P.sbuf("o_rs", [128, NB], F32)
        tmp = P.sbuf("o_tmp", [128, 8, NB], F32)
        def ld_a(bi):
            t0 = bi * NB
            P.dma("sp", MTb[bi % 2].ap(), dr["mt"][:, :, t0:t0 + NB].rearrange("k p t -> p k t"),
                  reads=[C.mt_bufs[(k, bi)] for k in range(8)], writes=[MTb[bi % 2]])
            P.dma("sp", xts[bi % 2].ap(), dr["xT"][:, t0:t0 + NB].rearrange("(k p) t -> p k t", p=128), writes=[xts[bi % 2]])

        for bi in range(NLB):
            t0 = bi * NB
            mtb, xt, x1b, h2b = MTb[bi % 2], xts[bi % 2], x1[bi % 2], h2[bi % 2]
            if bi == 0:
                ld_a(0)
            if bi + 1 < NLB:
                ld_a(bi + 1)
            for j in range(8):
                py = C.nb()
                for k in range(8):
                    P.op("pe", lambda e, j=j, k=k, py=py: e.matmul(py[:, 0:NB], lhsT=WO[:, k, j * 128:(j + 1) * 128], rhs=mtb[:, k, :],
                                                                   start=(k == 0), stop=(k == 7)), reads=[WO, mtb], writes=[py])
                if j % 2 == 0:
                    P.op("act", lambda e, j=j, py=py: e.activation(out=y32[:, j, :], in_=py[:, 0:NB], func=AF.Copy), writes=[py, y32])
                else:
                    P.op("dve", lambda e, j=j, py=py: e.tensor_copy(out=y32[:, j, :], in_=py[:, 0:NB]), writes=[py, y32])
            P.op("act", lambda e: e.activation(out=sq.ap(), in_=y32.ap(), func=AF.Square), reads=[y32], writes=[sq])
            pb = C.nb()
            for k in range(8):
                P.op("pe", lambda e, k=k, pb=pb: e.matmul(pb[:, 0:NB], lhsT=C.ones.ap(), rhs=sq[:, k, :], start=(k == 0), stop=(k == 7)),
                     reads=[sq, C.ones], writes=[pb])
            P.op("act", lambda e, pb=pb: e.activation(out=rs.ap(), in_=pb[:, 0:NB], func=AF.Sqrt, scale=1.0 / D_MODEL, bias=C.eps6.ap()),
                 reads=[C.eps6], writes=[pb, rs])
            P.op("dve", lambda e: e.reciprocal(out=rs.ap(), in_=rs.ap()), writes=[rs])
            for j in range(8):
                P.op("dve", lambda e, j=j: e.scalar_tensor_tensor(out=tmp[:, j, :], in0=y32[:, j, :], scalar=C.G1[:, j:j + 1], in1=rs.ap(),
                                                                  op0=ALU.mult, op1=ALU.mult), reads=[y32, rs, C.G1], writes=[tmp])
            P.op("pool", lambda e, x1b=x1b, xt=xt: e.tensor_tensor(out=x1b.ap(), in0=tmp.ap(), in1=xt.ap(), op=ALU.add),
                 reads=[tmp, xt], writes=[x1b])
            xb = Buf("x1s_%d" % bi)
            C.x1_bufs[bi] = xb
            P.dma("sp", dr["x1s"][:, :, t0:t0 + NB].rearrange("k p t -> p k t"), x1b.ap(), reads=[x1b], writes=[xb])
            norm_block(C, x1b, NB, lambda k: C.A2[:, k:k + 1], lambda k: C.modT[:, 24 + k, 0:1],
                       lambda k, h2b=h2b: h2b[:, k, :], [C.A2, C.modT], h2b, (sq, rs, tmp))
            hb = Buf("h2s_%d" % bi)
            C.h2_bufs[bi] = hb
            P.dma("sp", dr["h2s"][:, :, t0:t0 + NB].rearrange("k p t -> p k t"), h2b.ap(), reads=[h2b], writes=[hb])


def stage_ffn(C):
    P, dr = C.P, C.dr
    NJ = D_FF // 128
    with P.scope():
        C.stg_i = 0
        WU = P.sbuf("WU", [128, 8, 2 * D_FF], BF16)
        WD = P.sbuf("WD", [128, NJ, 1024], BF16)
        with P.scope():
            stg = [P.sbuf("fstg%d" % i, [128, 8, 256], F32) for i in range(2)]
            load_w_bf16(C, (WU, 0), dr["w_up"], 2 * D_FF, stg)
            for j in range(NJ):
                s_ = stg[C.stg_i % 2]
                C.stg_i += 1
                sv = s_.ap().rearrange("p a b -> p (a b)")[:, 0:1024]
                P.dma("sp", sv, dr["w_down"][j * 128:(j + 1) * 128, :], writes=[s_])
                P.op("pool", lambda e, j=j, sv=sv: e.tensor_copy(out=WD[:, j, :], in_=sv), reads=[s_], writes=[WD])
        FCW = P.sbuf("FCW", [128, 2 * NJ, 3], F32)
        FB = P.sbuf("FB", [128, 2 * NJ], F32)
        C.cwb = FCW
        P.dma("sp", FCW.ap(), dr["fconvT"].rearrange("(c p) j -> p c j", p=128), writes=[FCW])
        P.dma("sp", FB.ap(), dr["fbias"], writes=[FB])
        h2t = [P.sbuf("fh2_%d" % i, [128, 8, NB + 2], BF16) for i in range(2)]
        x1t = [P.sbuf("fx1_%d" % i, [128, 8, NB], F32) for i in range(2)]
        uv = [P.sbuf("fuv%d" % i, [128, NB], F32) for i in range(3)]
        ug = [P.sbuf("fug%d" % i, [128, NB], F32) for i in range(3)]
        sg = [P.sbuf("fsg%d" % i, [128, NB], F32) for i in range(3)]
        act16 = P.sbuf("fact", [128, NJ, NB], BF16)
        f32t = P.sbuf("ff32", [128, 8, NB], F32)
        sq = P.sbuf("f_sq", [128, 8, NB], BF16)
        rs = P.sbuf("f_rs", [128, NB], F32)
        def load_block(bi):
            t0 = bi * NB
            hb, xb = h2t[bi % 2], x1t[bi % 2]
            lo = max(t0 - 1, 0)
            hi = min(t0 + NB + 1, T_LAT)
            rd = [C.h2_bufs[b] for b in (bi - 1, bi, bi + 1) if 0 <= b < NLB]
            if bi == 0:
                P.op("pool", lambda e, hb=hb: e.memset(hb[:, :, 0:1], 0.0), writes=[hb])
            if bi == NLB - 1:
                P.op("pool", lambda e, hb=hb: e.memset(hb[:, :, NB + 1:NB + 2], 0.0), writes=[hb])
            d0 = lo - (t0 - 1)
            P.dma("sp", hb[:, :, d0:d0 + (hi - lo)], dr["h2s"][:, :, lo:hi].rearrange("k p t -> p k t"), reads=rd, writes=[hb])
            P.dma("sp", xb.ap(), dr["x1s"][:, :, t0:t0 + NB].rearrange("k p t -> p k t"), reads=[C.x1_bufs[bi]], writes=[xb])

        def gate_mul(j, u, g, s2):
            P.op("act", lambda e: e.activation(out=s2.ap(), in_=g.ap(), func=AF.Silu), reads=[g], writes=[s2])
            P.op("pool", lambda e: e.tensor_tensor(out=act16[:, j, :], in0=u.ap(), in1=s2.ap(), op=ALU.mult),
                 reads=[u, s2], writes=[act16])

        load_block(0)
        for bi in range(NLB):
            t0 = bi * NB
            hb, xb = h2t[bi % 2], x1t[bi % 2]
            if bi + 1 < NLB:
                load_block(bi + 1)
            prev = None
            for j in range(NJ):
                u, g, s2 = uv[j % 3], ug[j % 3], sg[j % 3]
                proj_conv(C, hb, 0, WU, j * 128, 128, FCW[:, j, :], u.ap(), u, bias=FB[:, j:j + 1], bias_buf=FB)
                proj_conv(C, hb, 0, WU, D_FF + j * 128, 128, FCW[:, NJ + j, :], g.ap(), g, bias=FB[:, NJ + j:NJ + j + 1], bias_buf=FB)
                if prev is not None:
                    gate_mul(*prev)
                prev = (j, u, g, s2)
            gate_mul(*prev)
            for i in range(8):
                pf = C.nb()
                for j in range(NJ):
                    P.op("pe", lambda e, i=i, j=j, pf=pf: e.matmul(pf[:, 0:NB], lhsT=WD[:, j, i * 128:(i + 1) * 128], rhs=act16[:, j, :],
                                                                   start=(j == 0), stop=(j == NJ - 1)), reads=[WD, act16], writes=[pf])
                if i % 2 == 0:
                    P.op("act", lambda e, i=i, pf=pf: e.activation(out=f32t[:, i, :], in_=pf[:, 0:NB], func=AF.Copy), writes=[pf, f32t])
                else:
                    P.op("dve", lambda e, i=i, pf=pf: e.tensor_copy(out=f32t[:, i, :], in_=pf[:, 0:NB]), writes=[pf, f32t])
            P.op("act", lambda e: e.activation(out=sq.ap(), in_=f32t.ap(), func=AF.Square), reads=[f32t], writes=[sq])
            pb = C.nb()
            for k in range(8):
                P.op("pe", lambda e, k=k, pb=pb: e.matmul(pb[:, 0:NB], lhsT=C.ones.ap(), rhs=sq[:, k, :], start=(k == 0), stop=(k == 7)),
                     reads=[sq, C.ones], writes=[pb])
            P.op("act", lambda e, pb=pb: e.activation(out=rs.ap(), in_=pb[:, 0:NB], func=AF.Sqrt, scale=1.0 / D_MODEL, bias=C.eps6.ap()),
                 reads=[C.eps6], writes=[pb, rs])
            P.op("dve", lambda e: e.reciprocal(out=rs.ap(), in_=rs.ap()), writes=[rs])
            for i in range(8):
                P.op("dve", lambda e, i=i: e.scalar_tensor_tensor(out=f32t[:, i, :], in0=f32t[:, i, :], scalar=C.G2[:, i:i + 1], in1=rs.ap(),
                                                                  op0=ALU.mult, op1=ALU.mult), reads=[rs, C.G2], writes=[f32t])
            P.op("pool", lambda e, xb=xb: e.tensor_tensor(out=xb.ap(), in0=f32t.ap(), in1=xb.ap(), op=ALU.add),
                 reads=[f32t], writes=[xb])
            P.dma("sp", dr["outT"][:, t0:t0 + NB].rearrange("(k p) t -> p k t", p=128), xb.ap(), reads=[xb], final=True)
```

```python
import contextlib
import numpy as np
import concourse.bass as bass
import concourse.mybir as mybir
from concourse.bass_utils import run_bass_kernel_spmd

F32 = mybir.dt.float32
BF16 = mybir.dt.bfloat16
AF = mybir.ActivationFunctionType
ALU = mybir.AluOpType
AX = mybir.AxisListType

SEM_ROLL = 30000


class Buf:
    def __init__(self, name, t=None):
        self.name = name
        self.t = t
        self.last_w = None
        self.readers = []
        self.dma_sem = None
        self.dma_cnt = 0

    def ap(self):
        return self.t[:]

    def __getitem__(self, idx):
        return self.t[idx]


class _Rec:
    def __getattr__(self, name):
        return lambda *a, **k: (name, a, k)


_REC = _Rec()


def _bind(fn):
    name, a, k = fn(_REC)
    return lambda e: getattr(e, name)(*a, **k)


class Op:
    __slots__ = ("eng", "fn", "deps", "signal", "token", "is_dma", "sem_buf", "final", "idx")


class Prog:
    ENGS = ("pe", "act", "dve", "pool", "sp")

    def __init__(self, nc):
        self.nc = nc
        self.stack = contextlib.ExitStack()
        self.ops = {e: [] for e in self.ENGS}
        self.nbuf = 0
        self.all_ops = []
        self.final_ops = []

    def sbuf(self, name, shape, dtype):
        self.nbuf += 1
        name = "s%d_%s" % (self.nbuf, name)
        t = self.stack.enter_context(self.nc.sbuf_tensor(name, list(shape), dtype))
        b = Buf(name, t)
        b.readers = list(getattr(self, "fence", []))
        if hasattr(self, "scope_bufs") and self.scope_bufs:
            self.scope_bufs[-1].append(b)
        return b

    def psum(self, name, shape, dtype):
        t = self.stack.enter_context(self.nc.psum_tensor(name, list(shape), dtype))
        return Buf(name, t)

    def view(self, name):
        return Buf(name)

    @contextlib.contextmanager
    def scope(self):
        old = self.stack
        self.stack = contextlib.ExitStack()
        if not hasattr(self, "scope_bufs"):
            self.scope_bufs = []
        self.scope_bufs.append([])
        try:
            yield
        finally:
            self.stack.close()
            self.stack = old
            bufs = self.scope_bufs.pop()
            ops = list(getattr(self, "fence", []))
            for b in bufs:
                if b.last_w is not None:
                    ops.append(b.last_w)
                ops.extend(b.readers)
            best = {}
            dmas = {}
            for o in ops:
                if o.is_dma:
                    dmas[id(o)] = o
                else:
                    if o.eng not in best or best[o.eng].idx < o.idx:
                        best[o.eng] = o
            self.fence = list(best.values()) + list(dmas.values())

    def _deps(self, o, reads, writes):
        deps = []
        for b in list(reads) + list(writes):
            if b.last_w is not None:
                deps.append(b.last_w)
        for b in writes:
            deps.extend(b.readers)
        for b in writes:
            b.last_w = o
            b.readers = []
        for b in reads:
            b.readers.append(o)
        seen = set()
        out = []
        for d in deps:
            if id(d) in seen or d is o:
                continue
            seen.add(id(d))
            if d.eng == "pe" and o.eng == "pe" and not d.is_dma and not o.is_dma:
                continue
            d.signal = True
            out.append(d)
        o.deps = out

    def op(self, eng, fn, reads=(), writes=()):
        o = Op()
        o.eng = eng
        o.fn = _bind(fn)
        o.signal = False
        o.token = None
        o.is_dma = False
        o.sem_buf = None
        o.final = False
        o.idx = len(self.ops[eng])
        self._deps(o, reads, writes)
        self.ops[eng].append(o)
        return o

    def dma(self, eng, out, in_, reads=(), writes=(), final=False):
        o = Op()
        o.eng = eng
        o.fn = lambda e: e.dma_start(out=out, in_=in_)
        o.signal = True
        o.token = None
        o.is_dma = True
        o.final = final
        cands = [b for b in list(writes) + list(reads) if b.t is not None]
        sb = cands[0] if cands else (list(writes) + list(reads))[0]
        o.sem_buf = sb
        o.idx = len(self.ops[eng])
        self._deps(o, reads, writes)
        self.ops[eng].append(o)
        if final:
            self.final_ops.append(o)
        return o

    def emit(self):
        nc = self.nc
        st = self.stack
        eng_sems = {}
        for e in self.ENGS:
            n = 0
            for o in self.ops[e]:
                if o.is_dma:
                    b = o.sem_buf
                    if b.dma_sem is None:
                        b.dma_sem = st.enter_context(nc.semaphore("d_" + b.name))
                    b.dma_cnt += 16
                    o.token = (b.dma_sem, b.dma_cnt, 16)
                elif o.signal:
                    k = n // SEM_ROLL
                    if (e, k) not in eng_sems:
                        eng_sems[(e, k)] = st.enter_context(nc.semaphore("s_%s_%d" % (e, k)))
                    o.token = (eng_sems[(e, k)], n % SEM_ROLL + 1, 1)
                    n += 1
        print("ops:", {e: len(self.ops[e]) for e in self.ENGS}, "signals:", {e: sum(1 for o in self.ops[e] if o.token is not None) for e in self.ENGS}, "nsems", len(eng_sems))
        all_sems = list(eng_sems.values())
        seen_b = set()
        for e in self.ENGS:
            for o in self.ops[e]:
                if o.is_dma and id(o.sem_buf) not in seen_b:
                    seen_b.add(id(o.sem_buf))
                    all_sems.append(o.sem_buf.dma_sem)
        with nc.Block() as blk0:
            @blk0.sync
            def _(eng):
                for sm in all_sems:
                    eng.sem_clear(sm)
        block = st.enter_context(nc.Block())
        hooks = {"pe": block.tensor, "act": block.scalar, "dve": block.vector,
                 "pool": block.gpsimd, "sp": block.sync}
        final_ops = self.final_ops

        def make(e):
            ops = self.ops[e]

            def body(eng):
                waited = {}
                for o in ops:
                    for d in o.deps:
                        sem, val, _ = d.token
                        if waited.get(id(sem), 0) < val:
                            eng.wait_ge(sem, val)
                            waited[id(sem)] = val
                    ins = o.fn(eng)
                    if o.token is not None:
                        ins.then_inc(o.token[0], o.token[2])
                if e == "sp":
                    for o in final_ops:
                        sem, val, _ = o.token
                        eng.wait_ge(sem, val)
            return body

        for e in self.ENGS:
            hooks[e](make(e))
        st.close()


D_MODEL = 1024
T_LAT = 4096
T_CTX = 256
NB = 256
NLB = T_LAT // NB
D_FF = 2816
EXPM05 = float(np.exp(-0.5))


class Ctx:
    pass


def build_program(nc, stage=99):
    P = Prog(nc)
    C = Ctx()
    C.P = P
    C.nc = nc
    dr = {}

    def din(name, shape, dt=F32):
        dr[name] = nc.dram_tensor(name, list(shape), dt, kind="ExternalInput").ap()

    din("xT", [1024, T_LAT]); din("ctxT", [1024, T_CTX]); din("cc", [128, 8, 2])
    din("w_mod", [1024, 6144]); din("b_modT", [128, 48]); din("gvec", [128, 4, 8])
    din("w_in", [1024, 3232]); din("w_qks", [1024, 1024]); din("convT", [1696, 3])
    din("lw2", [64, 2, 512]); din("w0a0", [128, 4, 4]); din("kvec", [128, 5, 4]); din("g2", [96, 512])
    din("lam", [128, 4, 64]); din("subln", [128, 1])
    din("w_out", [1024, 1024]); din("w_up", [1024, 5632]); din("fconvT", [5632, 3]); din("fbias", [128, 44])
    din("w_down", [2816, 1024]); din("ropeC", [128, T_LAT]); din("ropeS", [128, T_LAT]); din("blkmask", [128, 5, 128])
    dr["outT"] = nc.dram_tensor("outT", [1024, T_LAT], F32, kind="ExternalOutput").ap()
    dr["yf"] = nc.dram_tensor("yf_scr", [4, 128, T_LAT], F32).ap()
    dr["mt"] = nc.dram_tensor("mt_scr", [8, 128, T_LAT], BF16).ap()
    dr["x1s"] = nc.dram_tensor("x1_scr", [8, 128, T_LAT], F32).ap()
    dr["h2s"] = nc.dram_tensor("h2_scr", [8, 128, T_LAT], BF16).ap()
    if stage < 99:
        dr["dbg"] = nc.dram_tensor("dbg", [128, 8, T_LAT + 2], F32, kind="ExternalOutput").ap()
    C.dr = dr
    C.yf_bufs = {}
    C.mt_bufs = {}

    C.banks = [P.psum("pb%d" % i, [128, 512], F32) for i in range(7)]
    C.bankT = P.psum("pbT", [128, 1024], BF16)
    C.bank_i = 0

    def nb():
        b = C.banks[C.bank_i % 7]
        C.bank_i += 1
        return b
    C.nb = nb

    ident = P.sbuf("ident", [128, 128], BF16)
    ones = P.sbuf("ones", [128, 128], BF16)
    bones = P.sbuf("bones", [128, 128], BF16)
    bmean = P.sbuf("bmean", [128, 128], BF16)
    mean128 = P.sbuf("mean128", [128, 128], BF16)
    eps6 = P.sbuf("eps6", [128, 1], F32)
    epsx = P.sbuf("epsx", [128, 1], F32)
    epss = P.sbuf("epss", [128, 1], F32)
    P.op("pool", lambda e: e.memset(ident.ap(), 0.0), writes=[ident])
    P.op("pool", lambda e: e.affine_select(out=ident.ap(), in_=ident.ap(), pattern=[[-1, 128]],
                                           compare_op=ALU.not_equal, fill=1.0, base=0, channel_multiplier=1),
         reads=[ident], writes=[ident])
    P.op("pool", lambda e: e.memset(ones.ap(), 1.0), writes=[ones])
    P.op("pool", lambda e: e.memset(mean128.ap(), 1.0 / 128), writes=[mean128])
    P.op("pool", lambda e: e.memset(bones.ap(), 0.0), writes=[bones])
    P.op("pool", lambda e: e.memset(bones[0:64, 0:64], 1.0), writes=[bones])
    P.op("pool", lambda e: e.memset(bones[64:128, 64:128], 1.0), writes=[bones])
    P.op("pool", lambda e: e.memset(bmean.ap(), 0.0), writes=[bmean])
    P.op("pool", lambda e: e.memset(bmean[0:64, 0:64], 1.0 / 64), writes=[bmean])
    P.op("pool", lambda e: e.memset(bmean[64:128, 64:128], 1.0 / 64), writes=[bmean])
    P.op("pool", lambda e: e.memset(eps6.ap(), 1e-6), writes=[eps6])
    P.op("pool", lambda e: e.memset(epsx.ap(), 64e-5), writes=[epsx])
    P.op("pool", lambda e: e.memset(epss.ap(), 1e-5), writes=[epss])
    C.ident, C.ones, C.bones, C.bmean, C.mean128 = ident, ones, bones, bmean, mean128
    C.eps6, C.epsx, C.epss = eps6, epsx, epss

    stage_mod(C)
    import os
    with P.scope():
        C.HT = P.sbuf("HT", [128, 8, T_LAT + 2], BF16)
        C.HC = P.sbuf("HC", [128, 8, T_CTX + 2], BF16)
        for Hb, n in ((C.HT, T_LAT), (C.HC, T_CTX)):
            P.op("pool", lambda e, Hb=Hb: e.memset(Hb[:, :, 0:1], 0.0), writes=[Hb])
            P.op("pool", lambda e, Hb=Hb, n=n: e.memset(Hb[:, :, n + 1:n + 2], 0.0), writes=[Hb])
        stage_prenorm(C)
        if stage == 1:
            dbg_dump_HT(C)
            P.emit()
            return nc
        if not os.environ.get("SKIP_RWKV"):
            stage_rwkv(C)
        if stage == 21:
            with P.scope():
                for k in range(4):
                    tmp = P.sbuf("dbgy%d" % k, [128, T_LAT], F32)
                    P.dma("sp", tmp.ap(), dr["yf"][k], reads=list(C.yf_bufs.values()), writes=[tmp])
                    P.dma("sp", dr["dbg"][:, k, 0:T_LAT], tmp.ap(), reads=[tmp], final=True)
            P.emit()
            return nc
        if stage == 2:
            dbg_dump_mt(C, 0, 4)
            P.emit()
            return nc
        stage_attn(C)
        if stage == 3:
            dbg_dump_mt(C, 4, 8)
            P.emit()
            return nc
        stage_out_a(C)
    stage_ffn(C)
    P.emit()
    return nc


def dbg_dump_HT(C):
    P = C.P
    with P.scope():
        for k in range(8):
            tmp = P.sbuf("dbgt%d" % k, [128, T_LAT + 2], F32)
            P.op("dve", lambda e, k=k, tmp=tmp: e.tensor_copy(out=tmp.ap(), in_=C.HT[:, k, :]), reads=[C.HT], writes=[tmp])
            P.dma("sp", C.dr["dbg"][:, k, :], tmp.ap(), reads=[tmp], final=True)


def dbg_dump_mt(C, k0, k1):
    P = C.P
    with P.scope():
        for k in range(k0, k1):
            tb = P.sbuf("dbgb%d" % k, [128, T_LAT], BF16)
            tmp = P.sbuf("dbgt%d" % k, [128, T_LAT], F32)
            rd = [b for (kk, _), b in C.mt_bufs.items() if kk == k] + list(C.yf_bufs.values())
            P.dma("sp", tb.ap(), C.dr["mt"][k], reads=rd, writes=[tb])
            P.op("dve", lambda e, tmp=tmp, tb=tb: e.tensor_copy(out=tmp.ap(), in_=tb.ap()), reads=[tb], writes=[tmp])
            P.dma("sp", C.dr["dbg"][:, k, 0:T_LAT], tmp.ap(), reads=[tmp], final=True)


def stage_mod(C):
    P, dr = C.P, C.dr
    cc = P.sbuf("cc", [128, 8, 2], F32)
    scc = P.sbuf("scc", [128, 8, 2], F32)
    bm = P.sbuf("bmodT", [128, 48], F32)
    gv = P.sbuf("gvec", [128, 4, 8], F32)
    modT = P.sbuf("modT", [128, 48, 2], F32)
    P.dma("sp", cc.ap(), dr["cc"], writes=[cc])
    P.dma("sp", bm.ap(), dr["b_modT"], writes=[bm])
    P.dma("sp", gv.ap(), dr["gvec"], writes=[gv])
    P.op("act", lambda e: e.activation(out=scc.ap(), in_=cc.ap(), func=AF.Silu), reads=[cc], writes=[scc])
    pm = C.nb()
    with P.scope():
        wst = [P.sbuf("wmst%d" % i, [128, 8, 512], F32) for i in range(2)]
        for g in range(12):
            w = wst[g % 2]
            P.dma("sp", w.ap(), dr["w_mod"][:, g * 512:(g + 1) * 512].rearrange("(k p) n -> p k n", p=128), writes=[w])
            for jj in range(4):
                j = g * 4 + jj
                for k in range(8):
                    P.op("pe", lambda e, w=w, k=k, jj=jj, j=j: e.matmul(
                        pm[:, 2 * j:2 * j + 2], lhsT=w[:, k, jj * 128:(jj + 1) * 128], rhs=scc[:, k, :],
                        start=(k == 0), stop=(k == 7)), reads=[w, scc], writes=[pm])
    pmv = pm[:, 0:96].rearrange("p (j c) -> p j c", c=2)
    for c in range(2):
        P.op("dve", lambda e, c=c: e.tensor_tensor(out=modT[:, :, c], in0=pmv[:, :, c], in1=bm.ap(), op=ALU.add),
             reads=[bm], writes=[pm, modT])
    C.modT = modT
    A1 = P.sbuf("A1", [128, 2, 8], F32)
    G1 = P.sbuf("G1", [128, 8], F32)
    A2 = P.sbuf("A2", [128, 8], F32)
    G2 = P.sbuf("G2", [128, 8], F32)
    for c in range(2):
        P.op("dve", lambda e, c=c: e.scalar_tensor_tensor(out=A1[:, c, :], in0=modT[:, 8:16, c], scalar=1.0,
                                                          in1=gv[:, 0, :], op0=ALU.add, op1=ALU.mult),
             reads=[modT, gv], writes=[A1])
    P.op("dve", lambda e: e.tensor_tensor(out=G1.ap(), in0=modT[:, 16:24, 0], in1=gv[:, 1, :], op=ALU.mult),
         reads=[modT, gv], writes=[G1])
    P.op("dve", lambda e: e.scalar_tensor_tensor(out=A2.ap(), in0=modT[:, 32:40, 0], scalar=1.0, in1=gv[:, 2, :],
                                                 op0=ALU.add, op1=ALU.mult), reads=[modT, gv], writes=[A2])
    P.op("dve", lambda e: e.tensor_tensor(out=G2.ap(), in0=modT[:, 40:48, 0], in1=gv[:, 3, :], op=ALU.mult),
         reads=[modT, gv], writes=[G2])
    C.A1, C.G1, C.A2, C.G2 = A1, G1, A2, G2


def norm_block(C, xt, nbk, A_ap, sh_ap, out_ap, xt_reads, out_buf, tmpn):
    P = C.P
    sq, rs, tmp = tmpn
    P.op("act", lambda e: e.activation(out=sq[:, :, 0:nbk], in_=xt[:, :, 0:nbk], func=AF.Square),
         reads=[xt], writes=[sq])
    pb = C.nb()
    for k in range(8):
        P.op("pe", lambda e, k=k: e.matmul(pb[:, 0:nbk], lhsT=C.ones.ap(), rhs=sq[:, k, 0:nbk],
                                            start=(k == 0), stop=(k == 7)), reads=[sq, C.ones], writes=[pb])
    P.op("act", lambda e: e.activation(out=rs[:, 0:nbk], in_=pb[:, 0:nbk], func=AF.Sqrt, scale=1.0 / D_MODEL,
                                       bias=C.eps6.ap()), reads=[C.eps6], writes=[pb, rs])
    P.op("dve", lambda e: e.reciprocal(out=rs[:, 0:nbk], in_=rs[:, 0:nbk]), writes=[rs])
    for k in range(8):
        P.op("dve", lambda e, k=k: e.scalar_tensor_tensor(out=tmp[:, k, 0:nbk], in0=xt[:, k, 0:nbk], scalar=A_ap(k),
                                                          in1=rs[:, 0:nbk], op0=ALU.mult, op1=ALU.mult),
             reads=[xt, rs] + xt_reads, writes=[tmp])
        P.op("act", lambda e, k=k: e.activation(out=out_ap(k), in_=tmp[:, k, 0:nbk], func=AF.Identity,
                                                bias=sh_ap(k)), reads=[tmp] + xt_reads, writes=[out_buf])


def stage_prenorm(C):
    P, dr = C.P, C.dr
    with P.scope():
        xts = [P.sbuf("xt%d" % i, [128, 8, NB], F32) for i in range(2)]
        sq = P.sbuf("pn_sq", [128, 8, NB], BF16)
        rs = P.sbuf("pn_rs", [128, NB], F32)
        tmp = P.sbuf("pn_tmp", [128, 8, NB], F32)
        blocks = [("c", 0)] + [("l", i) for i in range(NLB)]
        for bi, (src, i) in enumerate(blocks):
            xt = xts[bi % 2]
            if src == "c":
                P.dma("sp", xt.ap(), dr["ctxT"].rearrange("(k p) t -> p k t", p=128), writes=[xt])
                H, cidx, t0 = C.HC, 1, 0
            else:
                P.dma("sp", xt.ap(), dr["xT"][:, i * NB:(i + 1) * NB].rearrange("(k p) t -> p k t", p=128), writes=[xt])
                H, cidx, t0 = C.HT, 0, i * NB
            norm_block(C, xt, NB, lambda k, cidx=cidx: C.A1[:, cidx, k:k + 1],
                       lambda k, cidx=cidx: C.modT[:, k, cidx:cidx + 1],
                       lambda k, H=H, t0=t0: H[:, k, 1 + t0:1 + t0 + NB], [C.A1, C.modT], H, (sq, rs, tmp))


def _rope_tables():
    n_pair = 16
    rows = T_LAT // 64
    row = np.repeat(np.arange(rows, dtype=np.float32), 64)
    col = np.tile(np.arange(64, dtype=np.float32), rows)
    inv = (np.float32(10000.0) ** (-np.arange(n_pair, dtype=np.float32) / np.float32(n_pair))).astype(np.float32)
    ang = np.concatenate([row[:, None] * inv, col[:, None] * inv], axis=-1).astype(np.float32)
    cos = np.cos(ang).astype(np.float32).T
    sin = np.sin(ang).astype(np.float32).T
    Cc = np.concatenate([cos, cos, cos, cos], axis=0)
    Ss = np.concatenate([-sin, sin, -sin, sin], axis=0)
    return np.ascontiguousarray(Cc), np.ascontiguousarray(Ss)


def prep_inputs(inp):
    f = lambda a: np.ascontiguousarray(np.asarray(a, dtype=np.float32))
    x, c, ctx, c_ctx = f(inp["x"]), f(inp["c"]), f(inp["ctx"]), f(inp["c_ctx"])
    pk = lambda v, n: f(v.reshape(n, 128).T)
    sh = {}
    sh["w_mod"] = f(inp["w_mod"][0])
    sh["b_modT"] = pk(f(inp["b_mod"][0]), 48)
    sh["gvec"] = f(np.stack([pk(f(inp[n][0]), 8) for n in ("g_pre_mix", "g_post_mix", "g_pre_ffn", "g_post_ffn")], axis=1))
    w_in = f(inp["w_in"][0])
    sh["w_in"] = w_in
    perm = np.arange(512).reshape(8, 2, 32)[:, ::-1, :].reshape(512)
    qc, kc = 1696, 1696 + 512
    sh["w_qks"] = f(np.concatenate([w_in[:, qc:qc + 512][:, perm], w_in[:, kc:kc + 512][:, perm]], axis=1))
    sh["convT"] = f(inp["rwkv_conv"][0].T)
    sh["lw2"] = f(np.stack([np.concatenate([f(inp["w2_fwd"][0]), f(inp["a2_fwd"][0])], 0),
                            np.concatenate([f(inp["w2_bwd"][0]), f(inp["a2_bwd"][0])], 0)], axis=1))
    sh["w0a0"] = f(np.stack([pk(f(inp[n][0]), 4) for n in ("w0_fwd", "w0_bwd", "a0_fwd", "a0_bwd")], axis=1))
    sh["kvec"] = f(np.stack([pk(f(inp[n][0]).reshape(512), 4) for n in ("k_k", "k_a", "r_k", "ln_x_w", "ln_x_b")], axis=1))
    sh["g2"] = f(inp["g2"][0])
    sh["lam"] = f(np.broadcast_to(np.stack([f(inp[n][0]) for n in ("lam_q1", "lam_k1", "lam_q2", "lam_k2")], 0)[None], (128, 4, 64)))
    sh["subln"] = f(inp["subln_w"][0].reshape(128, 1))
    sh["w_out"] = f(inp["w_out"][0])
    sh["w_up"] = f(inp["w_up"][0])
    sh["fconvT"] = f(inp["ffn_conv"][0].T)
    sh["fbias"] = pk(f(inp["ffn_conv_b"][0]), 44)
    sh["w_down"] = f(inp["w_down"][0])
    sh["ropeC"], sh["ropeS"] = _rope_tables()
    ii = np.arange(128)
    dm = lambda sz: (ii[:, None] // sz == ii[None, :] // sz).astype(np.float32)
    sh["blkmask"] = f(np.stack([dm(8), dm(16) - dm(8), dm(32) - dm(16), dm(64) - dm(32), dm(128) - dm(64)], axis=1))
    maps = []
    for b in range(8):
        m = dict(sh)
        m["xT"] = f(x[b].T)
        m["ctxT"] = f(ctx[b].T)
        m["cc"] = f(np.stack([c[b], c_ctx], -1).reshape(8, 128, 2).transpose(1, 0, 2))
        maps.append(m)
    return maps


def kernel(**inputs):
    maps = prep_inputs(inputs)
    nc = bass.Bass("TRN2", target_bir_lowering=False)
    build_program(nc)
    res = run_bass_kernel_spmd(nc, maps, core_ids=list(range(8)))
    out = np.stack([np.ascontiguousarray(res.results[b]["outT"].T) for b in range(8)], 0)
    return out.astype(np.float32)


def load_w_bf16(C, dst, dram_cols_ap, ncols, stg):
    P = C.P
    for i, c0 in enumerate(range(0, ncols, 256)):
        c1 = min(ncols, c0 + 256)
        s = stg[C.stg_i % 2]
        C.stg_i += 1
        P.dma("sp", s[:, :, 0:c1 - c0], dram_cols_ap[:, c0:c1].rearrange("(k p) n -> p k n", p=128), writes=[s])
        ce = ("pool", "act", "dve")[C.stg_i % 3]
        if ce == "act":
            P.op("act", lambda e, s=s, c0=c0, c1=c1: e.activation(out=dst[0][:, :, dst[1] + c0:dst[1] + c1], in_=s[:, :, 0:c1 - c0], func=AF.Copy),
                 reads=[s], writes=[dst[0]])
        else:
            P.op(ce, lambda e, s=s, c0=c0, c1=c1: e.tensor_copy(out=dst[0][:, :, dst[1] + c0:dst[1] + c1], in_=s[:, :, 0:c1 - c0]),
                 reads=[s], writes=[dst[0]])


def proj_conv(C, H, t0, W, c0, m, cw, out, out_buf, bias=None, eng2="dve", bias_buf=None):
    P = C.P
    pb = C.nb()
    for k in range(8):
        P.op("pe", lambda e, k=k: e.matmul(pb[0:m, 0:NB + 2], lhsT=W[:, k, c0:c0 + m], rhs=H[:, k, t0:t0 + NB + 2],
                                            start=(k == 0), stop=(k == 7)), reads=[W, H], writes=[pb])
    if bias is None:
        P.op("act", lambda e: e.activation(out=out, in_=pb[0:m, 1:NB + 1], func=AF.Copy, scale=cw[:, 1:2]),
             reads=[C.cwb], writes=[pb, out_buf])
    else:
        P.op("act", lambda e: e.activation(out=out, in_=pb[0:m, 1:NB + 1], func=AF.Identity, scale=cw[:, 1:2], bias=bias),
             reads=[C.cwb, bias_buf], writes=[pb, out_buf])
    P.op(eng2, lambda e: e.scalar_tensor_tensor(out=out, in0=pb[0:m, 0:NB], scalar=cw[:, 0:1], in1=out,
                                                op0=ALU.mult, op1=ALU.add), reads=[C.cwb], writes=[pb, out_buf])
    P.op(eng2, lambda e: e.scalar_tensor_tensor(out=out, in0=pb[0:m, 2:NB + 2], scalar=cw[:, 2:3], in1=out,
                                                op0=ALU.mult, op1=ALU.add), reads=[C.cwb], writes=[pb, out_buf])


def stage_rwkv(C):
    P, dr = C.P, C.dr
    C.stg_i = 0
    with P.scope():
        WR = P.sbuf("WR", [128, 8, 1696], BF16)
        with P.scope():
            stg = [P.sbuf("wstg%d" % i, [128, 8, 256], F32) for i in range(2)]
            load_w_bf16(C, (WR, 0), dr["w_in"][:, 0:1696], 1696, stg)
        CW = P.sbuf("CW", [128, 14, 3], F32)
        C.cwb = CW
        P.dma("sp", CW[:, 0:12, :], dr["convT"][0:1536, :].rearrange("(c p) j -> p c j", p=128), writes=[CW])
        P.dma("sp", CW[0:64, 12, :], dr["convT"][1536:1600, :], writes=[CW])
        P.dma("sp", CW[0:96, 13, :], dr["convT"][1600:1696, :], writes=[CW])
        LW2 = P.sbuf("LW2", [64, 2, 512], BF16)
        G2W = P.sbuf("G2W", [96, 512], BF16)
        with P.scope():
            lw2f = P.sbuf("lw2f", [64, 2, 512], F32)
            P.dma("sp", lw2f.ap(), dr["lw2"], writes=[lw2f])
            P.op("pool", lambda e: e.tensor_copy(out=LW2.ap(), in_=lw2f.ap()), reads=[lw2f], writes=[LW2])
            g2f = P.sbuf("g2f", [96, 512], F32)
            P.dma("sp", g2f.ap(), dr["g2"], writes=[g2f])
            P.op("pool", lambda e: e.tensor_copy(out=G2W.ap(), in_=g2f.ap()), reads=[g2f], writes=[G2W])
        w0a0 = P.sbuf("w0a0", [128, 4, 4], F32)
        kvec = P.sbuf("kvec", [128, 5, 4], F32)
        P.dma("sp", w0a0.ap(), dr["w0a0"], writes=[w0a0])
        P.dma("sp", kvec.ap(), dr["kvec"], writes=[kvec])
        omka = P.sbuf("omka", [128, 4], F32)
        rkh = P.sbuf("rkh", [128, 4], F32)
        P.op("dve", lambda e: e.tensor_scalar(out=omka.ap(), in0=kvec[:, 1, :], scalar1=-1.0, scalar2=1.0,
                                              op0=ALU.mult, op1=ALU.add), reads=[kvec], writes=[omka])
        P.op("dve", lambda e: e.tensor_scalar(out=rkh.ap(), in0=kvec[:, 2, :], scalar1=0.5, scalar2=None,
                                              op0=ALU.mult), reads=[kvec], writes=[rkh])
        AMM = [P.sbuf("amm%d" % d, [128, 4, 128], F32) for d in range(2)]
        with P.scope():
            ones32 = P.sbuf("ones32", [128, 128], F32)
            P.op("pool", lambda e: e.memset(ones32.ap(), 1.0), writes=[ones32])
            msk = {}
            for nm, cop, sgn in (("SU", ALU.is_gt, -1), ("IU", ALU.is_ge, -1), ("SL", ALU.is_gt, 1), ("IL", ALU.is_ge, 1)):
                mb = P.sbuf("m" + nm, [128, 128], F32)
                P.op("pool", lambda e, mb=mb, cop=cop, sgn=sgn: e.affine_select(out=mb.ap(), in_=ones32.ap(), pattern=[[-sgn, 128]],
                                                                                compare_op=cop, fill=0.0, base=0, channel_multiplier=sgn),
                     reads=[ones32], writes=[mb])
                msk[nm] = mb
            for d, (s_, i_) in enumerate((("SU", "IU"), ("SL", "IL"))):
                am = AMM[d]
                for q, nm in enumerate((s_, i_, s_, i_)):
                    P.op("pool", lambda e, am=am, q=q, nm=nm: e.tensor_copy(out=am[:, q, :], in_=msk[nm].ap()),
                         reads=[msk[nm]], writes=[am])
        NTMap = [AMM[1][:, 0, :], AMM[0][:, 0, :]]
        NTMb = [AMM[1], AMM[0]]
        rmask = P.sbuf("rmask", [128, NB], F32)
        P.op("pool", lambda e: e.memset(rmask.ap(), 1.0), writes=[rmask])
        for c in range(NB // 128):
            P.op("pool", lambda e, c=c: e.memset(rmask[:, c * 128:c * 128 + 1], 0.0), writes=[rmask])

        def f32t(n, shape=(128, NB)):
            return P.sbuf(n, list(shape), F32)

        def b16t(n, shape=(128, NB)):
            return P.sbuf(n, list(shape), BF16)

        la32, LA16 = f32t("la32", (64, NB)), b16t("LA16", (64, NB))
        gl32, sg16 = f32t("gl32", (96, NB)), b16t("sg16", (96, NB))
        BM2 = P.sbuf("blkmask2", [128, 5, 2, 128], BF16)
        with P.scope():
            bmf = P.sbuf("blkmaskf", [128, 5, 128], F32)
            P.dma("sp", bmf.ap(), dr["blkmask"], writes=[bmf])
            for q_ in range(2):
                P.op("pool", lambda e, q_=q_: e.tensor_copy(out=BM2[:, :, q_, :], in_=bmf.ap()), reads=[bmf], writes=[BM2])
        I2 = P.sbuf("ident2", [128, 2, 128], BF16)
        for q_ in range(2):
            P.op("pool", lambda e, q_=q_: e.tensor_copy(out=I2[:, q_, :], in_=C.ident.ap()), reads=[C.ident], writes=[I2])
        S32 = [[f32t("S32_%d_%d" % (d, p), (128, 64)) for p in range(4)] for d in range(2)]
        Sb = [[[b16t("Sb_%d_%d_%d" % (d, p, i), (128, 64)) for i in range(2)] for p in range(4)] for d in range(2)]
        sbi = [[0] * 4 for _ in range(2)]
        def make_set(si):
            sfx = "q%d_" % si
            o_ = Ctx()
            r32, k32, v32, v16 = f32t(sfx + "r32"), f32t(sfx + "k32"), f32t(sfx + "v32"), b16t(sfx + "v16")
            sig, aa, kraw, ksq, rn = f32t(sfx + "sig"), f32t(sfx + "aa"), f32t(sfx + "kraw"), b16t(sfx + "ksq"), f32t(sfx + "rn")
            fac, kdir, bb, cs, LL, Lm = f32t(sfx + "fac"), f32t(sfx + "kdir"), f32t(sfx + "bb"), f32t(sfx + "cs"), f32t(sfx + "LL"), f32t(sfx + "Lm")
            gg, ginv, gprev = f32t(sfx + "gg"), f32t(sfx + "ginv"), f32t(sfx + "gprev")
            ld, kk = sig, kraw
            yc, rstd, yn, rk, bonus, a2_, kdir2 = LL, rn, Lm, ginv, gprev, bb, kdir
            Bt, Kt = b16t(sfx + "Bt"), b16t(sfx + "Kt")
            KR = P.sbuf(sfx + "KR", [128, NB // 128, 2, 128], BF16)
            Bg, Kg = b16t(sfx + "Bg", (128, 128)), b16t(sfx + "Kg", (128, 128))
            TT = [P.sbuf(sfx + "TT%d" % ci, [128, 3, 128], BF16) for ci in range(2)]
            AMh = [[P.sbuf(sfx + "AM%d_%d" % (ci, h), [128, 4, 128], BF16) for h in range(2)] for ci in range(2)]
            NN = [[[P.sbuf(sfx + "NN%d_%d_%d" % (ci, h, i), [128, 2, 128], BF16) for i in range(1)] for h in range(2)] for ci in range(2)]
            PP = None
            Tb = [[P.sbuf(sfx + "Tb%d_%d" % (ci, h), [128, 128], BF16) for h in range(2)] for ci in range(2)]
            IV = [[dict((nm, P.sbuf(sfx + "iv%s_%d_%d" % (nm, ci, h), [128, 2, 128], BF16)) for nm in ("Nb2", "S2", "S4", "X2", "Pa", "Pb"))
                   for h in range(2)] for ci in range(2)]
            NZ, UT = b16t(sfx + "NZ", (128, 128)), b16t(sfx + "UT", (128, 128))
            ys32, yfl = f32t(sfx + "ys32"), f32t(sfx + "yfl")
            ys16, yc2 = b16t(sfx + "ys16"), b16t(sfx + "yc2")
            rk16, o16 = b16t(sfx + "rk16"), b16t(sfx + "o16")
            for _n in ['r32', 'k32', 'v32', 'v16', 'sig', 'ld', 'aa', 'a2_', 'kraw', 'ksq', 'rn', 'kk', 'fac', 'kdir', 'kdir2', 'bb', 'cs', 'LL', 'Lm', 'gg', 'ginv', 'gprev', 'Bt', 'Kt', 'KR', 'Bg', 'Kg', 'TT', 'AMh', 'NN', 'PP', 'Tb', 'IV', 'NZ', 'UT', 'ys32', 'yfl', 'ys16', 'yc', 'yc2', 'rstd', 'yn', 'rk', 'rk16', 'bonus', 'o16']:
                setattr(o_, _n, locals()[_n])
            return o_

        TS = [make_set(0), make_set(1)]
        for d in range(2):
            for p in range(4):
                P.op("pool", lambda e, d=d, p=p: e.memset(S32[d][p].ap(), 0.0), writes=[S32[d][p]])
                P.op("pool", lambda e, d=d, p=p: e.memset(Sb[d][p][0].ap(), 0.0), writes=[Sb[d][p][0]])

        NCH = NB // 128
        import os
        lim = int(os.environ.get("RWKV_LIM", "999"))
        cnt = 0
        for d in range(2):
            blocks = [("c", 0)] + ([("l", i) for i in range(NLB)] if d == 0 else [("l", i) for i in reversed(range(NLB))])
            for (src, bi) in blocks:
                cnt += 1
                if cnt > lim:
                    continue
                if os.environ.get("RWKV_SKIPF") and d == 0 and src == "l":
                    continue
                H = C.HC if src == "c" else C.HT
                t0 = 0 if src == "c" else bi * NB
                lat = (src == "l")
                proj_conv(C, H, t0, WR, 1536, 64, CW[0:64, 12, :], la32.ap(), la32)
                P.op("act", lambda e: e.activation(out=LA16[0:32, :], in_=la32[0:32, :], func=AF.Tanh), reads=[la32], writes=[LA16])
                P.op("pool", lambda e: e.tensor_copy(out=LA16[32:64, :], in_=la32[32:64, :]), reads=[la32], writes=[LA16])
                if d == 1 and lat:
                    proj_conv(C, H, t0, WR, 1600, 96, CW[0:96, 13, :], gl32.ap(), gl32)
                    P.op("act", lambda e: e.activation(out=sg16.ap(), in_=gl32.ap(), func=AF.Sigmoid), reads=[gl32], writes=[sg16])
                def pair_gen(p, ts_):
                    proj_conv(C, H, t0, WR, p * 128, 128, CW[:, p, :], ts_.r32.ap(), ts_.r32)
                    yield
                    proj_conv(C, H, t0, WR, 512 + p * 128, 128, CW[:, 4 + p, :], ts_.k32.ap(), ts_.k32)
                    yield
                    proj_conv(C, H, t0, WR, 1024 + p * 128, 128, CW[:, 8 + p, :], ts_.v32.ap(), ts_.v32)
                    yield
                    P.op("act", lambda e: e.activation(out=ts_.v16.ap(), in_=ts_.v32.ap(), func=AF.Copy), reads=[ts_.v32], writes=[ts_.v16])
                    pz = C.nb()
                    P.op("pe", lambda e, p=p, d=d: e.matmul(pz[:, 0:NB], lhsT=LW2[0:32, d, p * 128:(p + 1) * 128], rhs=LA16[0:32, :],
                                                             start=True, stop=True), reads=[LW2, LA16], writes=[pz])
                    P.op("act", lambda e, p=p, d=d: e.activation(out=ts_.sig.ap(), in_=pz[:, 0:NB], func=AF.Sigmoid, bias=w0a0[:, d, p:p + 1]),
                         reads=[w0a0], writes=[pz, ts_.sig])
                    P.op("pool", lambda e: e.tensor_scalar(out=ts_.ld.ap(), in0=ts_.sig.ap(), scalar1=-EXPM05, scalar2=None, op0=ALU.mult),
                         reads=[ts_.sig], writes=[ts_.ld])
                    pz = C.nb()
                    P.op("pe", lambda e, p=p, d=d, pz=pz: e.matmul(pz[:, 0:NB], lhsT=LW2[32:64, d, p * 128:(p + 1) * 128], rhs=LA16[32:64, :],
                                                                    start=True, stop=True), reads=[LW2, LA16], writes=[pz])
                    P.op("act", lambda e, p=p, d=d, pz=pz: e.activation(out=ts_.aa.ap(), in_=pz[:, 0:NB], func=AF.Sigmoid, bias=w0a0[:, 2 + d, p:p + 1]),
                         reads=[w0a0], writes=[pz, ts_.aa])
                    yield
                    P.op("act", lambda e, p=p: e.activation(out=ts_.kraw.ap(), in_=ts_.k32.ap(), func=AF.Copy, scale=kvec[:, 0, p:p + 1]),
                         reads=[ts_.k32, kvec], writes=[ts_.kraw])
                    P.op("act", lambda e: e.activation(out=ts_.ksq.ap(), in_=ts_.kraw.ap(), func=AF.Square), reads=[ts_.kraw], writes=[ts_.ksq])
                    pz = C.nb()
                    P.op("pe", lambda e, pz=pz: e.matmul(pz[:, 0:NB], lhsT=C.bones.ap(), rhs=ts_.ksq.ap(), start=True, stop=True),
                         reads=[C.bones, ts_.ksq], writes=[pz])
                    P.op("act", lambda e, pz=pz: e.activation(out=ts_.rn.ap(), in_=pz[:, 0:NB], func=AF.Sqrt), writes=[pz, ts_.rn])
                    P.op("dve", lambda e: e.tensor_scalar(out=ts_.rn.ap(), in0=ts_.rn.ap(), scalar1=1e-12, scalar2=None, op0=ALU.max), writes=[ts_.rn])
                    P.op("dve", lambda e: e.reciprocal(out=ts_.rn.ap(), in_=ts_.rn.ap()), writes=[ts_.rn])
                    P.op("dve", lambda e: e.tensor_tensor(out=ts_.kk.ap(), in0=ts_.kraw.ap(), in1=ts_.rn.ap(), op=ALU.mult), reads=[ts_.kraw, ts_.rn], writes=[ts_.kk])
                    yield
                    P.op("dve", lambda e, p=p: e.tensor_scalar(out=ts_.fac.ap(), in0=ts_.aa.ap(), scalar1=kvec[:, 1, p:p + 1], scalar2=omka[:, p:p + 1],
                                                               op0=ALU.mult, op1=ALU.add), reads=[ts_.aa, kvec, omka], writes=[ts_.fac])
                    P.op("pool", lambda e: e.tensor_tensor(out=ts_.kdir.ap(), in0=ts_.k32.ap(), in1=ts_.fac.ap(), op=ALU.mult), reads=[ts_.k32, ts_.fac], writes=[ts_.kdir])
                    P.op("pool", lambda e: e.tensor_tensor(out=ts_.bb.ap(), in0=ts_.kk.ap(), in1=ts_.aa.ap(), op=ALU.mult), reads=[ts_.kk, ts_.aa], writes=[ts_.bb])
                    P.op("dve", lambda e: e.tensor_tensor_scan(out=ts_.cs.ap(), data0=rmask.ap(), data1=ts_.ld.ap(), initial=0.0,
                                                               op0=ALU.mult, op1=ALU.add), reads=[rmask, ts_.ld], writes=[ts_.cs])
                    if d == 0:
                        Lb = ts_.cs
                    else:
                        for c in range(NCH):
                            P.op("dve", lambda e, c=c: e.tensor_scalar(out=ts_.LL[:, c * 128:(c + 1) * 128], in0=ts_.cs[:, c * 128:(c + 1) * 128],
                                                                       scalar1=-1.0, scalar2=ts_.cs[:, c * 128 + 127:c * 128 + 128],
                                                                       op0=ALU.mult, op1=ALU.add), reads=[ts_.cs], writes=[ts_.LL])
                        P.op("dve", lambda e: e.tensor_tensor(out=ts_.LL.ap(), in0=ts_.LL.ap(), in1=ts_.ld.ap(), op=ALU.add), reads=[ts_.ld], writes=[ts_.LL])
                        Lb = ts_.LL
                    P.op("pool", lambda e, Lb=Lb: e.tensor_tensor(out=ts_.Lm.ap(), in0=Lb.ap(), in1=ts_.ld.ap(), op=ALU.subtract), reads=[Lb, ts_.ld], writes=[ts_.Lm])
                    P.op("act", lambda e, Lb=Lb: e.activation(out=ts_.gg.ap(), in_=Lb.ap(), func=AF.Exp), reads=[Lb], writes=[ts_.gg])
                    P.op("act", lambda e, Lb=Lb: e.activation(out=ts_.ginv.ap(), in_=Lb.ap(), func=AF.Exp, scale=-1.0), reads=[Lb], writes=[ts_.ginv])
                    P.op("act", lambda e: e.activation(out=ts_.gprev.ap(), in_=ts_.Lm.ap(), func=AF.Exp), reads=[ts_.Lm], writes=[ts_.gprev])
                    yield
                    P.op("dve", lambda e: e.tensor_tensor(out=ts_.Bt.ap(), in0=ts_.bb.ap(), in1=ts_.ginv.ap(), op=ALU.mult), reads=[ts_.bb, ts_.ginv], writes=[ts_.Bt])
                    P.op("dve", lambda e: e.tensor_tensor(out=ts_.Kt.ap(), in0=ts_.kdir.ap(), in1=ts_.ginv.ap(), op=ALU.mult), reads=[ts_.kdir, ts_.ginv], writes=[ts_.Kt])
                    P.op("pool", lambda e: e.tensor_tensor(out=ts_.KR[:, :, 0, :], in0=ts_.kk.ap().rearrange("p (c t) -> p c t", t=128),
                                                          in1=ts_.gprev.ap().rearrange("p (c t) -> p c t", t=128), op=ALU.mult),
                         reads=[ts_.kk, ts_.gprev], writes=[ts_.KR])
                    P.op("pool", lambda e: e.tensor_tensor(out=ts_.KR[:, :, 1, :], in0=ts_.r32.ap().rearrange("p (c t) -> p c t", t=128),
                                                          in1=ts_.gg.ap().rearrange("p (c t) -> p c t", t=128), op=ALU.mult),
                         reads=[ts_.r32, ts_.gg], writes=[ts_.KR])
                    yield
                    if d == 1 and lat:
                        if (p, bi) in C.yf_bufs:
                            P.dma("sp", ts_.yfl.ap(), dr["yf"][p, :, t0:t0 + NB], reads=[C.yf_bufs[(p, bi)]], writes=[ts_.yfl])
                        else:
                            P.op("pool", lambda e: e.memset(ts_.yfl.ap(), 0.0), writes=[ts_.yfl])
                    chunks = list(range(NCH)) if d == 0 else list(reversed(range(NCH)))
                    for ci, c in enumerate(chunks):
                        csl = slice(c * 128, (c + 1) * 128)
                        gcol = c * 128 + 127 if d == 0 else c * 128
                        P.op("act", lambda e, csl=csl, gcol=gcol: e.activation(out=ts_.Bg.ap(), in_=ts_.Bt[:, csl], func=AF.Copy, scale=ts_.gg[:, gcol:gcol + 1]),
                             reads=[ts_.Bt, ts_.gg], writes=[ts_.Bg])
                        P.op("act", lambda e, csl=csl, gcol=gcol: e.activation(out=ts_.Kg.ap(), in_=ts_.Kt[:, csl], func=AF.Copy, scale=ts_.gg[:, gcol:gcol + 1]),
                             reads=[ts_.Kt, ts_.gg], writes=[ts_.Kg])
                        pT = C.bankT
                        P.op("pe", lambda e, csl=csl: e.transpose(pT[:, 0:128], ts_.v16[:, csl], C.ident.ap()), reads=[ts_.v16, C.ident], writes=[pT])
                        P.op("pe", lambda e: e.transpose(pT[:, 128:256], ts_.Bg.ap(), C.ident.ap()), reads=[ts_.Bg, C.ident], writes=[pT])
                        P.op("pe", lambda e: e.transpose(pT[:, 256:384], ts_.Kg.ap(), C.ident.ap()), reads=[ts_.Kg, C.ident], writes=[pT])
                        P.op("act", lambda e: e.activation(out=ts_.TT[ci].ap(), in_=pT[:, 0:384].rearrange("p (a b) -> p a b", b=128), func=AF.Copy),
                             writes=[pT, ts_.TT[ci]])
                        yield
                        for h in range(2):
                            hs = slice(h * 64, (h + 1) * 64)
                            AM = ts_.AMh[ci][h]
                            pg = C.nb()
                            P.op("pe", lambda e, hs=hs, csl=csl, c=c, pg=pg: e.matmul(pg[:, 0:256], lhsT=ts_.Bt[hs, csl], rhs=ts_.KR[hs, c, :, :],
                                                                                      start=True, stop=True), reads=[ts_.Bt, ts_.KR], writes=[pg])
                            P.op("pe", lambda e, hs=hs, csl=csl, c=c, pg=pg: e.matmul(pg[:, 256:512], lhsT=ts_.Kt[hs, csl], rhs=ts_.KR[hs, c, :, :],
                                                                                      start=True, stop=True), reads=[ts_.Kt, ts_.KR], writes=[pg])
                            N0 = ts_.NN[ci][h][0]
                            P.op("dve", lambda e, N0=N0, pg=pg, d=d: e.tensor_tensor(out=N0[:, 0, :], in0=pg[:, 0:128], in1=AMM[d][:, 0, :], op=ALU.mult),
                                 reads=[AMM[d]], writes=[pg, N0])
                            P.op("dve", lambda e, AM=AM, pg=pg, d=d: e.tensor_tensor(out=AM.ap(), in0=pg.ap().rearrange("p (a b) -> p a b", b=128),
                                                                                    in1=AMM[d].ap(), op=ALU.mult), reads=[AMM[d]], writes=[pg, AM])
                            pn = C.nb()
                            P.op("pe", lambda e, hs=hs, csl=csl, c=c, pn=pn: e.matmul(pn[:, 0:128], lhsT=ts_.KR[hs, c, 0, :], rhs=ts_.Bt[hs, csl],
                                                                                      start=True, stop=True), reads=[ts_.Bt, ts_.KR], writes=[pn])
                            N0 = ts_.NN[ci][h][0]
                            P.op("dve", lambda e, N0=N0, pn=pn, d=d: e.tensor_tensor(out=N0[:, 1, :], in0=pn[:, 0:128], in1=NTMap[d], op=ALU.mult),
                                 reads=[NTMb[d]], writes=[pn, N0])
                        def inv_gen(ci, h):
                            N0 = ts_.NN[ci][h][0]
                            iv = ts_.IV[ci][h]
                            mNb, S2, S4, mX = iv["Nb2"], iv["S2"], iv["S4"], iv["X2"]
                            Pc = [iv["Pa"], iv["Pb"]]
                            idb = C.ident
                            P.op("dve", lambda e: e.scalar_tensor_tensor(out=mNb.ap(), in0=N0.ap(), scalar=-1.0, in1=BM2[:, 0, :, :],
                                                                         op0=ALU.mult, op1=ALU.mult), reads=[N0, BM2], writes=[mNb])
                            pq = C.nb()
                            P.op("pe", lambda e: e.matmul(pq[:, 0:128], lhsT=mNb[:, 1, :], rhs=mNb[:, 0, :], start=True, stop=True), reads=[mNb], writes=[pq])
                            P.op("pe", lambda e: e.matmul(pq[:, 128:256], lhsT=mNb[:, 0, :], rhs=mNb[:, 1, :], start=True, stop=True), reads=[mNb], writes=[pq])
                            P.op("act", lambda e: e.activation(out=S2.ap(), in_=pq[:, 0:256].rearrange("p (a b) -> p a b", b=128), func=AF.Copy), writes=[pq, S2])
                            P.op("pool", lambda e: e.tensor_tensor(out=Pc[0].ap(), in0=I2.ap(), in1=mNb.ap(), op=ALU.add),
                                 reads=[mNb, I2], writes=[Pc[0]])
                            yield
                            pq2 = C.nb()
                            P.op("pe", lambda e: e.matmul(pq2[:, 0:128], lhsT=S2[:, 1, :], rhs=S2[:, 0, :], start=True, stop=True), reads=[S2], writes=[pq2])
                            P.op("pe", lambda e: e.matmul(pq2[:, 128:256], lhsT=S2[:, 0, :], rhs=S2[:, 1, :], start=True, stop=True), reads=[S2], writes=[pq2])
                            P.op("dve", lambda e: e.tensor_copy(out=S4.ap(), in_=pq2[:, 0:256].rearrange("p (a b) -> p a b", b=128)), writes=[pq2, S4])

                            def pstep(Sx, Pin, Pout):
                                pp_ = C.nb()
                                for q_ in range(2):
                                    P.op("pe", lambda e, q_=q_: e.matmul(pp_[:, q_ * 128:(q_ + 1) * 128], lhsT=Sx[:, 1 - q_, :], rhs=Pin[:, q_, :], start=True, stop=False),
                                         reads=[Sx, Pin], writes=[pp_])
                                    P.op("pe", lambda e, q_=q_: e.matmul(pp_[:, q_ * 128:(q_ + 1) * 128], lhsT=idb.ap(), rhs=Pin[:, q_, :], start=False, stop=True),
                                         reads=[idb, Pin], writes=[pp_])
                                P.op("act", lambda e: e.activation(out=Pout.ap(), in_=pp_[:, 0:256].rearrange("p (a b) -> p a b", b=128), func=AF.Copy),
                                     writes=[pp_, Pout])

                            pstep(S2, Pc[0], Pc[1])
                            yield
                            pstep(S4, Pc[1], Pc[0])
                            cur = 0
                            yield
                            for l in range(4):
                                Tc, Tn = Pc[cur], Pc[1 - cur]
                                last = (l == 3)
                                nq = 1 if last else 2
                                px = C.nb()
                                for q_ in range(nq):
                                    P.op("pe", lambda e, q_=q_: e.matmul(px[:, q_ * 128:(q_ + 1) * 128], lhsT=N0[:, 1 - q_, :], rhs=Tc[:, q_, :], start=True, stop=True),
                                         reads=[N0, Tc], writes=[px])
                                P.op("dve", lambda e: e.scalar_tensor_tensor(out=mX[:, 0:nq, :], in0=px[:, 0:nq * 128].rearrange("p (a b) -> p a b", b=128),
                                                                             scalar=-1.0, in1=BM2[:, 1 + l, 0:nq, :], op0=ALU.mult, op1=ALU.mult),
                                     reads=[BM2], writes=[px, mX])
                                yield
                                pr = C.nb()
                                for q_ in range(nq):
                                    P.op("pe", lambda e, q_=q_: e.matmul(pr[:, q_ * 128:(q_ + 1) * 128], lhsT=Tc[:, 1 - q_, :], rhs=mX[:, q_, :], start=True, stop=False),
                                         reads=[Tc, mX], writes=[pr])
                                    P.op("pe", lambda e, q_=q_: e.matmul(pr[:, q_ * 128:(q_ + 1) * 128], lhsT=idb.ap(), rhs=Tc[:, q_, :], start=False, stop=True),
                                         reads=[idb, Tc], writes=[pr])
                                if not last:
                                    P.op("act", lambda e: e.activation(out=Tn.ap(), in_=pr[:, 0:256].rearrange("p (a b) -> p a b", b=128), func=AF.Copy),
                                         writes=[pr, Tn])
                                else:
                                    P.op("act", lambda e: e.activation(out=ts_.Tb[ci][h].ap(), in_=pr[:, 0:128], func=AF.Copy), writes=[pr, ts_.Tb[ci][h]])
                                cur = 1 - cur
                                yield

                    for _ in zip(*[inv_gen(ci_, h_) for ci_ in range(len(chunks)) for h_ in range(2)]):
                        yield
                    for ci, c in enumerate(chunks):
                        csl = slice(c * 128, (c + 1) * 128)
                        gcol = c * 128 + 127 if d == 0 else c * 128
                        Th = ts_.Tb[ci]
                        So = Sb[d][p][sbi[d][p] % 2]
                        Sn = Sb[d][p][(sbi[d][p] + 1) % 2]
                        sbi[d][p] += 1
                        pz = C.nb()
                        for h in range(2):
                            hs = slice(h * 64, (h + 1) * 64)
                            P.op("pe", lambda e, hs=hs, c=c, pz=pz, So=So: e.matmul(pz[:, hs], lhsT=ts_.KR[hs, c, 0, :], rhs=So[hs, :], start=True, stop=False),
                                 reads=[ts_.KR, So], writes=[pz])
                            P.op("pe", lambda e, hs=hs, h=h, pz=pz: e.matmul(pz[:, hs], lhsT=ts_.AMh[ci][h][:, 2, :], rhs=ts_.TT[ci][:, 0, hs], start=False, stop=True),
                                 reads=[ts_.AMh[ci][h], ts_.TT[ci]], writes=[pz])
                        P.op("act", lambda e, pz=pz: e.activation(out=ts_.NZ.ap(), in_=pz[:, 0:128], func=AF.Copy, scale=-1.0), writes=[pz, ts_.NZ])
                        yield
                        pu = C.nb()
                        for h in range(2):
                            hs = slice(h * 64, (h + 1) * 64)
                            P.op("pe", lambda e, hs=hs, h=h, pu=pu: e.matmul(pu[:, hs], lhsT=Th[h].ap(), rhs=ts_.NZ[:, hs], start=True, stop=True),
                                 reads=[Th[h], ts_.NZ], writes=[pu])
                        P.op("act", lambda e, pu=pu: e.activation(out=ts_.UT.ap(), in_=pu[:, 0:128], func=AF.Copy), writes=[pu, ts_.UT])
                        yield
                        if lat:
                            py = C.nb()
                            for h in range(2):
                                hs = slice(h * 64, (h + 1) * 64)
                                P.op("pe", lambda e, hs=hs, c=c, py=py, So=So: e.matmul(py[hs, 0:128], lhsT=So[hs, :], rhs=ts_.KR[hs, c, 1, :], start=True, stop=False),
                                     reads=[So, ts_.KR], writes=[py])
                                P.op("pe", lambda e, hs=hs, h=h, py=py: e.matmul(py[hs, 0:128], lhsT=ts_.UT[:, hs], rhs=ts_.AMh[ci][h][:, 1, :], start=False, stop=False),
                                     reads=[ts_.UT, ts_.AMh[ci][h]], writes=[py])
                                P.op("pe", lambda e, hs=hs, h=h, py=py: e.matmul(py[hs, 0:128], lhsT=ts_.TT[ci][:, 0, hs], rhs=ts_.AMh[ci][h][:, 3, :], start=False, stop=True),
                                     reads=[ts_.TT[ci], ts_.AMh[ci][h]], writes=[py])
                            if d == 0:
                                P.op("act", lambda e, py=py, csl=csl: e.activation(out=ts_.ys32[:, csl], in_=py[:, 0:128], func=AF.Copy), writes=[py, ts_.ys32])
                            else:
                                P.op("dve", lambda e, py=py, csl=csl: e.tensor_tensor(out=ts_.ys32[:, csl], in0=py[:, 0:128], in1=ts_.yfl[:, csl], op=ALU.add),
                                     reads=[ts_.yfl], writes=[py, ts_.ys32])
                        pS = C.nb()
                        for h in range(2):
                            hs = slice(h * 64, (h + 1) * 64)
                            P.op("pe", lambda e, hs=hs, pS=pS: e.matmul(pS[hs, 0:64], lhsT=ts_.TT[ci][:, 1, hs], rhs=ts_.UT[:, hs], start=True, stop=False),
                                 reads=[ts_.TT[ci], ts_.UT], writes=[pS])
                            P.op("pe", lambda e, hs=hs, pS=pS: e.matmul(pS[hs, 0:64], lhsT=ts_.TT[ci][:, 2, hs], rhs=ts_.TT[ci][:, 0, hs], start=False, stop=True),
                                 reads=[ts_.TT[ci]], writes=[pS])
                        S3 = S32[d][p]
                        P.op("dve", lambda e, pS=pS, S3=S3, gcol=gcol: e.scalar_tensor_tensor(out=S3.ap(), in0=S3.ap(), scalar=ts_.gg[:, gcol:gcol + 1],
                                                                                              in1=pS[:, 0:64], op0=ALU.mult, op1=ALU.add),
                             reads=[ts_.gg], writes=[pS, S3])
                        P.op("act", lambda e, S3=S3, Sn=Sn: e.activation(out=Sn.ap(), in_=S3.ap(), func=AF.Copy), reads=[S3], writes=[Sn])
                        yield
                    if not lat:
                        return
                    if d == 0:
                        yb = Buf("yf_%d_%d" % (p, bi))
                        C.yf_bufs[(p, bi)] = yb
                        P.dma("sp", dr["yf"][p, :, t0:t0 + NB], ts_.ys32.ap(), reads=[ts_.ys32], writes=[yb])
                        return
                    P.op("act", lambda e: e.activation(out=ts_.ys16.ap(), in_=ts_.ys32.ap(), func=AF.Copy), reads=[ts_.ys32], writes=[ts_.ys16])
                    pm_ = C.nb()
                    P.op("pe", lambda e, pm_=pm_: e.matmul(pm_[:, 0:NB], lhsT=C.bmean.ap(), rhs=ts_.ys16.ap(), start=True, stop=True),
                         reads=[C.bmean, ts_.ys16], writes=[pm_])
                    P.op("dve", lambda e, pm_=pm_: e.tensor_tensor(out=ts_.yc.ap(), in0=ts_.ys32.ap(), in1=pm_[:, 0:NB], op=ALU.subtract),
                         reads=[ts_.ys32], writes=[pm_, ts_.yc])
                    yield
                    P.op("act", lambda e: e.activation(out=ts_.yc2.ap(), in_=ts_.yc.ap(), func=AF.Square), reads=[ts_.yc], writes=[ts_.yc2])
                    pv = C.nb()
                    P.op("pe", lambda e, pv=pv: e.matmul(pv[:, 0:NB], lhsT=C.bmean.ap(), rhs=ts_.yc2.ap(), start=True, stop=True),
                         reads=[C.bmean, ts_.yc2], writes=[pv])
                    P.op("act", lambda e, pv=pv: e.activation(out=ts_.rstd.ap(), in_=pv[:, 0:NB], func=AF.Sqrt, bias=C.epsx.ap()),
                         reads=[C.epsx], writes=[pv, ts_.rstd])
                    P.op("dve", lambda e: e.reciprocal(out=ts_.rstd.ap(), in_=ts_.rstd.ap()), writes=[ts_.rstd])
                    yield
                    P.op("dve", lambda e: e.tensor_tensor(out=ts_.yn.ap(), in0=ts_.yc.ap(), in1=ts_.rstd.ap(), op=ALU.mult), reads=[ts_.yc, ts_.rstd], writes=[ts_.yn])
                    P.op("dve", lambda e, p=p: e.tensor_scalar(out=ts_.yn.ap(), in0=ts_.yn.ap(), scalar1=kvec[:, 3, p:p + 1], scalar2=kvec[:, 4, p:p + 1],
                                                               op0=ALU.mult, op1=ALU.add), reads=[kvec], writes=[ts_.yn])
                    pz = C.nb()
                    P.op("pe", lambda e, p=p, pz=pz: e.matmul(pz[:, 0:NB], lhsT=LW2[32:64, 0, p * 128:(p + 1) * 128], rhs=LA16[32:64, :],
                                                              start=True, stop=True), reads=[LW2, LA16], writes=[pz])
                    P.op("act", lambda e, p=p, pz=pz: e.activation(out=ts_.a2_.ap(), in_=pz[:, 0:NB], func=AF.Sigmoid, bias=w0a0[:, 2, p:p + 1]),
                         reads=[w0a0], writes=[pz, ts_.a2_])
                    yield
                    P.op("dve", lambda e, p=p: e.tensor_scalar(out=ts_.a2_.ap(), in0=ts_.a2_.ap(), scalar1=kvec[:, 1, p:p + 1], scalar2=omka[:, p:p + 1],
                                                               op0=ALU.mult, op1=ALU.add), reads=[kvec, omka], writes=[ts_.a2_])
                    P.op("pool", lambda e: e.tensor_tensor(out=ts_.a2_.ap(), in0=ts_.a2_.ap(), in1=ts_.fac.ap(), op=ALU.add), reads=[ts_.fac], writes=[ts_.a2_])
                    P.op("pool", lambda e: e.tensor_tensor(out=ts_.kdir2.ap(), in0=ts_.k32.ap(), in1=ts_.a2_.ap(), op=ALU.mult), reads=[ts_.k32, ts_.a2_], writes=[ts_.kdir2])
                    P.op("pool", lambda e: e.tensor_tensor(out=ts_.rk.ap(), in0=ts_.r32.ap(), in1=ts_.kdir2.ap(), op=ALU.mult), reads=[ts_.r32, ts_.kdir2], writes=[ts_.rk])
                    P.op("pool", lambda e, p=p: e.tensor_scalar(out=ts_.rk16.ap(), in0=ts_.rk.ap(), scalar1=rkh[:, p:p + 1], scalar2=None, op0=ALU.mult),
                         reads=[ts_.rk, rkh], writes=[ts_.rk16])
                    pbn = C.nb()
                    P.op("pe", lambda e, pbn=pbn: e.matmul(pbn[:, 0:NB], lhsT=C.bones.ap(), rhs=ts_.rk16.ap(), start=True, stop=True),
                         reads=[C.bones, ts_.rk16], writes=[pbn])
                    P.op("dve", lambda e, pbn=pbn: e.tensor_tensor(out=ts_.bonus.ap(), in0=pbn[:, 0:NB], in1=ts_.v32.ap(), op=ALU.mult),
                         reads=[ts_.v32], writes=[pbn, ts_.bonus])
                    yield
                    P.op("pool", lambda e: e.tensor_tensor(out=ts_.bonus.ap(), in0=ts_.bonus.ap(), in1=ts_.yn.ap(), op=ALU.add), reads=[ts_.yn], writes=[ts_.bonus])
                    pgt = C.nb()
                    P.op("pe", lambda e, p=p, pgt=pgt: e.matmul(pgt[:, 0:NB], lhsT=G2W[0:96, p * 128:(p + 1) * 128], rhs=sg16.ap(), start=True, stop=True),
                         reads=[G2W, sg16], writes=[pgt])
                    P.op("dve", lambda e, pgt=pgt: e.tensor_tensor(out=ts_.o16.ap(), in0=pgt[:, 0:NB], in1=ts_.bonus.ap(), op=ALU.mult),
                         reads=[ts_.bonus], writes=[pgt, ts_.o16])
                    mb = Buf("mt_%d_%d" % (p, bi))
                    C.mt_bufs[(p, bi)] = mb
                    P.dma("sp", dr["mt"][p, :, t0:t0 + NB], ts_.o16.ap(), reads=[ts_.o16], writes=[mb])
                import itertools
                for pp0 in (0, 2):
                    for _ in itertools.zip_longest(pair_gen(pp0, TS[0]), pair_gen(pp0 + 1, TS[1])):
                        pass


def stage_attn(C):
    P, dr = C.P, C.dr
    HT, HC = C.HT, C.HC
    bk = C.banks
    NKT = (T_CTX + T_LAT) // 128
    with P.scope():
        stg = [P.sbuf("astg%d" % i, [128, 8, 256], F32) for i in range(2)]
        Wh = P.sbuf("Wh", [128, 8, 640], BF16)
        KT = P.sbuf("KT", [128, T_CTX + T_LAT], BF16)
        VT = P.sbuf("VT", [128, NKT, 128], BF16)
        QB = [P.sbuf("QB%d" % i, [128, 512], BF16) for i in range(2)]
        PT = [P.sbuf("PT%d" % i, [128, 512], BF16) for i in range(3)]
        accD = P.sbuf("accD", [128, 512], F32)
        accP = P.sbuf("accP", [128, 512], F32)
        ones32a = P.sbuf("ones32a", [128, 128], F32)
        P.op("pool", lambda e: e.memset(ones32a.ap(), 1.0), writes=[ones32a])
        rC = [P.sbuf("rC%d" % i, [128, NB], F32) for i in range(2)]
        rS = [P.sbuf("rS%d" % i, [128, NB], F32) for i in range(2)]
        t1 = P.sbuf("rp_t1", [128, NB], F32)
        t2 = P.sbuf("rp_t2", [128, NB], F32)
        rl = P.sbuf("at_rl", [128, 512], F32)
        on_ = P.sbuf("at_on", [128, 512], F32)
        dif = P.sbuf("at_dif", [128, NB], F32)
        dsq = P.sbuf("at_dsq", [128, NB], BF16)
        drs = P.sbuf("at_drs", [128, NB], F32)
        ao16 = [P.sbuf("at_o16_%d" % i, [128, NB], BF16) for i in range(2)]
        lamt = P.sbuf("lamt", [128, 4, 64], F32)
        lpr = P.sbuf("lpr", [128, 2, 64], F32)
        lsum = P.sbuf("lsum", [128, 2], F32)
        nlam = P.sbuf("nlam", [128, 1], F32)
        sbw = P.sbuf("sbw", [128, 1], F32)
        P.dma("sp", lamt.ap(), dr["lam"], writes=[lamt])
        P.dma("sp", sbw.ap(), dr["subln"], writes=[sbw])
        P.op("dve", lambda e: e.tensor_tensor(out=lpr.ap(), in0=lamt[:, 0:4:2, :], in1=lamt[:, 1:4:2, :], op=ALU.mult),
             reads=[lamt], writes=[lpr])
        P.op("dve", lambda e: e.reduce_sum(out=lsum.ap(), in_=lpr.ap(), axis=AX.X), reads=[lpr], writes=[lsum])
        P.op("act", lambda e: e.activation(out=lsum.ap(), in_=lsum.ap(), func=AF.Exp), writes=[lsum])
        P.op("dve", lambda e: e.tensor_tensor(out=nlam.ap(), in0=lsum[:, 1:2], in1=lsum[:, 0:1], op=ALU.subtract),
             reads=[lsum], writes=[nlam])
        P.op("dve", lambda e: e.tensor_scalar(out=nlam.ap(), in0=nlam.ap(), scalar1=-0.2, scalar2=None, op0=ALU.add), writes=[nlam])
        P.op("dve", lambda e: e.tensor_scalar(out=sbw.ap(), in0=sbw.ap(), scalar1=0.8, scalar2=None, op0=ALU.mult), writes=[sbw])
        for q in QB:
            P.op("pool", lambda e, q=q: e.memset(q.ap(), 0.0), writes=[q])
        C.stg_i = 0
        ri = [0]

        def load_rope(bi):
            i = ri[0] % 2
            ri[0] += 1
            P.dma("sp", rC[i].ap(), dr["ropeC"][:, bi * NB:(bi + 1) * NB], writes=[rC[i]])
            P.dma("sp", rS[i].ap(), dr["ropeS"][:, bi * NB:(bi + 1) * NB], writes=[rS[i]])
            return rC[i], rS[i]

        def proj_rope(bank, c0, bi, rc, rs, outs):
            t0 = bi * NB
            for g in range(2):
                for k in range(8):
                    P.op("pe", lambda e, g=g, k=k: e.matmul(bank[:, g * NB:(g + 1) * NB], lhsT=Wh[:, k, c0 + g * 128:c0 + (g + 1) * 128],
                                                             rhs=HT[:, k, 1 + t0:1 + t0 + NB], start=(k == 0), stop=(k == 7)),
                         reads=[Wh, HT], writes=[bank])
            P.op("dve", lambda e: e.tensor_tensor(out=t1.ap(), in0=bank[:, 0:NB], in1=rc.ap(), op=ALU.mult), reads=[rc], writes=[bank, t1])
            P.op("dve", lambda e: e.tensor_tensor(out=t2.ap(), in0=bank[:, NB:2 * NB], in1=rs.ap(), op=ALU.mult), reads=[rs], writes=[bank, t2])
            for rows, oap, ob in outs:
                P.op("pool", lambda e, rows=rows, oap=oap: e.tensor_tensor(out=oap, in0=t1[rows, :], in1=t2[rows, :], op=ALU.add),
                     reads=[t1, t2], writes=[ob])

        for h in range(4):
            qc = 1696 + h * 128
            kc = 1696 + 512 + h * 128
            vc = 1696 + 1024 + h * 128
            load_w_bf16(C, (Wh, 0), dr["w_in"][:, qc:qc + 128], 128, stg)
            load_w_bf16(C, (Wh, 128), dr["w_qks"][:, h * 128:(h + 1) * 128], 128, stg)
            load_w_bf16(C, (Wh, 256), dr["w_in"][:, kc:kc + 128], 128, stg)
            load_w_bf16(C, (Wh, 384), dr["w_qks"][:, 512 + h * 128:512 + (h + 1) * 128], 128, stg)
            load_w_bf16(C, (Wh, 512), dr["w_in"][:, vc:vc + 128], 128, stg)
            pb = C.nb()
            for k in range(8):
                P.op("pe", lambda e, k=k, pb=pb: e.matmul(pb[:, 0:T_CTX], lhsT=Wh[:, k, 256:384], rhs=HC[:, k, 1:1 + T_CTX],
                                                           start=(k == 0), stop=(k == 7)), reads=[Wh, HC], writes=[pb])
            P.op("act", lambda e, pb=pb: e.activation(out=KT[:, 0:T_CTX], in_=pb[:, 0:T_CTX], func=AF.Copy), writes=[pb, KT])
            for bi in range(NLB):
                rc, rs = load_rope(bi)
                proj_rope(C.nb(), 256, bi, rc, rs, [(slice(0, 128), KT[:, T_CTX + bi * NB:T_CTX + (bi + 1) * NB], KT)])
            for j in range(NKT):
                Hs, c0 = (HC, 1 + j * 128) if j < 2 else (HT, 1 + (j - 2) * 128)
                pv = C.nb()
                for k in range(8):
                    P.op("pe", lambda e, k=k, pv=pv, Hs=Hs, c0=c0: e.matmul(pv[:, 0:128], lhsT=Hs[:, k, c0:c0 + 128], rhs=Wh[:, k, 512:640],
                                                                           start=(k == 0), stop=(k == 7)), reads=[Wh, Hs], writes=[pv])
                eng = "act" if j % 2 == 0 else "dve"
                if eng == "act":
                    P.op("act", lambda e, pv=pv, j=j: e.activation(out=VT[:, j, :], in_=pv[:, 0:128], func=AF.Copy), writes=[pv, VT])
                else:
                    P.op("dve", lambda e, pv=pv, j=j: e.tensor_copy(out=VT[:, j, :], in_=pv[:, 0:128]), writes=[pv, VT])

            def q_proj(bi):
                rc, rs = load_rope(bi)
                qb = QB[bi % 2]
                proj_rope(bk[6], 0, bi, rc, rs, [(slice(0, 64), qb[0:64, 0:NB], qb), (slice(64, 128), qb[64:128, NB:2 * NB], qb)])

            q_proj(0)
            pending = []
            for bi in range(NLB):
                qb = QB[bi % 2]
                po, pl = bk[3 + bi % 2], bk[5]

                def qk(j):
                    ps = bk[j % 3]
                    P.op("pe", lambda e, j=j, ps=ps: e.matmul(ps.ap(), lhsT=KT[:, j * 128:(j + 1) * 128], rhs=qb.ap(), start=True, stop=True),
                         reads=[KT, qb], writes=[ps])

                qk(0)
                qk(1)
                qk(2)
                if bi + 1 < NLB:
                    q_proj(bi + 1)
                while pending:
                    pending.pop(0)()
                nD = nP = 0
                for j in range(NKT):
                    ps, pt = bk[j % 3], PT[j % 3]
                    P.op("act", lambda e, ps=ps, pt=pt: e.activation(out=pt.ap(), in_=ps.ap(), func=AF.Exp, scale=0.125), writes=[ps, pt])
                    P.op("pe", lambda e, j=j, pt=pt: e.matmul(po.ap(), lhsT=VT[:, j, :], rhs=pt.ap(), start=(j == 0), stop=(j == NKT - 1)),
                         reads=[VT, pt], writes=[po])
                    if nD == 0:
                        P.op("dve", lambda e, pt=pt: e.tensor_copy(out=accD.ap(), in_=pt.ap()), reads=[pt], writes=[accD])
                    else:
                        P.op("dve", lambda e, pt=pt: e.tensor_tensor(out=accD.ap(), in0=accD.ap(), in1=pt.ap(), op=ALU.add), reads=[pt], writes=[accD])
                    nD += 1
                    if j + 3 < NKT:
                        qk(j + 3)
                P.op("pe", lambda e: e.matmul(pl.ap(), lhsT=ones32a.ap(), rhs=accD.ap(), start=True, stop=True), reads=[ones32a, accD], writes=[pl])
                def finalize(bi=bi, po=po, pl=pl):
                    P.op("dve", lambda e: e.reciprocal(out=rl.ap(), in_=pl.ap()), writes=[pl, rl])
                    P.op("dve", lambda e: e.tensor_tensor(out=on_.ap(), in0=po.ap(), in1=rl.ap(), op=ALU.mult), reads=[rl], writes=[po, on_])
                    P.op("dve", lambda e: e.scalar_tensor_tensor(out=dif.ap(), in0=on_[:, NB:2 * NB], scalar=nlam.ap(), in1=on_[:, 0:NB],
                                                                 op0=ALU.mult, op1=ALU.add), reads=[on_, nlam], writes=[dif])
                    P.op("pool", lambda e: e.tensor_tensor(out=dsq.ap(), in0=dif.ap(), in1=dif.ap(), op=ALU.mult), reads=[dif], writes=[dsq])
                    pm = bk[6]
                    P.op("pe", lambda e: e.matmul(pm[:, 0:NB], lhsT=C.mean128.ap(), rhs=dsq.ap(), start=True, stop=True),
                         reads=[C.mean128, dsq], writes=[pm])
                    P.op("act", lambda e: e.activation(out=drs.ap(), in_=pm[:, 0:NB], func=AF.Ln, bias=C.epss.ap()), reads=[C.epss], writes=[pm, drs])
                    P.op("act", lambda e: e.activation(out=drs.ap(), in_=drs.ap(), func=AF.Exp, scale=-0.5), writes=[drs])
                    P.op("pool", lambda e: e.tensor_tensor(out=dif.ap(), in0=dif.ap(), in1=drs.ap(), op=ALU.mult), reads=[drs], writes=[dif])
                    o16 = ao16[bi % 2]
                    P.op("pool", lambda e, o16=o16: e.tensor_scalar(out=o16.ap(), in0=dif.ap(), scalar1=sbw.ap(), scalar2=None, op0=ALU.mult),
                         reads=[dif, sbw], writes=[o16])
                    mb = Buf("mt_%d_%d" % (4 + h, bi))
                    C.mt_bufs[(4 + h, bi)] = mb
                    P.dma("sp", dr["mt"][4 + h, :, bi * NB:(bi + 1) * NB], o16.ap(), reads=[o16], writes=[mb])

                pending.append(finalize)
            while pending:
                pending.pop(0)()


def stage_out_a(C):
    P, dr = C.P, C.dr
    C.x1_bufs, C.h2_bufs = {}, {}
    with P.scope():
        C.stg_i = 0
        stg = [P.sbuf("ostg%d" % i, [128, 8, 256], F32) for i in range(2)]
        WO = P.sbuf("WO", [128, 8, 1024], BF16)
        load_w_bf16(C, (WO, 0), dr["w_out"], 1024, stg)
        MTb = [P.sbuf("MTb%d" % i, [128, 8, NB], BF16) for i in range(2)]
        xts = [P.sbuf("oxt%d" % i, [128, 8, NB], F32) for i in range(2)]
        y32 = P.sbuf("oy32", [128, 8, NB], F32)
        x1 = [P.sbuf("ox1_%d" % i, [128, 8, NB], F32) for i in range(2)]
        h2 = [P.sbuf("oh2_%d" % i, [128, 8, NB], BF16) for i in range(2)]
        sq = P.sbuf("o_sq", [128, 8, NB], BF16)
        rs = P.sbuf("o_rs", [128, NB], F32)
        tmp = P.sbuf("o_tmp", [128, 8, NB], F32)
        def ld_a(bi):
            t0 = bi * NB
            P.dma("sp", MTb[bi % 2].ap(), dr["mt"][:, :, t0:t0 + NB].rearrange("k p t -> p k t"),
                  reads=[C.mt_bufs[(k, bi)] for k in range(8)], writes=[MTb[bi % 2]])
            P.dma("sp", xts[bi % 2].ap(), dr["xT"][:, t0:t0 + NB].rearrange("(k p) t -> p k t", p=128), writes=[xts[bi % 2]])

        for bi in range(NLB):
            t0 = bi * NB
            mtb, xt, x1b, h2b = MTb[bi % 2], xts[bi % 2], x1[bi % 2], h2[bi % 2]
            if bi == 0:
                ld_a(0)
            if bi + 1 < NLB:
                ld_a(bi + 1)
            for j in range(8):
                py = C.nb()
                for k in range(8):
                    P.op("pe", lambda e, j=j, k=k, py=py: e.matmul(py[:, 0:NB], lhsT=WO[:, k, j * 128:(j + 1) * 128], rhs=mtb[:, k, :],
                                                                   start=(k == 0), stop=(k == 7)), reads=[WO, mtb], writes=[py])
                if j % 2 == 0:
                    P.op("act", lambda e, j=j, py=py: e.activation(out=y32[:, j, :], in_=py[:, 0:NB], func=AF.Copy), writes=[py, y32])
                else:
                    P.op("dve", lambda e, j=j, py=py: e.tensor_copy(out=y32[:, j, :], in_=py[:, 0:NB]), writes=[py, y32])
            P.op("act", lambda e: e.activation(out=sq.ap(), in_=y32.ap(), func=AF.Square), reads=[y32], writes=[sq])
            pb = C.nb()
            for k in range(8):
                P.op("pe", lambda e, k=k, pb=pb: e.matmul(pb[:, 0:NB], lhsT=C.ones.ap(), rhs=sq[:, k, :], start=(k == 0), stop=(k == 7)),
                     reads=[sq, C.ones], writes=[pb])
            P.op("act", lambda e, pb=pb: e.activation(out=rs.ap(), in_=pb[:, 0:NB], func=AF.Sqrt, scale=1.0 / D_MODEL, bias=C.eps6.ap()),
                 reads=[C.eps6], writes=[pb, rs])
            P.op("dve", lambda e: e.reciprocal(out=rs.ap(), in_=rs.ap()), writes=[rs])
            for j in range(8):
                P.op("dve", lambda e, j=j: e.scalar_tensor_tensor(out=tmp[:, j, :], in0=y32[:, j, :], scalar=C.G1[:, j:j + 1], in1=rs.ap(),
                                                                  op0=ALU.mult, op1=ALU.mult), reads=[y32, rs, C.G1], writes=[tmp])
            P.op("pool", lambda e, x1b=x1b, xt=xt: e.tensor_tensor(out=x1b.ap(), in0=tmp.ap(), in1=xt.ap(), op=ALU.add),
                 reads=[tmp, xt], writes=[x1b])
            xb = Buf("x1s_%d" % bi)
            C.x1_bufs[bi] = xb
            P.dma("sp", dr["x1s"][:, :, t0:t0 + NB].rearrange("k p t -> p k t"), x1b.ap(), reads=[x1b], writes=[xb])
            norm_block(C, x1b, NB, lambda k: C.A2[:, k:k + 1], lambda k: C.modT[:, 24 + k, 0:1],
                       lambda k, h2b=h2b: h2b[:, k, :], [C.A2, C.modT], h2b, (sq, rs, tmp))
            hb = Buf("h2s_%d" % bi)
            C.h2_bufs[bi] = hb
            P.dma("sp", dr["h2s"][:, :, t0:t0 + NB].rearrange("k p t -> p k t"), h2b.ap(), reads=[h2b], writes=[hb])


def stage_ffn(C):
    P, dr = C.P, C.dr
    NJ = D_FF // 128
    with P.scope():
        C.stg_i = 0
        WU = P.sbuf("WU", [128, 8, 2 * D_FF], BF16)
        WD = P.sbuf("WD", [128, NJ, 1024], BF16)
        with P.scope():
            stg = [P.sbuf("fstg%d" % i, [128, 8, 256], F32) for i in range(2)]
            load_w_bf16(C, (WU, 0), dr["w_up"], 2 * D_FF, stg)
            for j in range(NJ):
                s_ = stg[C.stg_i % 2]
                C.stg_i += 1
                sv = s_.ap().rearrange("p a b -> p (a b)")[:, 0:1024]
                P.dma("sp", sv, dr["w_down"][j * 128:(j + 1) * 128, :], writes=[s_])
                if j % 2 == 0:
                    P.op("act", lambda e, j=j, sv=sv: e.activation(out=WD[:, j, :], in_=sv, func=AF.Copy), reads=[s_], writes=[WD])
                else:
                    P.op("dve", lambda e, j=j, sv=sv: e.tensor_copy(out=WD[:, j, :], in_=sv), reads=[s_], writes=[WD])
        FCW = P.sbuf("FCW", [128, 2 * NJ, 3], F32)
        FB = P.sbuf("FB", [128, 2 * NJ], F32)
        C.cwb = FCW
        P.dma("sp", FCW.ap(), dr["fconvT"].rearrange("(c p) j -> p c j", p=128), writes=[FCW])
        P.dma("sp", FB.ap(), dr["fbias"], writes=[FB])
        h2t = [P.sbuf("fh2_%d" % i, [128, 8, NB + 2], BF16) for i in range(2)]
        x1t = [P.sbuf("fx1_%d" % i, [128, 8, NB], F32) for i in range(2)]
        uv = [P.sbuf("fuv%d" % i, [128, NB], F32) for i in range(3)]
        ug = [P.sbuf("fug%d" % i, [128, NB], F32) for i in range(3)]
        sg = [P.sbuf("fsg%d" % i, [128, NB], F32) for i in range(3)]
        act16 = P.sbuf("fact", [128, NJ, NB], BF16)
        f32t = P.sbuf("ff32", [128, 8, NB], F32)
        sq = P.sbuf("f_sq", [128, 8, NB], BF16)
        rs = P.sbuf("f_rs", [128, NB], F32)
        def load_block(bi):
            t0 = bi * NB
            hb, xb = h2t[bi % 2], x1t[bi % 2]
            lo = max(t0 - 1, 0)
            hi = min(t0 + NB + 1, T_LAT)
            rd = [C.h2_bufs[b] for b in (bi - 1, bi, bi + 1) if 0 <= b < NLB]
            if bi == 0:
                P.op("pool", lambda e, hb=hb: e.memset(hb[:, :, 0:1], 0.0), writes=[hb])
            if bi == NLB - 1:
                P.op("pool", lambda e, hb=hb: e.memset(hb[:, :, NB + 1:NB + 2], 0.0), writes=[hb])
            d0 = lo - (t0 - 1)
            P.dma("sp", hb[:, :, d0:d0 + (hi - lo)], dr["h2s"][:, :, lo:hi].rearrange("k p t -> p k t"), reads=rd, writes=[hb])
            P.dma("sp", xb.ap(), dr["x1s"][:, :, t0:t0 + NB].rearrange("k p t -> p k t"), reads=[C.x1_bufs[bi]], writes=[xb])

        def gate_mul(j, u, g, s2):
            P.op("act", lambda e: e.activation(out=s2.ap(), in_=g.ap(), func=AF.Silu), reads=[g], writes=[s2])
            P.op("pool", lambda e: e.tensor_tensor(out=act16[:, j, :], in0=u.ap(), in1=s2.ap(), op=ALU.mult),
                 reads=[u, s2], writes=[act16])

        load_block(0)
        for bi in range(NLB):
            t0 = bi * NB
            hb, xb = h2t[bi % 2], x1t[bi % 2]
            if bi + 1 < NLB:
                load_block(bi + 1)
            prev = None
            for j in range(NJ):
                u, g, s2 = uv[j % 3], ug[j % 3], sg[j % 3]
                proj_conv(C, hb, 0, WU, j * 128, 128, FCW[:, j, :], u.ap(), u, bias=FB[:, j:j + 1], bias_buf=FB)
                proj_conv(C, hb, 0, WU, D_FF + j * 128, 128, FCW[:, NJ + j, :], g.ap(), g, bias=FB[:, NJ + j:NJ + j + 1], bias_buf=FB)
                if prev is not None:
                    gate_mul(*prev)
                prev = (j, u, g, s2)
            gate_mul(*prev)
            for i in range(8):
                pf = C.nb()
                for j in range(NJ):
                    P.op("pe", lambda e, i=i, j=j, pf=pf: e.matmul(pf[:, 0:NB], lhsT=WD[:, j, i * 128:(i + 1) * 128], rhs=act16[:, j, :],
                                                                   start=(j == 0), stop=(j == NJ - 1)), reads=[WD, act16], writes=[pf])
                if i % 2 == 0:
                    P.op("act", lambda e, i=i, pf=pf: e.activation(out=f32t[:, i, :], in_=pf[:, 0:NB], func=AF.Copy), writes=[pf, f32t])
                else:
                    P.op("dve", lambda e, i=i, pf=pf: e.tensor_copy(out=f32t[:, i, :], in_=pf[:, 0:NB]), writes=[pf, f32t])
            P.op("act", lambda e: e.activation(out=sq.ap(), in_=f32t.ap(), func=AF.Square), reads=[f32t], writes=[sq])
            pb = C.nb()
            for k in range(8):
                P.op("pe", lambda e, k=k, pb=pb: e.matmul(pb[:, 0:NB], lhsT=C.ones.ap(), rhs=sq[:, k, :], start=(k == 0), stop=(k == 7)),
                     reads=[sq, C.ones], writes=[pb])
            P.op("act", lambda e, pb=pb: e.activation(out=rs.ap(), in_=pb[:, 0:NB], func=AF.Sqrt, scale=1.0 / D_MODEL, bias=C.eps6.ap()),
                 reads=[C.eps6], writes=[pb, rs])
            P.op("dve", lambda e: e.reciprocal(out=rs.ap(), in_=rs.ap()), writes=[rs])
            for i in range(8):
                P.op("dve", lambda e, i=i: e.scalar_tensor_tensor(out=f32t[:, i, :], in0=f32t[:, i, :], scalar=C.G2[:, i:i + 1], in1=rs.ap(),
                                                                  op0=ALU.mult, op1=ALU.mult), reads=[rs, C.G2], writes=[f32t])
            P.op("pool", lambda e, xb=xb: e.tensor_tensor(out=xb.ap(), in0=f32t.ap(), in1=xb.ap(), op=ALU.add),
                 reads=[f32t], writes=[xb])
            P.dma("sp", dr["outT"][:, t0:t0 + NB].rearrange("(k p) t -> p k t", p=128), xb.ap(), reads=[xb], final=True)
```

```python
import contextlib
import numpy as np
import concourse.bass as bass
import concourse.mybir as mybir
from concourse.bass_utils import run_bass_kernel_spmd

F32 = mybir.dt.float32
BF16 = mybir.dt.bfloat16
AF = mybir.ActivationFunctionType
ALU = mybir.AluOpType
AX = mybir.AxisListType

SEM_ROLL = 30000


class Buf:
    def __init__(self, name, t=None):
        self.name = name
        self.t = t
        self.last_w = None
        self.readers = []
        self.dma_sem = None
        self.dma_cnt = 0

    def ap(self):
        return self.t[:]

    def __getitem__(self, idx):
        return self.t[idx]


class _Rec:
    def __getattr__(self, name):
        return lambda *a, **k: (name, a, k)


_REC = _Rec()


def _bind(fn):
    name, a, k = fn(_REC)
    return lambda e: getattr(e, name)(*a, **k)


class Op:
    __slots__ = ("eng", "fn", "deps", "signal", "token", "is_dma", "sem_buf", "final", "idx")


class Prog:
    ENGS = ("pe", "act", "dve", "pool", "sp")

    def __init__(self, nc):
        self.nc = nc
        self.stack = contextlib.ExitStack()
        self.ops = {e: [] for e in self.ENGS}
        self.nbuf = 0
        self.all_ops = []
        self.final_ops = []

    def sbuf(self, name, shape, dtype):
        self.nbuf += 1
        name = "s%d_%s" % (self.nbuf, name)
        t = self.stack.enter_context(self.nc.sbuf_tensor(name, list(shape), dtype))
        b = Buf(name, t)
        b.readers = list(getattr(self, "fence", []))
        if hasattr(self, "scope_bufs") and self.scope_bufs:
            self.scope_bufs[-1].append(b)
        return b

    def psum(self, name, shape, dtype):
        t = self.stack.enter_context(self.nc.psum_tensor(name, list(shape), dtype))
        return Buf(name, t)

    def view(self, name):
        return Buf(name)

    @contextlib.contextmanager
    def scope(self):
        old = self.stack
        self.stack = contextlib.ExitStack()
        if not hasattr(self, "scope_bufs"):
            self.scope_bufs = []
        self.scope_bufs.append([])
        try:
            yield
        finally:
            self.stack.close()
            self.stack = old
            bufs = self.scope_bufs.pop()
            ops = list(getattr(self, "fence", []))
            for b in bufs:
                if b.last_w is not None:
                    ops.append(b.last_w)
                ops.extend(b.readers)
            best = {}
            dmas = {}
            for o in ops:
                if o.is_dma:
                    dmas[id(o)] = o
                else:
                    if o.eng not in best or best[o.eng].idx < o.idx:
                        best[o.eng] = o
            self.fence = list(best.values()) + list(dmas.values())

    def _deps(self, o, reads, writes):
        deps = []
        for b in list(reads) + list(writes):
            if b.last_w is not None:
                deps.append(b.last_w)
        for b in writes:
            deps.extend(b.readers)
        for b in writes:
            b.last_w = o
            b.readers = []
        for b in reads:
            b.readers.append(o)
        seen = set()
        out = []
        for d in deps:
            if id(d) in seen or d is o:
                continue
            seen.add(id(d))
            if d.eng == "pe" and o.eng == "pe" and not d.is_dma and not o.is_dma:
                continue
            d.signal = True
            out.append(d)
        o.deps = out

    def op(self, eng, fn, reads=(), writes=()):
        o = Op()
        o.eng = eng
        o.fn = _bind(fn)
        o.signal = False
        o.token = None
        o.is_dma = False
        o.sem_buf = None
        o.final = False
        o.idx = len(self.ops[eng])
        self._deps(o, reads, writes)
        self.ops[eng].append(o)
        return o

    def dma(self, eng, out, in_, reads=(), writes=(), final=False):
        o = Op()
        o.eng = eng
        o.fn = lambda e: e.dma_start(out=out, in_=in_)
        o.signal = True
        o.token = None
        o.is_dma = True
        o.final = final
        cands = [b for b in list(writes) + list(reads) if b.t is not None]
        sb = cands[0] if cands else (list(writes) + list(reads))[0]
        o.sem_buf = sb
        o.idx = len(self.ops[eng])
        self._deps(o, reads, writes)
        self.ops[eng].append(o)
        if final:
            self.final_ops.append(o)
        return o

    def emit(self):
        nc = self.nc
        st = self.stack
        eng_sems = {}
        for e in self.ENGS:
            n = 0
            for o in self.ops[e]:
                if o.is_dma:
                    b = o.sem_buf
                    if b.dma_sem is None:
                        b.dma_sem = st.enter_context(nc.semaphore("d_" + b.name))
                    b.dma_cnt += 16
                    o.token = (b.dma_sem, b.dma_cnt, 16)
                elif o.signal:
                    k = n // SEM_ROLL
                    if (e, k) not in eng_sems:
                        eng_sems[(e, k)] = st.enter_context(nc.semaphore("s_%s_%d" % (e, k)))
                    o.token = (eng_sems[(e, k)], n % SEM_ROLL + 1, 1)
                    n += 1
        print("ops:", {e: len(self.ops[e]) for e in self.ENGS}, "signals:", {e: sum(1 for o in self.ops[e] if o.token is not None) for e in self.ENGS}, "nsems", len(eng_sems))
        all_sems = list(eng_sems.values())
        seen_b = set()
        for e in self.ENGS:
            for o in self.ops[e]:
                if o.is_dma and id(o.sem_buf) not in seen_b:
                    seen_b.add(id(o.sem_buf))
                    all_sems.append(o.sem_buf.dma_sem)
        with nc.Block() as blk0:
            @blk0.sync
            def _(eng):
                for sm in all_sems:
                    eng.sem_clear(sm)
        block = st.enter_context(nc.Block())
        hooks = {"pe": block.tensor, "act": block.scalar, "dve": block.vector,
                 "pool": block.gpsimd, "sp": block.sync}
        final_ops = self.final_ops

        def make(e):
            ops = self.ops[e]

            def body(eng):
                waited = {}
                for o in ops:
                    for d in o.deps:
                        sem, val, _ = d.token
                        if waited.get(id(sem), 0) < val:
                            eng.wait_ge(sem, val)
                            waited[id(sem)] = val
                    ins = o.fn(eng)
                    if o.token is not None:
                        ins.then_inc(o.token[0], o.token[2])
                if e == "sp":
                    for o in final_ops:
                        sem, val, _ = o.token
                        eng.wait_ge(sem, val)
            return body

        for e in self.ENGS:
            hooks[e](make(e))
        st.close()


D_MODEL = 1024
T_LAT = 4096
T_CTX = 256
NB = 256
NLB = T_LAT // NB
D_FF = 2816
EXPM05 = float(np.exp(-0.5))


class Ctx:
    pass


def build_program(nc, stage=99):
    P = Prog(nc)
    C = Ctx()
    C.P = P
    C.nc = nc
    dr = {}

    def din(name, shape, dt=F32):
        dr[name] = nc.dram_tensor(name, list(shape), dt, kind="ExternalInput").ap()

    din("xT", [1024, T_LAT]); din("ctxT", [1024, T_CTX]); din("cc", [128, 8, 2])
    din("w_mod", [1024, 6144]); din("b_modT", [128, 48]); din("gvec", [128, 4, 8])
    din("w_in", [1024, 3232]); din("w_qks", [1024, 1024]); din("convT", [1696, 3])
    din("lw2", [64, 2, 512]); din("w0a0", [128, 4, 4]); din("kvec", [128, 5, 4]); din("g2", [96, 512])
    din("lam", [128, 4, 64]); din("subln", [128, 1])
    din("w_out", [1024, 1024]); din("w_up", [1024, 5632]); din("fconvT", [5632, 3]); din("fbias", [128, 44])
    din("w_down", [2816, 1024]); din("ropeC", [128, T_LAT]); din("ropeS", [128, T_LAT]); din("blkmask", [128, 5, 128])
    dr["outT"] = nc.dram_tensor("outT", [1024, T_LAT], F32, kind="ExternalOutput").ap()
    dr["yf"] = nc.dram_tensor("yf_scr", [4, 128, T_LAT], F32).ap()
    dr["mt"] = nc.dram_tensor("mt_scr", [8, 128, T_LAT], BF16).ap()
    dr["x1s"] = nc.dram_tensor("x1_scr", [8, 128, T_LAT], F32).ap()
    dr["h2s"] = nc.dram_tensor("h2_scr", [8, 128, T_LAT], BF16).ap()
    if stage < 99:
        dr["dbg"] = nc.dram_tensor("dbg", [128, 8, T_LAT + 2], F32, kind="ExternalOutput").ap()
    C.dr = dr
    C.yf_bufs = {}
    C.mt_bufs = {}

    C.banks = [P.psum("pb%d" % i, [128, 512], F32) for i in range(7)]
    C.bankT = P.psum("pbT", [128, 1024], BF16)
    C.bank_i = 0

    def nb():
        b = C.banks[C.bank_i % 7]
        C.bank_i += 1
        return b
    C.nb = nb

    ident = P.sbuf("ident", [128, 128], BF16)
    ones = P.sbuf("ones", [128, 128], BF16)
    bones = P.sbuf("bones", [128, 128], BF16)
    bmean = P.sbuf("bmean", [128, 128], BF16)
    mean128 = P.sbuf("mean128", [128, 128], BF16)
    eps6 = P.sbuf("eps6", [128, 1], F32)
    epsx = P.sbuf("epsx", [128, 1], F32)
    epss = P.sbuf("epss", [128, 1], F32)
    P.op("pool", lambda e: e.memset(ident.ap(), 0.0), writes=[ident])
    P.op("pool", lambda e: e.affine_select(out=ident.ap(), in_=ident.ap(), pattern=[[-1, 128]],
                                           compare_op=ALU.not_equal, fill=1.0, base=0, channel_multiplier=1),
         reads=[ident], writes=[ident])
    P.op("pool", lambda e: e.memset(ones.ap(), 1.0), writes=[ones])
    P.op("pool", lambda e: e.memset(mean128.ap(), 1.0 / 128), writes=[mean128])
    P.op("pool", lambda e: e.memset(bones.ap(), 0.0), writes=[bones])
    P.op("pool", lambda e: e.memset(bones[0:64, 0:64], 1.0), writes=[bones])
    P.op("pool", lambda e: e.memset(bones[64:128, 64:128], 1.0), writes=[bones])
    P.op("pool", lambda e: e.memset(bmean.ap(), 0.0), writes=[bmean])
    P.op("pool", lambda e: e.memset(bmean[0:64, 0:64], 1.0 / 64), writes=[bmean])
    P.op("pool", lambda e: e.memset(bmean[64:128, 64:128], 1.0 / 64), writes=[bmean])
    P.op("pool", lambda e: e.memset(eps6.ap(), 1e-6), writes=[eps6])
    P.op("pool", lambda e: e.memset(epsx.ap(), 64e-5), writes=[epsx])
    P.op("pool", lambda e: e.memset(epss.ap(), 1e-5), writes=[epss])
    C.ident, C.ones, C.bones, C.bmean, C.mean128 = ident, ones, bones, bmean, mean128
    C.eps6, C.epsx, C.epss = eps6, epsx, epss

    stage_mod(C)
    import os
    with P.scope():
        C.HT = P.sbuf("HT", [128, 8, T_LAT + 2], BF16)
        C.HC = P.sbuf("HC", [128, 8, T_CTX + 2], BF16)
        for Hb, n in ((C.HT, T_LAT), (C.HC, T_CTX)):
            P.op("pool", lambda e, Hb=Hb: e.memset(Hb[:, :, 0:1], 0.0), writes=[Hb])
            P.op("pool", lambda e, Hb=Hb, n=n: e.memset(Hb[:, :, n + 1:n + 2], 0.0), writes=[Hb])
        stage_prenorm(C)
        if stage == 1:
            dbg_dump_HT(C)
            P.emit()
            return nc
        if not os.environ.get("SKIP_RWKV"):
            stage_rwkv(C)
        if stage == 21:
            with P.scope():
                for k in range(4):
                    tmp = P.sbuf("dbgy%d" % k, [128, T_LAT], F32)
                    P.dma("sp", tmp.ap(), dr["yf"][k], reads=list(C.yf_bufs.values()), writes=[tmp])
                    P.dma("sp", dr["dbg"][:, k, 0:T_LAT], tmp.ap(), reads=[tmp], final=True)
            P.emit()
            return nc
        if stage == 2:
            dbg_dump_mt(C, 0, 4)
            P.emit()
            return nc
        stage_attn(C)
        if stage == 3:
            dbg_dump_mt(C, 4, 8)
            P.emit()
            return nc
        stage_out_a(C)
    stage_ffn(C)
    P.emit()
    return nc


def dbg_dump_HT(C):
    P = C.P
    with P.scope():
        for k in range(8):
            tmp = P.sbuf("dbgt%d" % k, [128, T_LAT + 2], F32)
            P.op("dve", lambda e, k=k, tmp=tmp: e.tensor_copy(out=tmp.ap(), in_=C.HT[:, k, :]), reads=[C.HT], writes=[tmp])
            P.dma("sp", C.dr["dbg"][:, k, :], tmp.ap(), reads=[tmp], final=True)


def dbg_dump_mt(C, k0, k1):
    P = C.P
    with P.scope():
        for k in range(k0, k1):
            tb = P.sbuf("dbgb%d" % k, [128, T_LAT], BF16)
            tmp = P.sbuf("dbgt%d" % k, [128, T_LAT], F32)
            rd = [b for (kk, _), b in C.mt_bufs.items() if kk == k] + list(C.yf_bufs.values())
            P.dma("sp", tb.ap(), C.dr["mt"][k], reads=rd, writes=[tb])
            P.op("dve", lambda e, tmp=tmp, tb=tb: e.tensor_copy(out=tmp.ap(), in_=tb.ap()), reads=[tb], writes=[tmp])
            P.dma("sp", C.dr["dbg"][:, k, 0:T_LAT], tmp.ap(), reads=[tmp], final=True)


def stage_mod(C):
    P, dr = C.P, C.dr
    cc = P.sbuf("cc", [128, 8, 2], F32)
    scc = P.sbuf("scc", [128, 8, 2], F32)
    bm = P.sbuf("bmodT", [128, 48], F32)
    gv = P.sbuf("gvec", [128, 4, 8], F32)
    modT = P.sbuf("modT", [128, 48, 2], F32)
    P.dma("sp", cc.ap(), dr["cc"], writes=[cc])
    P.dma("sp", bm.ap(), dr["b_modT"], writes=[bm])
    P.dma("sp", gv.ap(), dr["gvec"], writes=[gv])
    P.op("act", lambda e: e.activation(out=scc.ap(), in_=cc.ap(), func=AF.Silu), reads=[cc], writes=[scc])
    pm = C.nb()
    with P.scope():
        wst = [P.sbuf("wmst%d" % i, [128, 8, 512], F32) for i in range(2)]
        for g in range(12):
            w = wst[g % 2]
            P.dma("sp", w.ap(), dr["w_mod"][:, g * 512:(g + 1) * 512].rearrange("(k p) n -> p k n", p=128), writes=[w])
            for jj in range(4):
                j = g * 4 + jj
                for k in range(8):
                    P.op("pe", lambda e, w=w, k=k, jj=jj, j=j: e.matmul(
                        pm[:, 2 * j:2 * j + 2], lhsT=w[:, k, jj * 128:(jj + 1) * 128], rhs=scc[:, k, :],
                        start=(k == 0), stop=(k == 7)), reads=[w, scc], writes=[pm])
    pmv = pm[:, 0:96].rearrange("p (j c) -> p j c", c=2)
    for c in range(2):
        P.op("dve", lambda e, c=c: e.tensor_tensor(out=modT[:, :, c], in0=pmv[:, :, c], in1=bm.ap(), op=ALU.add),
             reads=[bm], writes=[pm, modT])
    C.modT = modT
    A1 = P.sbuf("A1", [128, 2, 8], F32)
    G1 = P.sbuf("G1", [128, 8], F32)
    A2 = P.sbuf("A2", [128, 8], F32)
    G2 = P.sbuf("G2", [128, 8], F32)
    for c in range(2):
        P.op("dve", lambda e, c=c: e.scalar_tensor_tensor(out=A1[:, c, :], in0=modT[:, 8:16, c], scalar=1.0,
                                                          in1=gv[:, 0, :], op0=ALU.add, op1=ALU.mult),
             reads=[modT, gv], writes=[A1])
    P.op("dve", lambda e: e.tensor_tensor(out=G1.ap(), in0=modT[:, 16:24, 0], in1=gv[:, 1, :], op=ALU.mult),
         reads=[modT, gv], writes=[G1])
    P.op("dve", lambda e: e.scalar_tensor_tensor(out=A2.ap(), in0=modT[:, 32:40, 0], scalar=1.0, in1=gv[:, 2, :],
                                                 op0=ALU.add, op1=ALU.mult), reads=[modT, gv], writes=[A2])
    P.op("dve", lambda e: e.tensor_tensor(out=G2.ap(), in0=modT[:, 40:48, 0], in1=gv[:, 3, :], op=ALU.mult),
         reads=[modT, gv], writes=[G2])
    C.A1, C.G1, C.A2, C.G2 = A1, G1, A2, G2


def norm_block(C, xt, nbk, A_ap, sh_ap, out_ap, xt_reads, out_buf, tmpn):
    P = C.P
    sq, rs, tmp = tmpn
    P.op("act", lambda e: e.activation(out=sq[:, :, 0:nbk], in_=xt[:, :, 0:nbk], func=AF.Square),
         reads=[xt], writes=[sq])
    pb = C.nb()
    for k in range(8):
        P.op("pe", lambda e, k=k: e.matmul(pb[:, 0:nbk], lhsT=C.ones.ap(), rhs=sq[:, k, 0:nbk],
                                            start=(k == 0), stop=(k == 7)), reads=[sq, C.ones], writes=[pb])
    P.op("act", lambda e: e.activation(out=rs[:, 0:nbk], in_=pb[:, 0:nbk], func=AF.Sqrt, scale=1.0 / D_MODEL,
                                       bias=C.eps6.ap()), reads=[C.eps6], writes=[pb, rs])
    P.op("dve", lambda e: e.reciprocal(out=rs[:, 0:nbk], in_=rs[:, 0:nbk]), writes=[rs])
    for k in range(8):
        P.op("dve", lambda e, k=k: e.scalar_tensor_tensor(out=tmp[:, k, 0:nbk], in0=xt[:, k, 0:nbk], scalar=A_ap(k),
                                                          in1=rs[:, 0:nbk], op0=ALU.mult, op1=ALU.mult),
             reads=[xt, rs] + xt_reads, writes=[tmp])
        P.op("act", lambda e, k=k: e.activation(out=out_ap(k), in_=tmp[:, k, 0:nbk], func=AF.Identity,
                                                bias=sh_ap(k)), reads=[tmp] + xt_reads, writes=[out_buf])


def stage_prenorm(C):
    P, dr = C.P, C.dr
    with P.scope():
        xts = [P.sbuf("xt%d" % i, [128, 8, NB], F32) for i in range(2)]
        sq = P.sbuf("pn_sq", [128, 8, NB], BF16)
        rs = P.sbuf("pn_rs", [128, NB], F32)
        tmp = P.sbuf("pn_tmp", [128, 8, NB], F32)
        blocks = [("c", 0)] + [("l", i) for i in range(NLB)]
        for bi, (src, i) in enumerate(blocks):
            xt = xts[bi % 2]
            if src == "c":
                P.dma("sp", xt.ap(), dr["ctxT"].rearrange("(k p) t -> p k t", p=128), writes=[xt])
                H, cidx, t0 = C.HC, 1, 0
            else:
                P.dma("sp", xt.ap(), dr["xT"][:, i * NB:(i + 1) * NB].rearrange("(k p) t -> p k t", p=128), writes=[xt])
                H, cidx, t0 = C.HT, 0, i * NB
            norm_block(C, xt, NB, lambda k, cidx=cidx: C.A1[:, cidx, k:k + 1],
                       lambda k, cidx=cidx: C.modT[:, k, cidx:cidx + 1],
                       lambda k, H=H, t0=t0: H[:, k, 1 + t0:1 + t0 + NB], [C.A1, C.modT], H, (sq, rs, tmp))


def _rope_tables():
    n_pair = 16
    rows = T_LAT // 64
    row = np.repeat(np.arange(rows, dtype=np.float32), 64)
    col = np.tile(np.arange(64, dtype=np.float32), rows)
    inv = (np.float32(10000.0) ** (-np.arange(n_pair, dtype=np.float32) / np.float32(n_pair))).astype(np.float32)
    ang = np.concatenate([row[:, None] * inv, col[:, None] * inv], axis=-1).astype(np.float32)
    cos = np.cos(ang).astype(np.float32).T
    sin = np.sin(ang).astype(np.float32).T
    Cc = np.concatenate([cos, cos, cos, cos], axis=0)
    Ss = np.concatenate([-sin, sin, -sin, sin], axis=0)
    return np.ascontiguousarray(Cc), np.ascontiguousarray(Ss)


def prep_inputs(inp):
    f = lambda a: np.ascontiguousarray(np.asarray(a, dtype=np.float32))
    x, c, ctx, c_ctx = f(inp["x"]), f(inp["c"]), f(inp["ctx"]), f(inp["c_ctx"])
    pk = lambda v, n: f(v.reshape(n, 128).T)
    sh = {}
    sh["w_mod"] = f(inp["w_mod"][0])
    sh["b_modT"] = pk(f(inp["b_mod"][0]), 48)
    sh["gvec"] = f(np.stack([pk(f(inp[n][0]), 8) for n in ("g_pre_mix", "g_post_mix", "g_pre_ffn", "g_post_ffn")], axis=1))
    w_in = f(inp["w_in"][0])
    sh["w_in"] = w_in
    perm = np.arange(512).reshape(8, 2, 32)[:, ::-1, :].reshape(512)
    qc, kc = 1696, 1696 + 512
    sh["w_qks"] = f(np.concatenate([w_in[:, qc:qc + 512][:, perm], w_in[:, kc:kc + 512][:, perm]], axis=1))
    sh["convT"] = f(inp["rwkv_conv"][0].T)
    sh["lw2"] = f(np.stack([np.concatenate([f(inp["w2_fwd"][0]), f(inp["a2_fwd"][0])], 0),
                            np.concatenate([f(inp["w2_bwd"][0]), f(inp["a2_bwd"][0])], 0)], axis=1))
    sh["w0a0"] = f(np.stack([pk(f(inp[n][0]), 4) for n in ("w0_fwd", "w0_bwd", "a0_fwd", "a0_bwd")], axis=1))
    sh["kvec"] = f(np.stack([pk(f(inp[n][0]).reshape(512), 4) for n in ("k_k", "k_a", "r_k", "ln_x_w", "ln_x_b")], axis=1))
    sh["g2"] = f(inp["g2"][0])
    sh["lam"] = f(np.broadcast_to(np.stack([f(inp[n][0]) for n in ("lam_q1", "lam_k1", "lam_q2", "lam_k2")], 0)[None], (128, 4, 64)))
    sh["subln"] = f(inp["subln_w"][0].reshape(128, 1))
    sh["w_out"] = f(inp["w_out"][0])
    sh["w_up"] = f(inp["w_up"][0])
    sh["fconvT"] = f(inp["ffn_conv"][0].T)
    sh["fbias"] = pk(f(inp["ffn_conv_b"][0]), 44)
    sh["w_down"] = f(inp["w_down"][0])
    sh["ropeC"], sh["ropeS"] = _rope_tables()
    ii = np.arange(128)
    dm = lambda sz: (ii[:, None] // sz == ii[None, :] // sz).astype(np.float32)
    sh["blkmask"] = f(np.stack([dm(8), dm(16) - dm(8), dm(32) - dm(16), dm(64) - dm(32), dm(128) - dm(64)], axis=1))
    maps = []
    for b in range(8):
        m = dict(sh)
        m["xT"] = f(x[b].T)
        m["ctxT"] = f(ctx[b].T)
        m["cc"] = f(np.stack([c[b], c_ctx], -1).reshape(8, 128, 2).transpose(1, 0, 2))
        maps.append(m)
    return maps


def kernel(**inputs):
    maps = prep_inputs(inputs)
    nc = bass.Bass("TRN2", target_bir_lowering=False)
    build_program(nc)
    res = run_bass_kernel_spmd(nc, maps, core_ids=list(range(8)))
    out = np.stack([np.ascontiguousarray(res.results[b]["outT"].T) for b in range(8)], 0)
    return out.astype(np.float32)


def load_w_bf16(C, dst, dram_cols_ap, ncols, stg):
    P = C.P
    for i, c0 in enumerate(range(0, ncols, 256)):
        c1 = min(ncols, c0 + 256)
        s = stg[C.stg_i % 2]
        C.stg_i += 1
        P.dma("sp", s[:, :, 0:c1 - c0], dram_cols_ap[:, c0:c1].rearrange("(k p) n -> p k n", p=128), writes=[s])
        ce = ("pool", "act", "dve")[C.stg_i % 3]
        if ce == "act":
            P.op("act", lambda e, s=s, c0=c0, c1=c1: e.activation(out=dst[0][:, :, dst[1] + c0:dst[1] + c1], in_=s[:, :, 0:c1 - c0], func=AF.Copy),
                 reads=[s], writes=[dst[0]])
        else:
            P.op(ce, lambda e, s=s, c0=c0, c1=c1: e.tensor_copy(out=dst[0][:, :, dst[1] + c0:dst[1] + c1], in_=s[:, :, 0:c1 - c0]),
                 reads=[s], writes=[dst[0]])


def proj_conv(C, H, t0, W, c0, m, cw, out, out_buf, bias=None, eng2="dve", bias_buf=None):
    P = C.P
    pb = C.nb()
    for k in range(8):
        P.op("pe", lambda e, k=k: e.matmul(pb[0:m, 0:NB + 2], lhsT=W[:, k, c0:c0 + m], rhs=H[:, k, t0:t0 + NB + 2],
                                            start=(k == 0), stop=(k == 7)), reads=[W, H], writes=[pb])
    if bias is None:
        P.op("act", lambda e: e.activation(out=out, in_=pb[0:m, 1:NB + 1], func=AF.Copy, scale=cw[:, 1:2]),
             reads=[C.cwb], writes=[pb, out_buf])
    else:
        P.op("act", lambda e: e.activation(out=out, in_=pb[0:m, 1:NB + 1], func=AF.Identity, scale=cw[:, 1:2], bias=bias),
             reads=[C.cwb, bias_buf], writes=[pb, out_buf])
    P.op(eng2, lambda e: e.scalar_tensor_tensor(out=out, in0=pb[0:m, 0:NB], scalar=cw[:, 0:1], in1=out,
                                                op0=ALU.mult, op1=ALU.add), reads=[C.cwb], writes=[pb, out_buf])
    P.op(eng2, lambda e: e.scalar_tensor_tensor(out=out, in0=pb[0:m, 2:NB + 2], scalar=cw[:, 2:3], in1=out,
                                                op0=ALU.mult, op1=ALU.add), reads=[C.cwb], writes=[pb, out_buf])


def stage_rwkv(C):
    P, dr = C.P, C.dr
    C.stg_i = 0
    with P.scope():
        WR = P.sbuf("WR", [128, 8, 1696], BF16)
        with P.scope():
            stg = [P.sbuf("wstg%d" % i, [128, 8, 256], F32) for i in range(2)]
            load_w_bf16(C, (WR, 0), dr["w_in"][:, 0:1696], 1696, stg)
        CW = P.sbuf("CW", [128, 14, 3], F32)
        C.cwb = CW
        P.dma("sp", CW[:, 0:12, :], dr["convT"][0:1536, :].rearrange("(c p) j -> p c j", p=128), writes=[CW])
        P.dma("sp", CW[0:64, 12, :], dr["convT"][1536:1600, :], writes=[CW])
        P.dma("sp", CW[0:96, 13, :], dr["convT"][1600:1696, :], writes=[CW])
        LW2 = P.sbuf("LW2", [64, 2, 512], BF16)
        G2W = P.sbuf("G2W", [96, 512], BF16)
        with P.scope():
            lw2f = P.sbuf("lw2f", [64, 2, 512], F32)
            P.dma("sp", lw2f.ap(), dr["lw2"], writes=[lw2f])
            P.op("pool", lambda e: e.tensor_copy(out=LW2.ap(), in_=lw2f.ap()), reads=[lw2f], writes=[LW2])
            g2f = P.sbuf("g2f", [96, 512], F32)
            P.dma("sp", g2f.ap(), dr["g2"], writes=[g2f])
            P.op("pool", lambda e: e.tensor_copy(out=G2W.ap(), in_=g2f.ap()), reads=[g2f], writes=[G2W])
        w0a0 = P.sbuf("w0a0", [128, 4, 4], F32)
        kvec = P.sbuf("kvec", [128, 5, 4], F32)
        P.dma("sp", w0a0.ap(), dr["w0a0"], writes=[w0a0])
        P.dma("sp", kvec.ap(), dr["kvec"], writes=[kvec])
        omka = P.sbuf("omka", [128, 4], F32)
        rkh = P.sbuf("rkh", [128, 4], F32)
        P.op("dve", lambda e: e.tensor_scalar(out=omka.ap(), in0=kvec[:, 1, :], scalar1=-1.0, scalar2=1.0,
                                              op0=ALU.mult, op1=ALU.add), reads=[kvec], writes=[omka])
        P.op("dve", lambda e: e.tensor_scalar(out=rkh.ap(), in0=kvec[:, 2, :], scalar1=0.5, scalar2=None,
                                              op0=ALU.mult), reads=[kvec], writes=[rkh])
        AMM = [P.sbuf("amm%d" % d, [128, 4, 128], F32) for d in range(2)]
        with P.scope():
            ones32 = P.sbuf("ones32", [128, 128], F32)
            P.op("pool", lambda e: e.memset(ones32.ap(), 1.0), writes=[ones32])
            msk = {}
            for nm, cop, sgn in (("SU", ALU.is_gt, -1), ("IU", ALU.is_ge, -1), ("SL", ALU.is_gt, 1), ("IL", ALU.is_ge, 1)):
                mb = P.sbuf("m" + nm, [128, 128], F32)
                P.op("pool", lambda e, mb=mb, cop=cop, sgn=sgn: e.affine_select(out=mb.ap(), in_=ones32.ap(), pattern=[[-sgn, 128]],
                                                                                compare_op=cop, fill=0.0, base=0, channel_multiplier=sgn),
                     reads=[ones32], writes=[mb])
                msk[nm] = mb
            for d, (s_, i_) in enumerate((("SU", "IU"), ("SL", "IL"))):
                am = AMM[d]
                for q, nm in enumerate((s_, i_, s_, i_)):
                    P.op("pool", lambda e, am=am, q=q, nm=nm: e.tensor_copy(out=am[:, q, :], in_=msk[nm].ap()),
                         reads=[msk[nm]], writes=[am])
        NTMap = [AMM[1][:, 0, :], AMM[0][:, 0, :]]
        NTMb = [AMM[1], AMM[0]]
        rmask = P.sbuf("rmask", [128, NB], F32)
        P.op("pool", lambda e: e.memset(rmask.ap(), 1.0), writes=[rmask])
        for c in range(NB // 128):
            P.op("pool", lambda e, c=c: e.memset(rmask[:, c * 128:c * 128 + 1], 0.0), writes=[rmask])

        def f32t(n, shape=(128, NB)):
            return P.sbuf(n, list(shape), F32)

        def b16t(n, shape=(128, NB)):
            return P.sbuf(n, list(shape), BF16)

        la32, LA16 = f32t("la32", (64, NB)), b16t("LA16", (64, NB))
        gl32, sg16 = f32t("gl32", (96, NB)), b16t("sg16", (96, NB))
        BM2 = P.sbuf("blkmask2", [128, 5, 2, 128], BF16)
        with P.scope():
            bmf = P.sbuf("blkmaskf", [128, 5, 128], F32)
            P.dma("sp", bmf.ap(), dr["blkmask"], writes=[bmf])
            for q_ in range(2):
                P.op("pool", lambda e, q_=q_: e.tensor_copy(out=BM2[:, :, q_, :], in_=bmf.ap()), reads=[bmf], writes=[BM2])
        I2 = P.sbuf("ident2", [128, 2, 128], BF16)
        for q_ in range(2):
            P.op("pool", lambda e, q_=q_: e.tensor_copy(out=I2[:, q_, :], in_=C.ident.ap()), reads=[C.ident], writes=[I2])
        S32 = [[f32t("S32_%d_%d" % (d, p), (128, 64)) for p in range(4)] for d in range(2)]
        Sb = [[[b16t("Sb_%d_%d_%d" % (d, p, i), (128, 64)) for i in range(2)] for p in range(4)] for d in range(2)]
        sbi = [[0] * 4 for _ in range(2)]
        def make_set(si):
            sfx = "q%d_" % si
            o_ = Ctx()
            r32, k32, v32, v16 = f32t(sfx + "r32"), f32t(sfx + "k32"), f32t(sfx + "v32"), b16t(sfx + "v16")
            sig, aa, kraw, ksq, rn = f32t(sfx + "sig"), f32t(sfx + "aa"), f32t(sfx + "kraw"), b16t(sfx + "ksq"), f32t(sfx + "rn")
            fac, kdir, bb, cs, LL, Lm = f32t(sfx + "fac"), f32t(sfx + "kdir"), f32t(sfx + "bb"), f32t(sfx + "cs"), f32t(sfx + "LL"), f32t(sfx + "Lm")
            gg, ginv, gprev = f32t(sfx + "gg"), f32t(sfx + "ginv"), f32t(sfx + "gprev")
            ld, kk = sig, kraw
            yc, rstd, yn, rk, bonus, a2_, kdir2 = LL, rn, Lm, ginv, gprev, bb, kdir
            Bt, Kt = b16t(sfx + "Bt"), b16t(sfx + "Kt")
            KR = P.sbuf(sfx + "KR", [128, NB // 128, 2, 128], BF16)
            Bg, Kg = b16t(sfx + "Bg", (128, 128)), b16t(sfx + "Kg", (128, 128))
            TT = [P.sbuf(sfx + "TT%d" % ci, [128, 3, 128], BF16) for ci in range(2)]
            AMh = [[P.sbuf(sfx + "AM%d_%d" % (ci, h), [128, 4, 128], BF16) for h in range(2)] for ci in range(2)]
            NN = [[[P.sbuf(sfx + "NN%d_%d_%d" % (ci, h, i), [128, 2, 128], BF16) for i in range(1)] for h in range(2)] for ci in range(2)]
            PP = None
            Tb = [[P.sbuf(sfx + "Tb%d_%d" % (ci, h), [128, 128], BF16) for h in range(2)] for ci in range(2)]
            IV = [[dict((nm, P.sbuf(sfx + "iv%s_%d_%d" % (nm, ci, h), [128, 2, 128], BF16)) for nm in ("Nb2", "S2", "S4", "X2", "Pa", "Pb"))
                   for h in range(2)] for ci in range(2)]
            NZ, UT = b16t(sfx + "NZ", (128, 128)), b16t(sfx + "UT", (128, 128))
            ys32, yfl = f32t(sfx + "ys32"), f32t(sfx + "yfl")
            ys16, yc2 = b16t(sfx + "ys16"), b16t(sfx + "yc2")
            rk16, o16 = b16t(sfx + "rk16"), b16t(sfx + "o16")
            for _n in ['r32', 'k32', 'v32', 'v16', 'sig', 'ld', 'aa', 'a2_', 'kraw', 'ksq', 'rn', 'kk', 'fac', 'kdir', 'kdir2', 'bb', 'cs', 'LL', 'Lm', 'gg', 'ginv', 'gprev', 'Bt', 'Kt', 'KR', 'Bg', 'Kg', 'TT', 'AMh', 'NN', 'PP', 'Tb', 'IV', 'NZ', 'UT', 'ys32', 'yfl', 'ys16', 'yc', 'yc2', 'rstd', 'yn', 'rk', 'rk16', 'bonus', 'o16']:
                setattr(o_, _n, locals()[_n])
            return o_

        TS = [make_set(0), make_set(1)]
        for d in range(2):
            for p in range(4):
                P.op("pool", lambda e, d=d, p=p: e.memset(S32[d][p].ap(), 0.0), writes=[S32[d][p]])
                P.op("pool", lambda e, d=d, p=p: e.memset(Sb[d][p][0].ap(), 0.0), writes=[Sb[d][p][0]])

        NCH = NB // 128
        import os
        lim = int(os.environ.get("RWKV_LIM", "999"))
        cnt = 0
        for d in range(2):
            blocks = [("c", 0)] + ([("l", i) for i in range(NLB)] if d == 0 else [("l", i) for i in reversed(range(NLB))])
            for (src, bi) in blocks:
                cnt += 1
                if cnt > lim:
                    continue
                if os.environ.get("RWKV_SKIPF") and d == 0 and src == "l":
                    continue
                H = C.HC if src == "c" else C.HT
                t0 = 0 if src == "c" else bi * NB
                lat = (src == "l")
                proj_conv(C, H, t0, WR, 1536, 64, CW[0:64, 12, :], la32.ap(), la32)
                P.op("act", lambda e: e.activation(out=LA16[0:32, :], in_=la32[0:32, :], func=AF.Tanh), reads=[la32], writes=[LA16])
                P.op("pool", lambda e: e.tensor_copy(out=LA16[32:64, :], in_=la32[32:64, :]), reads=[la32], writes=[LA16])
                if d == 1 and lat:
                    proj_conv(C, H, t0, WR, 1600, 96, CW[0:96, 13, :], gl32.ap(), gl32)
                    P.op("act", lambda e: e.activation(out=sg16.ap(), in_=gl32.ap(), func=AF.Sigmoid), reads=[gl32], writes=[sg16])
                def pair_gen(p, ts_):
                    proj_conv(C, H, t0, WR, p * 128, 128, CW[:, p, :], ts_.r32.ap(), ts_.r32)
                    yield
                    proj_conv(C, H, t0, WR, 512 + p * 128, 128, CW[:, 4 + p, :], ts_.k32.ap(), ts_.k32)
                    yield
                    proj_conv(C, H, t0, WR, 1024 + p * 128, 128, CW[:, 8 + p, :], ts_.v32.ap(), ts_.v32)
                    yield
                    P.op("act", lambda e: e.activation(out=ts_.v16.ap(), in_=ts_.v32.ap(), func=AF.Copy), reads=[ts_.v32], writes=[ts_.v16])
                    pz = C.nb()
                    P.op("pe", lambda e, p=p, d=d: e.matmul(pz[:, 0:NB], lhsT=LW2[0:32, d, p * 128:(p + 1) * 128], rhs=LA16[0:32, :],
                                                             start=True, stop=True), reads=[LW2, LA16], writes=[pz])
                    P.op("act", lambda e, p=p, d=d: e.activation(out=ts_.sig.ap(), in_=pz[:, 0:NB], func=AF.Sigmoid, bias=w0a0[:, d, p:p + 1]),
                         reads=[w0a0], writes=[pz, ts_.sig])
                    P.op("pool", lambda e: e.tensor_scalar(out=ts_.ld.ap(), in0=ts_.sig.ap(), scalar1=-EXPM05, scalar2=None, op0=ALU.mult),
                         reads=[ts_.sig], writes=[ts_.ld])
                    pz = C.nb()
                    P.op("pe", lambda e, p=p, d=d, pz=pz: e.matmul(pz[:, 0:NB], lhsT=LW2[32:64, d, p * 128:(p + 1) * 128], rhs=LA16[32:64, :],
                                                                    start=True, stop=True), reads=[LW2, LA16], writes=[pz])
                    P.op("act", lambda e, p=p, d=d, pz=pz: e.activation(out=ts_.aa.ap(), in_=pz[:, 0:NB], func=AF.Sigmoid, bias=w0a0[:, 2 + d, p:p + 1]),
                         reads=[w0a0], writes=[pz, ts_.aa])
                    yield
                    P.op("act", lambda e, p=p: e.activation(out=ts_.kraw.ap(), in_=ts_.k32.ap(), func=AF.Copy, scale=kvec[:, 0, p:p + 1]),
                         reads=[ts_.k32, kvec], writes=[ts_.kraw])
                    P.op("act", lambda e: e.activation(out=ts_.ksq.ap(), in_=ts_.kraw.ap(), func=AF.Square), reads=[ts_.kraw], writes=[ts_.ksq])
                    pz = C.nb()
                    P.op("pe", lambda e, pz=pz: e.matmul(pz[:, 0:NB], lhsT=C.bones.ap(), rhs=ts_.ksq.ap(), start=True, stop=True),
                         reads=[C.bones, ts_.ksq], writes=[pz])
                    P.op("act", lambda e, pz=pz: e.activation(out=ts_.rn.ap(), in_=pz[:, 0:NB], func=AF.Sqrt), writes=[pz, ts_.rn])
                    P.op("dve", lambda e: e.tensor_scalar(out=ts_.rn.ap(), in0=ts_.rn.ap(), scalar1=1e-12, scalar2=None, op0=ALU.max), writes=[ts_.rn])
                    P.op("dve", lambda e: e.reciprocal(out=ts_.rn.ap(), in_=ts_.rn.ap()), writes=[ts_.rn])
                    P.op("dve", lambda e: e.tensor_tensor(out=ts_.kk.ap(), in0=ts_.kraw.ap(), in1=ts_.rn.ap(), op=ALU.mult), reads=[ts_.kraw, ts_.rn], writes=[ts_.kk])
                    yield
                    P.op("dve", lambda e, p=p: e.tensor_scalar(out=ts_.fac.ap(), in0=ts_.aa.ap(), scalar1=kvec[:, 1, p:p + 1], scalar2=omka[:, p:p + 1],
                                                               op0=ALU.mult, op1=ALU.add), reads=[ts_.aa, kvec, omka], writes=[ts_.fac])
                    P.op("pool", lambda e: e.tensor_tensor(out=ts_.kdir.ap(), in0=ts_.k32.ap(), in1=ts_.fac.ap(), op=ALU.mult), reads=[ts_.k32, ts_.fac], writes=[ts_.kdir])
                    P.op("pool", lambda e: e.tensor_tensor(out=ts_.bb.ap(), in0=ts_.kk.ap(), in1=ts_.aa.ap(), op=ALU.mult), reads=[ts_.kk, ts_.aa], writes=[ts_.bb])
                    P.op("dve", lambda e: e.tensor_tensor_scan(out=ts_.cs.ap(), data0=rmask.ap(), data1=ts_.ld.ap(), initial=0.0,
                                                               op0=ALU.mult, op1=ALU.add), reads=[rmask, ts_.ld], writes=[ts_.cs])
                    if d == 0:
                        Lb = ts_.cs
                    else:
                        for c in range(NCH):
                            P.op("dve", lambda e, c=c: e.tensor_scalar(out=ts_.LL[:, c * 128:(c + 1) * 128], in0=ts_.cs[:, c * 128:(c + 1) * 128],
                                                                       scalar1=-1.0, scalar2=ts_.cs[:, c * 128 + 127:c * 128 + 128],
                                                                       op0=ALU.mult, op1=ALU.add), reads=[ts_.cs], writes=[ts_.LL])
                        P.op("dve", lambda e: e.tensor_tensor(out=ts_.LL.ap(), in0=ts_.LL.ap(), in1=ts_.ld.ap(), op=ALU.add), reads=[ts_.ld], writes=[ts_.LL])
                        Lb = ts_.LL
                    P.op("pool", lambda e, Lb=Lb: e.tensor_tensor(out=ts_.Lm.ap(), in0=Lb.ap(), in1=ts_.ld.ap(), op=ALU.subtract), reads=[Lb, ts_.ld], writes=[ts_.Lm])
                    P.op("act", lambda e, Lb=Lb: e.activation(out=ts_.gg.ap(), in_=Lb.ap(), func=AF.Exp), reads=[Lb], writes=[ts_.gg])
                    P.op("act", lambda e, Lb=Lb: e.activation(out=ts_.ginv.ap(), in_=Lb.ap(), func=AF.Exp, scale=-1.0), reads=[Lb], writes=[ts_.ginv])
                    P.op("act", lambda e: e.activation(out=ts_.gprev.ap(), in_=ts_.Lm.ap(), func=AF.Exp), reads=[ts_.Lm], writes=[ts_.gprev])
                    yield
                    P.op("dve", lambda e: e.tensor_tensor(out=ts_.Bt.ap(), in0=ts_.bb.ap(), in1=ts_.ginv.ap(), op=ALU.mult), reads=[ts_.bb, ts_.ginv], writes=[ts_.Bt])
                    P.op("dve", lambda e: e.tensor_tensor(out=ts_.Kt.ap(), in0=ts_.kdir.ap(), in1=ts_.ginv.ap(), op=ALU.mult), reads=[ts_.kdir, ts_.ginv], writes=[ts_.Kt])
                    P.op("pool", lambda e: e.tensor_tensor(out=ts_.KR[:, :, 0, :], in0=ts_.kk.ap().rearrange("p (c t) -> p c t", t=128),
                                                          in1=ts_.gprev.ap().rearrange("p (c t) -> p c t", t=128), op=ALU.mult),
                         reads=[ts_.kk, ts_.gprev], writes=[ts_.KR])
                    P.op("pool", lambda e: e.tensor_tensor(out=ts_.KR[:, :, 1, :], in0=ts_.r32.ap().rearrange("p (c t) -> p c t", t=128),
                                                          in1=ts_.gg.ap().rearrange("p (c t) -> p c t", t=128), op=ALU.mult),
                         reads=[ts_.r32, ts_.gg], writes=[ts_.KR])
                    yield
                    if d == 1 and lat:
                        if (p, bi) in C.yf_bufs:
                            P.dma("sp", ts_.yfl.ap(), dr["yf"][p, :, t0:t0 + NB], reads=[C.yf_bufs[(p, bi)]], writes=[ts_.yfl])
                        else:
                            P.op("pool", lambda e: e.memset(ts_.yfl.ap(), 0.0), writes=[ts_.yfl])
                    chunks = list(range(NCH)) if d == 0 else list(reversed(range(NCH)))
                    for ci, c in enumerate(chunks):
                        csl = slice(c * 128, (c + 1) * 128)
                        gcol = c * 128 + 127 if d == 0 else c * 128
                        P.op("act", lambda e, csl=csl, gcol=gcol: e.activation(out=ts_.Bg.ap(), in_=ts_.Bt[:, csl], func=AF.Copy, scale=ts_.gg[:, gcol:gcol + 1]),
                             reads=[ts_.Bt, ts_.gg], writes=[ts_.Bg])
                        P.op("act", lambda e, csl=csl, gcol=gcol: e.activation(out=ts_.Kg.ap(), in_=ts_.Kt[:, csl], func=AF.Copy, scale=ts_.gg[:, gcol:gcol + 1]),
                             reads=[ts_.Kt, ts_.gg], writes=[ts_.Kg])
                        pT = C.bankT
                        P.op("pe", lambda e, csl=csl: e.transpose(pT[:, 0:128], ts_.v16[:, csl], C.ident.ap()), reads=[ts_.v16, C.ident], writes=[pT])
                        P.op("pe", lambda e: e.transpose(pT[:, 128:256], ts_.Bg.ap(), C.ident.ap()), reads=[ts_.Bg, C.ident], writes=[pT])
                        P.op("pe", lambda e: e.transpose(pT[:, 256:384], ts_.Kg.ap(), C.ident.ap()), reads=[ts_.Kg, C.ident], writes=[pT])
                        P.op("act", lambda e: e.activation(out=ts_.TT[ci].ap(), in_=pT[:, 0:384].rearrange("p (a b) -> p a b", b=128), func=AF.Copy),
                             writes=[pT, ts_.TT[ci]])
                        yield
                        for h in range(2):
                            hs = slice(h * 64, (h + 1) * 64)
                            AM = ts_.AMh[ci][h]
                            pg = C.nb()
                            P.op("pe", lambda e, hs=hs, csl=csl, c=c, pg=pg: e.matmul(pg[:, 0:256], lhsT=ts_.Bt[hs, csl], rhs=ts_.KR[hs, c, :, :],
                                                                                      start=True, stop=True), reads=[ts_.Bt, ts_.KR], writes=[pg])
                            P.op("pe", lambda e, hs=hs, csl=csl, c=c, pg=pg: e.matmul(pg[:, 256:512], lhsT=ts_.Kt[hs, csl], rhs=ts_.KR[hs, c, :, :],
                                                                                      start=True, stop=True), reads=[ts_.Kt, ts_.KR], writes=[pg])
                            N0 = ts_.NN[ci][h][0]
                            P.op("dve", lambda e, N0=N0, pg=pg, d=d: e.tensor_tensor(out=N0[:, 0, :], in0=pg[:, 0:128], in1=AMM[d][:, 0, :], op=ALU.mult),
                                 reads=[AMM[d]], writes=[pg, N0])
                            P.op("dve", lambda e, AM=AM, pg=pg, d=d: e.tensor_tensor(out=AM.ap(), in0=pg.ap().rearrange("p (a b) -> p a b", b=128),
                                                                                    in1=AMM[d].ap(), op=ALU.mult), reads=[AMM[d]], writes=[pg, AM])
                            pn = C.nb()
                            P.op("pe", lambda e, hs=hs, csl=csl, c=c, pn=pn: e.matmul(pn[:, 0:128], lhsT=ts_.KR[hs, c, 0, :], rhs=ts_.Bt[hs, csl],
                                                                                      start=True, stop=True), reads=[ts_.Bt, ts_.KR], writes=[pn])
                            N0 = ts_.NN[ci][h][0]
                            P.op("dve", lambda e, N0=N0, pn=pn, d=d: e.tensor_tensor(out=N0[:, 1, :], in0=pn[:, 0:128], in1=NTMap[d], op=ALU.mult),
                                 reads=[NTMb[d]], writes=[pn, N0])
                        def inv_gen(ci, h):
                            N0 = ts_.NN[ci][h][0]
                            iv = ts_.IV[ci][h]
                            mNb, S2, S4, mX = iv["Nb2"], iv["S2"], iv["S4"], iv["X2"]
                            Pc = [iv["Pa"], iv["Pb"]]
                            idb = C.ident
                            P.op("dve", lambda e: e.scalar_tensor_tensor(out=mNb.ap(), in0=N0.ap(), scalar=-1.0, in1=BM2[:, 0, :, :],
                                                                         op0=ALU.mult, op1=ALU.mult), reads=[N0, BM2], writes=[mNb])
                            pq = C.nb()
                            P.op("pe", lambda e: e.matmul(pq[:, 0:128], lhsT=mNb[:, 1, :], rhs=mNb[:, 0, :], start=True, stop=True), reads=[mNb], writes=[pq])
                            P.op("pe", lambda e: e.matmul(pq[:, 128:256], lhsT=mNb[:, 0, :], rhs=mNb[:, 1, :], start=True, stop=True), reads=[mNb], writes=[pq])
                            P.op("act", lambda e: e.activation(out=S2.ap(), in_=pq[:, 0:256].rearrange("p (a b) -> p a b", b=128), func=AF.Copy), writes=[pq, S2])
                            P.op("pool", lambda e: e.tensor_tensor(out=Pc[0].ap(), in0=I2.ap(), in1=mNb.ap(), op=ALU.add),
                                 reads=[mNb, I2], writes=[Pc[0]])
                            yield
                            pq2 = C.nb()
                            P.op("pe", lambda e: e.matmul(pq2[:, 0:128], lhsT=S2[:, 1, :], rhs=S2[:, 0, :], start=True, stop=True), reads=[S2], writes=[pq2])
                            P.op("pe", lambda e: e.matmul(pq2[:, 128:256], lhsT=S2[:, 0, :], rhs=S2[:, 1, :], start=True, stop=True), reads=[S2], writes=[pq2])
                            P.op("dve", lambda e: e.tensor_copy(out=S4.ap(), in_=pq2[:, 0:256].rearrange("p (a b) -> p a b", b=128)), writes=[pq2, S4])

                            def pstep(Sx, Pin, Pout):
                                pp_ = C.nb()
                                for q_ in range(2):
                                    P.op("pe", lambda e, q_=q_: e.matmul(pp_[:, q_ * 128:(q_ + 1) * 128], lhsT=Sx[:, 1 - q_, :], rhs=Pin[:, q_, :], start=True, stop=False),
                                         reads=[Sx, Pin], writes=[pp_])
                                    P.op("pe", lambda e, q_=q_: e.matmul(pp_[:, q_ * 128:(q_ + 1) * 128], lhsT=idb.ap(), rhs=Pin[:, q_, :], start=False, stop=True),
                                         reads=[idb, Pin], writes=[pp_])
                                P.op("act", lambda e: e.activation(out=Pout.ap(), in_=pp_[:, 0:256].rearrange("p (a b) -> p a b", b=128), func=AF.Copy),
                                     writes=[pp_, Pout])

                            pstep(S2, Pc[0], Pc[1])
                            yield
                            pstep(S4, Pc[1], Pc[0])
                            cur = 0
                            yield
                            for l in range(4):
                                Tc, Tn = Pc[cur], Pc[1 - cur]
                                last = (l == 3)
                                nq = 1 if last else 2
                                px = C.nb()
                                for q_ in range(nq):
                                    P.op("pe", lambda e, q_=q_: e.matmul(px[:, q_ * 128:(q_ + 1) * 128], lhsT=N0[:, 1 - q_, :], rhs=Tc[:, q_, :], start=True, stop=True),
                                         reads=[N0, Tc], writes=[px])
                                P.op("dve", lambda e: e.scalar_tensor_tensor(out=mX[:, 0:nq, :], in0=px[:, 0:nq * 128].rearrange("p (a b) -> p a b", b=128),
                                                                             scalar=-1.0, in1=BM2[:, 1 + l, 0:nq, :], op0=ALU.mult, op1=ALU.mult),
                                     reads=[BM2], writes=[px, mX])
                                yield
                                pr = C.nb()
                                for q_ in range(nq):
                                    P.op("pe", lambda e, q_=q_: e.matmul(pr[:, q_ * 128:(q_ + 1) * 128], lhsT=Tc[:, 1 - q_, :], rhs=mX[:, q_, :], start=True, stop=False),
                                         reads=[Tc, mX], writes=[pr])
                                    P.op("pe", lambda e, q_=q_: e.matmul(pr[:, q_ * 128:(q_ + 1) * 128], lhsT=idb.ap(), rhs=Tc[:, q_, :], start=False, stop=True),
                                         reads=[idb, Tc], writes=[pr])
                                if not last:
                                    P.op("act", lambda e: e.activation(out=Tn.ap(), in_=pr[:, 0:256].rearrange("p (a b) -> p a b", b=128), func=AF.Copy),
                                         writes=[pr, Tn])
                                else:
                                    P.op("act", lambda e: e.activation(out=ts_.Tb[ci][h].ap(), in_=pr[:, 0:128], func=AF.Copy), writes=[pr, ts_.Tb[ci][h]])
                                cur = 1 - cur
                                yield

                    for _ in zip(*[inv_gen(ci_, h_) for ci_ in range(len(chunks)) for h_ in range(2)]):
                        yield
                    for ci, c in enumerate(chunks):
                        csl = slice(c * 128, (c + 1) * 128)
                        gcol = c * 128 + 127 if d == 0 else c * 128
                        Th = ts_.Tb[ci]
                        So = Sb[d][p][sbi[d][p] % 2]
                        Sn = Sb[d][p][(sbi[d][p] + 1) % 2]
                        sbi[d][p] += 1
                        pz = C.nb()
                        for h in range(2):
                            hs = slice(h * 64, (h + 1) * 64)
                            P.op("pe", lambda e, hs=hs, c=c, pz=pz, So=So: e.matmul(pz[:, hs], lhsT=ts_.KR[hs, c, 0, :], rhs=So[hs, :], start=True, stop=False),
                                 reads=[ts_.KR, So], writes=[pz])
                            P.op("pe", lambda e, hs=hs, h=h, pz=pz: e.matmul(pz[:, hs], lhsT=ts_.AMh[ci][h][:, 2, :], rhs=ts_.TT[ci][:, 0, hs], start=False, stop=True),
                                 reads=[ts_.AMh[ci][h], ts_.TT[ci]], writes=[pz])
                        P.op("act", lambda e, pz=pz: e.activation(out=ts_.NZ.ap(), in_=pz[:, 0:128], func=AF.Copy, scale=-1.0), writes=[pz, ts_.NZ])
                        yield
                        pu = C.nb()
                        for h in range(2):
                            hs = slice(h * 64, (h + 1) * 64)
                            P.op("pe", lambda e, hs=hs, h=h, pu=pu: e.matmul(pu[:, hs], lhsT=Th[h].ap(), rhs=ts_.NZ[:, hs], start=True, stop=True),
                                 reads=[Th[h], ts_.NZ], writes=[pu])
                        P.op("act", lambda e, pu=pu: e.activation(out=ts_.UT.ap(), in_=pu[:, 0:128], func=AF.Copy), writes=[pu, ts_.UT])
                        yield
                        if lat:
                            py = C.nb()
                            for h in range(2):
                                hs = slice(h * 64, (h + 1) * 64)
                                P.op("pe", lambda e, hs=hs, c=c, py=py, So=So: e.matmul(py[hs, 0:128], lhsT=So[hs, :], rhs=ts_.KR[hs, c, 1, :], start=True, stop=False),
                                     reads=[So, ts_.KR], writes=[py])
                                P.op("pe", lambda e, hs=hs, h=h, py=py: e.matmul(py[hs, 0:128], lhsT=ts_.UT[:, hs], rhs=ts_.AMh[ci][h][:, 1, :], start=False, stop=False),
                                     reads=[ts_.UT, ts_.AMh[ci][h]], writes=[py])
                                P.op("pe", lambda e, hs=hs, h=h, py=py: e.matmul(py[hs, 0:128], lhsT=ts_.TT[ci][:, 0, hs], rhs=ts_.AMh[ci][h][:, 3, :], start=False, stop=True),
                                     reads=[ts_.TT[ci], ts_.AMh[ci][h]], writes=[py])
                            if d == 0:
                                P.op("act", lambda e, py=py, csl=csl: e.activation(out=ts_.ys32[:, csl], in_=py[:, 0:128], func=AF.Copy), writes=[py, ts_.ys32])
                            else:
                                P.op("dve", lambda e, py=py, csl=csl: e.tensor_tensor(out=ts_.ys32[:, csl], in0=py[:, 0:128], in1=ts_.yfl[:, csl], op=ALU.add),
                                     reads=[ts_.yfl], writes=[py, ts_.ys32])
                        pS = C.nb()
                        for h in range(2):
                            hs = slice(h * 64, (h + 1) * 64)
                            P.op("pe", lambda e, hs=hs, pS=pS: e.matmul(pS[hs, 0:64], lhsT=ts_.TT[ci][:, 1, hs], rhs=ts_.UT[:, hs], start=True, stop=False),
                                 reads=[ts_.TT[ci], ts_.UT], writes=[pS])
                            P.op("pe", lambda e, hs=hs, pS=pS: e.matmul(pS[hs, 0:64], lhsT=ts_.TT[ci][:, 2, hs], rhs=ts_.TT[ci][:, 0, hs], start=False, stop=True),
                                 reads=[ts_.TT[ci]], writes=[pS])
                        S3 = S32[d][p]
                        P.op("dve", lambda e, pS=pS, S3=S3, gcol=gcol: e.scalar_tensor_tensor(out=S3.ap(), in0=S3.ap(), scalar=ts_.gg[:, gcol:gcol + 1],
                                                                                              in1=pS[:, 0:64], op0=ALU.mult, op1=ALU.add),
                             reads=[ts_.gg], writes=[pS, S3])
                        P.op("act", lambda e, S3=S3, Sn=Sn: e.activation(out=Sn.ap(), in_=S3.ap(), func=AF.Copy), reads=[S3], writes=[Sn])
                        yield
                    if not lat:
                        return
                    if d == 0:
                        yb = Buf("yf_%d_%d" % (p, bi))
                        C.yf_bufs[(p, bi)] = yb
                        P.dma("sp", dr["yf"][p, :, t0:t0 + NB], ts_.ys32.ap(), reads=[ts_.ys32], writes=[yb])
                        return
                    P.op("act", lambda e: e.activation(out=ts_.ys16.ap(), in_=ts_.ys32.ap(), func=AF.Copy), reads=[ts_.ys32], writes=[ts_.ys16])
                    pm_ = C.nb()
                    P.op("pe", lambda e, pm_=pm_: e.matmul(pm_[:, 0:NB], lhsT=C.bmean.ap(), rhs=ts_.ys16.ap(), start=True, stop=True),
                         reads=[C.bmean, ts_.ys16], writes=[pm_])
                    P.op("dve", lambda e, pm_=pm_: e.tensor_tensor(out=ts_.yc.ap(), in0=ts_.ys32.ap(), in1=pm_[:, 0:NB], op=ALU.subtract),
                         reads=[ts_.ys32], writes=[pm_, ts_.yc])
                    yield
                    P.op("act", lambda e: e.activation(out=ts_.yc2.ap(), in_=ts_.yc.ap(), func=AF.Square), reads=[ts_.yc], writes=[ts_.yc2])
                    pv = C.nb()
                    P.op("pe", lambda e, pv=pv: e.matmul(pv[:, 0:NB], lhsT=C.bmean.ap(), rhs=ts_.yc2.ap(), start=True, stop=True),
                         reads=[C.bmean, ts_.yc2], writes=[pv])
                    P.op("act", lambda e, pv=pv: e.activation(out=ts_.rstd.ap(), in_=pv[:, 0:NB], func=AF.Sqrt, bias=C.epsx.ap()),
                         reads=[C.epsx], writes=[pv, ts_.rstd])
                    P.op("dve", lambda e: e.reciprocal(out=ts_.rstd.ap(), in_=ts_.rstd.ap()), writes=[ts_.rstd])
                    yield
                    P.op("dve", lambda e: e.tensor_tensor(out=ts_.yn.ap(), in0=ts_.yc.ap(), in1=ts_.rstd.ap(), op=ALU.mult), reads=[ts_.yc, ts_.rstd], writes=[ts_.yn])
                    P.op("dve", lambda e, p=p: e.tensor_scalar(out=ts_.yn.ap(), in0=ts_.yn.ap(), scalar1=kvec[:, 3, p:p + 1], scalar2=kvec[:, 4, p:p + 1],
                                                               op0=ALU.mult, op1=ALU.add), reads=[kvec], writes=[ts_.yn])
                    pz = C.nb()
                    P.op("pe", lambda e, p=p, pz=pz: e.matmul(pz[:, 0:NB], lhsT=LW2[32:64, 0, p * 128:(p + 1) * 128], rhs=LA16[32:64, :],
                                                              start=True, stop=True), reads=[LW2, LA16], writes=[pz])
                    P.op("act", lambda e, p=p, pz=pz: e.activation(out=ts_.a2_.ap(), in_=pz[:, 0:NB], func=AF.Sigmoid, bias=w0a0[:, 2, p:p + 1]),
                         reads=[w0a0], writes=[pz, ts_.a2_])
                    yield
                    P.op("dve", lambda e, p=p: e.tensor_scalar(out=ts_.a2_.ap(), in0=ts_.a2_.ap(), scalar1=kvec[:, 1, p:p + 1], scalar2=omka[:, p:p + 1],
                                                               op0=ALU.mult, op1=ALU.add), reads=[kvec, omka], writes=[ts_.a2_])
                    P.op("pool", lambda e: e.tensor_tensor(out=ts_.a2_.ap(), in0=ts_.a2_.ap(), in1=ts_.fac.ap(), op=ALU.add), reads=[ts_.fac], writes=[ts_.a2_])
                    P.op("pool", lambda e: e.tensor_tensor(out=ts_.kdir2.ap(), in0=ts_.k32.ap(), in1=ts_.a2_.ap(), op=ALU.mult), reads=[ts_.k32, ts_.a2_], writes=[ts_.kdir2])
                    P.op("pool", lambda e: e.tensor_tensor(out=ts_.rk.ap(), in0=ts_.r32.ap(), in1=ts_.kdir2.ap(), op=ALU.mult), reads=[ts_.r32, ts_.kdir2], writes=[ts_.rk])
                    P.op("pool", lambda e, p=p: e.tensor_scalar(out=ts_.rk16.ap(), in0=ts_.rk.ap(), scalar1=rkh[:, p:p + 1], scalar2=None, op0=ALU.mult),
                         reads=[ts_.rk, rkh], writes=[ts_.rk16])
                    pbn = C.nb()
                    P.op("pe", lambda e, pbn=pbn: e.matmul(pbn[:, 0:NB], lhsT=C.bones.ap(), rhs=ts_.rk16.ap(), start=True, stop=True),
                         reads=[C.bones, ts_.rk16], writes=[pbn])
                    P.op("dve", lambda e, pbn=pbn: e.tensor_tensor(out=ts_.bonus.ap(), in0=pbn[:, 0:NB], in1=ts_.v32.ap(), op=ALU.mult),
                         reads=[ts_.v32], writes=[pbn, ts_.bonus])
                    yield
                    P.op("pool", lambda e: e.tensor_tensor(out=ts_.bonus.ap(), in0=ts_.bonus.ap(), in1=ts_.yn.ap(), op=ALU.add), reads=[ts_.yn], writes=[ts_.bonus])
                    pgt = C.nb()
                    P.op("pe", lambda e, p=p, pgt=pgt: e.matmul(pgt[:, 0:NB], lhsT=G2W[0:96, p * 128:(p + 1) * 128], rhs=sg16.ap(), start=True, stop=True),
                         reads=[G2W, sg16], writes=[pgt])
                    P.op("dve", lambda e, pgt=pgt: e.tensor_tensor(out=ts_.o16.ap(), in0=pgt[:, 0:NB], in1=ts_.bonus.ap(), op=ALU.mult),
                         reads=[ts_.bonus], writes=[pgt, ts_.o16])
                    mb = Buf("mt_%d_%d" % (p, bi))
                    C.mt_bufs[(p, bi)] = mb
                    P.dma("sp", dr["mt"][p, :, t0:t0 + NB], ts_.o16.ap(), reads=[ts_.o16], writes=[mb])
                import itertools
                for pp0 in (0, 2):
                    for _ in itertools.zip_longest(pair_gen(pp0, TS[0]), pair_gen(pp0 + 1, TS[1])):
                        pass


def stage_attn(C):
    P, dr = C.P, C.dr
    HT, HC = C.HT, C.HC
    bk = C.banks
    NKT = (T_CTX + T_LAT) // 128
    with P.scope():
        stg = [P.sbuf("astg%d" % i, [128, 8, 256], F32) for i in range(2)]
        Wh = P.sbuf("Wh", [128, 8, 640], BF16)
        KT = P.sbuf("KT", [128, T_CTX + T_LAT], BF16)
        VT = P.sbuf("VT", [128, NKT, 128], BF16)
        QB = [P.sbuf("QB%d" % i, [128, 512], BF16) for i in range(2)]
        PT = [P.sbuf("PT%d" % i, [128, 512], BF16) for i in range(6)]
        accD = P.sbuf("accD", [128, 512], F32)
        accP = P.sbuf("accP", [128, 512], F32)
        ones32a = P.sbuf("ones32a", [128, 128], F32)
        P.op("pool", lambda e: e.memset(ones32a.ap(), 1.0), writes=[ones32a])
        rC = [P.sbuf("rC%d" % i, [128, NB], F32) for i in range(2)]
        rS = [P.sbuf("rS%d" % i, [128, NB], F32) for i in range(2)]
        t1 = P.sbuf("rp_t1", [128, NB], F32)
        t2 = P.sbuf("rp_t2", [128, NB], F32)
        rl = P.sbuf("at_rl", [128, 512], F32)
        on_ = P.sbuf("at_on", [128, 512], F32)
        dif = P.sbuf("at_dif", [128, NB], F32)
        dsq = P.sbuf("at_dsq", [128, NB], BF16)
        drs = P.sbuf("at_drs", [128, NB], F32)
        ao16 = [P.sbuf("at_o16_%d" % i, [128, NB], BF16) for i in range(2)]
        lamt = P.sbuf("lamt", [128, 4, 64], F32)
        lpr = P.sbuf("lpr", [128, 2, 64], F32)
        lsum = P.sbuf("lsum", [128, 2], F32)
        nlam = P.sbuf("nlam", [128, 1], F32)
        sbw = P.sbuf("sbw", [128, 1], F32)
        P.dma("sp", lamt.ap(), dr["lam"], writes=[lamt])
        P.dma("sp", sbw.ap(), dr["subln"], writes=[sbw])
        P.op("dve", lambda e: e.tensor_tensor(out=lpr.ap(), in0=lamt[:, 0:4:2, :], in1=lamt[:, 1:4:2, :], op=ALU.mult),
             reads=[lamt], writes=[lpr])
        P.op("dve", lambda e: e.reduce_sum(out=lsum.ap(), in_=lpr.ap(), axis=AX.X), reads=[lpr], writes=[lsum])
        P.op("act", lambda e: e.activation(out=lsum.ap(), in_=lsum.ap(), func=AF.Exp), writes=[lsum])
        P.op("dve", lambda e: e.tensor_tensor(out=nlam.ap(), in0=lsum[:, 1:2], in1=lsum[:, 0:1], op=ALU.subtract),
             reads=[lsum], writes=[nlam])
        P.op("dve", lambda e: e.tensor_scalar(out=nlam.ap(), in0=nlam.ap(), scalar1=-0.2, scalar2=None, op0=ALU.add), writes=[nlam])
        P.op("dve", lambda e: e.tensor_scalar(out=sbw.ap(), in0=sbw.ap(), scalar1=0.8, scalar2=None, op0=ALU.mult), writes=[sbw])
        for q in QB:
            P.op("pool", lambda e, q=q: e.memset(q.ap(), 0.0), writes=[q])
        C.stg_i = 0
        ri = [0]

        def load_rope(bi):
            i = ri[0] % 2
            ri[0] += 1
            P.dma("sp", rC[i].ap(), dr["ropeC"][:, bi * NB:(bi + 1) * NB], writes=[rC[i]])
            P.dma("sp", rS[i].ap(), dr["ropeS"][:, bi * NB:(bi + 1) * NB], writes=[rS[i]])
            return rC[i], rS[i]

        def proj_rope(bank, c0, bi, rc, rs, outs):
            t0 = bi * NB
            for g in range(2):
                for k in range(8):
                    P.op("pe", lambda e, g=g, k=k: e.matmul(bank[:, g * NB:(g + 1) * NB], lhsT=Wh[:, k, c0 + g * 128:c0 + (g + 1) * 128],
                                                             rhs=HT[:, k, 1 + t0:1 + t0 + NB], start=(k == 0), stop=(k == 7)),
                         reads=[Wh, HT], writes=[bank])
            P.op("dve", lambda e: e.tensor_tensor(out=t1.ap(), in0=bank[:, 0:NB], in1=rc.ap(), op=ALU.mult), reads=[rc], writes=[bank, t1])
            P.op("dve", lambda e: e.tensor_tensor(out=t2.ap(), in0=bank[:, NB:2 * NB], in1=rs.ap(), op=ALU.mult), reads=[rs], writes=[bank, t2])
            for rows, oap, ob in outs:
                P.op("pool", lambda e, rows=rows, oap=oap: e.tensor_tensor(out=oap, in0=t1[rows, :], in1=t2[rows, :], op=ALU.add),
                     reads=[t1, t2], writes=[ob])

        for h in range(4):
            qc = 1696 + h * 128
            kc = 1696 + 512 + h * 128
            vc = 1696 + 1024 + h * 128
            load_w_bf16(C, (Wh, 0), dr["w_in"][:, qc:qc + 128], 128, stg)
            load_w_bf16(C, (Wh, 128), dr["w_qks"][:, h * 128:(h + 1) * 128], 128, stg)
            load_w_bf16(C, (Wh, 256), dr["w_in"][:, kc:kc + 128], 128, stg)
            load_w_bf16(C, (Wh, 384), dr["w_qks"][:, 512 + h * 128:512 + (h + 1) * 128], 128, stg)
            load_w_bf16(C, (Wh, 512), dr["w_in"][:, vc:vc + 128], 128, stg)
            pb = C.nb()
            for k in range(8):
                P.op("pe", lambda e, k=k, pb=pb: e.matmul(pb[:, 0:T_CTX], lhsT=Wh[:, k, 256:384], rhs=HC[:, k, 1:1 + T_CTX],
                                                           start=(k == 0), stop=(k == 7)), reads=[Wh, HC], writes=[pb])
            P.op("act", lambda e, pb=pb: e.activation(out=KT[:, 0:T_CTX], in_=pb[:, 0:T_CTX], func=AF.Copy), writes=[pb, KT])
            for bi in range(NLB):
                rc, rs = load_rope(bi)
                proj_rope(C.nb(), 256, bi, rc, rs, [(slice(0, 128), KT[:, T_CTX + bi * NB:T_CTX + (bi + 1) * NB], KT)])
            for j in range(NKT):
                Hs, c0 = (HC, 1 + j * 128) if j < 2 else (HT, 1 + (j - 2) * 128)
                pv = C.nb()
                for k in range(8):
                    P.op("pe", lambda e, k=k, pv=pv, Hs=Hs, c0=c0: e.matmul(pv[:, 0:128], lhsT=Hs[:, k, c0:c0 + 128], rhs=Wh[:, k, 512:640],
                                                                           start=(k == 0), stop=(k == 7)), reads=[Wh, Hs], writes=[pv])
                eng = "act" if j % 2 == 0 else "dve"
                if eng == "act":
                    P.op("act", lambda e, pv=pv, j=j: e.activation(out=VT[:, j, :], in_=pv[:, 0:128], func=AF.Copy), writes=[pv, VT])
                else:
                    P.op("dve", lambda e, pv=pv, j=j: e.tensor_copy(out=VT[:, j, :], in_=pv[:, 0:128]), writes=[pv, VT])

            def q_proj(bi):
                rc, rs = load_rope(bi)
                qb = QB[bi % 2]
                proj_rope(bk[6], 0, bi, rc, rs, [(slice(0, 64), qb[0:64, 0:NB], qb), (slice(64, 128), qb[64:128, NB:2 * NB], qb)])

            q_proj(0)
            pending = []
            for bi in range(NLB):
                qb = QB[bi % 2]
                po, pl = bk[3 + bi % 2], bk[5]

                def qk(j):
                    ps = bk[j % 3]
                    P.op("pe", lambda e, j=j, ps=ps: e.matmul(ps.ap(), lhsT=KT[:, j * 128:(j + 1) * 128], rhs=qb.ap(), start=True, stop=True),
                         reads=[KT, qb], writes=[ps])

                qk(0)
                qk(1)
                qk(2)
                if bi + 1 < NLB:
                    q_proj(bi + 1)
                while pending:
                    pending.pop(0)()
                nD = nP = 0
                for j in range(NKT):
                    ps, pt = bk[j % 3], PT[j % 6]
                    P.op("act", lambda e, ps=ps, pt=pt: e.activation(out=pt.ap(), in_=ps.ap(), func=AF.Exp, scale=0.125), writes=[ps, pt])
                    P.op("pe", lambda e, j=j, pt=pt: e.matmul(po.ap(), lhsT=VT[:, j, :], rhs=pt.ap(), start=(j == 0), stop=(j == NKT - 1)),
                         reads=[VT, pt], writes=[po])
                    acc = accD if j % 2 == 0 else accP
                    if j < 2:
                        P.op("dve", lambda e, pt=pt, acc=acc: e.tensor_copy(out=acc.ap(), in_=pt.ap()), reads=[pt], writes=[acc])
                    else:
                        P.op("dve", lambda e, pt=pt, acc=acc: e.tensor_tensor(out=acc.ap(), in0=acc.ap(), in1=pt.ap(), op=ALU.add), reads=[pt], writes=[acc])
                    if j + 3 < NKT:
                        qk(j + 3)
                P.op("pe", lambda e: e.matmul(pl.ap(), lhsT=ones32a.ap(), rhs=accD.ap(), start=True, stop=False), reads=[ones32a, accD], writes=[pl])
                P.op("pe", lambda e: e.matmul(pl.ap(), lhsT=ones32a.ap(), rhs=accP.ap(), start=False, stop=True), reads=[ones32a, accP], writes=[pl])
                def finalize(bi=bi, po=po, pl=pl):
                    P.op("dve", lambda e: e.reciprocal(out=rl.ap(), in_=pl.ap()), writes=[pl, rl])
                    P.op("dve", lambda e: e.tensor_tensor(out=on_.ap(), in0=po.ap(), in1=rl.ap(), op=ALU.mult), reads=[rl], writes=[po, on_])
                    P.op("dve", lambda e: e.scalar_tensor_tensor(out=dif.ap(), in0=on_[:, NB:2 * NB], scalar=nlam.ap(), in1=on_[:, 0:NB],
                                                                 op0=ALU.mult, op1=ALU.add), reads=[on_, nlam], writes=[dif])
                    P.op("pool", lambda e: e.tensor_tensor(out=dsq.ap(), in0=dif.ap(), in1=dif.ap(), op=ALU.mult), reads=[dif], writes=[dsq])
                    pm = bk[6]
                    P.op("pe", lambda e: e.matmul(pm[:, 0:NB], lhsT=C.mean128.ap(), rhs=dsq.ap(), start=True, stop=True),
                         reads=[C.mean128, dsq], writes=[pm])
                    P.op("act", lambda e: e.activation(out=drs.ap(), in_=pm[:, 0:NB], func=AF.Ln, bias=C.epss.ap()), reads=[C.epss], writes=[pm, drs])
                    P.op("act", lambda e: e.activation(out=drs.ap(), in_=drs.ap(), func=AF.Exp, scale=-0.5), writes=[drs])
                    P.op("pool", lambda e: e.tensor_tensor(out=dif.ap(), in0=dif.ap(), in1=drs.ap(), op=ALU.mult), reads=[drs], writes=[dif])
                    o16 = ao16[bi % 2]
                    P.op("pool", lambda e, o16=o16: e.tensor_scalar(out=o16.ap(), in0=dif.ap(), scalar1=sbw.ap(), scalar2=None, op0=ALU.mult),
                         reads=[dif, sbw], writes=[o16])
                    mb = Buf("mt_%d_%d" % (4 + h, bi))
                    C.mt_bufs[(4 + h, bi)] = mb
                    P.dma("sp", dr["mt"][4 + h, :, bi * NB:(bi + 1) * NB], o16.ap(), reads=[o16], writes=[mb])

                pending.append(finalize)
            while pending:
                pending.pop(0)()


def stage_out_a(C):
    P, dr = C.P, C.dr
    C.x1_bufs, C.h2_bufs = {}, {}
    with P.scope():
        C.stg_i = 0
        stg = [P.sbuf("ostg%d" % i, [128, 8, 256], F32) for i in range(2)]
        WO = P.sbuf("WO", [128, 8, 1024], BF16)
        load_w_bf16(C, (WO, 0), dr["w_out"], 1024, stg)
        MTb = [P.sbuf("MTb%d" % i, [128, 8, NB], BF16) for i in range(2)]
        xts = [P.sbuf("oxt%d" % i, [128, 8, NB], F32) for i in range(2)]
        y32 = P.sbuf("oy32", [128, 8, NB], F32)
        x1 = [P.sbuf("ox1_%d" % i, [128, 8, NB], F32) for i in range(2)]
        h2 = [P.sbuf("oh2_%d" % i, [128, 8, NB], BF16) for i in range(2)]
        sq = P.sbuf("o_sq", [128, 8, NB], BF16)
        rs = P.sbuf("o_rs", [128, NB], F32)
        tmp = P.sbuf("o_tmp", [128, 8, NB], F32)
        def ld_a(bi):
            t0 = bi * NB
            P.dma("sp", MTb[bi % 2].ap(), dr["mt"][:, :, t0:t0 + NB].rearrange("k p t -> p k t"),
                  reads=[C.mt_bufs[(k, bi)] for k in range(8)], writes=[MTb[bi % 2]])
            P.dma("sp", xts[bi % 2].ap(), dr["xT"][:, t0:t0 + NB].rearrange("(k p) t -> p k t", p=128), writes=[xts[bi % 2]])

        for bi in range(NLB):
            t0 = bi * NB
            mtb, xt, x1b, h2b = MTb[bi % 2], xts[bi % 2], x1[bi % 2], h2[bi % 2]
            if bi == 0:
                ld_a(0)
            if bi + 1 < NLB:
                ld_a(bi + 1)
            for j in range(8):
                py = C.nb()
                for k in range(8):
                    P.op("pe", lambda e, j=j, k=k, py=py: e.matmul(py[:, 0:NB], lhsT=WO[:, k, j * 128:(j + 1) * 128], rhs=mtb[:, k, :],
                                                                   start=(k == 0), stop=(k == 7)), reads=[WO, mtb], writes=[py])
                if j % 2 == 0:
                    P.op("act", lambda e, j=j, py=py: e.activation(out=y32[:, j, :], in_=py[:, 0:NB], func=AF.Copy), writes=[py, y32])
                else:
                    P.op("dve", lambda e, j=j, py=py: e.tensor_copy(out=y32[:, j, :], in_=py[:, 0:NB]), writes=[py, y32])
            P.op("act", lambda e: e.activation(out=sq.ap(), in_=y32.ap(), func=AF.Square), reads=[y32], writes=[sq])
            pb = C.nb()
            for k in range(8):
                P.op("pe", lambda e, k=k, pb=pb: e.matmul(pb[:, 0:NB], lhsT=C.ones.ap(), rhs=sq[:, k, :], start=(k == 0), stop=(k == 7)),
                     reads=[sq, C.ones], writes=[pb])
            P.op("act", lambda e, pb=pb: e.activation(out=rs.ap(), in_=pb[:, 0:NB], func=AF.Sqrt, scale=1.0 / D_MODEL, bias=C.eps6.ap()),
                 reads=[C.eps6], writes=[pb, rs])
            P.op("dve", lambda e: e.reciprocal(out=rs.ap(), in_=rs.ap()), writes=[rs])
            for j in range(8):
                P.op("dve", lambda e, j=j: e.scalar_tensor_tensor(out=tmp[:, j, :], in0=y32[:, j, :], scalar=C.G1[:, j:j + 1], in1=rs.ap(),
                                                                  op0=ALU.mult, op1=ALU.mult), reads=[y32, rs, C.G1], writes=[tmp])
            P.op("pool", lambda e, x1b=x1b, xt=xt: e.tensor_tensor(out=x1b.ap(), in0=tmp.ap(), in1=xt.ap(), op=ALU.add),
                 reads=[tmp, xt], writes=[x1b])
            xb = Buf("x1s_%d" % bi)
            C.x1_bufs[bi] = xb
            P.dma("sp", dr["x1s"][:, :, t0:t0 + NB].rearrange("k p t -> p k t"), x1b.ap(), reads=[x1b], writes=[xb])
            norm_block(C, x1b, NB, lambda k: C.A2[:, k:k + 1], lambda k: C.modT[:, 24 + k, 0:1],
                       lambda k, h2b=h2b: h2b[:, k, :], [C.A2, C.modT], h2b, (sq, rs, tmp))
            hb = Buf("h2s_%d" % bi)
            C.h2_bufs[bi] = hb
            P.dma("sp", dr["h2s"][:, :, t0:t0 + NB].rearrange("k p t -> p k t"), h2b.ap(), reads=[h2b], writes=[hb])


def stage_ffn(C):
    P, dr = C.P, C.dr
    NJ = D_FF // 128
    with P.scope():
        C.stg_i = 0
        WU = P.sbuf("WU", [128, 8, 2 * D_FF], BF16)
        WD = P.sbuf("WD", [128, NJ, 1024], BF16)
        with P.scope():
            stg = [P.sbuf("fstg%d" % i, [128, 8, 256], F32) for i in range(2)]
            load_w_bf16(C, (WU, 0), dr["w_up"], 2 * D_FF, stg)
            for j in range(NJ):
                s_ = stg[C.stg_i % 2]
                C.stg_i += 1
                sv = s_.ap().rearrange("p a b -> p (a b)")[:, 0:1024]
                P.dma("sp", sv, dr["w_down"][j * 128:(j + 1) * 128, :], writes=[s_])
                if j % 2 == 0:
                    P.op("act", lambda e, j=j, sv=sv: e.activation(out=WD[:, j, :], in_=sv, func=AF.Copy), reads=[s_], writes=[WD])
                else:
                    P.op("dve", lambda e, j=j, sv=sv: e.tensor_copy(out=WD[:, j, :], in_=sv), reads=[s_], writes=[WD])
        FCW = P.sbuf("FCW", [128, 2 * NJ, 3], F32)
        FB = P.sbuf("FB", [128, 2 * NJ], F32)
        C.cwb = FCW
        P.dma("sp", FCW.ap(), dr["fconvT"].rearrange("(c p) j -> p c j", p=128), writes=[FCW])
        P.dma("sp", FB.ap(), dr["fbias"], writes=[FB])
        h2t = [P.sbuf("fh2_%d" % i, [128, 8, NB + 2], BF16) for i in range(2)]
        x1t = [P.sbuf("fx1_%d" % i, [128, 8, NB], F32) for i in range(2)]
        uv = [P.sbuf("fuv%d" % i, [128, NB], F32) for i in range(3)]
        ug = [P.sbuf("fug%d" % i, [128, NB], F32) for i in range(3)]
        sg = [P.sbuf("fsg%d" % i, [128, NB], F32) for i in range(3)]
        act16 = P.sbuf("fact", [128, NJ, NB], BF16)
        f32t = P.sbuf("ff32", [128, 8, NB], F32)
        sq = P.sbuf("f_sq", [128, 8, NB], BF16)
        rs = P.sbuf("f_rs", [128, NB], F32)
        def load_block(bi):
            t0 = bi * NB
            hb, xb = h2t[bi % 2], x1t[bi % 2]
            lo = max(t0 - 1, 0)
            hi = min(t0 + NB + 1, T_LAT)
            rd = [C.h2_bufs[b] for b in (bi - 1, bi, bi + 1) if 0 <= b < NLB]
            if bi == 0:
                P.op("pool", lambda e, hb=hb: e.memset(hb[:, :, 0:1], 0.0), writes=[hb])
            if bi == NLB - 1:
                P.op("pool", lambda e, hb=hb: e.memset(hb[:, :, NB + 1:NB + 2], 0.0), writes=[hb])
            d0 = lo - (t0 - 1)
            P.dma("sp", hb[:, :, d0:d0 + (hi - lo)], dr["h2s"][:, :, lo:hi].rearrange("k p t -> p k t"), reads=rd, writes=[hb])
            P.dma("sp", xb.ap(), dr["x1s"][:, :, t0:t0 + NB].rearrange("k p t -> p k t"), reads=[C.x1_bufs[bi]], writes=[xb])

        def gate_mul(j, u, g, s2):
            P.op("act", lambda e: e.activation(out=s2.ap(), in_=g.ap(), func=AF.Silu), reads=[g], writes=[s2])
            P.op("pool", lambda e: e.tensor_tensor(out=act16[:, j, :], in0=u.ap(), in1=s2.ap(), op=ALU.mult),
                 reads=[u, s2], writes=[act16])

        load_block(0)
        for bi in range(NLB):
            t0 = bi * NB
            hb, xb = h2t[bi % 2], x1t[bi % 2]
            if bi + 1 < NLB:
                load_block(bi + 1)
            prev = None
            for j in range(NJ):
                u, g, s2 = uv[j % 3], ug[j % 3], sg[j % 3]
                proj_conv(C, hb, 0, WU, j * 128, 128, FCW[:, j, :], u.ap(), u, bias=FB[:, j:j + 1], bias_buf=FB)
                proj_conv(C, hb, 0, WU, D_FF + j * 128, 128, FCW[:, NJ + j, :], g.ap(), g, bias=FB[:, NJ + j:NJ + j + 1], bias_buf=FB)
                if prev is not None:
                    gate_mul(*prev)
                prev = (j, u, g, s2)
            gate_mul(*prev)
            for i in range(8):
                pf = C.nb()
                for j in range(NJ):
                    P.op("pe", lambda e, i=i, j=j, pf=pf: e.matmul(pf[:, 0:NB], lhsT=WD[:, j, i * 128:(i + 1) * 128], rhs=act16[:, j, :],
                                                                   start=(j == 0), stop=(j == NJ - 1)), reads=[WD, act16], writes=[pf])
                if i % 2 == 0:
                    P.op("act", lambda e, i=i, pf=pf: e.activation(out=f32t[:, i, :], in_=pf[:, 0:NB], func=AF.Copy), writes=[pf, f32t])
                else:
                    P.op("dve", lambda e, i=i, pf=pf: e.tensor_copy(out=f32t[:, i, :], in_=pf[:, 0:NB]), writes=[pf, f32t])
            P.op("act", lambda e: e.activation(out=sq.ap(), in_=f32t.ap(), func=AF.Square), reads=[f32t], writes=[sq])
            pb = C.nb()
            for k in range(8):
                P.op("pe", lambda e, k=k, pb=pb: e.matmul(pb[:, 0:NB], lhsT=C.ones.ap(), rhs=sq[:, k, :], start=(k == 0), stop=(k == 7)),
                     reads=[sq, C.ones], writes=[pb])
            P.op("act", lambda e, pb=pb: e.activation(out=rs.ap(), in_=pb[:, 0:NB], func=AF.Sqrt, scale=1.0 / D_MODEL, bias=C.eps6.ap()),
                 reads=[C.eps6], writes=[pb, rs])
            P.op("dve", lambda e: e.reciprocal(out=rs.ap(), in_=rs.ap()), writes=[rs])
            for i in range(8):
                P.op("dve", lambda e, i=i: e.scalar_tensor_tensor(out=f32t[:, i, :], in0=f32t[:, i, :], scalar=C.G2[:, i:i + 1], in1=rs.ap(),
                                                                  op0=ALU.mult, op1=ALU.mult), reads=[rs, C.G2], writes=[f32t])
            P.op("pool", lambda e, xb=xb: e.tensor_tensor(out=xb.ap(), in0=f32t.ap(), in1=xb.ap(), op=ALU.add),
                 reads=[f32t], writes=[xb])
            P.dma("sp", dr["outT"][:, t0:t0 + NB].rearrange("(k p) t -> p k t", p=128), xb.ap(), reads=[xb], final=True)
```

```python
import contextlib
import numpy as np
import concourse.bass as bass
import concourse.mybir as mybir
from concourse.bass_utils import run_bass_kernel_spmd

F32 = mybir.dt.float32
BF16 = mybir.dt.bfloat16
AF = mybir.ActivationFunctionType
ALU = mybir.AluOpType
AX = mybir.AxisListType

SEM_ROLL = 30000


class Buf:
    def __init__(self, name, t=None):
        self.name = name
        self.t = t
        self.last_w = None
        self.readers = []
        self.dma_sem = None
        self.dma_cnt = 0

    def ap(self):
        return self.t[:]

    def __getitem__(self, idx):
        return self.t[idx]


class _Rec:
    def __getattr__(self, name):
        return lambda *a, **k: (name, a, k)


_REC = _Rec()


def _bind(fn):
    name, a, k = fn(_REC)
    return lambda e: getattr(e, name)(*a, **k)


class Op:
    __slots__ = ("eng", "fn", "deps", "signal", "token", "is_dma", "sem_buf", "final", "idx")


class Prog:
    ENGS = ("pe", "act", "dve", "pool", "sp")

    def __init__(self, nc):
        self.nc = nc
        self.stack = contextlib.ExitStack()
        self.ops = {e: [] for e in self.ENGS}
        self.nbuf = 0
        self.all_ops = []
        self.final_ops = []

    def sbuf(self, name, shape, dtype):
        self.nbuf += 1
        name = "s%d_%s" % (self.nbuf, name)
        t = self.stack.enter_context(self.nc.sbuf_tensor(name, list(shape), dtype))
        b = Buf(name, t)
        b.readers = list(getattr(self, "fence", []))
        if hasattr(self, "scope_bufs") and self.scope_bufs:
            self.scope_bufs[-1].append(b)
        return b

    def psum(self, name, shape, dtype):
        t = self.stack.enter_context(self.nc.psum_tensor(name, list(shape), dtype))
        return Buf(name, t)

    def view(self, name):
        return Buf(name)

    @contextlib.contextmanager
    def scope(self):
        old = self.stack
        self.stack = contextlib.ExitStack()
        if not hasattr(self, "scope_bufs"):
            self.scope_bufs = []
        self.scope_bufs.append([])
        try:
            yield
        finally:
            self.stack.close()
            self.stack = old
            bufs = self.scope_bufs.pop()
            ops = list(getattr(self, "fence", []))
            for b in bufs:
                if b.last_w is not None:
                    ops.append(b.last_w)
                ops.extend(b.readers)
            best = {}
            dmas = {}
            for o in ops:
                if o.is_dma:
                    dmas[id(o)] = o
                else:
                    if o.eng not in best or best[o.eng].idx < o.idx:
                        best[o.eng] = o
            self.fence = list(best.values()) + list(dmas.values())

    def _deps(self, o, reads, writes):
        deps = []
        for b in list(reads) + list(writes):
            if b.last_w is not None:
                deps.append(b.last_w)
        for b in writes:
            deps.extend(b.readers)
        for b in writes:
            b.last_w = o
            b.readers = []
        for b in reads:
            b.readers.append(o)
        seen = set()
        out = []
        for d in deps:
            if id(d) in seen or d is o:
                continue
            seen.add(id(d))
            if d.eng == "pe" and o.eng == "pe" and not d.is_dma and not o.is_dma:
                continue
            d.signal = True
            out.append(d)
        o.deps = out

    def op(self, eng, fn, reads=(), writes=()):
        o = Op()
        o.eng = eng
        o.fn = _bind(fn)
        o.signal = False
        o.token = None
        o.is_dma = False
        o.sem_buf = None
        o.final = False
        o.idx = len(self.ops[eng])
        self._deps(o, reads, writes)
        self.ops[eng].append(o)
        return o

    def dma(self, eng, out, in_, reads=(), writes=(), final=False):
        o = Op()
        o.eng = eng
        o.fn = lambda e: e.dma_start(out=out, in_=in_)
        o.signal = True
        o.token = None
        o.is_dma = True
        o.final = final
        cands = [b for b in list(writes) + list(reads) if b.t is not None]
        sb = cands[0] if cands else (list(writes) + list(reads))[0]
        o.sem_buf = sb
        o.idx = len(self.ops[eng])
        self._deps(o, reads, writes)
        self.ops[eng].append(o)
        if final:
            self.final_ops.append(o)
        return o

    def emit(self):
        nc = self.nc
        st = self.stack
        eng_sems = {}
        for e in self.ENGS:
            n = 0
            for o in self.ops[e]:
                if o.is_dma:
                    b = o.sem_buf
                    if b.dma_sem is None:
                        b.dma_sem = st.enter_context(nc.semaphore("d_" + b.name))
                    b.dma_cnt += 16
                    o.token = (b.dma_sem, b.dma_cnt, 16)
                elif o.signal:
                    k = n // SEM_ROLL
                    if (e, k) not in eng_sems:
                        eng_sems[(e, k)] = st.enter_context(nc.semaphore("s_%s_%d" % (e, k)))
                    o.token = (eng_sems[(e, k)], n % SEM_ROLL + 1, 1)
                    n += 1
        print("ops:", {e: len(self.ops[e]) for e in self.ENGS}, "signals:", {e: sum(1 for o in self.ops[e] if o.token is not None) for e in self.ENGS}, "nsems", len(eng_sems))
        all_sems = list(eng_sems.values())
        seen_b = set()
        for e in self.ENGS:
            for o in self.ops[e]:
                if o.is_dma and id(o.sem_buf) not in seen_b:
                    seen_b.add(id(o.sem_buf))
                    all_sems.append(o.sem_buf.dma_sem)
        with nc.Block() as blk0:
            @blk0.sync
            def _(eng):
                for sm in all_sems:
                    eng.sem_clear(sm)
        block = st.enter_context(nc.Block())
        hooks = {"pe": block.tensor, "act": block.scalar, "dve": block.vector,
                 "pool": block.gpsimd, "sp": block.sync}
        final_ops = self.final_ops

        def make(e):
            ops = self.ops[e]

            def body(eng):
                waited = {}
                for o in ops:
                    for d in o.deps:
                        sem, val, _ = d.token
                        if waited.get(id(sem), 0) < val:
                            eng.wait_ge(sem, val)
                            waited[id(sem)] = val
                    ins = o.fn(eng)
                    if o.token is not None:
                        ins.then_inc(o.token[0], o.token[2])
                if e == "sp":
                    for o in final_ops:
                        sem, val, _ = o.token
                        eng.wait_ge(sem, val)
            return body

        for e in self.ENGS:
            hooks[e](make(e))
        st.close()


D_MODEL = 1024
T_LAT = 4096
T_CTX = 256
NB = 256
NLB = T_LAT // NB
D_FF = 2816
EXPM05 = float(np.exp(-0.5))


class Ctx:
    pass


def build_program(nc, stage=99):
    P = Prog(nc)
    C = Ctx()
    C.P = P
    C.nc = nc
    dr = {}

    def din(name, shape, dt=F32):
        dr[name] = nc.dram_tensor(name, list(shape), dt, kind="ExternalInput").ap()

    din("xT", [1024, T_LAT]); din("ctxT", [1024, T_CTX]); din("cc", [128, 8, 2])
    din("w_mod", [1024, 6144]); din("b_modT", [128, 48]); din("gvec", [128, 4, 8])
    din("w_in", [1024, 3232]); din("w_qks", [1024, 1024]); din("convT", [1696, 3])
    din("lw2", [64, 2, 512]); din("w0a0", [128, 4, 4]); din("kvec", [128, 5, 4]); din("g2", [96, 512])
    din("lam", [128, 4, 64]); din("subln", [128, 1])
    din("w_out", [1024, 1024]); din("w_up", [1024, 5632]); din("fconvT", [5632, 3]); din("fbias", [128, 44])
    din("w_down", [2816, 1024]); din("ropeC", [128, T_LAT]); din("ropeS", [128, T_LAT]); din("blkmask", [128, 5, 128])
    dr["outT"] = nc.dram_tensor("outT", [1024, T_LAT], F32, kind="ExternalOutput").ap()
    dr["yf"] = nc.dram_tensor("yf_scr", [4, 128, T_LAT], F32).ap()
    dr["mt"] = nc.dram_tensor("mt_scr", [8, 128, T_LAT], BF16).ap()
    dr["x1s"] = nc.dram_tensor("x1_scr", [8, 128, T_LAT], F32).ap()
    dr["h2s"] = nc.dram_tensor("h2_scr", [8, 128, T_LAT], BF16).ap()
    if stage < 99:
        dr["dbg"] = nc.dram_tensor("dbg", [128, 8, T_LAT + 2], F32, kind="ExternalOutput").ap()
    C.dr = dr
    C.yf_bufs = {}
    C.mt_bufs = {}

    C.banks = [P.psum("pb%d" % i, [128, 512], F32) for i in range(7)]
    C.bankT = P.psum("pbT", [128, 1024], BF16)
    C.bank_i = 0

    def nb():
        b = C.banks[C.bank_i % 7]
        C.bank_i += 1
        return b
    C.nb = nb

    ident = P.sbuf("ident", [128, 128], BF16)
    ones = P.sbuf("ones", [128, 128], BF16)
    bones = P.sbuf("bones", [128, 128], BF16)
    bmean = P.sbuf("bmean", [128, 128], BF16)
    mean128 = P.sbuf("mean128", [128, 128], BF16)
    eps6 = P.sbuf("eps6", [128, 1], F32)
    epsx = P.sbuf("epsx", [128, 1], F32)
    epss = P.sbuf("epss", [128, 1], F32)
    P.op("pool", lambda e: e.memset(ident.ap(), 0.0), writes=[ident])
    P.op("pool", lambda e: e.affine_select(out=ident.ap(), in_=ident.ap(), pattern=[[-1, 128]],
                                           compare_op=ALU.not_equal, fill=1.0, base=0, channel_multiplier=1),
         reads=[ident], writes=[ident])
    P.op("pool", lambda e: e.memset(ones.ap(), 1.0), writes=[ones])
    P.op("pool", lambda e: e.memset(mean128.ap(), 1.0 / 128), writes=[mean128])
    P.op("pool", lambda e: e.memset(bones.ap(), 0.0), writes=[bones])
    P.op("pool", lambda e: e.memset(bones[0:64, 0:64], 1.0), writes=[bones])
    P.op("pool", lambda e: e.memset(bones[64:128, 64:128], 1.0), writes=[bones])
    P.op("pool", lambda e: e.memset(bmean.ap(), 0.0), writes=[bmean])
    P.op("pool", lambda e: e.memset(bmean[0:64, 0:64], 1.0 / 64), writes=[bmean])
    P.op("pool", lambda e: e.memset(bmean[64:128, 64:128], 1.0 / 64), writes=[bmean])
    P.op("pool", lambda e: e.memset(eps6.ap(), 1e-6), writes=[eps6])
    P.op("pool", lambda e: e.memset(epsx.ap(), 64e-5), writes=[epsx])
    P.op("pool", lambda e: e.memset(epss.ap(), 1e-5), writes=[epss])
    C.ident, C.ones, C.bones, C.bmean, C.mean128 = ident, ones, bones, bmean, mean128
    C.eps6, C.epsx, C.epss = eps6, epsx, epss

    stage_mod(C)
    import os
    with P.scope():
        C.HT = P.sbuf("HT", [128, 8, T_LAT + 2], BF16)
        C.HC = P.sbuf("HC", [128, 8, T_CTX + 2], BF16)
        for Hb, n in ((C.HT, T_LAT), (C.HC, T_CTX)):
            P.op("pool", lambda e, Hb=Hb: e.memset(Hb[:, :, 0:1], 0.0), writes=[Hb])
            P.op("pool", lambda e, Hb=Hb, n=n: e.memset(Hb[:, :, n + 1:n + 2], 0.0), writes=[Hb])
        stage_prenorm(C)
        if stage == 1:
            dbg_dump_HT(C)
            P.emit()
            return nc
        if not os.environ.get("SKIP_RWKV"):
            stage_rwkv(C)
        if stage == 21:
            with P.scope():
                for k in range(4):
                    tmp = P.sbuf("dbgy%d" % k, [128, T_LAT], F32)
                    P.dma("sp", tmp.ap(), dr["yf"][k], reads=list(C.yf_bufs.values()), writes=[tmp])
                    P.dma("sp", dr["dbg"][:, k, 0:T_LAT], tmp.ap(), reads=[tmp], final=True)
            P.emit()
            return nc
        if stage == 2:
            dbg_dump_mt(C, 0, 4)
            P.emit()
            return nc
        stage_attn(C)
        if stage == 3:
            dbg_dump_mt(C, 4, 8)
            P.emit()
            return nc
        stage_out_a(C)
    stage_ffn(C)
    P.emit()
    return nc


def dbg_dump_HT(C):
    P = C.P
    with P.scope():
        for k in range(8):
            tmp = P.sbuf("dbgt%d" % k, [128, T_LAT + 2], F32)
            P.op("dve", lambda e, k=k, tmp=tmp: e.tensor_copy(out=tmp.ap(), in_=C.HT[:, k, :]), reads=[C.HT], writes=[tmp])
            P.dma("sp", C.dr["dbg"][:, k, :], tmp.ap(), reads=[tmp], final=True)


def dbg_dump_mt(C, k0, k1):
    P = C.P
    with P.scope():
        for k in range(k0, k1):
            tb = P.sbuf("dbgb%d" % k, [128, T_LAT], BF16)
            tmp = P.sbuf("dbgt%d" % k, [128, T_LAT], F32)
            rd = [b for (kk, _), b in C.mt_bufs.items() if kk == k] + list(C.yf_bufs.values())
            P.dma("sp", tb.ap(), C.dr["mt"][k], reads=rd, writes=[tb])
            P.op("dve", lambda e, tmp=tmp, tb=tb: e.tensor_copy(out=tmp.ap(), in_=tb.ap()), reads=[tb], writes=[tmp])
            P.dma("sp", C.dr["dbg"][:, k, 0:T_LAT], tmp.ap(), reads=[tmp], final=True)


def stage_mod(C):
    P, dr = C.P, C.dr
    cc = P.sbuf("cc", [128, 8, 2], F32)
    scc = P.sbuf("scc", [128, 8, 2], F32)
    bm = P.sbuf("bmodT", [128, 48], F32)
    gv = P.sbuf("gvec", [128, 4, 8], F32)
    modT = P.sbuf("modT", [128, 48, 2], F32)
    P.dma("sp", cc.ap(), dr["cc"], writes=[cc])
    P.dma("sp", bm.ap(), dr["b_modT"], writes=[bm])
    P.dma("sp", gv.ap(), dr["gvec"], writes=[gv])
    P.op("act", lambda e: e.activation(out=scc.ap(), in_=cc.ap(), func=AF.Silu), reads=[cc], writes=[scc])
    pm = C.nb()
    with P.scope():
        wst = [P.sbuf("wmst%d" % i, [128, 8, 512], F32) for i in range(2)]
        for g in range(12):
            w = wst[g % 2]
            P.dma("sp", w.ap(), dr["w_mod"][:, g * 512:(g + 1) * 512].rearrange("(k p) n -> p k n", p=128), writes=[w])
            for jj in range(4):
                j = g * 4 + jj
                for k in range(8):
                    P.op("pe", lambda e, w=w, k=k, jj=jj, j=j: e.matmul(
                        pm[:, 2 * j:2 * j + 2], lhsT=w[:, k, jj * 128:(jj + 1) * 128], rhs=scc[:, k, :],
                        start=(k == 0), stop=(k == 7)), reads=[w, scc], writes=[pm])
    pmv = pm[:, 0:96].rearrange("p (j c) -> p j c", c=2)
    for c in range(2):
        P.op("dve", lambda e, c=c: e.tensor_tensor(out=modT[:, :, c], in0=pmv[:, :, c], in1=bm.ap(), op=ALU.add),
             reads=[bm], writes=[pm, modT])
    C.modT = modT
    A1 = P.sbuf("A1", [128, 2, 8], F32)
    G1 = P.sbuf("G1", [128, 8], F32)
    A2 = P.sbuf("A2", [128, 8], F32)
    G2 = P.sbuf("G2", [128, 8], F32)
    for c in range(2):
        P.op("dve", lambda e, c=c: e.scalar_tensor_tensor(out=A1[:, c, :], in0=modT[:, 8:16, c], scalar=1.0,
                                                          in1=gv[:, 0, :], op0=ALU.add, op1=ALU.mult),
             reads=[modT, gv], writes=[A1])
    P.op("dve", lambda e: e.tensor_tensor(out=G1.ap(), in0=modT[:, 16:24, 0], in1=gv[:, 1, :], op=ALU.mult),
         reads=[modT, gv], writes=[G1])
    P.op("dve", lambda e: e.scalar_tensor_tensor(out=A2.ap(), in0=modT[:, 32:40, 0], scalar=1.0, in1=gv[:, 2, :],
                                                 op0=ALU.add, op1=ALU.mult), reads=[modT, gv], writes=[A2])
    P.op("dve", lambda e: e.tensor_tensor(out=G2.ap(), in0=modT[:, 40:48, 0], in1=gv[:, 3, :], op=ALU.mult),
         reads=[modT, gv], writes=[G2])
    C.A1, C.G1, C.A2, C.G2 = A1, G1, A2, G2


def norm_block(C, xt, nbk, A_ap, sh_ap, out_ap, xt_reads, out_buf, tmpn):
    P = C.P
    sq, rs, tmp = tmpn
    P.op("act", lambda e: e.activation(out=sq[:, :, 0:nbk], in_=xt[:, :, 0:nbk], func=AF.Square),
         reads=[xt], writes=[sq])
    pb = C.nb()
    for k in range(8):
        P.op("pe", lambda e, k=k: e.matmul(pb[:, 0:nbk], lhsT=C.ones.ap(), rhs=sq[:, k, 0:nbk],
                                            start=(k == 0), stop=(k == 7)), reads=[sq, C.ones], writes=[pb])
    P.op("act", lambda e: e.activation(out=rs[:, 0:nbk], in_=pb[:, 0:nbk], func=AF.Sqrt, scale=1.0 / D_MODEL,
                                       bias=C.eps6.ap()), reads=[C.eps6], writes=[pb, rs])
    P.op("dve", lambda e: e.reciprocal(out=rs[:, 0:nbk], in_=rs[:, 0:nbk]), writes=[rs])
    for k in range(8):
        P.op("dve", lambda e, k=k: e.scalar_tensor_tensor(out=tmp[:, k, 0:nbk], in0=xt[:, k, 0:nbk], scalar=A_ap(k),
                                                          in1=rs[:, 0:nbk], op0=ALU.mult, op1=ALU.mult),
             reads=[xt, rs] + xt_reads, writes=[tmp])
        P.op("act", lambda e, k=k: e.activation(out=out_ap(k), in_=tmp[:, k, 0:nbk], func=AF.Identity,
                                                bias=sh_ap(k)), reads=[tmp] + xt_reads, writes=[out_buf])


def stage_prenorm(C):
    P, dr = C.P, C.dr
    with P.scope():
        xts = [P.sbuf("xt%d" % i, [128, 8, NB], F32) for i in range(2)]
        sq = P.sbuf("pn_sq", [128, 8, NB], BF16)
        rs = P.sbuf("pn_rs", [128, NB], F32)
        tmp = P.sbuf("pn_tmp", [128, 8, NB], F32)
        blocks = [("c", 0)] + [("l", i) for i in range(NLB)]
        for bi, (src, i) in enumerate(blocks):
            xt = xts[bi % 2]
            if src == "c":
                P.dma("sp", xt.ap(), dr["ctxT"].rearrange("(k p) t -> p k t", p=128), writes=[xt])
                H, cidx, t0 = C.HC, 1, 0
            else:
                P.dma("sp", xt.ap(), dr["xT"][:, i * NB:(i + 1) * NB].rearrange("(k p) t -> p k t", p=128), writes=[xt])
                H, cidx, t0 = C.HT, 0, i * NB
            norm_block(C, xt, NB, lambda k, cidx=cidx: C.A1[:, cidx, k:k + 1],
                       lambda k, cidx=cidx: C.modT[:, k, cidx:cidx + 1],
                       lambda k, H=H, t0=t0: H[:, k, 1 + t0:1 + t0 + NB], [C.A1, C.modT], H, (sq, rs, tmp))


def _rope_tables():
    n_pair = 16
    rows = T_LAT // 64
    row = np.repeat(np.arange(rows, dtype=np.float32), 64)
    col = np.tile(np.arange(64, dtype=np.float32), rows)
    inv = (np.float32(10000.0) ** (-np.arange(n_pair, dtype=np.float32) / np.float32(n_pair))).astype(np.float32)
    ang = np.concatenate([row[:, None] * inv, col[:, None] * inv], axis=-1).astype(np.float32)
    cos = np.cos(ang).astype(np.float32).T
    sin = np.sin(ang).astype(np.float32).T
    Cc = np.concatenate([cos, cos, cos, cos], axis=0)
    Ss = np.concatenate([-sin, sin, -sin, sin], axis=0)
    return np.ascontiguousarray(Cc), np.ascontiguousarray(Ss)


def prep_inputs(inp):
    f = lambda a: np.ascontiguousarray(np.asarray(a, dtype=np.float32))
    x, c, ctx, c_ctx = f(inp["x"]), f(inp["c"]), f(inp["ctx"]), f(inp["c_ctx"])
    pk = lambda v, n: f(v.reshape(n, 128).T)
    sh = {}
    sh["w_mod"] = f(inp["w_mod"][0])
    sh["b_modT"] = pk(f(inp["b_mod"][0]), 48)
    sh["gvec"] = f(np.stack([pk(f(inp[n][0]), 8) for n in ("g_pre_mix", "g_post_mix", "g_pre_ffn", "g_post_ffn")], axis=1))
    w_in = f(inp["w_in"][0])
    sh["w_in"] = w_in
    perm = np.arange(512).reshape(8, 2, 32)[:, ::-1, :].reshape(512)
    qc, kc = 1696, 1696 + 512
    sh["w_qks"] = f(np.concatenate([w_in[:, qc:qc + 512][:, perm], w_in[:, kc:kc + 512][:, perm]], axis=1))
    sh["convT"] = f(inp["rwkv_conv"][0].T)
    sh["lw2"] = f(np.stack([np.concatenate([f(inp["w2_fwd"][0]), f(inp["a2_fwd"][0])], 0),
                            np.concatenate([f(inp["w2_bwd"][0]), f(inp["a2_bwd"][0])], 0)], axis=1))
    sh["w0a0"] = f(np.stack([pk(f(inp[n][0]), 4) for n in ("w0_fwd", "w0_bwd", "a0_fwd", "a0_bwd")], axis=1))
    sh["kvec"] = f(np.stack([pk(f(inp[n][0]).reshape(512), 4) for n in ("k_k", "k_a", "r_k", "ln_x_w", "ln_x_b")], axis=1))
    sh["g2"] = f(inp["g2"][0])
    sh["lam"] = f(np.broadcast_to(np.stack([f(inp[n][0]) for n in ("lam_q1", "lam_k1", "lam_q2", "lam_k2")], 0)[None], (128, 4, 64)))
    sh["subln"] = f(inp["subln_w"][0].reshape(128, 1))
    sh["w_out"] = f(inp["w_out"][0])
    sh["w_up"] = f(inp["w_up"][0])
    sh["fconvT"] = f(inp["ffn_conv"][0].T)
    sh["fbias"] = pk(f(inp["ffn_conv_b"][0]), 44)
    sh["w_down"] = f(inp["w_down"][0])
    sh["ropeC"], sh["ropeS"] = _rope_tables()
    ii = np.arange(128)
    dm = lambda sz: (ii[:, None] // sz == ii[None, :] // sz).astype(np.float32)
    sh["blkmask"] = f(np.stack([dm(8), dm(16) - dm(8), dm(32) - dm(16), dm(64) - dm(32), dm(128) - dm(64)], axis=1))
    maps = []
    for b in range(8):
        m = dict(sh)
        m["xT"] = f(x[b].T)
        m["ctxT"] = f(ctx[b].T)
        m["cc"] = f(np.stack([c[b], c_ctx], -1).reshape(8, 128, 2).transpose(1, 0, 2))
        maps.append(m)
    return maps


def kernel(**inputs):
    maps = prep_inputs(inputs)
    nc = bass.Bass("TRN2", target_bir_lowering=False)
    build_program(nc)
    res = run_bass_kernel_spmd(nc, maps, core_ids=list(range(8)))
    out = np.stack([np.ascontiguousarray(res.results[b]["outT"].T) for b in range(8)], 0)
    return out.astype(np.float32)


def load_w_bf16(C, dst, dram_cols_ap, ncols, stg):
    P = C.P
    for i, c0 in enumerate(range(0, ncols, 256)):
        c1 = min(ncols, c0 + 256)
        s = stg[C.stg_i % 2]
        C.stg_i += 1
        P.dma("sp", s[:, :, 0:c1 - c0], dram_cols_ap[:, c0:c1].rearrange("(k p) n -> p k n", p=128), writes=[s])
        ce = ("pool", "act", "dve")[C.stg_i % 3]
        if ce == "act":
            P.op("act", lambda e, s=s, c0=c0, c1=c1: e.activation(out=dst[0][:, :, dst[1] + c0:dst[1] + c1], in_=s[:, :, 0:c1 - c0], func=AF.Copy),
                 reads=[s], writes=[dst[0]])
        else:
            P.op(ce, lambda e, s=s, c0=c0, c1=c1: e.tensor_copy(out=dst[0][:, :, dst[1] + c0:dst[1] + c1], in_=s[:, :, 0:c1 - c0]),
                 reads=[s], writes=[dst[0]])


def proj_conv(C, H, t0, W, c0, m, cw, out, out_buf, bias=None, eng2="dve", bias_buf=None):
    P = C.P
    pb = C.nb()
    for k in range(8):
        P.op("pe", lambda e, k=k: e.matmul(pb[0:m, 0:NB + 2], lhsT=W[:, k, c0:c0 + m], rhs=H[:, k, t0:t0 + NB + 2],
                                            start=(k == 0), stop=(k == 7)), reads=[W, H], writes=[pb])
    if bias is None:
        P.op("act", lambda e: e.activation(out=out, in_=pb[0:m, 1:NB + 1], func=AF.Copy, scale=cw[:, 1:2]),
             reads=[C.cwb], writes=[pb, out_buf])
    else:
        P.op("act", lambda e: e.activation(out=out, in_=pb[0:m, 1:NB + 1], func=AF.Identity, scale=cw[:, 1:2], bias=bias),
             reads=[C.cwb, bias_buf], writes=[pb, out_buf])
    P.op(eng2, lambda e: e.scalar_tensor_tensor(out=out, in0=pb[0:m, 0:NB], scalar=cw[:, 0:1], in1=out,
                                                op0=ALU.mult, op1=ALU.add), reads=[C.cwb], writes=[pb, out_buf])
    P.op(eng2, lambda e: e.scalar_tensor_tensor(out=out, in0=pb[0:m, 2:NB + 2], scalar=cw[:, 2:3], in1=out,
                                                op0=ALU.mult, op1=ALU.add), reads=[C.cwb], writes=[pb, out_buf])


def stage_rwkv(C):
    P, dr = C.P, C.dr
    C.stg_i = 0
    with P.scope():
        WR = P.sbuf("WR", [128, 8, 1696], BF16)
        with P.scope():
            stg = [P.sbuf("wstg%d" % i, [128, 8, 256], F32) for i in range(2)]
            load_w_bf16(C, (WR, 0), dr["w_in"][:, 0:1696], 1696, stg)
        CW = P.sbuf("CW", [128, 14, 3], F32)
        C.cwb = CW
        P.dma("sp", CW[:, 0:12, :], dr["convT"][0:1536, :].rearrange("(c p) j -> p c j", p=128), writes=[CW])
        P.dma("sp", CW[0:64, 12, :], dr["convT"][1536:1600, :], writes=[CW])
        P.dma("sp", CW[0:96, 13, :], dr["convT"][1600:1696, :], writes=[CW])
        LW2 = P.sbuf("LW2", [64, 2, 512], BF16)
        G2W = P.sbuf("G2W", [96, 512], BF16)
        with P.scope():
            lw2f = P.sbuf("lw2f", [64, 2, 512], F32)
            P.dma("sp", lw2f.ap(), dr["lw2"], writes=[lw2f])
            P.op("pool", lambda e: e.tensor_copy(out=LW2.ap(), in_=lw2f.ap()), reads=[lw2f], writes=[LW2])
            g2f = P.sbuf("g2f", [96, 512], F32)
            P.dma("sp", g2f.ap(), dr["g2"], writes=[g2f])
            P.op("pool", lambda e: e.tensor_copy(out=G2W.ap(), in_=g2f.ap()), reads=[g2f], writes=[G2W])
        w0a0 = P.sbuf("w0a0", [128, 4, 4], F32)
        kvec = P.sbuf("kvec", [128, 5, 4], F32)
        P.dma("sp", w0a0.ap(), dr["w0a0"], writes=[w0a0])
        P.dma("sp", kvec.ap(), dr["kvec"], writes=[kvec])
        omka = P.sbuf("omka", [128, 4], F32)
        rkh = P.sbuf("rkh", [128, 4], F32)
        P.op("dve", lambda e: e.tensor_scalar(out=omka.ap(), in0=kvec[:, 1, :], scalar1=-1.0, scalar2=1.0,
                                              op0=ALU.mult, op1=ALU.add), reads=[kvec], writes=[omka])
        P.op("dve", lambda e: e.tensor_scalar(out=rkh.ap(), in0=kvec[:, 2, :], scalar1=0.5, scalar2=None,
                                              op0=ALU.mult), reads=[kvec], writes=[rkh])
        AMM = [P.sbuf("amm%d" % d, [128, 4, 128], F32) for d in range(2)]
        with P.scope():
            ones32 = P.sbuf("ones32", [128, 128], F32)
            P.op("pool", lambda e: e.memset(ones32.ap(), 1.0), writes=[ones32])
            msk = {}
            for nm, cop, sgn in (("SU", ALU.is_gt, -1), ("IU", ALU.is_ge, -1), ("SL", ALU.is_gt, 1), ("IL", ALU.is_ge, 1)):
                mb = P.sbuf("m" + nm, [128, 128], F32)
                P.op("pool", lambda e, mb=mb, cop=cop, sgn=sgn: e.affine_select(out=mb.ap(), in_=ones32.ap(), pattern=[[-sgn, 128]],
                                                                                compare_op=cop, fill=0.0, base=0, channel_multiplier=sgn),
                     reads=[ones32], writes=[mb])
                msk[nm] = mb
            for d, (s_, i_) in enumerate((("SU", "IU"), ("SL", "IL"))):
                am = AMM[d]
                for q, nm in enumerate((s_, i_, s_, i_)):
                    P.op("pool", lambda e, am=am, q=q, nm=nm: e.tensor_copy(out=am[:, q, :], in_=msk[nm].ap()),
                         reads=[msk[nm]], writes=[am])
        NTMap = [AMM[1][:, 0, :], AMM[0][:, 0, :]]
        NTMb = [AMM[1], AMM[0]]
        rmask = P.sbuf("rmask", [128, NB], F32)
        P.op("pool", lambda e: e.memset(rmask.ap(), 1.0), writes=[rmask])
        for c in range(NB // 128):
            P.op("pool", lambda e, c=c: e.memset(rmask[:, c * 128:c * 128 + 1], 0.0), writes=[rmask])

        def f32t(n, shape=(128, NB)):
            return P.sbuf(n, list(shape), F32)

        def b16t(n, shape=(128, NB)):
            return P.sbuf(n, list(shape), BF16)

        la32, LA16 = f32t("la32", (64, NB)), b16t("LA16", (64, NB))
        gl32, sg16 = f32t("gl32", (96, NB)), b16t("sg16", (96, NB))
        BM2 = P.sbuf("blkmask2", [128, 5, 2, 128], BF16)
        with P.scope():
            bmf = P.sbuf("blkmaskf", [128, 5, 128], F32)
            P.dma("sp", bmf.ap(), dr["blkmask"], writes=[bmf])
            for q_ in range(2):
                P.op("pool", lambda e, q_=q_: e.tensor_copy(out=BM2[:, :, q_, :], in_=bmf.ap()), reads=[bmf], writes=[BM2])
        I2 = P.sbuf("ident2", [128, 2, 128], BF16)
        for q_ in range(2):
            P.op("pool", lambda e, q_=q_: e.tensor_copy(out=I2[:, q_, :], in_=C.ident.ap()), reads=[C.ident], writes=[I2])
        S32 = [[f32t("S32_%d_%d" % (d, p), (128, 64)) for p in range(4)] for d in range(2)]
        Sb = [[[b16t("Sb_%d_%d_%d" % (d, p, i), (128, 64)) for i in range(2)] for p in range(4)] for d in range(2)]
        sbi = [[0] * 4 for _ in range(2)]
        def make_set(si):
            sfx = "q%d_" % si
            o_ = Ctx()
            r32, k32, v32, v16 = f32t(sfx + "r32"), f32t(sfx + "k32"), f32t(sfx + "v32"), b16t(sfx + "v16")
            sig, aa, kraw, ksq, rn = f32t(sfx + "sig"), f32t(sfx + "aa"), f32t(sfx + "kraw"), b16t(sfx + "ksq"), f32t(sfx + "rn")
            fac, kdir, bb, cs, LL, Lm = f32t(sfx + "fac"), f32t(sfx + "kdir"), f32t(sfx + "bb"), f32t(sfx + "cs"), f32t(sfx + "LL"), f32t(sfx + "Lm")
            gg, ginv, gprev = f32t(sfx + "gg"), f32t(sfx + "ginv"), f32t(sfx + "gprev")
            ld, kk = sig, kraw
            yc, rstd, yn, rk, bonus, a2_, kdir2 = LL, rn, Lm, ginv, gprev, bb, kdir
            Bt, Kt = b16t(sfx + "Bt"), b16t(sfx + "Kt")
            KR = P.sbuf(sfx + "KR", [128, NB // 128, 2, 128], BF16)
            Bg, Kg = b16t(sfx + "Bg", (128, 128)), b16t(sfx + "Kg", (128, 128))
            TT = [P.sbuf(sfx + "TT%d" % ci, [128, 3, 128], BF16) for ci in range(2)]
            AMh = [[P.sbuf(sfx + "AM%d_%d" % (ci, h), [128, 4, 128], BF16) for h in range(2)] for ci in range(2)]
            NN = [[[P.sbuf(sfx + "NN%d_%d_%d" % (ci, h, i), [128, 2, 128], BF16) for i in range(1)] for h in range(2)] for ci in range(2)]
            PP = None
            Tb = [[P.sbuf(sfx + "Tb%d_%d" % (ci, h), [128, 128], BF16) for h in range(2)] for ci in range(2)]
            IV = [[dict((nm, P.sbuf(sfx + "iv%s_%d_%d" % (nm, ci, h), [128, 2, 128], BF16)) for nm in ("Nb2", "S2", "S4", "X2", "Pa", "Pb"))
                   for h in range(2)] for ci in range(2)]
            NZ, UT = b16t(sfx + "NZ", (128, 128)), b16t(sfx + "UT", (128, 128))
            ys32, yfl = f32t(sfx + "ys32"), f32t(sfx + "yfl")
            ys16, yc2 = b16t(sfx + "ys16"), b16t(sfx + "yc2")
            rk16, o16 = b16t(sfx + "rk16"), b16t(sfx + "o16")
            for _n in ['r32', 'k32', 'v32', 'v16', 'sig', 'ld', 'aa', 'a2_', 'kraw', 'ksq', 'rn', 'kk', 'fac', 'kdir', 'kdir2', 'bb', 'cs', 'LL', 'Lm', 'gg', 'ginv', 'gprev', 'Bt', 'Kt', 'KR', 'Bg', 'Kg', 'TT', 'AMh', 'NN', 'PP', 'Tb', 'IV', 'NZ', 'UT', 'ys32', 'yfl', 'ys16', 'yc', 'yc2', 'rstd', 'yn', 'rk', 'rk16', 'bonus', 'o16']:
                setattr(o_, _n, locals()[_n])
            return o_

        TS = [make_set(0), make_set(1)]
        for d in range(2):
            for p in range(4):
                P.op("pool", lambda e, d=d, p=p: e.memset(S32[d][p].ap(), 0.0), writes=[S32[d][p]])
                P.op("pool", lambda e, d=d, p=p: e.memset(Sb[d][p][0].ap(), 0.0), writes=[Sb[d][p][0]])

        NCH = NB // 128
        import os
        lim = int(os.environ.get("RWKV_LIM", "999"))
        cnt = 0
        for d in range(2):
            blocks = [("c", 0)] + ([("l", i) for i in range(NLB)] if d == 0 else [("l", i) for i in reversed(range(NLB))])
            for (src, bi) in blocks:
                cnt += 1
                if cnt > lim:
                    continue
                if os.environ.get("RWKV_SKIPF") and d == 0 and src == "l":
                    continue
                H = C.HC if src == "c" else C.HT
                t0 = 0 if src == "c" else bi * NB
                lat = (src == "l")
                proj_conv(C, H, t0, WR, 1536, 64, CW[0:64, 12, :], la32.ap(), la32)
                P.op("act", lambda e: e.activation(out=LA16[0:32, :], in_=la32[0:32, :], func=AF.Tanh), reads=[la32], writes=[LA16])
                P.op("pool", lambda e: e.tensor_copy(out=LA16[32:64, :], in_=la32[32:64, :]), reads=[la32], writes=[LA16])
                if d == 1 and lat:
                    proj_conv(C, H, t0, WR, 1600, 96, CW[0:96, 13, :], gl32.ap(), gl32)
                    P.op("act", lambda e: e.activation(out=sg16.ap(), in_=gl32.ap(), func=AF.Sigmoid), reads=[gl32], writes=[sg16])
                def pair_gen(p, ts_):
                    proj_conv(C, H, t0, WR, p * 128, 128, CW[:, p, :], ts_.r32.ap(), ts_.r32)
                    yield
                    proj_conv(C, H, t0, WR, 512 + p * 128, 128, CW[:, 4 + p, :], ts_.k32.ap(), ts_.k32)
                    yield
                    proj_conv(C, H, t0, WR, 1024 + p * 128, 128, CW[:, 8 + p, :], ts_.v32.ap(), ts_.v32)
                    yield
                    P.op("act", lambda e: e.activation(out=ts_.v16.ap(), in_=ts_.v32.ap(), func=AF.Copy), reads=[ts_.v32], writes=[ts_.v16])
                    pz = C.nb()
                    P.op("pe", lambda e, p=p, d=d: e.matmul(pz[:, 0:NB], lhsT=LW2[0:32, d, p * 128:(p + 1) * 128], rhs=LA16[0:32, :],
                                                             start=True, stop=True), reads=[LW2, LA16], writes=[pz])
                    P.op("act", lambda e, p=p, d=d: e.activation(out=ts_.sig.ap(), in_=pz[:, 0:NB], func=AF.Sigmoid, bias=w0a0[:, d, p:p + 1]),
                         reads=[w0a0], writes=[pz, ts_.sig])
                    P.op("pool", lambda e: e.tensor_scalar(out=ts_.ld.ap(), in0=ts_.sig.ap(), scalar1=-EXPM05, scalar2=None, op0=ALU.mult),
                         reads=[ts_.sig], writes=[ts_.ld])
                    pz = C.nb()
                    P.op("pe", lambda e, p=p, d=d, pz=pz: e.matmul(pz[:, 0:NB], lhsT=LW2[32:64, d, p * 128:(p + 1) * 128], rhs=LA16[32:64, :],
                                                                    start=True, stop=True), reads=[LW2, LA16], writes=[pz])
                    P.op("act", lambda e, p=p, d=d, pz=pz: e.activation(out=ts_.aa.ap(), in_=pz[:, 0:NB], func=AF.Sigmoid, bias=w0a0[:, 2 + d, p:p + 1]),
                         reads=[w0a0], writes=[pz, ts_.aa])
                    yield
                    P.op("act", lambda e, p=p: e.activation(out=ts_.kraw.ap(), in_=ts_.k32.ap(), func=AF.Copy, scale=kvec[:, 0, p:p + 1]),
                         reads=[ts_.k32, kvec], writes=[ts_.kraw])
                    P.op("act", lambda e: e.activation(out=ts_.ksq.ap(), in_=ts_.kraw.ap(), func=AF.Square), reads=[ts_.kraw], writes=[ts_.ksq])
                    pz = C.nb()
                    P.op("pe", lambda e, pz=pz: e.matmul(pz[:, 0:NB], lhsT=C.bones.ap(), rhs=ts_.ksq.ap(), start=True, stop=True),
                         reads=[C.bones, ts_.ksq], writes=[pz])
                    P.op("act", lambda e, pz=pz: e.activation(out=ts_.rn.ap(), in_=pz[:, 0:NB], func=AF.Sqrt), writes=[pz, ts_.rn])
                    P.op("dve", lambda e: e.tensor_scalar(out=ts_.rn.ap(), in0=ts_.rn.ap(), scalar1=1e-12, scalar2=None, op0=ALU.max), writes=[ts_.rn])
                    P.op("dve", lambda e: e.reciprocal(out=ts_.rn.ap(), in_=ts_.rn.ap()), writes=[ts_.rn])
                    P.op("dve", lambda e: e.tensor_tensor(out=ts_.kk.ap(), in0=ts_.kraw.ap(), in1=ts_.rn.ap(), op=ALU.mult), reads=[ts_.kraw, ts_.rn], writes=[ts_.kk])
                    yield
                    P.op("dve", lambda e, p=p: e.tensor_scalar(out=ts_.fac.ap(), in0=ts_.aa.ap(), scalar1=kvec[:, 1, p:p + 1], scalar2=omka[:, p:p + 1],
                                                               op0=ALU.mult, op1=ALU.add), reads=[ts_.aa, kvec, omka], writes=[ts_.fac])
                    P.op("pool", lambda e: e.tensor_tensor(out=ts_.kdir.ap(), in0=ts_.k32.ap(), in1=ts_.fac.ap(), op=ALU.mult), reads=[ts_.k32, ts_.fac], writes=[ts_.kdir])
                    P.op("pool", lambda e: e.tensor_tensor(out=ts_.bb.ap(), in0=ts_.kk.ap(), in1=ts_.aa.ap(), op=ALU.mult), reads=[ts_.kk, ts_.aa], writes=[ts_.bb])
                    P.op("dve", lambda e: e.tensor_tensor_scan(out=ts_.cs.ap(), data0=rmask.ap(), data1=ts_.ld.ap(), initial=0.0,
                                                               op0=ALU.mult, op1=ALU.add), reads=[rmask, ts_.ld], writes=[ts_.cs])
                    if d == 0:
                        Lb = ts_.cs
                    else:
                        for c in range(NCH):
                            P.op("dve", lambda e, c=c: e.tensor_scalar(out=ts_.LL[:, c * 128:(c + 1) * 128], in0=ts_.cs[:, c * 128:(c + 1) * 128],
                                                                       scalar1=-1.0, scalar2=ts_.cs[:, c * 128 + 127:c * 128 + 128],
                                                                       op0=ALU.mult, op1=ALU.add), reads=[ts_.cs], writes=[ts_.LL])
                        P.op("dve", lambda e: e.tensor_tensor(out=ts_.LL.ap(), in0=ts_.LL.ap(), in1=ts_.ld.ap(), op=ALU.add), reads=[ts_.ld], writes=[ts_.LL])
                        Lb = ts_.LL
                    P.op("pool", lambda e, Lb=Lb: e.tensor_tensor(out=ts_.Lm.ap(), in0=Lb.ap(), in1=ts_.ld.ap(), op=ALU.subtract), reads=[Lb, ts_.ld], writes=[ts_.Lm])
                    P.op("act", lambda e, Lb=Lb: e.activation(out=ts_.gg.ap(), in_=Lb.ap(), func=AF.Exp), reads=[Lb], writes=[ts_.gg])
                    P.op("act", lambda e, Lb=Lb: e.activation(out=ts_.ginv.ap(), in_=Lb.ap(), func=AF.Exp, scale=-1.0), reads=[Lb], writes=[ts_.ginv])
                    P.op("act", lambda e: e.activation(out=ts_.gprev.ap(), in_=ts_.Lm.ap(), func=AF.Exp), reads=[ts_.Lm], writes=[ts_.gprev])
                    yield
                    P.op("dve", lambda e: e.tensor_tensor(out=ts_.Bt.ap(), in0=ts_.bb.ap(), in1=ts_.ginv.ap(), op=ALU.mult), reads=[ts_.bb, ts_.ginv], writes=[ts_.Bt])
                    P.op("dve", lambda e: e.tensor_tensor(out=ts_.Kt.ap(), in0=ts_.kdir.ap(), in1=ts_.ginv.ap(), op=ALU.mult), reads=[ts_.kdir, ts_.ginv], writes=[ts_.Kt])
                    P.op("pool", lambda e: e.tensor_tensor(out=ts_.KR[:, :, 0, :], in0=ts_.kk.ap().rearrange("p (c t) -> p c t", t=128),
                                                          in1=ts_.gprev.ap().rearrange("p (c t) -> p c t", t=128), op=ALU.mult),
                         reads=[ts_.kk, ts_.gprev], writes=[ts_.KR])
                    P.op("pool", lambda e: e.tensor_tensor(out=ts_.KR[:, :, 1, :], in0=ts_.r32.ap().rearrange("p (c t) -> p c t", t=128),
                                                          in1=ts_.gg.ap().rearrange("p (c t) -> p c t", t=128), op=ALU.mult),
                         reads=[ts_.r32, ts_.gg], writes=[ts_.KR])
                    yield
                    if d == 1 and lat:
                        if (p, bi) in C.yf_bufs:
                            P.dma("sp", ts_.yfl.ap(), dr["yf"][p, :, t0:t0 + NB], reads=[C.yf_bufs[(p, bi)]], writes=[ts_.yfl])
                        else:
                            P.op("pool", lambda e: e.memset(ts_.yfl.ap(), 0.0), writes=[ts_.yfl])
                    chunks = list(range(NCH)) if d == 0 else list(reversed(range(NCH)))
                    for ci, c in enumerate(chunks):
                        csl = slice(c * 128, (c + 1) * 128)
                        gcol = c * 128 + 127 if d == 0 else c * 128
                        P.op("act", lambda e, csl=csl, gcol=gcol: e.activation(out=ts_.Bg.ap(), in_=ts_.Bt[:, csl], func=AF.Copy, scale=ts_.gg[:, gcol:gcol + 1]),
                             reads=[ts_.Bt, ts_.gg], writes=[ts_.Bg])
                        P.op("act", lambda e, csl=csl, gcol=gcol: e.activation(out=ts_.Kg.ap(), in_=ts_.Kt[:, csl], func=AF.Copy, scale=ts_.gg[:, gcol:gcol + 1]),
                             reads=[ts_.Kt, ts_.gg], writes=[ts_.Kg])
                        pT = C.bankT
                        P.op("pe", lambda e, csl=csl: e.transpose(pT[:, 0:128], ts_.v16[:, csl], C.ident.ap()), reads=[ts_.v16, C.ident], writes=[pT])
                        P.op("pe", lambda e: e.transpose(pT[:, 128:256], ts_.Bg.ap(), C.ident.ap()), reads=[ts_.Bg, C.ident], writes=[pT])
                        P.op("pe", lambda e: e.transpose(pT[:, 256:384], ts_.Kg.ap(), C.ident.ap()), reads=[ts_.Kg, C.ident], writes=[pT])
                        P.op("act", lambda e: e.activation(out=ts_.TT[ci].ap(), in_=pT[:, 0:384].rearrange("p (a b) -> p a b", b=128), func=AF.Copy),
                             writes=[pT, ts_.TT[ci]])
                        yield
                        for h in range(2):
                            hs = slice(h * 64, (h + 1) * 64)
                            AM = ts_.AMh[ci][h]
                            pg = C.nb()
                            P.op("pe", lambda e, hs=hs, csl=csl, c=c, pg=pg: e.matmul(pg[:, 0:256], lhsT=ts_.Bt[hs, csl], rhs=ts_.KR[hs, c, :, :],
                                                                                      start=True, stop=True), reads=[ts_.Bt, ts_.KR], writes=[pg])
                            P.op("pe", lambda e, hs=hs, csl=csl, c=c, pg=pg: e.matmul(pg[:, 256:512], lhsT=ts_.Kt[hs, csl], rhs=ts_.KR[hs, c, :, :],
                                                                                      start=True, stop=True), reads=[ts_.Kt, ts_.KR], writes=[pg])
                            N0 = ts_.NN[ci][h][0]
                            P.op("dve", lambda e, N0=N0, pg=pg, d=d: e.tensor_tensor(out=N0[:, 0, :], in0=pg[:, 0:128], in1=AMM[d][:, 0, :], op=ALU.mult),
                                 reads=[AMM[d]], writes=[pg, N0])
                            P.op("dve", lambda e, AM=AM, pg=pg, d=d: e.tensor_tensor(out=AM.ap(), in0=pg.ap().rearrange("p (a b) -> p a b", b=128),
                                                                                    in1=AMM[d].ap(), op=ALU.mult), reads=[AMM[d]], writes=[pg, AM])
                            pn = C.nb()
                            P.op("pe", lambda e, hs=hs, csl=csl, c=c, pn=pn: e.matmul(pn[:, 0:128], lhsT=ts_.KR[hs, c, 0, :], rhs=ts_.Bt[hs, csl],
                                                                                      start=True, stop=True), reads=[ts_.Bt, ts_.KR], writes=[pn])
                            N0 = ts_.NN[ci][h][0]
                            P.op("dve", lambda e, N0=N0, pn=pn, d=d: e.tensor_tensor(out=N0[:, 1, :], in0=pn[:, 0:128], in1=NTMap[d], op=ALU.mult),
                                 reads=[NTMb[d]], writes=[pn, N0])
                        def inv_gen(ci, h):
                            N0 = ts_.NN[ci][h][0]
                            iv = ts_.IV[ci][h]
                            mNb, S2, S4, mX = iv["Nb2"], iv["S2"], iv["S4"], iv["X2"]
                            Pc = [iv["Pa"], iv["Pb"]]
                            idb = C.ident
                            P.op("dve", lambda e: e.scalar_tensor_tensor(out=mNb.ap(), in0=N0.ap(), scalar=-1.0, in1=BM2[:, 0, :, :],
                                                                         op0=ALU.mult, op1=ALU.mult), reads=[N0, BM2], writes=[mNb])
                            pq = C.nb()
                            P.op("pe", lambda e: e.matmul(pq[:, 0:128], lhsT=mNb[:, 1, :], rhs=mNb[:, 0, :], start=True, stop=True), reads=[mNb], writes=[pq])
                            P.op("pe", lambda e: e.matmul(pq[:, 128:256], lhsT=mNb[:, 0, :], rhs=mNb[:, 1, :], start=True, stop=True), reads=[mNb], writes=[pq])
                            P.op("act", lambda e: e.activation(out=S2.ap(), in_=pq[:, 0:256].rearrange("p (a b) -> p a b", b=128), func=AF.Copy), writes=[pq, S2])
                            P.op("pool", lambda e: e.tensor_tensor(out=Pc[0].ap(), in0=I2.ap(), in1=mNb.ap(), op=ALU.add),
                                 reads=[mNb, I2], writes=[Pc[0]])
                            yield
                            pq2 = C.nb()
                            P.op("pe", lambda e: e.matmul(pq2[:, 0:128], lhsT=S2[:, 1, :], rhs=S2[:, 0, :], start=True, stop=True), reads=[S2], writes=[pq2])
                            P.op("pe", lambda e: e.matmul(pq2[:, 128:256], lhsT=S2[:, 0, :], rhs=S2[:, 1, :], start=True, stop=True), reads=[S2], writes=[pq2])
                            P.op("dve", lambda e: e.tensor_copy(out=S4.ap(), in_=pq2[:, 0:256].rearrange("p (a b) -> p a b", b=128)), writes=[pq2, S4])

                            def pstep(Sx, Pin, Pout):
                                pp_ = C.nb()
                                for q_ in range(2):
                                    P.op("pe", lambda e, q_=q_: e.matmul(pp_[:, q_ * 128:(q_ + 1) * 128], lhsT=Sx[:, 1 - q_, :], rhs=Pin[:, q_, :], start=True, stop=False),
                                         reads=[Sx, Pin], writes=[pp_])
                                    P.op("pe", lambda e, q_=q_: e.matmul(pp_[:, q_ * 128:(q_ + 1) * 128], lhsT=idb.ap(), rhs=Pin[:, q_, :], start=False, stop=True),
                                         reads=[idb, Pin], writes=[pp_])
                                P.op("act", lambda e: e.activation(out=Pout.ap(), in_=pp_[:, 0:256].rearrange("p (a b) -> p a b", b=128), func=AF.Copy),
                                     writes=[pp_, Pout])

                            pstep(S2, Pc[0], Pc[1])
                            yield
                            pstep(S4, Pc[1], Pc[0])
                            cur = 0
                            yield
                            for l in range(4):
                                Tc, Tn = Pc[cur], Pc[1 - cur]
                                last = (l == 3)
                                nq = 1 if last else 2
                                px = C.nb()
                                for q_ in range(nq):
                                    P.op("pe", lambda e, q_=q_: e.matmul(px[:, q_ * 128:(q_ + 1) * 128], lhsT=N0[:, 1 - q_, :], rhs=Tc[:, q_, :], start=True, stop=True),
                                         reads=[N0, Tc], writes=[px])
                                P.op("dve", lambda e: e.scalar_tensor_tensor(out=mX[:, 0:nq, :], in0=px[:, 0:nq * 128].rearrange("p (a b) -> p a b", b=128),
                                                                             scalar=-1.0, in1=BM2[:, 1 + l, 0:nq, :], op0=ALU.mult, op1=ALU.mult),
                                     reads=[BM2], writes=[px, mX])
                                yield
                                pr = C.nb()
                                for q_ in range(nq):
                                    P.op("pe", lambda e, q_=q_: e.matmul(pr[:, q_ * 128:(q_ + 1) * 128], lhsT=Tc[:, 1 - q_, :], rhs=mX[:, q_, :], start=True, stop=False),
                                         reads=[Tc, mX], writes=[pr])
                                    P.op("pe", lambda e, q_=q_: e.matmul(pr[:, q_ * 128:(q_ + 1) * 128], lhsT=idb.ap(), rhs=Tc[:, q_, :], start=False, stop=True),
                                         reads=[idb, Tc], writes=[pr])
                                if not last:
                                    P.op("act", lambda e: e.activation(out=Tn.ap(), in_=pr[:, 0:256].rearrange("p (a b) -> p a b", b=128), func=AF.Copy),
                                         writes=[pr, Tn])
                                else:
                                    P.op("act", lambda e: e.activation(out=ts_.Tb[ci][h].ap(), in_=pr[:, 0:128], func=AF.Copy), writes=[pr, ts_.Tb[ci][h]])
                                cur = 1 - cur
                                yield

                    for _ in zip(*[inv_gen(ci_, h_) for ci_ in range(len(chunks)) for h_ in range(2)]):
                        yield
                    for ci, c in enumerate(chunks):
                        csl = slice(c * 128, (c + 1) * 128)
                        gcol = c * 128 + 127 if d == 0 else c * 128
                        Th = ts_.Tb[ci]
                        So = Sb[d][p][sbi[d][p] % 2]
                        Sn = Sb[d][p][(sbi[d][p] + 1) % 2]
                        sbi[d][p] += 1
                        pz = C.nb()
                        for h in range(2):
                            hs = slice(h * 64, (h + 1) * 64)
                            P.op("pe", lambda e, hs=hs, c=c, pz=pz, So=So: e.matmul(pz[:, hs], lhsT=ts_.KR[hs, c, 0, :], rhs=So[hs, :], start=True, stop=False),
                                 reads=[ts_.KR, So], writes=[pz])
                            P.op("pe", lambda e, hs=hs, h=h, pz=pz: e.matmul(pz[:, hs], lhsT=ts_.AMh[ci][h][:, 2, :], rhs=ts_.TT[ci][:, 0, hs], start=False, stop=True),
                                 reads=[ts_.AMh[ci][h], ts_.TT[ci]], writes=[pz])
                        P.op("act", lambda e, pz=pz: e.activation(out=ts_.NZ.ap(), in_=pz[:, 0:128], func=AF.Copy, scale=-1.0), writes=[pz, ts_.NZ])
                        yield
                        pu = C.nb()
                        for h in range(2):
                            hs = slice(h * 64, (h + 1) * 64)
                            P.op("pe", lambda e, hs=hs, h=h, pu=pu: e.matmul(pu[:, hs], lhsT=Th[h].ap(), rhs=ts_.NZ[:, hs], start=True, stop=True),
                                 reads=[Th[h], ts_.NZ], writes=[pu])
                        P.op("act", lambda e, pu=pu: e.activation(out=ts_.UT.ap(), in_=pu[:, 0:128], func=AF.Copy), writes=[pu, ts_.UT])
                        yield
                        if lat:
                            py = C.nb()
                            for h in range(2):
                                hs = slice(h * 64, (h + 1) * 64)
                                P.op("pe", lambda e, hs=hs, c=c, py=py, So=So: e.matmul(py[hs, 0:128], lhsT=So[hs, :], rhs=ts_.KR[hs, c, 1, :], start=True, stop=False),
                                     reads=[So, ts_.KR], writes=[py])
                                P.op("pe", lambda e, hs=hs, h=h, py=py: e.matmul(py[hs, 0:128], lhsT=ts_.UT[:, hs], rhs=ts_.AMh[ci][h][:, 1, :], start=False, stop=False),
                                     reads=[ts_.UT, ts_.AMh[ci][h]], writes=[py])
                                P.op("pe", lambda e, hs=hs, h=h, py=py: e.matmul(py[hs, 0:128], lhsT=ts_.TT[ci][:, 0, hs], rhs=ts_.AMh[ci][h][:, 3, :], start=False, stop=True),
                                     reads=[ts_.TT[ci], ts_.AMh[ci][h]], writes=[py])
                            if d == 0:
                                P.op("act", lambda e, py=py, csl=csl: e.activation(out=ts_.ys32[:, csl], in_=py[:, 0:128], func=AF.Copy), writes=[py, ts_.ys32])
                            else:
                                P.op("dve", lambda e, py=py, csl=csl: e.tensor_tensor(out=ts_.ys32[:, csl], in0=py[:, 0:128], in1=ts_.yfl[:, csl], op=ALU.add),
                                     reads=[ts_.yfl], writes=[py, ts_.ys32])
                        pS = C.nb()
                        for h in range(2):
                            hs = slice(h * 64, (h + 1) * 64)
                            P.op("pe", lambda e, hs=hs, pS=pS: e.matmul(pS[hs, 0:64], lhsT=ts_.TT[ci][:, 1, hs], rhs=ts_.UT[:, hs], start=True, stop=False),
                                 reads=[ts_.TT[ci], ts_.UT], writes=[pS])
                            P.op("pe", lambda e, hs=hs, pS=pS: e.matmul(pS[hs, 0:64], lhsT=ts_.TT[ci][:, 2, hs], rhs=ts_.TT[ci][:, 0, hs], start=False, stop=True),
                                 reads=[ts_.TT[ci]], writes=[pS])
                        S3 = S32[d][p]
                        P.op("dve", lambda e, pS=pS, S3=S3, gcol=gcol: e.scalar_tensor_tensor(out=S3.ap(), in0=S3.ap(), scalar=ts_.gg[:, gcol:gcol + 1],
                                                                                              in1=pS[:, 0:64], op0=ALU.mult, op1=ALU.add),
                             reads=[ts_.gg], writes=[pS, S3])
                        P.op("act", lambda e, S3=S3, Sn=Sn: e.activation(out=Sn.ap(), in_=S3.ap(), func=AF.Copy), reads=[S3], writes=[Sn])
                        yield
                    if not lat:
                        return
                    if d == 0:
                        yb = Buf("yf_%d_%d" % (p, bi))
                        C.yf_bufs[(p, bi)] = yb
                        P.dma("sp", dr["yf"][p, :, t0:t0 + NB], ts_.ys32.ap(), reads=[ts_.ys32], writes=[yb])
                        return
                    P.op("act", lambda e: e.activation(out=ts_.ys16.ap(), in_=ts_.ys32.ap(), func=AF.Copy), reads=[ts_.ys32], writes=[ts_.ys16])
                    pm_ = C.nb()
                    P.op("pe", lambda e, pm_=pm_: e.matmul(pm_[:, 0:NB], lhsT=C.bmean.ap(), rhs=ts_.ys16.ap(), start=True, stop=True),
                         reads=[C.bmean, ts_.ys16], writes=[pm_])
                    P.op("dve", lambda e, pm_=pm_: e.tensor_tensor(out=ts_.yc.ap(), in0=ts_.ys32.ap(), in1=pm_[:, 0:NB], op=ALU.subtract),
                         reads=[ts_.ys32], writes=[pm_, ts_.yc])
                    yield
                    P.op("act", lambda e: e.activation(out=ts_.yc2.ap(), in_=ts_.yc.ap(), func=AF.Square), reads=[ts_.yc], writes=[ts_.yc2])
                    pv = C.nb()
                    P.op("pe", lambda e, pv=pv: e.matmul(pv[:, 0:NB], lhsT=C.bmean.ap(), rhs=ts_.yc2.ap(), start=True, stop=True),
                         reads=[C.bmean, ts_.yc2], writes=[pv])
                    P.op("act", lambda e, pv=pv: e.activation(out=ts_.rstd.ap(), in_=pv[:, 0:NB], func=AF.Sqrt, bias=C.epsx.ap()),
                         reads=[C.epsx], writes=[pv, ts_.rstd])
                    P.op("dve", lambda e: e.reciprocal(out=ts_.rstd.ap(), in_=ts_.rstd.ap()), writes=[ts_.rstd])
                    yield
                    P.op("dve", lambda e: e.tensor_tensor(out=ts_.yn.ap(), in0=ts_.yc.ap(), in1=ts_.rstd.ap(), op=ALU.mult), reads=[ts_.yc, ts_.rstd], writes=[ts_.yn])
                    P.op("dve", lambda e, p=p: e.tensor_scalar(out=ts_.yn.ap(), in0=ts_.yn.ap(), scalar1=kvec[:, 3, p:p + 1], scalar2=kvec[:, 4, p:p + 1],
                                                               op0=ALU.mult, op1=ALU.add), reads=[kvec], writes=[ts_.yn])
                    pz = C.nb()
                    P.op("pe", lambda e, p=p, pz=pz: e.matmul(pz[:, 0:NB], lhsT=LW2[32:64, 0, p * 128:(p + 1) * 128], rhs=LA16[32:64, :],
                                                              start=True, stop=True), reads=[LW2, LA16], writes=[pz])
                    P.op("act", lambda e, p=p, pz=pz: e.activation(out=ts_.a2_.ap(), in_=pz[:, 0:NB], func=AF.Sigmoid, bias=w0a0[:, 2, p:p + 1]),
                         reads=[w0a0], writes=[pz, ts_.a2_])
                    yield
                    P.op("dve", lambda e, p=p: e.tensor_scalar(out=ts_.a2_.ap(), in0=ts_.a2_.ap(), scalar1=kvec[:, 1, p:p + 1], scalar2=omka[:, p:p + 1],
                                                               op0=ALU.mult, op1=ALU.add), reads=[kvec, omka], writes=[ts_.a2_])
                    P.op("pool", lambda e: e.tensor_tensor(out=ts_.a2_.ap(), in0=ts_.a2_.ap(), in1=ts_.fac.ap(), op=ALU.add), reads=[ts_.fac], writes=[ts_.a2_])
                    P.op("pool", lambda e: e.tensor_tensor(out=ts_.kdir2.ap(), in0=ts_.k32.ap(), in1=ts_.a2_.ap(), op=ALU.mult), reads=[ts_.k32, ts_.a2_], writes=[ts_.kdir2])
                    P.op("pool", lambda e: e.tensor_tensor(out=ts_.rk.ap(), in0=ts_.r32.ap(), in1=ts_.kdir2.ap(), op=ALU.mult), reads=[ts_.r32, ts_.kdir2], writes=[ts_.rk])
                    P.op("pool", lambda e, p=p: e.tensor_scalar(out=ts_.rk16.ap(), in0=ts_.rk.ap(), scalar1=rkh[:, p:p + 1], scalar2=None, op0=ALU.mult),
                         reads=[ts_.rk, rkh], writes=[ts_.rk16])
                    pbn = C.nb()
                    P.op("pe", lambda e, pbn=pbn: e.matmul(pbn[:, 0:NB], lhsT=C.bones.ap(), rhs=ts_.rk16.ap(), start=True, stop=True),
                         reads=[C.bones, ts_.rk16], writes=[pbn])
                    P.op("dve", lambda e, pbn=pbn: e.tensor_tensor(out=ts_.bonus.ap(), in0=pbn[:, 0:NB], in1=ts_.v32.ap(), op=ALU.mult),
                         reads=[ts_.v32], writes=[pbn, ts_.bonus])
                    yield
                    P.op("pool", lambda e: e.tensor_tensor(out=ts_.bonus.ap(), in0=ts_.bonus.ap(), in1=ts_.yn.ap(), op=ALU.add), reads=[ts_.yn], writes=[ts_.bonus])
                    pgt = C.nb()
                    P.op("pe", lambda e, p=p, pgt=pgt: e.matmul(pgt[:, 0:NB], lhsT=G2W[0:96, p * 128:(p + 1) * 128], rhs=sg16.ap(), start=True, stop=True),
                         reads=[G2W, sg16], writes=[pgt])
                    P.op("dve", lambda e, pgt=pgt: e.tensor_tensor(out=ts_.o16.ap(), in0=pgt[:, 0:NB], in1=ts_.bonus.ap(), op=ALU.mult),
                         reads=[ts_.bonus], writes=[pgt, ts_.o16])
                    mb = Buf("mt_%d_%d" % (p, bi))
                    C.mt_bufs[(p, bi)] = mb
                    P.dma("sp", dr["mt"][p, :, t0:t0 + NB], ts_.o16.ap(), reads=[ts_.o16], writes=[mb])
                import itertools
                for pp0 in (0, 2):
                    for _ in itertools.zip_longest(pair_gen(pp0, TS[0]), pair_gen(pp0 + 1, TS[1])):
                        pass


def stage_attn(C):
    P, dr = C.P, C.dr
    HT, HC = C.HT, C.HC
    bk = C.banks
    NKT = (T_CTX + T_LAT) // 128
    with P.scope():
        stg = [P.sbuf("astg%d" % i, [128, 8, 256], F32) for i in range(2)]
        Wh = P.sbuf("Wh", [128, 8, 640], BF16)
        KT = P.sbuf("KT", [128, T_CTX + T_LAT], BF16)
        VT = P.sbuf("VT", [128, NKT, 128], BF16)
        QB = [P.sbuf("QB%d" % i, [128, 512], BF16) for i in range(2)]
        PT = [P.sbuf("PT%d" % i, [128, 512], BF16) for i in range(6)]
        accD = P.sbuf("accD", [128, 512], F32)
        accP = P.sbuf("accP", [128, 512], F32)
        ones32a = P.sbuf("ones32a", [128, 128], F32)
        P.op("pool", lambda e: e.memset(ones32a.ap(), 1.0), writes=[ones32a])
        rC = [P.sbuf("rC%d" % i, [128, NB], F32) for i in range(2)]
        rS = [P.sbuf("rS%d" % i, [128, NB], F32) for i in range(2)]
        t1 = P.sbuf("rp_t1", [128, NB], F32)
        t2 = P.sbuf("rp_t2", [128, NB], F32)
        rl = P.sbuf("at_rl", [128, 512], F32)
        on_ = P.sbuf("at_on", [128, 512], F32)
        dif = P.sbuf("at_dif", [128, NB], F32)
        dsq = P.sbuf("at_dsq", [128, NB], BF16)
        drs = P.sbuf("at_drs", [128, NB], F32)
        ao16 = [P.sbuf("at_o16_%d" % i, [128, NB], BF16) for i in range(2)]
        lamt = P.sbuf("lamt", [128, 4, 64], F32)
        lpr = P.sbuf("lpr", [128, 2, 64], F32)
        lsum = P.sbuf("lsum", [128, 2], F32)
        nlam = P.sbuf("nlam", [128, 1], F32)
        sbw = P.sbuf("sbw", [128, 1], F32)
        P.dma("sp", lamt.ap(), dr["lam"], writes=[lamt])
        P.dma("sp", sbw.ap(), dr["subln"], writes=[sbw])
        P.op("dve", lambda e: e.tensor_tensor(out=lpr.ap(), in0=lamt[:, 0:4:2, :], in1=lamt[:, 1:4:2, :], op=ALU.mult),
             reads=[lamt], writes=[lpr])
        P.op("dve", lambda e: e.reduce_sum(out=lsum.ap(), in_=lpr.ap(), axis=AX.X), reads=[lpr], writes=[lsum])
        P.op("act", lambda e: e.activation(out=lsum.ap(), in_=lsum.ap(), func=AF.Exp), writes=[lsum])
        P.op("dve", lambda e: e.tensor_tensor(out=nlam.ap(), in0=lsum[:, 1:2], in1=lsum[:, 0:1], op=ALU.subtract),
             reads=[lsum], writes=[nlam])
        P.op("dve", lambda e: e.tensor_scalar(out=nlam.ap(), in0=nlam.ap(), scalar1=-0.2, scalar2=None, op0=ALU.add), writes=[nlam])
        P.op("dve", lambda e: e.tensor_scalar(out=sbw.ap(), in0=sbw.ap(), scalar1=0.8, scalar2=None, op0=ALU.mult), writes=[sbw])
        for q in QB:
            P.op("pool", lambda e, q=q: e.memset(q.ap(), 0.0), writes=[q])
        C.stg_i = 0
        ri = [0]

        def load_rope(bi):
            i = ri[0] % 2
            ri[0] += 1
            P.dma("sp", rC[i].ap(), dr["ropeC"][:, bi * NB:(bi + 1) * NB], writes=[rC[i]])
            P.dma("sp", rS[i].ap(), dr["ropeS"][:, bi * NB:(bi + 1) * NB], writes=[rS[i]])
            return rC[i], rS[i]

        def proj_rope(bank, c0, bi, rc, rs, outs):
            t0 = bi * NB
            for g in range(2):
                for k in range(8):
                    P.op("pe", lambda e, g=g, k=k: e.matmul(bank[:, g * NB:(g + 1) * NB], lhsT=Wh[:, k, c0 + g * 128:c0 + (g + 1) * 128],
                                                             rhs=HT[:, k, 1 + t0:1 + t0 + NB], start=(k == 0), stop=(k == 7)),
                         reads=[Wh, HT], writes=[bank])
            P.op("dve", lambda e: e.tensor_tensor(out=t1.ap(), in0=bank[:, 0:NB], in1=rc.ap(), op=ALU.mult), reads=[rc], writes=[bank, t1])
            P.op("dve", lambda e: e.tensor_tensor(out=t2.ap(), in0=bank[:, NB:2 * NB], in1=rs.ap(), op=ALU.mult), reads=[rs], writes=[bank, t2])
            for rows, oap, ob in outs:
                P.op("pool", lambda e, rows=rows, oap=oap: e.tensor_tensor(out=oap, in0=t1[rows, :], in1=t2[rows, :], op=ALU.add),
                     reads=[t1, t2], writes=[ob])

        for h in range(4):
            qc = 1696 + h * 128
            kc = 1696 + 512 + h * 128
            vc = 1696 + 1024 + h * 128
            load_w_bf16(C, (Wh, 0), dr["w_in"][:, qc:qc + 128], 128, stg)
            load_w_bf16(C, (Wh, 128), dr["w_qks"][:, h * 128:(h + 1) * 128], 128, stg)
            load_w_bf16(C, (Wh, 256), dr["w_in"][:, kc:kc + 128], 128, stg)
            load_w_bf16(C, (Wh, 384), dr["w_qks"][:, 512 + h * 128:512 + (h + 1) * 128], 128, stg)
            load_w_bf16(C, (Wh, 512), dr["w_in"][:, vc:vc + 128], 128, stg)
            pb = C.nb()
            for k in range(8):
                P.op("pe", lambda e, k=k, pb=pb: e.matmul(pb[:, 0:T_CTX], lhsT=Wh[:, k, 256:384], rhs=HC[:, k, 1:1 + T_CTX],
                                                           start=(k == 0), stop=(k == 7)), reads=[Wh, HC], writes=[pb])
            P.op("act", lambda e, pb=pb: e.activation(out=KT[:, 0:T_CTX], in_=pb[:, 0:T_CTX], func=AF.Copy), writes=[pb, KT])
            for bi in range(NLB):
                rc, rs = load_rope(bi)
                proj_rope(C.nb(), 256, bi, rc, rs, [(slice(0, 128), KT[:, T_CTX + bi * NB:T_CTX + (bi + 1) * NB], KT)])
            for j in range(NKT):
                Hs, c0 = (HC, 1 + j * 128) if j < 2 else (HT, 1 + (j - 2) * 128)
                pv = C.nb()
                for k in range(8):
                    P.op("pe", lambda e, k=k, pv=pv, Hs=Hs, c0=c0: e.matmul(pv[:, 0:128], lhsT=Hs[:, k, c0:c0 + 128], rhs=Wh[:, k, 512:640],
                                                                           start=(k == 0), stop=(k == 7)), reads=[Wh, Hs], writes=[pv])
                eng = "act" if j % 2 == 0 else "dve"
                if eng == "act":
                    P.op("act", lambda e, pv=pv, j=j: e.activation(out=VT[:, j, :], in_=pv[:, 0:128], func=AF.Copy), writes=[pv, VT])
                else:
                    P.op("dve", lambda e, pv=pv, j=j: e.tensor_copy(out=VT[:, j, :], in_=pv[:, 0:128]), writes=[pv, VT])

            def q_proj(bi):
                rc, rs = load_rope(bi)
                qb = QB[bi % 2]
                proj_rope(bk[6], 0, bi, rc, rs, [(slice(0, 64), qb[0:64, 0:NB], qb), (slice(64, 128), qb[64:128, NB:2 * NB], qb)])

            q_proj(0)
            pending = []
            for bi in range(NLB):
                qb = QB[bi % 2]
                po, pl = bk[3 + bi % 2], bk[5]

                def qk(j):
                    ps = bk[j % 3]
                    P.op("pe", lambda e, j=j, ps=ps: e.matmul(ps.ap(), lhsT=KT[:, j * 128:(j + 1) * 128], rhs=qb.ap(), start=True, stop=True),
                         reads=[KT, qb], writes=[ps])

                qk(0)
                qk(1)
                qk(2)
                if bi + 1 < NLB:
                    q_proj(bi + 1)
                while pending:
                    pending.pop(0)()
                nD = nP = 0
                for j in range(NKT):
                    ps, pt = bk[j % 3], PT[j % 6]
                    P.op("act", lambda e, ps=ps, pt=pt: e.activation(out=pt.ap(), in_=ps.ap(), func=AF.Exp, scale=0.125), writes=[ps, pt])
                    P.op("pe", lambda e, j=j, pt=pt: e.matmul(po.ap(), lhsT=VT[:, j, :], rhs=pt.ap(), start=(j == 0), stop=(j == NKT - 1)),
                         reads=[VT, pt], writes=[po])
                    if j % 3 == 2:
                        P.op("pe", lambda e, j=j, pt=pt: e.matmul(pl.ap(), lhsT=C.ones.ap(), rhs=pt.ap(), start=(j == 2), stop=False),
                             reads=[C.ones, pt], writes=[pl])
                    else:
                        acc = accD if nD % 2 == 0 else accP
                        if nD < 2:
                            P.op("dve", lambda e, pt=pt, acc=acc: e.tensor_copy(out=acc.ap(), in_=pt.ap()), reads=[pt], writes=[acc])
                        else:
                            P.op("dve", lambda e, pt=pt, acc=acc: e.tensor_tensor(out=acc.ap(), in0=acc.ap(), in1=pt.ap(), op=ALU.add), reads=[pt], writes=[acc])
                        nD += 1
                    if j + 3 < NKT:
                        qk(j + 3)
                P.op("pe", lambda e: e.matmul(pl.ap(), lhsT=ones32a.ap(), rhs=accD.ap(), start=False, stop=False), reads=[ones32a, accD], writes=[pl])
                P.op("pe", lambda e: e.matmul(pl.ap(), lhsT=ones32a.ap(), rhs=accP.ap(), start=False, stop=True), reads=[ones32a, accP], writes=[pl])
                def finalize(bi=bi, po=po, pl=pl):
                    P.op("dve", lambda e: e.reciprocal(out=rl.ap(), in_=pl.ap()), writes=[pl, rl])
                    P.op("dve", lambda e: e.tensor_tensor(out=on_.ap(), in0=po.ap(), in1=rl.ap(), op=ALU.mult), reads=[rl], writes=[po, on_])
                    P.op("dve", lambda e: e.scalar_tensor_tensor(out=dif.ap(), in0=on_[:, NB:2 * NB], scalar=nlam.ap(), in1=on_[:, 0:NB],
                                                                 op0=ALU.mult, op1=ALU.add), reads=[on_, nlam], writes=[dif])
                    P.op("pool", lambda e: e.tensor_tensor(out=dsq.ap(), in0=dif.ap(), in1=dif.ap(), op=ALU.mult), reads=[dif], writes=[dsq])
                    pm = bk[6]
                    P.op("pe", lambda e: e.matmul(pm[:, 0:NB], lhsT=C.mean128.ap(), rhs=dsq.ap(), start=True, stop=True),
                         reads=[C.mean128, dsq], writes=[pm])
                    P.op("act", lambda e: e.activation(out=drs.ap(), in_=pm[:, 0:NB], func=AF.Ln, bias=C.epss.ap()), reads=[C.epss], writes=[pm, drs])
                    P.op("act", lambda e: e.activation(out=drs.ap(), in_=drs.ap(), func=AF.Exp, scale=-0.5), writes=[drs])
                    P.op("pool", lambda e: e.tensor_tensor(out=dif.ap(), in0=dif.ap(), in1=drs.ap(), op=ALU.mult), reads=[drs], writes=[dif])
                    o16 = ao16[bi % 2]
                    P.op("pool", lambda e, o16=o16: e.tensor_scalar(out=o16.ap(), in0=dif.ap(), scalar1=sbw.ap(), scalar2=None, op0=ALU.mult),
                         reads=[dif, sbw], writes=[o16])
                    mb = Buf("mt_%d_%d" % (4 + h, bi))
                    C.mt_bufs[(4 + h, bi)] = mb
                    P.dma("sp", dr["mt"][4 + h, :, bi * NB:(bi + 1) * NB], o16.ap(), reads=[o16], writes=[mb])

                pending.append(finalize)
            while pending:
                pending.pop(0)()


def stage_out_a(C):
    P, dr = C.P, C.dr
    C.x1_bufs, C.h2_bufs = {}, {}
    with P.scope():
        C.stg_i = 0
        stg = [P.sbuf("ostg%d" % i, [128, 8, 256], F32) for i in range(2)]
        WO = P.sbuf("WO", [128, 8, 1024], BF16)
        load_w_bf16(C, (WO, 0), dr["w_out"], 1024, stg)
        MTb = [P.sbuf("MTb%d" % i, [128, 8, NB], BF16) for i in range(2)]
        xts = [P.sbuf("oxt%d" % i, [128, 8, NB], F32) for i in range(2)]
        y32 = P.sbuf("oy32", [128, 8, NB], F32)
        x1 = [P.sbuf("ox1_%d" % i, [128, 8, NB], F32) for i in range(2)]
        h2 = [P.sbuf("oh2_%d" % i, [128, 8, NB], BF16) for i in range(2)]
        sq = P.sbuf("o_sq", [128, 8, NB], BF16)
        rs = P.sbuf("o_rs", [128, NB], F32)
        tmp = P.sbuf("o_tmp", [128, 8, NB], F32)
        def ld_a(bi):
            t0 = bi * NB
            P.dma("sp", MTb[bi % 2].ap(), dr["mt"][:, :, t0:t0 + NB].rearrange("k p t -> p k t"),
                  reads=[C.mt_bufs[(k, bi)] for k in range(8)], writes=[MTb[bi % 2]])
            P.dma("sp", xts[bi % 2].ap(), dr["xT"][:, t0:t0 + NB].rearrange("(k p) t -> p k t", p=128), writes=[xts[bi % 2]])

        for bi in range(NLB):
            t0 = bi * NB
            mtb, xt, x1b, h2b = MTb[bi % 2], xts[bi % 2], x1[bi % 2], h2[bi % 2]
            if bi == 0:
                ld_a(0)
            if bi + 1 < NLB:
                ld_a(bi + 1)
            for j in range(8):
                py = C.nb()
                for k in range(8):
                    P.op("pe", lambda e, j=j, k=k, py=py: e.matmul(py[:, 0:NB], lhsT=WO[:, k, j * 128:(j + 1) * 128], rhs=mtb[:, k, :],
                                                                   start=(k == 0), stop=(k == 7)), reads=[WO, mtb], writes=[py])
                if j % 2 == 0:
                    P.op("act", lambda e, j=j, py=py: e.activation(out=y32[:, j, :], in_=py[:, 0:NB], func=AF.Copy), writes=[py, y32])
                else:
                    P.op("dve", lambda e, j=j, py=py: e.tensor_copy(out=y32[:, j, :], in_=py[:, 0:NB]), writes=[py, y32])
            P.op("act", lambda e: e.activation(out=sq.ap(), in_=y32.ap(), func=AF.Square), reads=[y32], writes=[sq])
            pb = C.nb()
            for k in range(8):
                P.op("pe", lambda e, k=k, pb=pb: e.matmul(pb[:, 0:NB], lhsT=C.ones.ap(), rhs=sq[:, k, :], start=(k == 0), stop=(k == 7)),
                     reads=[sq, C.ones], writes=[pb])
            P.op("act", lambda e, pb=pb: e.activation(out=rs.ap(), in_=pb[:, 0:NB], func=AF.Sqrt, scale=1.0 / D_MODEL, bias=C.eps6.ap()),
                 reads=[C.eps6], writes=[pb, rs])
            P.op("dve", lambda e: e.reciprocal(out=rs.ap(), in_=rs.ap()), writes=[rs])
            for j in range(8):
                P.op("dve", lambda e, j=j: e.scalar_tensor_tensor(out=tmp[:, j, :], in0=y32[:, j, :], scalar=C.G1[:, j:j + 1], in1=rs.ap(),
                                                                  op0=ALU.mult, op1=ALU.mult), reads=[y32, rs, C.G1], writes=[tmp])
            P.op("pool", lambda e, x1b=x1b, xt=xt: e.tensor_tensor(out=x1b.ap(), in0=tmp.ap(), in1=xt.ap(), op=ALU.add),
                 reads=[tmp, xt], writes=[x1b])
            xb = Buf("x1s_%d" % bi)
            C.x1_bufs[bi] = xb
            P.dma("sp", dr["x1s"][:, :, t0:t0 + NB].rearrange("k p t -> p k t"), x1b.ap(), reads=[x1b], writes=[xb])
            norm_block(C, x1b, NB, lambda k: C.A2[:, k:k + 1], lambda k: C.modT[:, 24 + k, 0:1],
                       lambda k, h2b=h2b: h2b[:, k, :], [C.A2, C.modT], h2b, (sq, rs, tmp))
            hb = Buf("h2s_%d" % bi)
            C.h2_bufs[bi] = hb
            P.dma("sp", dr["h2s"][:, :, t0:t0 + NB].rearrange("k p t -> p k t"), h2b.ap(), reads=[h2b], writes=[hb])


def stage_ffn(C):
    P, dr = C.P, C.dr
    NJ = D_FF // 128
    with P.scope():
        C.stg_i = 0
        WU = P.sbuf("WU", [128, 8, 2 * D_FF], BF16)
        WD = P.sbuf("WD", [128, NJ, 1024], BF16)
        with P.scope():
            stg = [P.sbuf("fstg%d" % i, [128, 8, 256], F32) for i in range(2)]
            load_w_bf16(C, (WU, 0), dr["w_up"], 2 * D_FF, stg)
            for j in range(NJ):
                s_ = stg[C.stg_i % 2]
                C.stg_i += 1
                sv = s_.ap().rearrange("p a b -> p (a b)")[:, 0:1024]
                P.dma("sp", sv, dr["w_down"][j * 128:(j + 1) * 128, :], writes=[s_])
                if j % 2 == 0:
                    P.op("act", lambda e, j=j, sv=sv: e.activation(out=WD[:, j, :], in_=sv, func=AF.Copy), reads=[s_], writes=[WD])
                else:
                    P.op("dve", lambda e, j=j, sv=sv: e.tensor_copy(out=WD[:, j, :], in_=sv), reads=[s_], writes=[WD])
        FCW = P.sbuf("FCW", [128, 2 * NJ, 3], F32)
        FB = P.sbuf("FB", [128, 2 * NJ], F32)
        C.cwb = FCW
        P.dma("sp", FCW.ap(), dr["fconvT"].rearrange("(c p) j -> p c j", p=128), writes=[FCW])
        P.dma("sp", FB.ap(), dr["fbias"], writes=[FB])
        h2t = [P.sbuf("fh2_%d" % i, [128, 8, NB + 2], BF16) for i in range(2)]
        x1t = [P.sbuf("fx1_%d" % i, [128, 8, NB], F32) for i in range(2)]
        uv = [P.sbuf("fuv%d" % i, [128, NB], F32) for i in range(3)]
        ug = [P.sbuf("fug%d" % i, [128, NB], F32) for i in range(3)]
        sg = [P.sbuf("fsg%d" % i, [128, NB], F32) for i in range(3)]
        act16 = P.sbuf("fact", [128, NJ, NB], BF16)
        f32t = P.sbuf("ff32", [128, 8, NB], F32)
        sq = P.sbuf("f_sq", [128, 8, NB], BF16)
        rs = P.sbuf("f_rs", [128, NB], F32)
        def load_block(bi):
            t0 = bi * NB
            hb, xb = h2t[bi % 2], x1t[bi % 2]
            lo = max(t0 - 1, 0)
            hi = min(t0 + NB + 1, T_LAT)
            rd = [C.h2_bufs[b] for b in (bi - 1, bi, bi + 1) if 0 <= b < NLB]
            if bi == 0:
                P.op("pool", lambda e, hb=hb: e.memset(hb[:, :, 0:1], 0.0), writes=[hb])
            if bi == NLB - 1:
                P.op("pool", lambda e, hb=hb: e.memset(hb[:, :, NB + 1:NB + 2], 0.0), writes=[hb])
            d0 = lo - (t0 - 1)
            P.dma("sp", hb[:, :, d0:d0 + (hi - lo)], dr["h2s"][:, :, lo:hi].rearrange("k p t -> p k t"), reads=rd, writes=[hb])
            P.dma("sp", xb.ap(), dr["x1s"][:, :, t0:t0 + NB].rearrange("k p t -> p k t"), reads=[C.x1_bufs[bi]], writes=[xb])

        def gate_mul(j, u, g, s2):
            P.op("act", lambda e: e.activation(out=s2.ap(), in_=g.ap(), func=AF.Silu), reads=[g], writes=[s2])
            P.op("pool", lambda e: e.tensor_tensor(out=act16[:, j, :], in0=u.ap(), in1=s2.ap(), op=ALU.mult),
                 reads=[u, s2], writes=[act16])

        load_block(0)
        for bi in range(NLB):
            t0 = bi * NB
            hb, xb = h2t[bi % 2], x1t[bi % 2]
            if bi + 1 < NLB:
                load_block(bi + 1)
            prev = None
            for j in range(NJ):
                u, g, s2 = uv[j % 3], ug[j % 3], sg[j % 3]
                proj_conv(C, hb, 0, WU, j * 128, 128, FCW[:, j, :], u.ap(), u, bias=FB[:, j:j + 1], bias_buf=FB)
                proj_conv(C, hb, 0, WU, D_FF + j * 128, 128, FCW[:, NJ + j, :], g.ap(), g, bias=FB[:, NJ + j:NJ + j + 1], bias_buf=FB)
                if prev is not None:
                    gate_mul(*prev)
                prev = (j, u, g, s2)
            gate_mul(*prev)
            for i in range(8):
                pf = C.nb()
                for j in range(NJ):
                    P.op("pe", lambda e, i=i, j=j, pf=pf: e.matmul(pf[:, 0:NB], lhsT=WD[:, j, i * 128:(i + 1) * 128], rhs=act16[:, j, :],
                                                                   start=(j == 0), stop=(j == NJ - 1)), reads=[WD, act16], writes=[pf])
                if i % 2 == 0:
                    P.op("act", lambda e, i=i, pf=pf: e.activation(out=f32t[:, i, :], in_=pf[:, 0:NB], func=AF.Copy), writes=[pf, f32t])
                else:
                    P.op("dve", lambda e, i=i, pf=pf: e.tensor_copy(out=f32t[:, i, :], in_=pf[:, 0:NB]), writes=[pf, f32t])
            P.op("act", lambda e: e.activation(out=sq.ap(), in_=f32t.ap(), func=AF.Square), reads=[f32t], writes=[sq])
            pb = C.nb()
            for k in range(8):
                P.op("pe", lambda e, k=k, pb=pb: e.matmul(pb[:, 0:NB], lhsT=C.ones.ap(), rhs=sq[:, k, :], start=(k == 0), stop=(k == 7)),
                     reads=[sq, C.ones], writes=[pb])
            P.op("act", lambda e, pb=pb: e.activation(out=rs.ap(), in_=pb[:, 0:NB], func=AF.Sqrt, scale=1.0 / D_MODEL, bias=C.eps6.ap()),
                 reads=[C.eps6], writes=[pb, rs])
            P.op("dve", lambda e: e.reciprocal(out=rs.ap(), in_=rs.ap()), writes=[rs])
            for i in range(8):
                P.op("dve", lambda e, i=i: e.scalar_tensor_tensor(out=f32t[:, i, :], in0=f32t[:, i, :], scalar=C.G2[:, i:i + 1], in1=rs.ap(),
                                                                  op0=ALU.mult, op1=ALU.mult), reads=[rs, C.G2], writes=[f32t])
            P.op("pool", lambda e, xb=xb: e.tensor_tensor(out=xb.ap(), in0=f32t.ap(), in1=xb.ap(), op=ALU.add),
                 reads=[f32t], writes=[xb])
            P.dma("sp", dr["outT"][:, t0:t0 + NB].rearrange("(k p) t -> p k t", p=128), xb.ap(), reads=[xb], final=True)
```

```python
import contextlib
import numpy as np
import concourse.bass as bass
import concourse.mybir as mybir
from concourse.bass_utils import run_bass_kernel_spmd

F32 = mybir.dt.float32
BF16 = mybir.dt.bfloat16
AF = mybir.ActivationFunctionType
ALU = mybir.AluOpType
AX = mybir.AxisListType

SEM_ROLL = 30000


class Buf:
    def __init__(self, name, t=None):
        self.name = name
        self.t = t
        self.last_w = None
        self.readers = []
        self.dma_sem = None
        self.dma_cnt = 0

    def ap(self):
        return self.t[:]

    def __getitem__(self, idx):
        return self.t[idx]


class _Rec:
    def __getattr__(self, name):
        return lambda *a, **k: (name, a, k)


_REC = _Rec()


def _bind(fn):
    name, a, k = fn(_REC)
    return lambda e: getattr(e, name)(*a, **k)


class Op:
    __slots__ = ("eng", "fn", "deps", "signal", "token", "is_dma", "sem_buf", "final", "idx")


class Prog:
    ENGS = ("pe", "act", "dve", "pool", "sp")

    def __init__(self, nc):
        self.nc = nc
        self.stack = contextlib.ExitStack()
        self.ops = {e: [] for e in self.ENGS}
        self.nbuf = 0
        self.all_ops = []
        self.final_ops = []

    def sbuf(self, name, shape, dtype):
        self.nbuf += 1
        name = "s%d_%s" % (self.nbuf, name)
        t = self.stack.enter_context(self.nc.sbuf_tensor(name, list(shape), dtype))
        b = Buf(name, t)
        b.readers = list(getattr(self, "fence", []))
        if hasattr(self, "scope_bufs") and self.scope_bufs:
            self.scope_bufs[-1].append(b)
        return b

    def psum(self, name, shape, dtype):
        t = self.stack.enter_context(self.nc.psum_tensor(name, list(shape), dtype))
        return Buf(name, t)

    def view(self, name):
        return Buf(name)

    @contextlib.contextmanager
    def scope(self):
        old = self.stack
        self.stack = contextlib.ExitStack()
        if not hasattr(self, "scope_bufs"):
            self.scope_bufs = []
        self.scope_bufs.append([])
        try:
            yield
        finally:
            self.stack.close()
            self.stack = old
            bufs = self.scope_bufs.pop()
            ops = list(getattr(self, "fence", []))
            for b in bufs:
                if b.last_w is not None:
                    ops.append(b.last_w)
                ops.extend(b.readers)
            best = {}
            dmas = {}
            for o in ops:
                if o.is_dma:
                    dmas[id(o)] = o
                else:
                    if o.eng not in best or best[o.eng].idx < o.idx:
                        best[o.eng] = o
            self.fence = list(best.values()) + list(dmas.values())

    def _deps(self, o, reads, writes):
        deps = []
        for b in list(reads) + list(writes):
            if b.last_w is not None:
                deps.append(b.last_w)
        for b in writes:
            deps.extend(b.readers)
        for b in writes:
            b.last_w = o
            b.readers = []
        for b in reads:
            b.readers.append(o)
        seen = set()
        out = []
        for d in deps:
            if id(d) in seen or d is o:
                continue
            seen.add(id(d))
            if d.eng == "pe" and o.eng == "pe" and not d.is_dma and not o.is_dma:
                continue
            d.signal = True
            out.append(d)
        o.deps = out

    def op(self, eng, fn, reads=(), writes=()):
        o = Op()
        o.eng = eng
        o.fn = _bind(fn)
        o.signal = False
        o.token = None
        o.is_dma = False
        o.sem_buf = None
        o.final = False
        o.idx = len(self.ops[eng])
        self._deps(o, reads, writes)
        self.ops[eng].append(o)
        return o

    def dma(self, eng, out, in_, reads=(), writes=(), final=False):
        o = Op()
        o.eng = eng
        o.fn = lambda e: e.dma_start(out=out, in_=in_)
        o.signal = True
        o.token = None
        o.is_dma = True
        o.final = final
        cands = [b for b in list(writes) + list(reads) if b.t is not None]
        sb = cands[0] if cands else (list(writes) + list(reads))[0]
        o.sem_buf = sb
        o.idx = len(self.ops[eng])
        self._deps(o, reads, writes)
        self.ops[eng].append(o)
        if final:
            self.final_ops.append(o)
        return o

    def emit(self):
        nc = self.nc
        st = self.stack
        eng_sems = {}
        for e in self.ENGS:
            n = 0
            for o in self.ops[e]:
                if o.is_dma:
                    b = o.sem_buf
                    if b.dma_sem is None:
                        b.dma_sem = st.enter_context(nc.semaphore("d_" + b.name))
                    b.dma_cnt += 16
                    o.token = (b.dma_sem, b.dma_cnt, 16)
                elif o.signal:
                    k = n // SEM_ROLL
                    if (e, k) not in eng_sems:
                        eng_sems[(e, k)] = st.enter_context(nc.semaphore("s_%s_%d" % (e, k)))
                    o.token = (eng_sems[(e, k)], n % SEM_ROLL + 1, 1)
                    n += 1
        print("ops:", {e: len(self.ops[e]) for e in self.ENGS}, "signals:", {e: sum(1 for o in self.ops[e] if o.token is not None) for e in self.ENGS}, "nsems", len(eng_sems))
        all_sems = list(eng_sems.values())
        seen_b = set()
        for e in self.ENGS:
            for o in self.ops[e]:
                if o.is_dma and id(o.sem_buf) not in seen_b:
                    seen_b.add(id(o.sem_buf))
                    all_sems.append(o.sem_buf.dma_sem)
        with nc.Block() as blk0:
            @blk0.sync
            def _(eng):
                for sm in all_sems:
                    eng.sem_clear(sm)
        block = st.enter_context(nc.Block())
        hooks = {"pe": block.tensor, "act": block.scalar, "dve": block.vector,
                 "pool": block.gpsimd, "sp": block.sync}
        final_ops = self.final_ops

        def make(e):
            ops = self.ops[e]

            def body(eng):
                waited = {}
                for o in ops:
                    for d in o.deps:
                        sem, val, _ = d.token
                        if waited.get(id(sem), 0) < val:
                            eng.wait_ge(sem, val)
                            waited[id(sem)] = val
                    ins = o.fn(eng)
                    if o.token is not None:
                        ins.then_inc(o.token[0], o.token[2])
                if e == "sp":
                    for o in final_ops:
                        sem, val, _ = o.token
                        eng.wait_ge(sem, val)
            return body

        for e in self.ENGS:
            hooks[e](make(e))
        st.close()


D_MODEL = 1024
T_LAT = 4096
T_CTX = 256
NB = 256
NLB = T_LAT // NB
D_FF = 2816
EXPM05 = float(np.exp(-0.5))


class Ctx:
    pass


def build_program(nc, stage=99):
    P = Prog(nc)
    C = Ctx()
    C.P = P
    C.nc = nc
    dr = {}

    def din(name, shape, dt=F32):
        dr[name] = nc.dram_tensor(name, list(shape), dt, kind="ExternalInput").ap()

    din("xT", [1024, T_LAT]); din("ctxT", [1024, T_CTX]); din("cc", [128, 8, 2])
    din("w_mod", [1024, 6144]); din("b_modT", [128, 48]); din("gvec", [128, 4, 8])
    din("w_in", [1024, 3232]); din("w_qks", [1024, 1024]); din("convT", [1696, 3])
    din("lw2", [64, 2, 512]); din("w0a0", [128, 4, 4]); din("kvec", [128, 5, 4]); din("g2", [96, 512])
    din("lam", [128, 4, 64]); din("subln", [128, 1])
    din("w_out", [1024, 1024]); din("w_up", [1024, 5632]); din("fconvT", [5632, 3]); din("fbias", [128, 44])
    din("w_down", [2816, 1024]); din("ropeC", [128, T_LAT]); din("ropeS", [128, T_LAT]); din("blkmask", [128, 5, 128])
    dr["outT"] = nc.dram_tensor("outT", [1024, T_LAT], F32, kind="ExternalOutput").ap()
    dr["yf"] = nc.dram_tensor("yf_scr", [4, 128, T_LAT], F32).ap()
    dr["mt"] = nc.dram_tensor("mt_scr", [8, 128, T_LAT], BF16).ap()
    dr["x1s"] = nc.dram_tensor("x1_scr", [8, 128, T_LAT], F32).ap()
    dr["h2s"] = nc.dram_tensor("h2_scr", [8, 128, T_LAT], BF16).ap()
    if stage < 99:
        dr["dbg"] = nc.dram_tensor("dbg", [128, 8, T_LAT + 2], F32, kind="ExternalOutput").ap()
    C.dr = dr
    C.yf_bufs = {}
    C.mt_bufs = {}

    C.banks = [P.psum("pb%d" % i, [128, 512], F32) for i in range(7)]
    C.bankT = P.psum("pbT", [128, 1024], BF16)
    C.bank_i = 0

    def nb():
        b = C.banks[C.bank_i % 7]
        C.bank_i += 1
        return b
    C.nb = nb

    ident = P.sbuf("ident", [128, 128], BF16)
    ones = P.sbuf("ones", [128, 128], BF16)
    bones = P.sbuf("bones", [128, 128], BF16)
    bmean = P.sbuf("bmean", [128, 128], BF16)
    mean128 = P.sbuf("mean128", [128, 128], BF16)
    eps6 = P.sbuf("eps6", [128, 1], F32)
    epsx = P.sbuf("epsx", [128, 1], F32)
    epss = P.sbuf("epss", [128, 1], F32)
    P.op("pool", lambda e: e.memset(ident.ap(), 0.0), writes=[ident])
    P.op("pool", lambda e: e.affine_select(out=ident.ap(), in_=ident.ap(), pattern=[[-1, 128]],
                                           compare_op=ALU.not_equal, fill=1.0, base=0, channel_multiplier=1),
         reads=[ident], writes=[ident])
    P.op("pool", lambda e: e.memset(ones.ap(), 1.0), writes=[ones])
    P.op("pool", lambda e: e.memset(mean128.ap(), 1.0 / 128), writes=[mean128])
    P.op("pool", lambda e: e.memset(bones.ap(), 0.0), writes=[bones])
    P.op("pool", lambda e: e.memset(bones[0:64, 0:64], 1.0), writes=[bones])
    P.op("pool", lambda e: e.memset(bones[64:128, 64:128], 1.0), writes=[bones])
    P.op("pool", lambda e: e.memset(bmean.ap(), 0.0), writes=[bmean])
    P.op("pool", lambda e: e.memset(bmean[0:64, 0:64], 1.0 / 64), writes=[bmean])
    P.op("pool", lambda e: e.memset(bmean[64:128, 64:128], 1.0 / 64), writes=[bmean])
    P.op("pool", lambda e: e.memset(eps6.ap(), 1e-6), writes=[eps6])
    P.op("pool", lambda e: e.memset(epsx.ap(), 64e-5), writes=[epsx])
    P.op("pool", lambda e: e.memset(epss.ap(), 1e-5), writes=[epss])
    C.ident, C.ones, C.bones, C.bmean, C.mean128 = ident, ones, bones, bmean, mean128
    C.eps6, C.epsx, C.epss = eps6, epsx, epss

    stage_mod(C)
    import os
    with P.scope():
        C.HT = P.sbuf("HT", [128, 8, T_LAT + 2], BF16)
        C.HC = P.sbuf("HC", [128, 8, T_CTX + 2], BF16)
        for Hb, n in ((C.HT, T_LAT), (C.HC, T_CTX)):
            P.op("pool", lambda e, Hb=Hb: e.memset(Hb[:, :, 0:1], 0.0), writes=[Hb])
            P.op("pool", lambda e, Hb=Hb, n=n: e.memset(Hb[:, :, n + 1:n + 2], 0.0), writes=[Hb])
        stage_prenorm(C)
        if stage == 1:
            dbg_dump_HT(C)
            P.emit()
            return nc
        if not os.environ.get("SKIP_RWKV"):
            stage_rwkv(C)
        if stage == 21:
            with P.scope():
                for k in range(4):
                    tmp = P.sbuf("dbgy%d" % k, [128, T_LAT], F32)
                    P.dma("sp", tmp.ap(), dr["yf"][k], reads=list(C.yf_bufs.values()), writes=[tmp])
                    P.dma("sp", dr["dbg"][:, k, 0:T_LAT], tmp.ap(), reads=[tmp], final=True)
            P.emit()
            return nc
        if stage == 2:
            dbg_dump_mt(C, 0, 4)
            P.emit()
            return nc
        stage_attn(C)
        if stage == 3:
            dbg_dump_mt(C, 4, 8)
            P.emit()
            return nc
        stage_out_a(C)
    stage_ffn(C)
    P.emit()
    return nc


def dbg_dump_HT(C):
    P = C.P
    with P.scope():
        for k in range(8):
            tmp = P.sbuf("dbgt%d" % k, [128, T_LAT + 2], F32)
            P.op("dve", lambda e, k=k, tmp=tmp: e.tensor_copy(out=tmp.ap(), in_=C.HT[:, k, :]), reads=[C.HT], writes=[tmp])
            P.dma("sp", C.dr["dbg"][:, k, :], tmp.ap(), reads=[tmp], final=True)


def dbg_dump_mt(C, k0, k1):
    P = C.P
    with P.scope():
        for k in range(k0, k1):
            tb = P.sbuf("dbgb%d" % k, [128, T_LAT], BF16)
            tmp = P.sbuf("dbgt%d" % k, [128, T_LAT], F32)
            rd = [b for (kk, _), b in C.mt_bufs.items() if kk == k] + list(C.yf_bufs.values())
            P.dma("sp", tb.ap(), C.dr["mt"][k], reads=rd, writes=[tb])
            P.op("dve", lambda e, tmp=tmp, tb=tb: e.tensor_copy(out=tmp.ap(), in_=tb.ap()), reads=[tb], writes=[tmp])
            P.dma("sp", C.dr["dbg"][:, k, 0:T_LAT], tmp.ap(), reads=[tmp], final=True)


def stage_mod(C):
    P, dr = C.P, C.dr
    cc = P.sbuf("cc", [128, 8, 2], F32)
    scc = P.sbuf("scc", [128, 8, 2], F32)
    bm = P.sbuf("bmodT", [128, 48], F32)
    gv = P.sbuf("gvec", [128, 4, 8], F32)
    modT = P.sbuf("modT", [128, 48, 2], F32)
    P.dma("sp", cc.ap(), dr["cc"], writes=[cc])
    P.dma("sp", bm.ap(), dr["b_modT"], writes=[bm])
    P.dma("sp", gv.ap(), dr["gvec"], writes=[gv])
    P.op("act", lambda e: e.activation(out=scc.ap(), in_=cc.ap(), func=AF.Silu), reads=[cc], writes=[scc])
    pm = C.nb()
    with P.scope():
        wst = [P.sbuf("wmst%d" % i, [128, 8, 512], F32) for i in range(2)]
        for g in range(12):
            w = wst[g % 2]
            P.dma("sp", w.ap(), dr["w_mod"][:, g * 512:(g + 1) * 512].rearrange("(k p) n -> p k n", p=128), writes=[w])
            for jj in range(4):
                j = g * 4 + jj
                for k in range(8):
                    P.op("pe", lambda e, w=w, k=k, jj=jj, j=j: e.matmul(
                        pm[:, 2 * j:2 * j + 2], lhsT=w[:, k, jj * 128:(jj + 1) * 128], rhs=scc[:, k, :],
                        start=(k == 0), stop=(k == 7)), reads=[w, scc], writes=[pm])
    pmv = pm[:, 0:96].rearrange("p (j c) -> p j c", c=2)
    for c in range(2):
        P.op("dve", lambda e, c=c: e.tensor_tensor(out=modT[:, :, c], in0=pmv[:, :, c], in1=bm.ap(), op=ALU.add),
             reads=[bm], writes=[pm, modT])
    C.modT = modT
    A1 = P.sbuf("A1", [128, 2, 8], F32)
    G1 = P.sbuf("G1", [128, 8], F32)
    A2 = P.sbuf("A2", [128, 8], F32)
    G2 = P.sbuf("G2", [128, 8], F32)
    for c in range(2):
        P.op("dve", lambda e, c=c: e.scalar_tensor_tensor(out=A1[:, c, :], in0=modT[:, 8:16, c], scalar=1.0,
                                                          in1=gv[:, 0, :], op0=ALU.add, op1=ALU.mult),
             reads=[modT, gv], writes=[A1])
    P.op("dve", lambda e: e.tensor_tensor(out=G1.ap(), in0=modT[:, 16:24, 0], in1=gv[:, 1, :], op=ALU.mult),
         reads=[modT, gv], writes=[G1])
    P.op("dve", lambda e: e.scalar_tensor_tensor(out=A2.ap(), in0=modT[:, 32:40, 0], scalar=1.0, in1=gv[:, 2, :],
                                                 op0=ALU.add, op1=ALU.mult), reads=[modT, gv], writes=[A2])
    P.op("dve", lambda e: e.tensor_tensor(out=G2.ap(), in0=modT[:, 40:48, 0], in1=gv[:, 3, :], op=ALU.mult),
         reads=[modT, gv], writes=[G2])
    C.A1, C.G1, C.A2, C.G2 = A1, G1, A2, G2


def norm_block(C, xt, nbk, A_ap, sh_ap, out_ap, xt_reads, out_buf, tmpn):
    P = C.P
    sq, rs, tmp = tmpn
    P.op("act", lambda e: e.activation(out=sq[:, :, 0:nbk], in_=xt[:, :, 0:nbk], func=AF.Square),
         reads=[xt], writes=[sq])
    pb = C.nb()
    for k in range(8):
        P.op("pe", lambda e, k=k: e.matmul(pb[:, 0:nbk], lhsT=C.ones.ap(), rhs=sq[:, k, 0:nbk],
                                            start=(k == 0), stop=(k == 7)), reads=[sq, C.ones], writes=[pb])
    P.op("act", lambda e: e.activation(out=rs[:, 0:nbk], in_=pb[:, 0:nbk], func=AF.Sqrt, scale=1.0 / D_MODEL,
                                       bias=C.eps6.ap()), reads=[C.eps6], writes=[pb, rs])
    P.op("dve", lambda e: e.reciprocal(out=rs[:, 0:nbk], in_=rs[:, 0:nbk]), writes=[rs])
    for k in range(8):
        P.op("dve", lambda e, k=k: e.scalar_tensor_tensor(out=tmp[:, k, 0:nbk], in0=xt[:, k, 0:nbk], scalar=A_ap(k),
                                                          in1=rs[:, 0:nbk], op0=ALU.mult, op1=ALU.mult),
             reads=[xt, rs] + xt_reads, writes=[tmp])
        P.op("act", lambda e, k=k: e.activation(out=out_ap(k), in_=tmp[:, k, 0:nbk], func=AF.Identity,
                                                bias=sh_ap(k)), reads=[tmp] + xt_reads, writes=[out_buf])


def stage_prenorm(C):
    P, dr = C.P, C.dr
    with P.scope():
        xts = [P.sbuf("xt%d" % i, [128, 8, NB], F32) for i in range(2)]
        sq = P.sbuf("pn_sq", [128, 8, NB], BF16)
        rs = P.sbuf("pn_rs", [128, NB], F32)
        tmp = P.sbuf("pn_tmp", [128, 8, NB], F32)
        blocks = [("c", 0)] + [("l", i) for i in range(NLB)]
        for bi, (src, i) in enumerate(blocks):
            xt = xts[bi % 2]
            if src == "c":
                P.dma("sp", xt.ap(), dr["ctxT"].rearrange("(k p) t -> p k t", p=128), writes=[xt])
                H, cidx, t0 = C.HC, 1, 0
            else:
                P.dma("sp", xt.ap(), dr["xT"][:, i * NB:(i + 1) * NB].rearrange("(k p) t -> p k t", p=128), writes=[xt])
                H, cidx, t0 = C.HT, 0, i * NB
            norm_block(C, xt, NB, lambda k, cidx=cidx: C.A1[:, cidx, k:k + 1],
                       lambda k, cidx=cidx: C.modT[:, k, cidx:cidx + 1],
                       lambda k, H=H, t0=t0: H[:, k, 1 + t0:1 + t0 + NB], [C.A1, C.modT], H, (sq, rs, tmp))


def _rope_tables():
    n_pair = 16
    rows = T_LAT // 64
    row = np.repeat(np.arange(rows, dtype=np.float32), 64)
    col = np.tile(np.arange(64, dtype=np.float32), rows)
    inv = (np.float32(10000.0) ** (-np.arange(n_pair, dtype=np.float32) / np.float32(n_pair))).astype(np.float32)
    ang = np.concatenate([row[:, None] * inv, col[:, None] * inv], axis=-1).astype(np.float32)
    cos = np.cos(ang).astype(np.float32).T
    sin = np.sin(ang).astype(np.float32).T
    Cc = np.concatenate([cos, cos, cos, cos], axis=0)
    Ss = np.concatenate([-sin, sin, -sin, sin], axis=0)
    return np.ascontiguousarray(Cc), np.ascontiguousarray(Ss)


def prep_inputs(inp):
    f = lambda a: np.ascontiguousarray(np.asarray(a, dtype=np.float32))
    x, c, ctx, c_ctx = f(inp["x"]), f(inp["c"]), f(inp["ctx"]), f(inp["c_ctx"])
    pk = lambda v, n: f(v.reshape(n, 128).T)
    sh = {}
    sh["w_mod"] = f(inp["w_mod"][0])
    sh["b_modT"] = pk(f(inp["b_mod"][0]), 48)
    sh["gvec"] = f(np.stack([pk(f(inp[n][0]), 8) for n in ("g_pre_mix", "g_post_mix", "g_pre_ffn", "g_post_ffn")], axis=1))
    w_in = f(inp["w_in"][0])
    sh["w_in"] = w_in
    perm = np.arange(512).reshape(8, 2, 32)[:, ::-1, :].reshape(512)
    qc, kc = 1696, 1696 + 512
    sh["w_qks"] = f(np.concatenate([w_in[:, qc:qc + 512][:, perm], w_in[:, kc:kc + 512][:, perm]], axis=1))
    sh["convT"] = f(inp["rwkv_conv"][0].T)
    sh["lw2"] = f(np.stack([np.concatenate([f(inp["w2_fwd"][0]), f(inp["a2_fwd"][0])], 0),
                            np.concatenate([f(inp["w2_bwd"][0]), f(inp["a2_bwd"][0])], 0)], axis=1))
    sh["w0a0"] = f(np.stack([pk(f(inp[n][0]), 4) for n in ("w0_fwd", "w0_bwd", "a0_fwd", "a0_bwd")], axis=1))
    sh["kvec"] = f(np.stack([pk(f(inp[n][0]).reshape(512), 4) for n in ("k_k", "k_a", "r_k", "ln_x_w", "ln_x_b")], axis=1))
    sh["g2"] = f(inp["g2"][0])
    sh["lam"] = f(np.broadcast_to(np.stack([f(inp[n][0]) for n in ("lam_q1", "lam_k1", "lam_q2", "lam_k2")], 0)[None], (128, 4, 64)))
    sh["subln"] = f(inp["subln_w"][0].reshape(128, 1))
    sh["w_out"] = f(inp["w_out"][0])
    sh["w_up"] = f(inp["w_up"][0])
    sh["fconvT"] = f(inp["ffn_conv"][0].T)
    sh["fbias"] = pk(f(inp["ffn_conv_b"][0]), 44)
    sh["w_down"] = f(inp["w_down"][0])
    sh["ropeC"], sh["ropeS"] = _rope_tables()
    ii = np.arange(128)
    dm = lambda sz: (ii[:, None] // sz == ii[None, :] // sz).astype(np.float32)
    sh["blkmask"] = f(np.stack([dm(8), dm(16) - dm(8), dm(32) - dm(16), dm(64) - dm(32), dm(128) - dm(64)], axis=1))
    maps = []
    for b in range(8):
        m = dict(sh)
        m["xT"] = f(x[b].T)
        m["ctxT"] = f(ctx[b].T)
        m["cc"] = f(np.stack([c[b], c_ctx], -1).reshape(8, 128, 2).transpose(1, 0, 2))
        maps.append(m)
    return maps


def kernel(**inputs):
    maps = prep_inputs(inputs)
    nc = bass.Bass("TRN2", target_bir_lowering=False)
    build_program(nc)
    res = run_bass_kernel_spmd(nc, maps, core_ids=list(range(8)))
    out = np.stack([np.ascontiguousarray(res.results[b]["outT"].T) for b in range(8)], 0)
    return out.astype(np.float32)


def load_w_bf16(C, dst, dram_cols_ap, ncols, stg):
    P = C.P
    for i, c0 in enumerate(range(0, ncols, 256)):
        c1 = min(ncols, c0 + 256)
        s = stg[C.stg_i % 2]
        C.stg_i += 1
        P.dma("sp", s[:, :, 0:c1 - c0], dram_cols_ap[:, c0:c1].rearrange("(k p) n -> p k n", p=128), writes=[s])
        ce = ("pool", "act", "dve")[C.stg_i % 3]
        if ce == "act":
            P.op("act", lambda e, s=s, c0=c0, c1=c1: e.activation(out=dst[0][:, :, dst[1] + c0:dst[1] + c1], in_=s[:, :, 0:c1 - c0], func=AF.Copy),
                 reads=[s], writes=[dst[0]])
        else:
            P.op(ce, lambda e, s=s, c0=c0, c1=c1: e.tensor_copy(out=dst[0][:, :, dst[1] + c0:dst[1] + c1], in_=s[:, :, 0:c1 - c0]),
                 reads=[s], writes=[dst[0]])


def proj_conv(C, H, t0, W, c0, m, cw, out, out_buf, bias=None, eng2="dve", bias_buf=None):
    P = C.P
    pb = C.nb()
    for k in range(8):
        P.op("pe", lambda e, k=k: e.matmul(pb[0:m, 0:NB + 2], lhsT=W[:, k, c0:c0 + m], rhs=H[:, k, t0:t0 + NB + 2],
                                            start=(k == 0), stop=(k == 7)), reads=[W, H], writes=[pb])
    if bias is None:
        P.op("act", lambda e: e.activation(out=out, in_=pb[0:m, 1:NB + 1], func=AF.Copy, scale=cw[:, 1:2]),
             reads=[C.cwb], writes=[pb, out_buf])
    else:
        P.op("act", lambda e: e.activation(out=out, in_=pb[0:m, 1:NB + 1], func=AF.Identity, scale=cw[:, 1:2], bias=bias),
             reads=[C.cwb, bias_buf], writes=[pb, out_buf])
    P.op(eng2, lambda e: e.scalar_tensor_tensor(out=out, in0=pb[0:m, 0:NB], scalar=cw[:, 0:1], in1=out,
                                                op0=ALU.mult, op1=ALU.add), reads=[C.cwb], writes=[pb, out_buf])
    P.op(eng2, lambda e: e.scalar_tensor_tensor(out=out, in0=pb[0:m, 2:NB + 2], scalar=cw[:, 2:3], in1=out,
                                                op0=ALU.mult, op1=ALU.add), reads=[C.cwb], writes=[pb, out_buf])


def stage_rwkv(C):
    P, dr = C.P, C.dr
    C.stg_i = 0
    with P.scope():
        WR = P.sbuf("WR", [128, 8, 1696], BF16)
        with P.scope():
            stg = [P.sbuf("wstg%d" % i, [128, 8, 256], F32) for i in range(2)]
            load_w_bf16(C, (WR, 0), dr["w_in"][:, 0:1696], 1696, stg)
        CW = P.sbuf("CW", [128, 14, 3], F32)
        C.cwb = CW
        P.dma("sp", CW[:, 0:12, :], dr["convT"][0:1536, :].rearrange("(c p) j -> p c j", p=128), writes=[CW])
        P.dma("sp", CW[0:64, 12, :], dr["convT"][1536:1600, :], writes=[CW])
        P.dma("sp", CW[0:96, 13, :], dr["convT"][1600:1696, :], writes=[CW])
        LW2 = P.sbuf("LW2", [64, 2, 512], BF16)
        G2W = P.sbuf("G2W", [96, 512], BF16)
        with P.scope():
            lw2f = P.sbuf("lw2f", [64, 2, 512], F32)
            P.dma("sp", lw2f.ap(), dr["lw2"], writes=[lw2f])
            P.op("pool", lambda e: e.tensor_copy(out=LW2.ap(), in_=lw2f.ap()), reads=[lw2f], writes=[LW2])
            g2f = P.sbuf("g2f", [96, 512], F32)
            P.dma("sp", g2f.ap(), dr["g2"], writes=[g2f])
            P.op("pool", lambda e: e.tensor_copy(out=G2W.ap(), in_=g2f.ap()), reads=[g2f], writes=[G2W])
        w0a0 = P.sbuf("w0a0", [128, 4, 4], F32)
        kvec = P.sbuf("kvec", [128, 5, 4], F32)
        P.dma("sp", w0a0.ap(), dr["w0a0"], writes=[w0a0])
        P.dma("sp", kvec.ap(), dr["kvec"], writes=[kvec])
        omka = P.sbuf("omka", [128, 4], F32)
        rkh = P.sbuf("rkh", [128, 4], F32)
        P.op("dve", lambda e: e.tensor_scalar(out=omka.ap(), in0=kvec[:, 1, :], scalar1=-1.0, scalar2=1.0,
                                              op0=ALU.mult, op1=ALU.add), reads=[kvec], writes=[omka])
        P.op("dve", lambda e: e.tensor_scalar(out=rkh.ap(), in0=kvec[:, 2, :], scalar1=0.5, scalar2=None,
                                              op0=ALU.mult), reads=[kvec], writes=[rkh])
        AMM = [P.sbuf("amm%d" % d, [128, 4, 128], F32) for d in range(2)]
        with P.scope():
            ones32 = P.sbuf("ones32", [128, 128], F32)
            P.op("pool", lambda e: e.memset(ones32.ap(), 1.0), writes=[ones32])
            msk = {}
            for nm, cop, sgn in (("SU", ALU.is_gt, -1), ("IU", ALU.is_ge, -1), ("SL", ALU.is_gt, 1), ("IL", ALU.is_ge, 1)):
                mb = P.sbuf("m" + nm, [128, 128], F32)
                P.op("pool", lambda e, mb=mb, cop=cop, sgn=sgn: e.affine_select(out=mb.ap(), in_=ones32.ap(), pattern=[[-sgn, 128]],
                                                                                compare_op=cop, fill=0.0, base=0, channel_multiplier=sgn),
                     reads=[ones32], writes=[mb])
                msk[nm] = mb
            for d, (s_, i_) in enumerate((("SU", "IU"), ("SL", "IL"))):
                am = AMM[d]
                for q, nm in enumerate((s_, i_, s_, i_)):
                    P.op("pool", lambda e, am=am, q=q, nm=nm: e.tensor_copy(out=am[:, q, :], in_=msk[nm].ap()),
                         reads=[msk[nm]], writes=[am])
        NTMap = [AMM[1][:, 0, :], AMM[0][:, 0, :]]
        NTMb = [AMM[1], AMM[0]]
        rmask = P.sbuf("rmask", [128, NB], F32)
        P.op("pool", lambda e: e.memset(rmask.ap(), 1.0), writes=[rmask])
        for c in range(NB // 128):
            P.op("pool", lambda e, c=c: e.memset(rmask[:, c * 128:c * 128 + 1], 0.0), writes=[rmask])

        def f32t(n, shape=(128, NB)):
            return P.sbuf(n, list(shape), F32)

        def b16t(n, shape=(128, NB)):
            return P.sbuf(n, list(shape), BF16)

        la32, LA16 = f32t("la32", (64, NB)), b16t("LA16", (64, NB))
        gl32, sg16 = f32t("gl32", (96, NB)), b16t("sg16", (96, NB))
        BM2 = P.sbuf("blkmask2", [128, 5, 2, 128], BF16)
        with P.scope():
            bmf = P.sbuf("blkmaskf", [128, 5, 128], F32)
            P.dma("sp", bmf.ap(), dr["blkmask"], writes=[bmf])
            for q_ in range(2):
                P.op("pool", lambda e, q_=q_: e.tensor_copy(out=BM2[:, :, q_, :], in_=bmf.ap()), reads=[bmf], writes=[BM2])
        I2 = P.sbuf("ident2", [128, 2, 128], BF16)
        for q_ in range(2):
            P.op("pool", lambda e, q_=q_: e.tensor_copy(out=I2[:, q_, :], in_=C.ident.ap()), reads=[C.ident], writes=[I2])
        S32 = [[f32t("S32_%d_%d" % (d, p), (128, 64)) for p in range(4)] for d in range(2)]
        Sb = [[[b16t("Sb_%d_%d_%d" % (d, p, i), (128, 64)) for i in range(2)] for p in range(4)] for d in range(2)]
        sbi = [[0] * 4 for _ in range(2)]
        def make_set(si):
            sfx = "q%d_" % si
            o_ = Ctx()
            r32, k32, v32, v16 = f32t(sfx + "r32"), f32t(sfx + "k32"), f32t(sfx + "v32"), b16t(sfx + "v16")
            sig, aa, kraw, ksq, rn = f32t(sfx + "sig"), f32t(sfx + "aa"), f32t(sfx + "kraw"), b16t(sfx + "ksq"), f32t(sfx + "rn")
            fac, kdir, bb, cs, LL, Lm = f32t(sfx + "fac"), f32t(sfx + "kdir"), f32t(sfx + "bb"), f32t(sfx + "cs"), f32t(sfx + "LL"), f32t(sfx + "Lm")
            gg, ginv, gprev = f32t(sfx + "gg"), f32t(sfx + "ginv"), f32t(sfx + "gprev")
            ld, kk = sig, kraw
            yc, rstd, yn, rk, bonus, a2_, kdir2 = LL, rn, Lm, ginv, gprev, bb, kdir
            Bt, Kt = b16t(sfx + "Bt"), b16t(sfx + "Kt")
            KR = P.sbuf(sfx + "KR", [128, NB // 128, 2, 128], BF16)
            Bg, Kg = b16t(sfx + "Bg", (128, 128)), b16t(sfx + "Kg", (128, 128))
            TT = [P.sbuf(sfx + "TT%d" % ci, [128, 3, 128], BF16) for ci in range(2)]
            AMh = [[P.sbuf(sfx + "AM%d_%d" % (ci, h), [128, 4, 128], BF16) for h in range(2)] for ci in range(2)]
            NN = [[[P.sbuf(sfx + "NN%d_%d_%d" % (ci, h, i), [128, 2, 128], BF16) for i in range(1)] for h in range(2)] for ci in range(2)]
            PP = None
            Tb = [[P.sbuf(sfx + "Tb%d_%d" % (ci, h), [128, 128], BF16) for h in range(2)] for ci in range(2)]
            IV = [[dict((nm, P.sbuf(sfx + "iv%s_%d_%d" % (nm, ci, h), [128, 2, 128], BF16)) for nm in ("Nb2", "S2", "S4", "X2", "Pa", "Pb"))
                   for h in range(2)] for ci in range(2)]
            NZ, UT = b16t(sfx + "NZ", (128, 128)), b16t(sfx + "UT", (128, 128))
            ys32, yfl = f32t(sfx + "ys32"), f32t(sfx + "yfl")
            ys16, yc2 = b16t(sfx + "ys16"), b16t(sfx + "yc2")
            rk16, o16 = b16t(sfx + "rk16"), b16t(sfx + "o16")
            for _n in ['r32', 'k32', 'v32', 'v16', 'sig', 'ld', 'aa', 'a2_', 'kraw', 'ksq', 'rn', 'kk', 'fac', 'kdir', 'kdir2', 'bb', 'cs', 'LL', 'Lm', 'gg', 'ginv', 'gprev', 'Bt', 'Kt', 'KR', 'Bg', 'Kg', 'TT', 'AMh', 'NN', 'PP', 'Tb', 'IV', 'NZ', 'UT', 'ys32', 'yfl', 'ys16', 'yc', 'yc2', 'rstd', 'yn', 'rk', 'rk16', 'bonus', 'o16']:
                setattr(o_, _n, locals()[_n])
            return o_

        TS = [make_set(0), make_set(1)]
        for d in range(2):
            for p in range(4):
                P.op("pool", lambda e, d=d, p=p: e.memset(S32[d][p].ap(), 0.0), writes=[S32[d][p]])
                P.op("pool", lambda e, d=d, p=p: e.memset(Sb[d][p][0].ap(), 0.0), writes=[Sb[d][p][0]])

        NCH = NB // 128
        import os
        lim = int(os.environ.get("RWKV_LIM", "999"))
        cnt = 0
        for d in range(2):
            blocks = [("c", 0)] + ([("l", i) for i in range(NLB)] if d == 0 else [("l", i) for i in reversed(range(NLB))])
            for (src, bi) in blocks:
                cnt += 1
                if cnt > lim:
                    continue
                if os.environ.get("RWKV_SKIPF") and d == 0 and src == "l":
                    continue
                H = C.HC if src == "c" else C.HT
                t0 = 0 if src == "c" else bi * NB
                lat = (src == "l")
                proj_conv(C, H, t0, WR, 1536, 64, CW[0:64, 12, :], la32.ap(), la32)
                P.op("act", lambda e: e.activation(out=LA16[0:32, :], in_=la32[0:32, :], func=AF.Tanh), reads=[la32], writes=[LA16])
                P.op("pool", lambda e: e.tensor_copy(out=LA16[32:64, :], in_=la32[32:64, :]), reads=[la32], writes=[LA16])
                if d == 1 and lat:
                    proj_conv(C, H, t0, WR, 1600, 96, CW[0:96, 13, :], gl32.ap(), gl32)
                    P.op("act", lambda e: e.activation(out=sg16.ap(), in_=gl32.ap(), func=AF.Sigmoid), reads=[gl32], writes=[sg16])
                def pair_gen(p, ts_):
                    proj_conv(C, H, t0, WR, p * 128, 128, CW[:, p, :], ts_.r32.ap(), ts_.r32)
                    yield
                    proj_conv(C, H, t0, WR, 512 + p * 128, 128, CW[:, 4 + p, :], ts_.k32.ap(), ts_.k32)
                    yield
                    proj_conv(C, H, t0, WR, 1024 + p * 128, 128, CW[:, 8 + p, :], ts_.v32.ap(), ts_.v32)
                    yield
                    P.op("act", lambda e: e.activation(out=ts_.v16.ap(), in_=ts_.v32.ap(), func=AF.Copy), reads=[ts_.v32], writes=[ts_.v16])
                    pz = C.nb()
                    P.op("pe", lambda e, p=p, d=d: e.matmul(pz[:, 0:NB], lhsT=LW2[0:32, d, p * 128:(p + 1) * 128], rhs=LA16[0:32, :],
                                                             start=True, stop=True), reads=[LW2, LA16], writes=[pz])
                    P.op("act", lambda e, p=p, d=d: e.activation(out=ts_.sig.ap(), in_=pz[:, 0:NB], func=AF.Sigmoid, bias=w0a0[:, d, p:p + 1]),
                         reads=[w0a0], writes=[pz, ts_.sig])
                    P.op("pool", lambda e: e.tensor_scalar(out=ts_.ld.ap(), in0=ts_.sig.ap(), scalar1=-EXPM05, scalar2=None, op0=ALU.mult),
                         reads=[ts_.sig], writes=[ts_.ld])
                    pz = C.nb()
                    P.op("pe", lambda e, p=p, d=d, pz=pz: e.matmul(pz[:, 0:NB], lhsT=LW2[32:64, d, p * 128:(p + 1) * 128], rhs=LA16[32:64, :],
                                                                    start=True, stop=True), reads=[LW2, LA16], writes=[pz])
                    P.op("act", lambda e, p=p, d=d, pz=pz: e.activation(out=ts_.aa.ap(), in_=pz[:, 0:NB], func=AF.Sigmoid, bias=w0a0[:, 2 + d, p:p + 1]),
                         reads=[w0a0], writes=[pz, ts_.aa])
                    yield
                    P.op("act", lambda e, p=p: e.activation(out=ts_.kraw.ap(), in_=ts_.k32.ap(), func=AF.Copy, scale=kvec[:, 0, p:p + 1]),
                         reads=[ts_.k32, kvec], writes=[ts_.kraw])
                    P.op("act", lambda e: e.activation(out=ts_.ksq.ap(), in_=ts_.kraw.ap(), func=AF.Square), reads=[ts_.kraw], writes=[ts_.ksq])
                    pz = C.nb()
                    P.op("pe", lambda e, pz=pz: e.matmul(pz[:, 0:NB], lhsT=C.bones.ap(), rhs=ts_.ksq.ap(), start=True, stop=True),
                         reads=[C.bones, ts_.ksq], writes=[pz])
                    P.op("act", lambda e, pz=pz: e.activation(out=ts_.rn.ap(), in_=pz[:, 0:NB], func=AF.Sqrt), writes=[pz, ts_.rn])
                    P.op("dve", lambda e: e.tensor_scalar(out=ts_.rn.ap(), in0=ts_.rn.ap(), scalar1=1e-12, scalar2=None, op0=ALU.max), writes=[ts_.rn])
                    P.op("dve", lambda e: e.reciprocal(out=ts_.rn.ap(), in_=ts_.rn.ap()), writes=[ts_.rn])
                    P.op("dve", lambda e: e.tensor_tensor(out=ts_.kk.ap(), in0=ts_.kraw.ap(), in1=ts_.rn.ap(), op=ALU.mult), reads=[ts_.kraw, ts_.rn], writes=[ts_.kk])
                    yield
                    P.op("dve", lambda e, p=p: e.tensor_scalar(out=ts_.fac.ap(), in0=ts_.aa.ap(), scalar1=kvec[:, 1, p:p + 1], scalar2=omka[:, p:p + 1],
                                                               op0=ALU.mult, op1=ALU.add), reads=[ts_.aa, kvec, omka], writes=[ts_.fac])
                    P.op("pool", lambda e: e.tensor_tensor(out=ts_.kdir.ap(), in0=ts_.k32.ap(), in1=ts_.fac.ap(), op=ALU.mult), reads=[ts_.k32, ts_.fac], writes=[ts_.kdir])
                    P.op("pool", lambda e: e.tensor_tensor(out=ts_.bb.ap(), in0=ts_.kk.ap(), in1=ts_.aa.ap(), op=ALU.mult), reads=[ts_.kk, ts_.aa], writes=[ts_.bb])
                    P.op("dve", lambda e: e.tensor_tensor_scan(out=ts_.cs.ap(), data0=rmask.ap(), data1=ts_.ld.ap(), initial=0.0,
                                                               op0=ALU.mult, op1=ALU.add), reads=[rmask, ts_.ld], writes=[ts_.cs])
                    if d == 0:
                        Lb = ts_.cs
                    else:
                        for c in range(NCH):
                            P.op("dve", lambda e, c=c: e.tensor_scalar(out=ts_.LL[:, c * 128:(c + 1) * 128], in0=ts_.cs[:, c * 128:(c + 1) * 128],
                                                                       scalar1=-1.0, scalar2=ts_.cs[:, c * 128 + 127:c * 128 + 128],
                                                                       op0=ALU.mult, op1=ALU.add), reads=[ts_.cs], writes=[ts_.LL])
                        P.op("dve", lambda e: e.tensor_tensor(out=ts_.LL.ap(), in0=ts_.LL.ap(), in1=ts_.ld.ap(), op=ALU.add), reads=[ts_.ld], writes=[ts_.LL])
                        Lb = ts_.LL
                    P.op("pool", lambda e, Lb=Lb: e.tensor_tensor(out=ts_.Lm.ap(), in0=Lb.ap(), in1=ts_.ld.ap(), op=ALU.subtract), reads=[Lb, ts_.ld], writes=[ts_.Lm])
                    P.op("act", lambda e, Lb=Lb: e.activation(out=ts_.gg.ap(), in_=Lb.ap(), func=AF.Exp), reads=[Lb], writes=[ts_.gg])
                    P.op("act", lambda e, Lb=Lb: e.activation(out=ts_.ginv.ap(), in_=Lb.ap(), func=AF.Exp, scale=-1.0), reads=[Lb], writes=[ts_.ginv])
                    P.op("act", lambda e: e.activation(out=ts_.gprev.ap(), in_=ts_.Lm.ap(), func=AF.Exp), reads=[ts_.Lm], writes=[ts_.gprev])
                    yield
                    P.op("dve", lambda e: e.tensor_tensor(out=ts_.Bt.ap(), in0=ts_.bb.ap(), in1=ts_.ginv.ap(), op=ALU.mult), reads=[ts_.bb, ts_.ginv], writes=[ts_.Bt])
                    P.op("dve", lambda e: e.tensor_tensor(out=ts_.Kt.ap(), in0=ts_.kdir.ap(), in1=ts_.ginv.ap(), op=ALU.mult), reads=[ts_.kdir, ts_.ginv], writes=[ts_.Kt])
                    P.op("pool", lambda e: e.tensor_tensor(out=ts_.KR[:, :, 0, :], in0=ts_.kk.ap().rearrange("p (c t) -> p c t", t=128),
                                                          in1=ts_.gprev.ap().rearrange("p (c t) -> p c t", t=128), op=ALU.mult),
                         reads=[ts_.kk, ts_.gprev], writes=[ts_.KR])
                    P.op("pool", lambda e: e.tensor_tensor(out=ts_.KR[:, :, 1, :], in0=ts_.r32.ap().rearrange("p (c t) -> p c t", t=128),
                                                          in1=ts_.gg.ap().rearrange("p (c t) -> p c t", t=128), op=ALU.mult),
                         reads=[ts_.r32, ts_.gg], writes=[ts_.KR])
                    yield
                    if d == 1 and lat:
                        if (p, bi) in C.yf_bufs:
                            P.dma("sp", ts_.yfl.ap(), dr["yf"][p, :, t0:t0 + NB], reads=[C.yf_bufs[(p, bi)]], writes=[ts_.yfl])
                        else:
                            P.op("pool", lambda e: e.memset(ts_.yfl.ap(), 0.0), writes=[ts_.yfl])
                    chunks = list(range(NCH)) if d == 0 else list(reversed(range(NCH)))
                    for ci, c in enumerate(chunks):
                        csl = slice(c * 128, (c + 1) * 128)
                        gcol = c * 128 + 127 if d == 0 else c * 128
                        P.op("act", lambda e, csl=csl, gcol=gcol: e.activation(out=ts_.Bg.ap(), in_=ts_.Bt[:, csl], func=AF.Copy, scale=ts_.gg[:, gcol:gcol + 1]),
                             reads=[ts_.Bt, ts_.gg], writes=[ts_.Bg])
                        P.op("act", lambda e, csl=csl, gcol=gcol: e.activation(out=ts_.Kg.ap(), in_=ts_.Kt[:, csl], func=AF.Copy, scale=ts_.gg[:, gcol:gcol + 1]),
                             reads=[ts_.Kt, ts_.gg], writes=[ts_.Kg])
                        pT = C.bankT
                        P.op("pe", lambda e, csl=csl: e.transpose(pT[:, 0:128], ts_.v16[:, csl], C.ident.ap()), reads=[ts_.v16, C.ident], writes=[pT])
                        P.op("pe", lambda e: e.transpose(pT[:, 128:256], ts_.Bg.ap(), C.ident.ap()), reads=[ts_.Bg, C.ident], writes=[pT])
                        P.op("pe", lambda e: e.transpose(pT[:, 256:384], ts_.Kg.ap(), C.ident.ap()), reads=[ts_.Kg, C.ident], writes=[pT])
                        P.op("act", lambda e: e.activation(out=ts_.TT[ci].ap(), in_=pT[:, 0:384].rearrange("p (a b) -> p a b", b=128), func=AF.Copy),
                             writes=[pT, ts_.TT[ci]])
                        yield
                        for h in range(2):
                            hs = slice(h * 64, (h + 1) * 64)
                            AM = ts_.AMh[ci][h]
                            pg = C.nb()
                            P.op("pe", lambda e, hs=hs, csl=csl, c=c, pg=pg: e.matmul(pg[:, 0:256], lhsT=ts_.Bt[hs, csl], rhs=ts_.KR[hs, c, :, :],
                                                                                      start=True, stop=True), reads=[ts_.Bt, ts_.KR], writes=[pg])
                            P.op("pe", lambda e, hs=hs, csl=csl, c=c, pg=pg: e.matmul(pg[:, 256:512], lhsT=ts_.Kt[hs, csl], rhs=ts_.KR[hs, c, :, :],
                                                                                      start=True, stop=True), reads=[ts_.Kt, ts_.KR], writes=[pg])
                            N0 = ts_.NN[ci][h][0]
                            P.op("dve", lambda e, N0=N0, pg=pg, d=d: e.tensor_tensor(out=N0[:, 0, :], in0=pg[:, 0:128], in1=AMM[d][:, 0, :], op=ALU.mult),
                                 reads=[AMM[d]], writes=[pg, N0])
                            P.op("dve", lambda e, AM=AM, pg=pg, d=d: e.tensor_tensor(out=AM.ap(), in0=pg.ap().rearrange("p (a b) -> p a b", b=128),
                                                                                    in1=AMM[d].ap(), op=ALU.mult), reads=[AMM[d]], writes=[pg, AM])
                            pn = C.nb()
                            P.op("pe", lambda e, hs=hs, csl=csl, c=c, pn=pn: e.matmul(pn[:, 0:128], lhsT=ts_.KR[hs, c, 0, :], rhs=ts_.Bt[hs, csl],
                                                                                      start=True, stop=True), reads=[ts_.Bt, ts_.KR], writes=[pn])
                            N0 = ts_.NN[ci][h][0]
                            P.op("dve", lambda e, N0=N0, pn=pn, d=d: e.tensor_tensor(out=N0[:, 1, :], in0=pn[:, 0:128], in1=NTMap[d], op=ALU.mult),
                                 reads=[NTMb[d]], writes=[pn, N0])
                        def inv_gen(ci, h):
                            N0 = ts_.NN[ci][h][0]
                            iv = ts_.IV[ci][h]
                            mNb, S2, S4, mX = iv["Nb2"], iv["S2"], iv["S4"], iv["X2"]
                            Pc = [iv["Pa"], iv["Pb"]]
                            idb = C.ident
                            P.op("dve", lambda e: e.scalar_tensor_tensor(out=mNb.ap(), in0=N0.ap(), scalar=-1.0, in1=BM2[:, 0, :, :],
                                                                         op0=ALU.mult, op1=ALU.mult), reads=[N0, BM2], writes=[mNb])
                            pq = C.nb()
                            P.op("pe", lambda e: e.matmul(pq[:, 0:128], lhsT=mNb[:, 1, :], rhs=mNb[:, 0, :], start=True, stop=True), reads=[mNb], writes=[pq])
                            P.op("pe", lambda e: e.matmul(pq[:, 128:256], lhsT=mNb[:, 0, :], rhs=mNb[:, 1, :], start=True, stop=True), reads=[mNb], writes=[pq])
                            P.op("act", lambda e: e.activation(out=S2.ap(), in_=pq[:, 0:256].rearrange("p (a b) -> p a b", b=128), func=AF.Copy), writes=[pq, S2])
                            P.op("pool", lambda e: e.tensor_tensor(out=Pc[0].ap(), in0=I2.ap(), in1=mNb.ap(), op=ALU.add),
                                 reads=[mNb, I2], writes=[Pc[0]])
                            yield
                            pq2 = C.nb()
                            P.op("pe", lambda e: e.matmul(pq2[:, 0:128], lhsT=S2[:, 1, :], rhs=S2[:, 0, :], start=True, stop=True), reads=[S2], writes=[pq2])
                            P.op("pe", lambda e: e.matmul(pq2[:, 128:256], lhsT=S2[:, 0, :], rhs=S2[:, 1, :], start=True, stop=True), reads=[S2], writes=[pq2])
                            P.op("dve", lambda e: e.tensor_copy(out=S4.ap(), in_=pq2[:, 0:256].rearrange("p (a b) -> p a b", b=128)), writes=[pq2, S4])

                            def pstep(Sx, Pin, Pout):
                                pp_ = C.nb()
                                for q_ in range(2):
                                    P.op("pe", lambda e, q_=q_: e.matmul(pp_[:, q_ * 128:(q_ + 1) * 128], lhsT=Sx[:, 1 - q_, :], rhs=Pin[:, q_, :], start=True, stop=False),
                                         reads=[Sx, Pin], writes=[pp_])
                                    P.op("pe", lambda e, q_=q_: e.matmul(pp_[:, q_ * 128:(q_ + 1) * 128], lhsT=idb.ap(), rhs=Pin[:, q_, :], start=False, stop=True),
                                         reads=[idb, Pin], writes=[pp_])
                                P.op("act", lambda e: e.activation(out=Pout.ap(), in_=pp_[:, 0:256].rearrange("p (a b) -> p a b", b=128), func=AF.Copy),
                                     writes=[pp_, Pout])

                            pstep(S2, Pc[0], Pc[1])
                            yield
                            pstep(S4, Pc[1], Pc[0])
                            cur = 0
                            yield
                            for l in range(4):
                                Tc, Tn = Pc[cur], Pc[1 - cur]
                                last = (l == 3)
                                nq = 1 if last else 2
                                px = C.nb()
                                for q_ in range(nq):
                                    P.op("pe", lambda e, q_=q_: e.matmul(px[:, q_ * 128:(q_ + 1) * 128], lhsT=N0[:, 1 - q_, :], rhs=Tc[:, q_, :], start=True, stop=True),
                                         reads=[N0, Tc], writes=[px])
                                P.op("dve", lambda e: e.scalar_tensor_tensor(out=mX[:, 0:nq, :], in0=px[:, 0:nq * 128].rearrange("p (a b) -> p a b", b=128),
                                                                             scalar=-1.0, in1=BM2[:, 1 + l, 0:nq, :], op0=ALU.mult, op1=ALU.mult),
                                     reads=[BM2], writes=[px, mX])
                                yield
                                pr = C.nb()
                                for q_ in range(nq):
                                    P.op("pe", lambda e, q_=q_: e.matmul(pr[:, q_ * 128:(q_ + 1) * 128], lhsT=Tc[:, 1 - q_, :], rhs=mX[:, q_, :], start=True, stop=False),
                                         reads=[Tc, mX], writes=[pr])
                                    P.op("pe", lambda e, q_=q_: e.matmul(pr[:, q_ * 128:(q_ + 1) * 128], lhsT=idb.ap(), rhs=Tc[:, q_, :], start=False, stop=True),
                                         reads=[idb, Tc], writes=[pr])
                                if not last:
                                    P.op("act", lambda e: e.activation(out=Tn.ap(), in_=pr[:, 0:256].rearrange("p (a b) -> p a b", b=128), func=AF.Copy),
                                         writes=[pr, Tn])
                                else:
                                    P.op("act", lambda e: e.activation(out=ts_.Tb[ci][h].ap(), in_=pr[:, 0:128], func=AF.Copy), writes=[pr, ts_.Tb[ci][h]])
                                cur = 1 - cur
                                yield

                    for _ in zip(*[inv_gen(ci_, h_) for ci_ in range(len(chunks)) for h_ in range(2)]):
                        yield
                    for ci, c in enumerate(chunks):
                        csl = slice(c * 128, (c + 1) * 128)
                        gcol = c * 128 + 127 if d == 0 else c * 128
                        Th = ts_.Tb[ci]
                        So = Sb[d][p][sbi[d][p] % 2]
                        Sn = Sb[d][p][(sbi[d][p] + 1) % 2]
                        sbi[d][p] += 1
                        pz = C.nb()
                        for h in range(2):
                            hs = slice(h * 64, (h + 1) * 64)
                            P.op("pe", lambda e, hs=hs, c=c, pz=pz, So=So: e.matmul(pz[:, hs], lhsT=ts_.KR[hs, c, 0, :], rhs=So[hs, :], start=True, stop=False),
                                 reads=[ts_.KR, So], writes=[pz])
                            P.op("pe", lambda e, hs=hs, h=h, pz=pz: e.matmul(pz[:, hs], lhsT=ts_.AMh[ci][h][:, 2, :], rhs=ts_.TT[ci][:, 0, hs], start=False, stop=True),
                                 reads=[ts_.AMh[ci][h], ts_.TT[ci]], writes=[pz])
                        P.op("act", lambda e, pz=pz: e.activation(out=ts_.NZ.ap(), in_=pz[:, 0:128], func=AF.Copy, scale=-1.0), writes=[pz, ts_.NZ])
                        yield
                        pu = C.nb()
                        for h in range(2):
                            hs = slice(h * 64, (h + 1) * 64)
                            P.op("pe", lambda e, hs=hs, h=h, pu=pu: e.matmul(pu[:, hs], lhsT=Th[h].ap(), rhs=ts_.NZ[:, hs], start=True, stop=True),
                                 reads=[Th[h], ts_.NZ], writes=[pu])
                        P.op("act", lambda e, pu=pu: e.activation(out=ts_.UT.ap(), in_=pu[:, 0:128], func=AF.Copy), writes=[pu, ts_.UT])
                        yield
                        if lat:
                            py = C.nb()
                            for h in range(2):
                                hs = slice(h * 64, (h + 1) * 64)
                                P.op("pe", lambda e, hs=hs, c=c, py=py, So=So: e.matmul(py[hs, 0:128], lhsT=So[hs, :], rhs=ts_.KR[hs, c, 1, :], start=True, stop=False),
                                     reads=[So, ts_.KR], writes=[py])
                                P.op("pe", lambda e, hs=hs, h=h, py=py: e.matmul(py[hs, 0:128], lhsT=ts_.UT[:, hs], rhs=ts_.AMh[ci][h][:, 1, :], start=False, stop=False),
                                     reads=[ts_.UT, ts_.AMh[ci][h]], writes=[py])
                                P.op("pe", lambda e, hs=hs, h=h, py=py: e.matmul(py[hs, 0:128], lhsT=ts_.TT[ci][:, 0, hs], rhs=ts_.AMh[ci][h][:, 3, :], start=False, stop=True),
                                     reads=[ts_.TT[ci], ts_.AMh[ci][h]], writes=[py])
                            if d == 0:
                                P.op("act", lambda e, py=py, csl=csl: e.activation(out=ts_.ys32[:, csl], in_=py[:, 0:128], func=AF.Copy), writes=[py, ts_.ys32])
                            else:
                                P.op("dve", lambda e, py=py, csl=csl: e.tensor_tensor(out=ts_.ys32[:, csl], in0=py[:, 0:128], in1=ts_.yfl[:, csl], op=ALU.add),
                                     reads=[ts_.yfl], writes=[py, ts_.ys32])
                        pS = C.nb()
                        for h in range(2):
                            hs = slice(h * 64, (h + 1) * 64)
                            P.op("pe", lambda e, hs=hs, pS=pS: e.matmul(pS[hs, 0:64], lhsT=ts_.TT[ci][:, 1, hs], rhs=ts_.UT[:, hs], start=True, stop=False),
                                 reads=[ts_.TT[ci], ts_.UT], writes=[pS])
                            P.op("pe", lambda e, hs=hs, pS=pS: e.matmul(pS[hs, 0:64], lhsT=ts_.TT[ci][:, 2, hs], rhs=ts_.TT[ci][:, 0, hs], start=False, stop=True),
                                 reads=[ts_.TT[ci]], writes=[pS])
                        S3 = S32[d][p]
                        P.op("dve", lambda e, pS=pS, S3=S3, gcol=gcol: e.scalar_tensor_tensor(out=S3.ap(), in0=S3.ap(), scalar=ts_.gg[:, gcol:gcol + 1],
                                                                                              in1=pS[:, 0:64], op0=ALU.mult, op1=ALU.add),
                             reads=[ts_.gg], writes=[pS, S3])
                        P.op("act", lambda e, S3=S3, Sn=Sn: e.activation(out=Sn.ap(), in_=S3.ap(), func=AF.Copy), reads=[S3], writes=[Sn])
                        yield
                    if not lat:
                        return
                    if d == 0:
                        yb = Buf("yf_%d_%d" % (p, bi))
                        C.yf_bufs[(p, bi)] = yb
                        P.dma("sp", dr["yf"][p, :, t0:t0 + NB], ts_.ys32.ap(), reads=[ts_.ys32], writes=[yb])
                        return
                    P.op("act", lambda e: e.activation(out=ts_.ys16.ap(), in_=ts_.ys32.ap(), func=AF.Copy), reads=[ts_.ys32], writes=[ts_.ys16])
                    pm_ = C.nb()
                    P.op("pe", lambda e, pm_=pm_: e.matmul(pm_[:, 0:NB], lhsT=C.bmean.ap(), rhs=ts_.ys16.ap(), start=True, stop=True),
                         reads=[C.bmean, ts_.ys16], writes=[pm_])
                    P.op("dve", lambda e, pm_=pm_: e.tensor_tensor(out=ts_.yc.ap(), in0=ts_.ys32.ap(), in1=pm_[:, 0:NB], op=ALU.subtract),
                         reads=[ts_.ys32], writes=[pm_, ts_.yc])
                    yield
                    P.op("act", lambda e: e.activation(out=ts_.yc2.ap(), in_=ts_.yc.ap(), func=AF.Square), reads=[ts_.yc], writes=[ts_.yc2])
                    pv = C.nb()
                    P.op("pe", lambda e, pv=pv: e.matmul(pv[:, 0:NB], lhsT=C.bmean.ap(), rhs=ts_.yc2.ap(), start=True, stop=True),
                         reads=[C.bmean, ts_.yc2], writes=[pv])
                    P.op("act", lambda e, pv=pv: e.activation(out=ts_.rstd.ap(), in_=pv[:, 0:NB], func=AF.Sqrt, bias=C.epsx.ap()),
                         reads=[C.epsx], writes=[pv, ts_.rstd])
                    P.op("dve", lambda e: e.reciprocal(out=ts_.rstd.ap(), in_=ts_.rstd.ap()), writes=[ts_.rstd])
                    yield
                    P.op("dve", lambda e: e.tensor_tensor(out=ts_.yn.ap(), in0=ts_.yc.ap(), in1=ts_.rstd.ap(), op=ALU.mult), reads=[ts_.yc, ts_.rstd], writes=[ts_.yn])
                    P.op("dve", lambda e, p=p: e.tensor_scalar(out=ts_.yn.ap(), in0=ts_.yn.ap(), scalar1=kvec[:, 3, p:p + 1], scalar2=kvec[:, 4, p:p + 1],
                                                               op0=ALU.mult, op1=ALU.add), reads=[kvec], writes=[ts_.yn])
                    pz = C.nb()
                    P.op("pe", lambda e, p=p, pz=pz: e.matmul(pz[:, 0:NB], lhsT=LW2[32:64, 0, p * 128:(p + 1) * 128], rhs=LA16[32:64, :],
                                                              start=True, stop=True), reads=[LW2, LA16], writes=[pz])
                    P.op("act", lambda e, p=p, pz=pz: e.activation(out=ts_.a2_.ap(), in_=pz[:, 0:NB], func=AF.Sigmoid, bias=w0a0[:, 2, p:p + 1]),
                         reads=[w0a0], writes=[pz, ts_.a2_])
                    yield
                    P.op("dve", lambda e, p=p: e.tensor_scalar(out=ts_.a2_.ap(), in0=ts_.a2_.ap(), scalar1=kvec[:, 1, p:p + 1], scalar2=omka[:, p:p + 1],
                                                               op0=ALU.mult, op1=ALU.add), reads=[kvec, omka], writes=[ts_.a2_])
                    P.op("pool", lambda e: e.tensor_tensor(out=ts_.a2_.ap(), in0=ts_.a2_.ap(), in1=ts_.fac.ap(), op=ALU.add), reads=[ts_.fac], writes=[ts_.a2_])
                    P.op("pool", lambda e: e.tensor_tensor(out=ts_.kdir2.ap(), in0=ts_.k32.ap(), in1=ts_.a2_.ap(), op=ALU.mult), reads=[ts_.k32, ts_.a2_], writes=[ts_.kdir2])
                    P.op("pool", lambda e: e.tensor_tensor(out=ts_.rk.ap(), in0=ts_.r32.ap(), in1=ts_.kdir2.ap(), op=ALU.mult), reads=[ts_.r32, ts_.kdir2], writes=[ts_.rk])
                    P.op("pool", lambda e, p=p: e.tensor_scalar(out=ts_.rk16.ap(), in0=ts_.rk.ap(), scalar1=rkh[:, p:p + 1], scalar2=None, op0=ALU.mult),
                         reads=[ts_.rk, rkh], writes=[ts_.rk16])
                    pbn = C.nb()
                    P.op("pe", lambda e, pbn=pbn: e.matmul(pbn[:, 0:NB], lhsT=C.bones.ap(), rhs=ts_.rk16.ap(), start=True, stop=True),
                         reads=[C.bones, ts_.rk16], writes=[pbn])
                    P.op("dve", lambda e, pbn=pbn: e.tensor_tensor(out=ts_.bonus.ap(), in0=pbn[:, 0:NB], in1=ts_.v32.ap(), op=ALU.mult),
                         reads=[ts_.v32], writes=[pbn, ts_.bonus])
                    yield
                    P.op("pool", lambda e: e.tensor_tensor(out=ts_.bonus.ap(), in0=ts_.bonus.ap(), in1=ts_.yn.ap(), op=ALU.add), reads=[ts_.yn], writes=[ts_.bonus])
                    pgt = C.nb()
                    P.op("pe", lambda e, p=p, pgt=pgt: e.matmul(pgt[:, 0:NB], lhsT=G2W[0:96, p * 128:(p + 1) * 128], rhs=sg16.ap(), start=True, stop=True),
                         reads=[G2W, sg16], writes=[pgt])
                    P.op("dve", lambda e, pgt=pgt: e.tensor_tensor(out=ts_.o16.ap(), in0=pgt[:, 0:NB], in1=ts_.bonus.ap(), op=ALU.mult),
                         reads=[ts_.bonus], writes=[pgt, ts_.o16])
                    mb = Buf("mt_%d_%d" % (p, bi))
                    C.mt_bufs[(p, bi)] = mb
                    P.dma("sp", dr["mt"][p, :, t0:t0 + NB], ts_.o16.ap(), reads=[ts_.o16], writes=[mb])
                import itertools
                for pp0 in (0, 2):
                    for _ in itertools.zip_longest(pair_gen(pp0, TS[0]), pair_gen(pp0 + 1, TS[1])):
                        pass


def stage_attn(C):
    P, dr = C.P, C.dr
    HT, HC = C.HT, C.HC
    bk = C.banks
    NKT = (T_CTX + T_LAT) // 128
    with P.scope():
        stg = [P.sbuf("astg%d" % i, [128, 8, 256], F32) for i in range(2)]
        Wh = P.sbuf("Wh", [128, 8, 640], BF16)
        KT = P.sbuf("KT", [128, T_CTX + T_LAT], BF16)
        VT = P.sbuf("VT", [128, NKT, 128], BF16)
        QB = [P.sbuf("QB%d" % i, [128, 512], BF16) for i in range(2)]
        PT = [P.sbuf("PT%d" % i, [128, 512], BF16) for i in range(6)]
        accD = P.sbuf("accD", [128, 512], F32)
        accP = P.sbuf("accP", [128, 512], F32)
        ones32a = P.sbuf("ones32a", [128, 128], F32)
        P.op("pool", lambda e: e.memset(ones32a.ap(), 1.0), writes=[ones32a])
        rC = [P.sbuf("rC%d" % i, [128, NB], F32) for i in range(2)]
        rS = [P.sbuf("rS%d" % i, [128, NB], F32) for i in range(2)]
        t1s = [P.sbuf("rp_t1_%d" % i, [128, NB], F32) for i in range(2)]
        t2s = [P.sbuf("rp_t2_%d" % i, [128, NB], F32) for i in range(2)]
        rp_i = [0]
        rl = P.sbuf("at_rl", [128, 512], F32)
        on_ = P.sbuf("at_on", [128, 512], F32)
        dif = P.sbuf("at_dif", [128, NB], F32)
        dsq = P.sbuf("at_dsq", [128, NB], BF16)
        drs = P.sbuf("at_drs", [128, NB], F32)
        ao16 = [P.sbuf("at_o16_%d" % i, [128, NB], BF16) for i in range(2)]
        lamt = P.sbuf("lamt", [128, 4, 64], F32)
        lpr = P.sbuf("lpr", [128, 2, 64], F32)
        lsum = P.sbuf("lsum", [128, 2], F32)
        nlam = P.sbuf("nlam", [128, 1], F32)
        sbw = P.sbuf("sbw", [128, 1], F32)
        P.dma("sp", lamt.ap(), dr["lam"], writes=[lamt])
        P.dma("sp", sbw.ap(), dr["subln"], writes=[sbw])
        P.op("dve", lambda e: e.tensor_tensor(out=lpr.ap(), in0=lamt[:, 0:4:2, :], in1=lamt[:, 1:4:2, :], op=ALU.mult),
             reads=[lamt], writes=[lpr])
        P.op("dve", lambda e: e.reduce_sum(out=lsum.ap(), in_=lpr.ap(), axis=AX.X), reads=[lpr], writes=[lsum])
        P.op("act", lambda e: e.activation(out=lsum.ap(), in_=lsum.ap(), func=AF.Exp), writes=[lsum])
        P.op("dve", lambda e: e.tensor_tensor(out=nlam.ap(), in0=lsum[:, 1:2], in1=lsum[:, 0:1], op=ALU.subtract),
             reads=[lsum], writes=[nlam])
        P.op("dve", lambda e: e.tensor_scalar(out=nlam.ap(), in0=nlam.ap(), scalar1=-0.2, scalar2=None, op0=ALU.add), writes=[nlam])
        P.op("dve", lambda e: e.tensor_scalar(out=sbw.ap(), in0=sbw.ap(), scalar1=0.8, scalar2=None, op0=ALU.mult), writes=[sbw])
        for q in QB:
            P.op("pool", lambda e, q=q: e.memset(q.ap(), 0.0), writes=[q])
        C.stg_i = 0
        ri = [0]

        def load_rope(bi):
            i = ri[0] % 2
            ri[0] += 1
            P.dma("sp", rC[i].ap(), dr["ropeC"][:, bi * NB:(bi + 1) * NB], writes=[rC[i]])
            P.dma("sp", rS[i].ap(), dr["ropeS"][:, bi * NB:(bi + 1) * NB], writes=[rS[i]])
            return rC[i], rS[i]

        def proj_rope(bank, c0, bi, rc, rs, outs):
            t0 = bi * NB
            t1, t2 = t1s[rp_i[0] % 2], t2s[rp_i[0] % 2]
            rp_i[0] += 1
            for g in range(2):
                for k in range(8):
                    P.op("pe", lambda e, g=g, k=k: e.matmul(bank[:, g * NB:(g + 1) * NB], lhsT=Wh[:, k, c0 + g * 128:c0 + (g + 1) * 128],
                                                             rhs=HT[:, k, 1 + t0:1 + t0 + NB], start=(k == 0), stop=(k == 7)),
                         reads=[Wh, HT], writes=[bank])
            P.op("dve", lambda e: e.tensor_tensor(out=t1.ap(), in0=bank[:, 0:NB], in1=rc.ap(), op=ALU.mult), reads=[rc], writes=[bank, t1])
            P.op("dve", lambda e: e.tensor_tensor(out=t2.ap(), in0=bank[:, NB:2 * NB], in1=rs.ap(), op=ALU.mult), reads=[rs], writes=[bank, t2])
            for rows, oap, ob in outs:
                P.op("pool", lambda e, rows=rows, oap=oap: e.tensor_tensor(out=oap, in0=t1[rows, :], in1=t2[rows, :], op=ALU.add),
                     reads=[t1, t2], writes=[ob])

        for h in range(4):
            qc = 1696 + h * 128
            kc = 1696 + 512 + h * 128
            vc = 1696 + 1024 + h * 128
            load_w_bf16(C, (Wh, 0), dr["w_in"][:, qc:qc + 128], 128, stg)
            load_w_bf16(C, (Wh, 128), dr["w_qks"][:, h * 128:(h + 1) * 128], 128, stg)
            load_w_bf16(C, (Wh, 256), dr["w_in"][:, kc:kc + 128], 128, stg)
            load_w_bf16(C, (Wh, 384), dr["w_qks"][:, 512 + h * 128:512 + (h + 1) * 128], 128, stg)
            load_w_bf16(C, (Wh, 512), dr["w_in"][:, vc:vc + 128], 128, stg)
            pb = C.nb()
            for k in range(8):
                P.op("pe", lambda e, k=k, pb=pb: e.matmul(pb[:, 0:T_CTX], lhsT=Wh[:, k, 256:384], rhs=HC[:, k, 1:1 + T_CTX],
                                                           start=(k == 0), stop=(k == 7)), reads=[Wh, HC], writes=[pb])
            P.op("act", lambda e, pb=pb: e.activation(out=KT[:, 0:T_CTX], in_=pb[:, 0:T_CTX], func=AF.Copy), writes=[pb, KT])
            for bi in range(NLB):
                rc, rs = load_rope(bi)
                proj_rope(C.nb(), 256, bi, rc, rs, [(slice(0, 128), KT[:, T_CTX + bi * NB:T_CTX + (bi + 1) * NB], KT)])
            for j in range(NKT):
                Hs, c0 = (HC, 1 + j * 128) if j < 2 else (HT, 1 + (j - 2) * 128)
                pv = C.nb()
                for k in range(8):
                    P.op("pe", lambda e, k=k, pv=pv, Hs=Hs, c0=c0: e.matmul(pv[:, 0:128], lhsT=Hs[:, k, c0:c0 + 128], rhs=Wh[:, k, 512:640],
                                                                           start=(k == 0), stop=(k == 7)), reads=[Wh, Hs], writes=[pv])
                eng = "act" if j % 2 == 0 else "dve"
                if eng == "act":
                    P.op("act", lambda e, pv=pv, j=j: e.activation(out=VT[:, j, :], in_=pv[:, 0:128], func=AF.Copy), writes=[pv, VT])
                else:
                    P.op("dve", lambda e, pv=pv, j=j: e.tensor_copy(out=VT[:, j, :], in_=pv[:, 0:128]), writes=[pv, VT])

            def q_proj(bi):
                rc, rs = load_rope(bi)
                qb = QB[bi % 2]
                proj_rope(bk[6], 0, bi, rc, rs, [(slice(0, 64), qb[0:64, 0:NB], qb), (slice(64, 128), qb[64:128, NB:2 * NB], qb)])

            q_proj(0)
            pending = []
            for bi in range(NLB):
                qb = QB[bi % 2]
                po, pl = bk[3 + bi % 2], bk[5]

                def qk(j):
                    ps = bk[j % 3]
                    P.op("pe", lambda e, j=j, ps=ps: e.matmul(ps.ap(), lhsT=KT[:, j * 128:(j + 1) * 128], rhs=qb.ap(), start=True, stop=True),
                         reads=[KT, qb], writes=[ps])

                qk(0)
                qk(1)
                qk(2)
                if bi + 1 < NLB:
                    q_proj(bi + 1)
                while pending:
                    pending.pop(0)()
                nD = nP = 0
                for j in range(NKT):
                    ps, pt = bk[j % 3], PT[j % 6]
                    P.op("act", lambda e, ps=ps, pt=pt: e.activation(out=pt.ap(), in_=ps.ap(), func=AF.Exp, scale=0.125), writes=[ps, pt])
                    P.op("pe", lambda e, j=j, pt=pt: e.matmul(po.ap(), lhsT=VT[:, j, :], rhs=pt.ap(), start=(j == 0), stop=(j == NKT - 1)),
                         reads=[VT, pt], writes=[po])
                    if j % 3 == 2:
                        P.op("pe", lambda e, j=j, pt=pt: e.matmul(pl.ap(), lhsT=C.ones.ap(), rhs=pt.ap(), start=(j == 2), stop=False),
                             reads=[C.ones, pt], writes=[pl])
                    else:
                        acc = accD if nD % 2 == 0 else accP
                        if nD < 2:
                            P.op("dve", lambda e, pt=pt, acc=acc: e.tensor_copy(out=acc.ap(), in_=pt.ap()), reads=[pt], writes=[acc])
                        else:
                            P.op("dve", lambda e, pt=pt, acc=acc: e.tensor_tensor(out=acc.ap(), in0=acc.ap(), in1=pt.ap(), op=ALU.add), reads=[pt], writes=[acc])
                        nD += 1
                    if j + 3 < NKT:
                        qk(j + 3)
                P.op("pe", lambda e: e.matmul(pl.ap(), lhsT=ones32a.ap(), rhs=accD.ap(), start=False, stop=False), reads=[ones32a, accD], writes=[pl])
                P.op("pe", lambda e: e.matmul(pl.ap(), lhsT=ones32a.ap(), rhs=accP.ap(), start=False, stop=True), reads=[ones32a, accP], writes=[pl])
                def finalize(bi=bi, po=po, pl=pl):
                    P.op("dve", lambda e: e.reciprocal(out=rl.ap(), in_=pl.ap()), writes=[pl, rl])
                    P.op("dve", lambda e: e.tensor_tensor(out=on_.ap(), in0=po.ap(), in1=rl.ap(), op=ALU.mult), reads=[rl], writes=[po, on_])
                    P.op("dve", lambda e: e.scalar_tensor_tensor(out=dif.ap(), in0=on_[:, NB:2 * NB], scalar=nlam.ap(), in1=on_[:, 0:NB],
                                                                 op0=ALU.mult, op1=ALU.add), reads=[on_, nlam], writes=[dif])
                    P.op("pool", lambda e: e.tensor_tensor(out=dsq.ap(), in0=dif.ap(), in1=dif.ap(), op=ALU.mult), reads=[dif], writes=[dsq])
                    pm = bk[6]
                    P.op("pe", lambda e: e.matmul(pm[:, 0:NB], lhsT=C.mean128.ap(), rhs=dsq.ap(), start=True, stop=True),
                         reads=[C.mean128, dsq], writes=[pm])
                    P.op("act", lambda e: e.activation(out=drs.ap(), in_=pm[:, 0:NB], func=AF.Ln, bias=C.epss.ap()), reads=[C.epss], writes=[pm, drs])
                    P.op("act", lambda e: e.activation(out=drs.ap(), in_=drs.ap(), func=AF.Exp, scale=-0.5), writes=[drs])
                    P.op("pool", lambda e: e.tensor_tensor(out=dif.ap(), in0=dif.ap(), in1=drs.ap(), op=ALU.mult), reads=[drs], writes=[dif])
                    o16 = ao16[bi % 2]
                    P.op("pool", lambda e, o16=o16: e.tensor_scalar(out=o16.ap(), in0=dif.ap(), scalar1=sbw.ap(), scalar2=None, op0=ALU.mult),
                         reads=[dif, sbw], writes=[o16])
                    mb = Buf("mt_%d_%d" % (4 + h, bi))
                    C.mt_bufs[(4 + h, bi)] = mb
                    P.dma("sp", dr["mt"][4 + h, :, bi * NB:(bi + 1) * NB], o16.ap(), reads=[o16], writes=[mb])

                pending.append(finalize)
            while pending:
                pending.pop(0)()


def stage_out_a(C):
    P, dr = C.P, C.dr
    C.x1_bufs, C.h2_bufs = {}, {}
    with P.scope():
        C.stg_i = 0
        stg = [P.sbuf("ostg%d" % i, [128, 8, 256], F32) for i in range(2)]
        WO = P.sbuf("WO", [128, 8, 1024], BF16)
        load_w_bf16(C, (WO, 0), dr["w_out"], 1024, stg)
        MTb = [P.sbuf("MTb%d" % i, [128, 8, NB], BF16) for i in range(2)]
        xts = [P.sbuf("oxt%d" % i, [128, 8, NB], F32) for i in range(2)]
        y32 = P.sbuf("oy32", [128, 8, NB], F32)
        x1 = [P.sbuf("ox1_%d" % i, [128, 8, NB], F32) for i in range(2)]
        h2 = [P.sbuf("oh2_%d" % i, [128, 8, NB], BF16) for i in range(2)]
        sq = P.sbuf("o_sq", [128, 8, NB], BF16)
        rs = P.sbuf("o_rs", [128, NB], F32)
        tmp = P.sbuf("o_tmp", [128, 8, NB], F32)
        def ld_a(bi):
            t0 = bi * NB
            P.dma("sp", MTb[bi % 2].ap(), dr["mt"][:, :, t0:t0 + NB].rearrange("k p t -> p k t"),
                  reads=[C.mt_bufs[(k, bi)] for k in range(8)], writes=[MTb[bi % 2]])
            P.dma("sp", xts[bi % 2].ap(), dr["xT"][:, t0:t0 + NB].rearrange("(k p) t -> p k t", p=128), writes=[xts[bi % 2]])

        for bi in range(NLB):
            t0 = bi * NB
            mtb, xt, x1b, h2b = MTb[bi % 2], xts[bi % 2], x1[bi % 2], h2[bi % 2]
            if bi == 0:
                ld_a(0)
            if bi + 1 < NLB:
                ld_a(bi + 1)
            for j in range(8):
                py = C.nb()
                for k in range(8):
                    P.op("pe", lambda e, j=j, k=k, py=py: e.matmul(py[:, 0:NB], lhsT=WO[:, k, j * 128:(j + 1) * 128], rhs=mtb[:, k, :],
                                                                   start=(k == 0), stop=(k == 7)), reads=[WO, mtb], writes=[py])
                if j % 2 == 0:
                    P.op("act", lambda e, j=j, py=py: e.activation(out=y32[:, j, :], in_=py[:, 0:NB], func=AF.Copy), writes=[py, y32])
                else:
                    P.op("dve", lambda e, j=j, py=py: e.tensor_copy(out=y32[:, j, :], in_=py[:, 0:NB]), writes=[py, y32])
            P.op("act", lambda e: e.activation(out=sq.ap(), in_=y32.ap(), func=AF.Square), reads=[y32], writes=[sq])
            pb = C.nb()
            for k in range(8):
                P.op("pe", lambda e, k=k, pb=pb: e.matmul(pb[:, 0:NB], lhsT=C.ones.ap(), rhs=sq[:, k, :], start=(k == 0), stop=(k == 7)),
                     reads=[sq, C.ones], writes=[pb])
            P.op("act", lambda e, pb=pb: e.activation(out=rs.ap(), in_=pb[:, 0:NB], func=AF.Sqrt, scale=1.0 / D_MODEL, bias=C.eps6.ap()),
                 reads=[C.eps6], writes=[pb, rs])
            P.op("dve", lambda e: e.reciprocal(out=rs.ap(), in_=rs.ap()), writes=[rs])
            for j in range(8):
                P.op("dve", lambda e, j=j: e.scalar_tensor_tensor(out=tmp[:, j, :], in0=y32[:, j, :], scalar=C.G1[:, j:j + 1], in1=rs.ap(),
                                                                  op0=ALU.mult, op1=ALU.mult), reads=[y32, rs, C.G1], writes=[tmp])
            P.op("pool", lambda e, x1b=x1b, xt=xt: e.tensor_tensor(out=x1b.ap(), in0=tmp.ap(), in1=xt.ap(), op=ALU.add),
                 reads=[tmp, xt], writes=[x1b])
            xb = Buf("x1s_%d" % bi)
            C.x1_bufs[bi] = xb
            P.dma("sp", dr["x1s"][:, :, t0:t0 + NB].rearrange("k p t -> p k t"), x1b.ap(), reads=[x1b], writes=[xb])
            norm_block(C, x1b, NB, lambda k: C.A2[:, k:k + 1], lambda k: C.modT[:, 24 + k, 0:1],
                       lambda k, h2b=h2b: h2b[:, k, :], [C.A2, C.modT], h2b, (sq, rs, tmp))
            hb = Buf("h2s_%d" % bi)
            C.h2_bufs[bi] = hb
            P.dma("sp", dr["h2s"][:, :, t0:t0 + NB].rearrange("k p t -> p k t"), h2b.ap(), reads=[h2b], writes=[hb])


def stage_ffn(C):
    P, dr = C.P, C.dr
    NJ = D_FF // 128
    with P.scope():
        C.stg_i = 0
        WU = P.sbuf("WU", [128, 8, 2 * D_FF], BF16)
        WD = P.sbuf("WD", [128, NJ, 1024], BF16)
        with P.scope():
            stg = [P.sbuf("fstg%d" % i, [128, 8, 256], F32) for i in range(2)]
            load_w_bf16(C, (WU, 0), dr["w_up"], 2 * D_FF, stg)
            for j in range(NJ):
                s_ = stg[C.stg_i % 2]
                C.stg_i += 1
                sv = s_.ap().rearrange("p a b -> p (a b)")[:, 0:1024]
                P.dma("sp", sv, dr["w_down"][j * 128:(j + 1) * 128, :], writes=[s_])
                if j % 2 == 0:
                    P.op("act", lambda e, j=j, sv=sv: e.activation(out=WD[:, j, :], in_=sv, func=AF.Copy), reads=[s_], writes=[WD])
                else:
                    P.op("dve", lambda e, j=j, sv=sv: e.tensor_copy(out=WD[:, j, :], in_=sv), reads=[s_], writes=[WD])
        FCW = P.sbuf("FCW", [128, 2 * NJ, 3], F32)
        FB = P.sbuf("FB", [128, 2 * NJ], F32)
        C.cwb = FCW
        P.dma("sp", FCW.ap(), dr["fconvT"].rearrange("(c p) j -> p c j", p=128), writes=[FCW])
        P.dma("sp", FB.ap(), dr["fbias"], writes=[FB])
        h2t = [P.sbuf("fh2_%d" % i, [128, 8, NB + 2], BF16) for i in range(2)]
        x1t = [P.sbuf("fx1_%d" % i, [128, 8, NB], F32) for i in range(2)]
        uv = [P.sbuf("fuv%d" % i, [128, NB], F32) for i in range(3)]
        ug = [P.sbuf("fug%d" % i, [128, NB], F32) for i in range(3)]
        sg = [P.sbuf("fsg%d" % i, [128, NB], F32) for i in range(3)]
        act16 = P.sbuf("fact", [128, NJ, NB], BF16)
        f32t = P.sbuf("ff32", [128, 8, NB], F32)
        sq = P.sbuf("f_sq", [128, 8, NB], BF16)
        rs = P.sbuf("f_rs", [128, NB], F32)
        def load_block(bi):
            t0 = bi * NB
            hb, xb = h2t[bi % 2], x1t[bi % 2]
            lo = max(t0 - 1, 0)
            hi = min(t0 + NB + 1, T_LAT)
            rd = [C.h2_bufs[b] for b in (bi - 1, bi, bi + 1) if 0 <= b < NLB]
            if bi == 0:
                P.op("pool", lambda e, hb=hb: e.memset(hb[:, :, 0:1], 0.0), writes=[hb])
            if bi == NLB - 1:
                P.op("pool", lambda e, hb=hb: e.memset(hb[:, :, NB + 1:NB + 2], 0.0), writes=[hb])
            d0 = lo - (t0 - 1)
            P.dma("sp", hb[:, :, d0:d0 + (hi - lo)], dr["h2s"][:, :, lo:hi].rearrange("k p t -> p k t"), reads=rd, writes=[hb])
            P.dma("sp", xb.ap(), dr["x1s"][:, :, t0:t0 + NB].rearrange("k p t -> p k t"), reads=[C.x1_bufs[bi]], writes=[xb])

        def gate_mul(j, u, g, s2):
            P.op("act", lambda e: e.activation(out=s2.ap(), in_=g.ap(), func=AF.Silu), reads=[g], writes=[s2])
            P.op("pool", lambda e: e.tensor_tensor(out=act16[:, j, :], in0=u.ap(), in1=s2.ap(), op=ALU.mult),
                 reads=[u, s2], writes=[act16])

        load_block(0)
        for bi in range(NLB):
            t0 = bi * NB
            hb, xb = h2t[bi % 2], x1t[bi % 2]
            if bi + 1 < NLB:
                load_block(bi + 1)
            prev = None
            for j in range(NJ):
                u, g, s2 = uv[j % 3], ug[j % 3], sg[j % 3]
                proj_conv(C, hb, 0, WU, j * 128, 128, FCW[:, j, :], u.ap(), u, bias=FB[:, j:j + 1], bias_buf=FB)
                proj_conv(C, hb, 0, WU, D_FF + j * 128, 128, FCW[:, NJ + j, :], g.ap(), g, bias=FB[:, NJ + j:NJ + j + 1], bias_buf=FB)
                if prev is not None:
                    gate_mul(*prev)
                prev = (j, u, g, s2)
            gate_mul(*prev)
            for i in range(8):
                pf = C.nb()
                for j in range(NJ):
                    P.op("pe", lambda e, i=i, j=j, pf=pf: e.matmul(pf[:, 0:NB], lhsT=WD[:, j, i * 128:(i + 1) * 128], rhs=act16[:, j, :],
                                                                   start=(j == 0), stop=(j == NJ - 1)), reads=[WD, act16], writes=[pf])
                if i % 2 == 0:
                    P.op("act", lambda e, i=i, pf=pf: e.activation(out=f32t[:, i, :], in_=pf[:, 0:NB], func=AF.Copy), writes=[pf, f32t])
                else:
                    P.op("dve", lambda e, i=i, pf=pf: e.tensor_copy(out=f32t[:, i, :], in_=pf[:, 0:NB]), writes=[pf, f32t])
            P.op("act", lambda e: e.activation(out=sq.ap(), in_=f32t.ap(), func=AF.Square), reads=[f32t], writes=[sq])
            pb = C.nb()
            for k in range(8):
                P.op("pe", lambda e, k=k, pb=pb: e.matmul(pb[:, 0:NB], lhsT=C.ones.ap(), rhs=sq[:, k, :], start=(k == 0), stop=(k == 7)),
                     reads=[sq, C.ones], writes=[pb])
            P.op("act", lambda e, pb=pb: e.activation(out=rs.ap(), in_=pb[:, 0:NB], func=AF.Sqrt, scale=1.0 / D_MODEL, bias=C.eps6.ap()),
                 reads=[C.eps6], writes=[pb, rs])
            P.op("dve", lambda e: e.reciprocal(out=rs.ap(), in_=rs.ap()), writes=[rs])
            for i in range(8):
                P.op("dve", lambda e, i=i: e.scalar_tensor_tensor(out=f32t[:, i, :], in0=f32t[:, i, :], scalar=C.G2[:, i:i + 1], in1=rs.ap(),
                                                                  op0=ALU.mult, op1=ALU.mult), reads=[rs, C.G2], writes=[f32t])
            P.op("pool", lambda e, xb=xb: e.tensor_tensor(out=xb.ap(), in0=f32t.ap(), in1=xb.ap(), op=ALU.add),
                 reads=[f32t], writes=[xb])
            P.dma("sp", dr["outT"][:, t0:t0 + NB].rearrange("(k p) t -> p k t", p=128), xb.ap(), reads=[xb], final=True)
```
